# Optimizing a Trainium2 kernel written in Bass

```python
import math
import functools
import jax
import jax.numpy as jnp
from jax import lax
import numpy as np

D_MODEL = 1024
BATCH = 32
SEQ = 256
DEPTH = 2
DEC_BATCH = 2
DEC_SEQ = 1024
PAST_LEN = 512

GRID_W = 64
HEAD_DIM = 64
MIX_WIDTH = D_MODEL
GROUP_WIDTH = MIX_WIDTH // 4
H_SWA = GROUP_WIDTH // HEAD_DIM
KV_SWA = H_SWA // 2
WINDOW = 128
BLOCK = 128
H_GQA = GROUP_WIDTH // HEAD_DIM
KV_GQA = H_GQA // 2
HY_CH = GROUP_WIDTH
HY_ORDER = 2
HY_BANDS = 16
HY_EMB = 2 * HY_BANDS + 1
HY_FH = 64
HY_TARGET = 1e-2
HY_FAST = 0.3
HY_SLOW = 1.5
H_GLA = 4
DV_GLA = GROUP_WIDTH // H_GLA
DK_GLA = DV_GLA // 2
GLA_RANK = 16
GLA_CHUNK = 64
GLA_NORM = 16.0
D_FF = 2816
N_MOD = 9
ROPE_THETA = 10000.0
EPS = 1e-6
IN_SIZES = (H_SWA * HEAD_DIM, KV_SWA * HEAD_DIM, KV_SWA * HEAD_DIM,
            3 * HY_CH,
            H_GQA * HEAD_DIM, KV_GQA * HEAD_DIM, KV_GQA * HEAD_DIM,
            H_GLA * DK_GLA, H_GLA * DK_GLA, H_GLA * DV_GLA, 2 * GLA_RANK, H_GLA * DV_GLA)
IN_WIDTH = sum(IN_SIZES)

kernel_name = 'hymba_style_diffusion_prefix_step'


def rms_norm(x, g):
    xf = x.astype(jnp.float32)
    y = xf * lax.rsqrt(jnp.mean(jnp.square(xf), axis=-1, keepdims=True) + EPS)
    return (y * g.astype(jnp.float32)).astype(x.dtype)


def modulation(cond, w_mod, b_mod):
    m = jax.nn.silu(cond) @ w_mod + b_mod
    return jnp.split(m[..., None, :], N_MOD, axis=-1)


def modulate(x, shift, scale):
    return x * (1 + scale) + shift


def swiglu(x, w_in, w_out):
    a, b = jnp.split(x @ w_in, 2, axis=-1)
    return (jax.nn.silu(a) * b) @ w_out


def grid_rope(L):
    rows = L // GRID_W
    r = jnp.repeat(jnp.arange(rows, dtype=jnp.float32), GRID_W)
    col = jnp.tile(jnp.arange(GRID_W, dtype=jnp.float32), rows)
    nf = HEAD_DIM // 4
    inv = ROPE_THETA ** (-jnp.arange(nf, dtype=jnp.float32) / nf)
    ang = jnp.concatenate([r[:, None] * inv, col[:, None] * inv], axis=-1)
    return jnp.cos(ang), jnp.sin(ang)


def apply_rope(x, cos, sin):
    xf = x.astype(jnp.float32)
    x1, x2 = xf[..., :HEAD_DIM // 2], xf[..., HEAD_DIM // 2:]
    c, s = cos[:, None, :], sin[:, None, :]
    return jnp.concatenate([x1 * c - x2 * s, x2 * c + x1 * s], axis=-1).astype(x.dtype)


def dense_attention(q, k, v, sink):
    B, Lq, H, hd = q.shape
    KV = k.shape[2]
    G = H // KV
    nb = Lq // BLOCK
    kf = k.astype(jnp.float32)
    vf = v.astype(jnp.float32)
    qb = q.astype(jnp.float32).reshape(B, nb, BLOCK, KV, G, hd).transpose(1, 0, 2, 3, 4, 5) * (hd ** -0.5)

    def block(qblk):
        s = jnp.einsum('bqkgd,bskd->bkgqs', qblk, kf)
        if sink is not None:
            sk = jnp.broadcast_to(sink.astype(jnp.float32).reshape(1, KV, G, 1, 1), s.shape[:-1] + (1,))
            s = jnp.concatenate([s, sk], axis=-1)
        p = jax.nn.softmax(s, axis=-1)
        if sink is not None:
            p = p[..., :-1]
        return jnp.einsum('bkgqs,bskd->bqkgd', p, vf)

    o = lax.map(block, qb)
    return o.transpose(1, 0, 2, 3, 4, 5).reshape(B, Lq, H, hd).astype(q.dtype)


def window_attention(q, k, v, k_ctx, v_ctx, sink):
    B, L, H, hd = q.shape
    KV = k.shape[2]
    G = H // KV
    nb = L // BLOCK
    f32 = jnp.float32
    qb = q.astype(f32).reshape(B, nb, BLOCK, KV, G, hd) * (hd ** -0.5)

    def band(x):
        xb = x.astype(f32).reshape(B, nb, BLOCK, KV, hd)
        z = jnp.zeros_like(xb[:, :1])
        prev = jnp.concatenate([z, xb[:, :-1]], axis=1)
        nxt = jnp.concatenate([xb[:, 1:], z], axis=1)
        return jnp.concatenate([prev, xb, nxt], axis=2)

    kb, vb = band(k), band(v)
    qpos = jnp.arange(nb)[:, None] * BLOCK + jnp.arange(BLOCK)[None, :]
    kpos = (jnp.arange(nb)[:, None] - 1) * BLOCK + jnp.arange(3 * BLOCK)[None, :]
    valid = ((jnp.abs(qpos[:, :, None] - kpos[:, None, :]) <= WINDOW)
             & (kpos[:, None, :] >= 0) & (kpos[:, None, :] < L))
    s_loc = jnp.where(valid, jnp.einsum('bnqkgd,bnskd->bkgnqs', qb, kb), -1e30)
    s_ctx = jnp.einsum('bnqkgd,bckd->bkgnqc', qb, k_ctx.astype(f32))
    s_sink = jnp.broadcast_to(sink.astype(f32).reshape(1, KV, G, 1, 1, 1), s_loc.shape[:-1] + (1,))
    p = jax.nn.softmax(jnp.concatenate([s_loc, s_ctx, s_sink], axis=-1), axis=-1)
    nloc = 3 * BLOCK
    nctx = k_ctx.shape[1]
    o = (jnp.einsum('bkgnqs,bnskd->bnqkgd', p[..., :nloc], vb)
         + jnp.einsum('bkgnqc,bckd->bnqkgd', p[..., nloc:nloc + nctx], v_ctx.astype(f32)))
    return o.reshape(B, L, H, hd).astype(q.dtype)


def short_conv(u, w, b):
    up = jnp.pad(u, ((0, 0), (1, 1), (0, 0)))
    return up[:, :-2] * w[0] + up[:, 1:-1] * w[1] + up[:, 2:] * w[2] + b


def hyena_kernels(L, lp):
    f32 = jnp.float32
    t = jnp.linspace(0.0, 1.0, L, dtype=f32)[:, None]
    w = (2.0 * math.pi / L) * jnp.arange(L, dtype=f32)[:, None]
    bands = jnp.linspace(1e-4, HY_BANDS - 1, HY_BANDS, dtype=f32)[None, :]
    z = jnp.concatenate([t, jnp.cos(bands * w), -jnp.sin(bands * w)], axis=-1)
    freq = lp['hy_freq'].astype(f32)
    h = jnp.sin(freq[0] * (z @ lp['hy_w1'].astype(f32) + lp['hy_b1'].astype(f32)))
    h = jnp.sin(freq[1] * (h @ lp['hy_w2'].astype(f32) + lp['hy_b2'].astype(f32)))
    h = (h @ lp['hy_w3'].astype(f32)).reshape(L, 2, HY_ORDER, HY_CH)
    deltas = jnp.linspace(math.log(HY_TARGET) / HY_SLOW, math.log(HY_TARGET) / HY_FAST, HY_CH, dtype=f32)
    h = h * jnp.exp(-t * jnp.abs(deltas))[:, None, None, :]
    k2 = jnp.concatenate([h[:, 0], jnp.zeros((1, HY_ORDER, HY_CH), f32), h[:0:-1, 1]], axis=0)
    return jnp.fft.rfft(k2, axis=0)


def long_conv(u, kf, bias):
    L = u.shape[1]
    U = jnp.fft.rfft(u, n=2 * L, axis=1)
    y = jnp.fft.irfft(U * kf[None], n=2 * L, axis=1)[:, :L]
    return y + u * bias


def hyena_mixer(z, lp):
    L = z.shape[1]
    z = short_conv(z, lp['hy_conv_w'], lp['hy_conv_b']).astype(jnp.float32)
    v, x1, x2 = jnp.split(z, 3, axis=-1)
    kf = hyena_kernels(L, lp)
    bias = lp['hy_bias'].astype(jnp.float32)
    y = x1 * long_conv(v, kf[:, 0], bias[0])
    y = x2 * long_conv(y, kf[:, 1], bias[1])
    return y


def gla_scan(q, k, v, g, s0):
    B, L, H, DK = q.shape
    DV = v.shape[-1]
    C = GLA_CHUNK
    n = L // C
    q, k, v, g = (x.astype(jnp.float32).reshape(B, n, C, H, x.shape[-1]) for x in (q, k, v, g))
    b = jnp.cumsum(g, axis=2)
    b_last = b[:, :, -1:]
    q_t = q * jnp.exp(b) * (DK ** -0.5)
    k_t = k * jnp.exp(-b)
    k_e = k * jnp.exp(b_last - b)
    mask = jnp.tril(jnp.ones((C, C), dtype=bool))
    att = jnp.where(mask, jnp.einsum('bnihd,bnjhd->bnhij', q_t, k_t), 0.0)
    o_intra = jnp.einsum('bnhij,bnjhv->bnihv', att, v)
    contrib = jnp.einsum('bnjhd,bnjhv->nbhdv', k_e, v)
    decay = jnp.exp(b_last[:, :, 0]).transpose(1, 0, 2, 3)

    def step(S, inp):
        dec, add = inp
        return dec[..., None] * S + add, S

    s_final, s_prev = lax.scan(step, s0.astype(jnp.float32), (decay, contrib))
    o_inter = jnp.einsum('bnihd,nbhdv->bnihv', q_t, s_prev)
    return (o_intra + o_inter).reshape(B, L, H, DV), s_final


def gla_bidir(q, k, v, gl, lp, s0_fwd, s0_bwd):
    B, L = q.shape[:2]
    logits = jnp.einsum('blzr,zrd->blzd', gl, lp['gla_gate_w']) + lp['gla_gate_b']
    g = (jax.nn.log_sigmoid(logits.astype(jnp.float32)) / GLA_NORM).reshape(B, L, 2, H_GLA, DK_GLA)
    o_f, s_f = gla_scan(q, k, v, g[:, :, 0], s0_fwd)
    flip = lambda x: x[:, ::-1]
    o_b, s_b = gla_scan(flip(q), flip(k), flip(v), flip(g[:, :, 1]), s0_bwd)
    return o_f + flip(o_b), jnp.stack([s_f, s_b], axis=1)


def project(u, lp):
    B, L, _ = u.shape
    idx = np.cumsum(IN_SIZES)[:-1].tolist()
    qa, ka, va, hy, qc, kc, vc, qd, kd, vd, gd, rd = jnp.split(u @ lp['w_in'], idx, axis=-1)
    heads = lambda t, h, d: t.reshape(B, L, h, d)
    qkg = lp['qk_norm_g']
    return (heads(qa, H_SWA, HEAD_DIM), heads(ka, KV_SWA, HEAD_DIM), heads(va, KV_SWA, HEAD_DIM), hy,
            rms_norm(heads(qc, H_GQA, HEAD_DIM), qkg[0]), rms_norm(heads(kc, KV_GQA, HEAD_DIM), qkg[1]),
            heads(vc, KV_GQA, HEAD_DIM),
            heads(qd, H_GLA, DK_GLA), heads(kd, H_GLA, DK_GLA), heads(vd, H_GLA, DV_GLA),
            gd.reshape(B, L, 2, GLA_RANK), rd)


def merge_groups(oa, ob, oc, od, rd, lp):
    B, L = oa.shape[:2]
    ga, gb, gc, gd = jnp.split(lp['mix_g'], 4)
    dt = rd.dtype
    ya = rms_norm(oa.reshape(B, L, GROUP_WIDTH), ga)
    yb = rms_norm(ob.astype(dt), gb)
    yc = rms_norm(oc.reshape(B, L, GROUP_WIDTH), gc)
    yd = rms_norm(od.astype(dt), gd.reshape(H_GLA, DV_GLA)).reshape(B, L, GROUP_WIDTH) * jax.nn.silu(rd)
    return jnp.concatenate([ya, yb.astype(dt), yc, yd], axis=-1) @ lp['w_out']


def mixer_context(u, lp):
    B = u.shape[0]
    qa, ka, va, hy, qc, kc, vc, qd, kd, vd, gd, rd = project(u, lp)
    oa = dense_attention(qa, ka, va, lp['swa_sink'])
    ob = hyena_mixer(hy, lp)
    oc = dense_attention(qc, kc, vc, None)
    s0 = jnp.zeros((B, H_GLA, DK_GLA, DV_GLA), jnp.float32)
    od, st = gla_bidir(qd, kd, vd, gd, lp, s0, s0)
    return merge_groups(oa, ob, oc, od, rd, lp), (ka, va, kc, vc, st.astype(u.dtype))


def mixer_latent(u, lp, ck_a, cv_a, ck_c, cv_c, st):
    L = u.shape[1]
    cos, sin = grid_rope(L)
    qa, ka, va, hy, qc, kc, vc, qd, kd, vd, gd, rd = project(u, lp)
    oa = window_attention(apply_rope(qa, cos, sin), apply_rope(ka, cos, sin), va, ck_a, cv_a, lp['swa_sink'])
    ob = hyena_mixer(hy, lp)
    k_all = jnp.concatenate([apply_rope(kc, cos, sin), ck_c.astype(kc.dtype)], axis=1)
    v_all = jnp.concatenate([vc, cv_c.astype(vc.dtype)], axis=1)
    oc = dense_attention(apply_rope(qc, cos, sin), k_all, v_all, None)
    od, _ = gla_bidir(qd, kd, vd, gd, lp, st[:, 0], st[:, 1])
    return merge_groups(oa, ob, oc, od, rd, lp), ()


def trunk_layer(x, cond, lp, mixer):
    sh1, sc1, g1, sh2, sc2, g2, sh3, sc3, g3 = modulation(cond, lp['w_mod'], lp['b_mod'])
    ng = lp['norm_g']
    h = x + 0.5 * g1 * swiglu(modulate(rms_norm(x, ng[0]), sh1, sc1), lp['ffn_w_in'][0], lp['ffn_w_out'][0])
    mix, ctx_tensors = mixer(modulate(rms_norm(h, ng[1]), sh2, sc2))
    h = h + g2 * mix
    h = h + 0.5 * g3 * swiglu(modulate(rms_norm(h, ng[2]), sh3, sc3), lp['ffn_w_in'][1], lp['ffn_w_out'][1])
    return h, ctx_tensors


def setup_inputs(seed: int = 0) -> dict:
    key = jax.random.key(seed)
    ks = jax.random.split(key, 32)
    f32 = jnp.float32
    nrm = lambda i, shape, scale: jax.random.normal(ks[i], shape, f32) * scale
    D = D_MODEL
    return {
        'x_prompt': nrm(0, (BATCH, SEQ, D), 1.0),
        'x_sample': nrm(1, (DEC_BATCH, DEC_SEQ, D), 1.0),
        'cache_swa_k': nrm(2, (DEC_BATCH, DEPTH, PAST_LEN, KV_SWA, HEAD_DIM), 1.0),
        'cache_swa_v': nrm(3, (DEC_BATCH, DEPTH, PAST_LEN, KV_SWA, HEAD_DIM), 1.0),
        'cache_gqa_k': nrm(4, (DEC_BATCH, DEPTH, PAST_LEN, KV_GQA, HEAD_DIM), 1.0),
        'cache_gqa_v': nrm(5, (DEC_BATCH, DEPTH, PAST_LEN, KV_GQA, HEAD_DIM), 1.0),
        'state_gla': nrm(6, (DEC_BATCH, DEPTH, 2, H_GLA, DK_GLA, DV_GLA), 0.3),
        'c': nrm(7, (DEC_BATCH, D), 1.0),
        'c_ctx': nrm(8, (D,), 1.0),
        'w_mod': nrm(9, (DEPTH, D, N_MOD * D), D ** -0.5),
        'b_mod': nrm(10, (DEPTH, N_MOD * D), 0.02),
        'norm_g': 1.0 + nrm(11, (DEPTH, 3, D), 0.02),
        'ffn_w_in': nrm(12, (DEPTH, 2, D, 2 * D_FF), D ** -0.5),
        'ffn_w_out': nrm(13, (DEPTH, 2, D_FF, D), D_FF ** -0.5),
        'w_in': nrm(14, (DEPTH, D, IN_WIDTH), D ** -0.5),
        'w_out': nrm(15, (DEPTH, MIX_WIDTH, D), MIX_WIDTH ** -0.5),
        'mix_g': 1.0 + nrm(16, (DEPTH, MIX_WIDTH), 0.02),
        'swa_sink': nrm(17, (DEPTH, H_SWA), 0.5),
        'qk_norm_g': 1.0 + nrm(18, (DEPTH, 2, HEAD_DIM), 0.02),
        'hy_conv_w': nrm(19, (DEPTH, 3, 3 * HY_CH), 3 ** -0.5),
        'hy_conv_b': nrm(20, (DEPTH, 3 * HY_CH), 0.02),
        'hy_w1': nrm(21, (DEPTH, HY_EMB, HY_FH), HY_EMB ** -0.5),
        'hy_b1': nrm(22, (DEPTH, HY_FH), 0.02),
        'hy_w2': nrm(23, (DEPTH, HY_FH, HY_FH), HY_FH ** -0.5),
        'hy_b2': nrm(24, (DEPTH, HY_FH), 0.02),
        'hy_w3': nrm(25, (DEPTH, HY_FH, 2 * HY_ORDER * HY_CH), HY_FH ** -0.5),
        'hy_freq': 1.0 + nrm(26, (DEPTH, 2, HY_FH), 0.02),
        'hy_bias': nrm(27, (DEPTH, HY_ORDER, HY_CH), 0.1),
        'gla_gate_w': nrm(28, (DEPTH, 2, GLA_RANK, H_GLA * DK_GLA), GLA_RANK ** -0.5),
        'gla_gate_b': nrm(29, (DEPTH, 2, H_GLA * DK_GLA), 0.02),
        'final_g': 1.0 + nrm(30, (D,), 0.02),
    }


def reference(x_prompt, x_sample, cache_swa_k, cache_swa_v, cache_gqa_k, cache_gqa_v, state_gla, c,
              c_ctx, w_mod, b_mod, norm_g, ffn_w_in, ffn_w_out, w_in, w_out, mix_g, swa_sink, qk_norm_g,
              hy_conv_w, hy_conv_b, hy_w1, hy_b1, hy_w2, hy_b2, hy_w3, hy_freq, hy_bias,
              gla_gate_w, gla_gate_b, final_g):
    def layer_params(l):
        return {'w_mod': w_mod[l], 'b_mod': b_mod[l], 'norm_g': norm_g[l], 'ffn_w_in': ffn_w_in[l],
                'ffn_w_out': ffn_w_out[l], 'w_in': w_in[l], 'w_out': w_out[l], 'mix_g': mix_g[l],
                'swa_sink': swa_sink[l], 'qk_norm_g': qk_norm_g[l], 'hy_conv_w': hy_conv_w[l],
                'hy_conv_b': hy_conv_b[l], 'hy_w1': hy_w1[l], 'hy_b1': hy_b1[l], 'hy_w2': hy_w2[l],
                'hy_b2': hy_b2[l], 'hy_w3': hy_w3[l], 'hy_freq': hy_freq[l], 'hy_bias': hy_bias[l],
                'gla_gate_w': gla_gate_w[l], 'gla_gate_b': gla_gate_b[l]}

    hp = x_prompt
    ks_a, vs_a, ks_c, vs_c, sts = [], [], [], [], []
    for l in range(DEPTH):
        lp = layer_params(l)
        hp, (ka, va, kc, vc, st) = trunk_layer(hp, c_ctx, lp, functools.partial(mixer_context, lp=lp))
        ks_a.append(ka)
        vs_a.append(va)
        ks_c.append(kc)
        vs_c.append(vc)
        sts.append(st)
    y_prompt = rms_norm(hp, final_g)
    new_swa_k = jnp.stack(ks_a, axis=1)
    new_swa_v = jnp.stack(vs_a, axis=1)
    new_gqa_k = jnp.stack(ks_c, axis=1)
    new_gqa_v = jnp.stack(vs_c, axis=1)
    new_state_gla = jnp.stack(sts, axis=1)

    hs = x_sample
    for l in range(DEPTH):
        lp = layer_params(l)
        mixer = functools.partial(mixer_latent, lp=lp, ck_a=cache_swa_k[:, l], cv_a=cache_swa_v[:, l],
                                  ck_c=cache_gqa_k[:, l], cv_c=cache_gqa_v[:, l], st=state_gla[:, l])
        hs, _ = trunk_layer(hs, c, lp, mixer)
    y_sample = rms_norm(hs, final_g)
    return (y_prompt, y_sample, new_swa_k, new_swa_v, new_gqa_k, new_gqa_v, new_state_gla)
```

```python
import math
from contextlib import ExitStack
import numpy as np
import ml_dtypes
import concourse.bass as bass
import concourse.mybir as mybir
from concourse.bass_utils import run_bass_kernel_spmd

F32 = mybir.dt.float32
BF16 = mybir.dt.bfloat16
AF = mybir.ActivationFunctionType
ALU = mybir.AluOpType
AX = mybir.AxisListType
NPBF16 = ml_dtypes.bfloat16

STAGES = {"ffn1": True, "mixer": True, "ffn2": True, "A": True, "C": True, "B": True, "D": True}
NLAYERS = 2
DEBUG = False
PROFILE_SCOPES = False
PROFILE_ENGINE = "pe"
GLA_STEPS = 99
GLA_PART = 9
GLA_VAR = 0
ONE_CORE = False
LAST = {}

ENGS = ("pe", "act", "dve", "pool", "sp")
DMA_NSEM = {"sp": 12, "act": 4, "pool": 12}


class Buf:
    __slots__ = ("name", "w", "r")

    def __init__(self, name=""):
        self.name = name
        self.w = None
        self.r = []


class Op:
    __slots__ = ("eng", "idx", "fn", "deps", "signal", "dma", "dsem", "dval", "cnt", "phase")

    def __init__(self, eng, idx, fn, dma):
        self.eng = eng
        self.idx = idx
        self.fn = fn
        self.deps = []
        self.signal = False
        self.dma = dma
        self.dsem = None
        self.dval = 0
        self.cnt = 0


class Sched:
    def __init__(self, nc, same_engine_sync=True):
        self.nc = nc
        self.ops = {e: [] for e in ENGS}
        self.ndma = {e: 0 for e in DMA_NSEM}
        self.dma_ops = {e: [] for e in DMA_NSEM}
        self.same = same_engine_sync
        self.out_dmas = []
        self.phase = None

    def op(self, eng, fn, reads=(), writes=(), dma=False, is_out=False):
        lst = self.ops[eng]
        o = Op(eng, len(lst), fn, dma)
        o.phase = self.phase
        deps = {}
        for b in reads:
            if b.w is not None:
                deps[id(b.w)] = b.w
        for b in writes:
            if b.w is not None:
                deps[id(b.w)] = b.w
            for r in b.r:
                deps[id(r)] = r
        if dma:
            j = self.ndma[eng]
            k = DMA_NSEM[eng]
            o.dsem = (eng, j % k)
            o.dval = 16 * (j // k + 1)
            if j >= k:
                p = self.dma_ops[eng][j - k]
                deps[id(p)] = p
            self.ndma[eng] += 1
            self.dma_ops[eng].append(o)
            if is_out:
                self.out_dmas.append(o)
        best = {}
        for d in deps.values():
            if d is o:
                continue
            if d.dma:
                o.deps.append(d)
                continue
            if d.eng == eng and (eng in ("pe", "sp") or not self.same):
                continue
            if d.eng not in best or best[d.eng].idx < d.idx:
                best[d.eng] = d
        for d in best.values():
            o.deps.append(d)
            d.signal = True
        for b in reads:
            if not dma:
                b.r = [r for r in b.r if r.dma or r.eng != eng]
            b.r.append(o)
        for b in writes:
            b.w = o
            b.r = []
        lst.append(o)
        return o

    def emit(self, stack):
        nc = self.nc
        CH = 2000
        fin = Op("sp", len(self.ops["sp"]), None, False)
        fin.deps = list(self.out_dmas)
        fin.phase = None
        self.ops["sp"].append(fin)
        for e in ENGS:
            c = 0
            for o in self.ops[e]:
                if o.signal:
                    c += 1
                o.cnt = c
        esem = {}
        for e in ENGS:
            n = (self.ops[e][-1].cnt if self.ops[e] else 0)
            for i in range(max(1, (n + CH - 1) // CH)):
                esem[(e, i)] = stack.enter_context(nc.semaphore("es_%s%d" % (e, i)))
        dsem = {}
        for e, k in DMA_NSEM.items():
            for i in range(k):
                dsem[(e, i)] = stack.enter_context(nc.semaphore("ds_%s%d" % (e, i)))
        block = stack.enter_context(nc.Block())

        def run(e, engine):
            known = {}
            kn_eng = {}
            cur = [None, None]

            def set_phase(ph):
                if not PROFILE_SCOPES or ph == cur[0] or e != PROFILE_ENGINE:
                    return
                if cur[1] is not None:
                    cur[1].__exit__(None, None, None)
                    cur[1] = None
                cur[0] = ph
                if ph is not None:
                    cur[1] = nc.named_scope(ph)
                    cur[1].__enter__()

            for o in self.ops[e] + [None]:
                if o is None:
                    set_phase(None)
                    break
                set_phase(o.phase)
                need = {}
                for d in o.deps:
                    if d.dma:
                        key, val = ("d",) + d.dsem, d.dval
                        if known.get(key, 0) >= val:
                            continue
                    else:
                        if kn_eng.get(d.eng, 0) >= d.cnt:
                            continue
                        key, val = ("e", d.eng, (d.cnt - 1) // CH), (d.cnt - 1) % CH + 1
                        kn_eng[d.eng] = d.cnt
                    if need.get(key, 0) < val:
                        need[key] = val
                for key, val in need.items():
                    s = dsem[key[1:]] if key[0] == "d" else esem[key[1:]]
                    engine.wait_ge(s, val)
                    if key[0] == "d":
                        known[key] = val
                if o.fn is None:
                    continue
                ins = o.fn(engine)
                if o.dma:
                    ins.then_inc(dsem[o.dsem], 16)
                elif o.signal:
                    ins.then_inc(esem[(e, (o.cnt - 1) // CH)], 1)

        @block.tensor
        def _(eng):
            run("pe", eng)

        @block.scalar
        def _(eng):
            run("act", eng)

        @block.vector
        def _(eng):
            run("dve", eng)

        @block.gpsimd
        def _(eng):
            run("pool", eng)

        @block.sync
        def _(eng):
            run("sp", eng)


D = 1024
T = 1280
TT = [(0, 256, 0), (256, 512, 1), (768, 512, 1)]
NTB = T // 128
DFF = 2816
NHC = DFF // 128
EPS = 1e-6
RING_SLOTS = 3
SLOT_ELEMS = 8192


class Prog:
    pass


def build_program():
    nc = bass.Bass("TRN2", target_bir_lowering=False)
    P = Prog()
    st = ExitStack()
    with st:
        S = Sched(nc)

        def din(name, shape, dt=F32):
            return nc.dram_tensor(name, list(shape), dt, kind="ExternalInput").ap()

        def dout(name, shape, dt=F32):
            return nc.dram_tensor(name, list(shape), dt, kind="ExternalOutput").ap()

        _n = [0]

        def sb(shape, dt, name=None):
            _n[0] += 1
            return st.enter_context(nc.sbuf_tensor(name or ("t%d" % _n[0]), list(shape), dt))

        x_d = din("x_in", [T, D])
        cond_d = din("cond_in", [2, D])
        wmod_d = din("w_mod", [2, D, 9 * D])
        bmod_d = din("b_mod", [2, 9 * D])
        normg_d = din("norm_g", [2, 3, D])
        fwin_d = din("ffn_w_in", [2, 2, D, 2 * DFF])
        fwout_d = din("ffn_w_out", [2, 2, DFF, D])
        finalg_d = din("final_g", [D])
        ident_d = din("ident_in", [128, 128])
        y_d = dout("y", [T, D])
        win_d = din("w_in", [2, D, 2592])
        wout_d = din("w_out", [2, D, D])
        mixg_d = din("mix_g", [2, D])
        sink_d = din("swa_sink", [2, 4])
        qkg_d = din("qk_norm_g", [2, 2, 64])
        cos_d = din("rope_cos", [T, 32])
        sin_d = din("rope_sin", [T, 32])
        ctx_d = din("ctx_kv", [4, 2, 512, 128])
        ctxbias_d = din("ctx_bias", [128, 1])
        amask_d = din("attn_mask", [4, 128, 1920], BF16)
        kv_d = dout("kv_out", [4, 2, T, 128])
        hcw_d = din("hy_conv_w", [2, 3, 768])
        hcb_d = din("hy_conv_b", [2, 768])
        hw1_d = din("hy_w1", [2, 33, 64])
        hb1_d = din("hy_b1", [2, 64])
        hw2_d = din("hy_w2", [2, 64, 64])
        hb2_d = din("hy_b2", [2, 64])
        hw3_d = din("hy_w3", [2, 64, 1024])
        hfr_d = din("hy_freq", [2, 2, 64])
        hbias_d = din("hy_bias", [2, 2, 256])
        hz_d = din("hy_zT", [33, T])
        hdec_d = din("hy_dec", [T, 2, 256])
        hfwg_d = din("hy_fw_g", [1024, 2048], BF16)
        hivg_d = din("hy_iv_g", [2048, 1024], BF16)
        hfwp_d = din("hy_fw_p", [256, 512], BF16)
        hivp_d = din("hy_iv_p", [512, 256], BF16)
        hflag_d = din("hy_flag", [128, 1])
        gw_d = din("gla_gate_w", [2, 2, 16, 128])
        gb_d = din("gla_gate_b", [2, 2, 128])
        gs0_d = din("gla_s0", [2, 2, 4, 32, 64])
        tri_d = din("tri_in", [4, 128, 128])
        gout_d = dout("gla_out", [2, 2, 5, 4, 32, 64])

        xres = sb([128, 8, T], F32, "xres")
        u = sb([128, 8, T], BF16, "u")
        hid = sb([128, 12, T], BF16, "hid")
        ring = [sb([128, SLOT_ELEMS], BF16, "ring%d" % i) for i in range(RING_SLOTS)]
        ring_b = [Buf("ring%d" % i) for i in range(RING_SLOTS)]
        stg = [sb([128, D], F32, "stg%d" % i) for i in range(2)]
        stg_b = [Buf() for _ in range(2)]
        tmp = [sb([128, 512], F32, "tmp%d" % i) for i in range(3)]
        tmp_b = [Buf() for _ in range(3)]
        rstd = sb([128, T], F32, "rstd")
        rstd_b = Buf()
        ident = sb([128, 128], F32, "ident")
        ident_b = Buf()
        ones_bf = sb([128, 128], BF16, "ones")
        ones_b = Buf()
        vecs_in = [sb([128, 128], F32, "vin%d" % i) for i in range(2)]
        vecs = [sb([128, 128], F32, "vec%d" % i) for i in range(2)]
        vecs_b = [Buf() for _ in range(2)]
        vin_b = [Buf() for _ in range(2)]
        condT = sb([128, 8, 2], BF16, "condT")
        condT_b = Buf()
        mod = sb([128, 2, 72, 2], F32, "mod")
        mod_b = Buf()
        modA = sb([128, 2, 3, 8, 2], F32, "modA")
        modG = sb([128, 2, 3, 8, 2], F32, "modG")
        modA_b = Buf()
        xres_b = [Buf("xres%d" % i) for i in range(3)]
        u_b = [Buf("u%d" % i) for i in range(3)]
        hid_b = [[Buf() for _ in range(3)] for _ in range(12)]

        arena = hid[:].rearrange("p a b -> p (a b)")
        qT = arena[0:64, 0:5120].rearrange("p (h t) -> p h t", h=4)
        kT = arena[0:64, 5120:7680].rearrange("p (h t) -> p h t", h=2)
        vtok = arena[:, 7680:8980].rearrange("p (b g f) -> p b g f", b=NTB, g=2)
        oT = arena[0:64, 8980:14100].rearrange("p (h t) -> p h t", h=4)
        qT_b, kT_b, vtok_b, oT_b = Buf("qT"), Buf("kT"), Buf("vtok"), Buf("oT")
        amask = sb([128, 4, 1920], BF16, "amask")
        amask_b = Buf()
        ropec = sb([128, NTB, 32], F32, "ropec")
        ropes = sb([128, NTB, 32], F32, "ropes")
        rope_b = Buf()
        gq = sb([128, 2, 6, 64], F32, "gq")
        gq_b = Buf()
        ctxbias = sb([128, 1], F32, "ctxbias")
        esink = sb([64, 8], F32, "esink")
        misc_b = Buf()
        ctxkT = sb([64, 2, 512], BF16, "ctxkT")
        ctxv = sb([128, 4, 2, 65], BF16, "ctxv")
        esink128 = sb([128, 8], F32, "esink128")
        ones_f = sb([128, 64], F32, "ones_f")
        rrow = sb([128, 512], F32, "rrow")
        rrow_b = Buf()
        ctxkT_b, ctxv_b = Buf(), Buf()
        kvst = [sb([128, 2, 128], F32, "kvst%d" % i) for i in range(2)]
        kvst_b = [Buf() for _ in range(2)]
        qkn = [sb([128, 384], F32, "qkn%d" % i) for i in range(2)]
        qkn_b = [Buf() for _ in range(2)]
        qkr = [sb([128, 384], F32, "qkr%d" % i) for i in range(2)]
        qkr_b = [Buf() for _ in range(2)]
        rtmp = sb([128, 192], F32, "rtmp")
        rtmp_b = Buf()
        ssq = sb([128, 8], F32, "ssq")
        ssq_b = Buf()
        etile = [sb([128, 512], BF16, "etile%d" % i) for i in range(4)]
        etile_b = [Buf() for _ in range(4)]
        rden = sb([64, 512], F32, "rden")
        rden_b = Buf()
        vin64 = sb([128, 64], F32, "vin64")
        vec64 = sb([64, 128], F32, "vec64")
        vec64_b = Buf()

        hyF = arena[:, 0:2560].bitcast(F32)
        hyx1 = arena[:, 2560:3840]
        hyx2 = arena[:, 3840:5120]
        hyvtok = arena[:, 5120:6400].rearrange("p (b c) -> p b c", b=NTB)
        hyhp = arena[:, 6400:7680].rearrange("p (b c) -> p b c", b=NTB)
        hyhm = arena[:, 7680:8960].rearrange("p (b c) -> p b c", b=NTB)
        hyY = arena[:, 8960:11520].rearrange("p (r c) -> p r c", r=20)
        hyob0 = arena[:, 11520:12800]
        hyh2 = arena[0:64, 12800:14080]
        hyF_b, hyx1_b, hyx2_b, hyvtok_b, hyh_b, hyY_b, hyob0_b, hyh2_b = (Buf() for _ in range(8))
        ARENA_BUFS = [qT_b, kT_b, vtok_b, oT_b, hyF_b, hyx1_b, hyx2_b, hyvtok_b, hyh_b, hyY_b, hyob0_b, hyh2_b]
        gqT = arena[0:64, 0:2560].rearrange("p (a t) -> p a t", a=2)
        gkT = arena[0:64, 2560:5120].rearrange("p (a t) -> p a t", a=2)
        gktok = arena[:, 5120:6400].rearrange("p (b c) -> p b c", b=NTB)
        gvtok = arena[:, 6400:8960].rearrange("p (b c) -> p b c", b=NTB)
        gtri = arena[:, 8960:9984].bitcast(F32).rearrange("p (a c) -> p a c", a=4)
        ggw = arena[0:16, 9984:10240]
        ggb = arena[0:1, 10240:10496]
        gS = arena[0:64, 10496:11008].bitcast(F32).rearrange("p (a v) -> p a v", a=4)
        gSb = arena[0:64, 11008:11264].rearrange("p (a v) -> p a v", a=4)
        geb = arena[0:64, 11264:12288].bitcast(F32).rearrange("p (a t) -> p a t", a=4)
        gqk = arena[0:64, 12288:12800]
        gke = arena[:, 12800:12928]
        gqk_b, gke_b = Buf(), Buf()
        a2 = amask[:].rearrange("p a n -> p (a n)")
        godT = a2[0:64, 0:5120].rearrange("p (h t) -> p h t", h=4)
        ggdT = a2[0:16, 5120:7680].rearrange("p (z t) -> p z t", z=2)
        gqT_b, gkT_b, gktok_b, gvtok_b, gconst_b, gS_b, geb_b, godT_b, ggdT_b = (Buf() for _ in range(9))
        fdummy = sb([128, 1], F32, "fdummy")
        vin3 = sb([128, 128], F32, "vin3")
        vec3 = sb([128, 128], F32, "vec3")
        vec3_b = Buf()
        w3b = sb([64, 1024], BF16, "w3b")
        hyw12 = sb([64, 128], F32, "hyw12")
        hyw_b = Buf()
        hysc = sb([128, 2, 6, 4], F32, "hysc")
        hyfb = sb([64, 4], F32, "hyfb")
        hyflag = sb([128, 1], F32, "hyflag")
        hysc_b = Buf()

        ps = [st.enter_context(nc.psum_tensor("ps%d" % i, [128, 512], F32)) for i in range(8)]
        ps_b = [Buf("ps%d" % i) for i in range(8)]
        _pi = [0]

        def next_ps(n=8):
            i = _pi[0] % n
            _pi[0] += 1
            return ps[i], ps_b[i]

        _ri = [0]

        def next_slot():
            i = _ri[0] % RING_SLOTS
            _ri[0] += 1
            return ring[i], ring_b[i]

        _ti = [0]

        def next_tmp():
            i = _ti[0] % 3
            _ti[0] += 1
            return tmp[i], tmp_b[i]

        S.op("sp", lambda e: e.dma_start(out=ident[:], in_=ident_d), writes=[ident_b], dma=True)
        S.op("dve", lambda e: e.memset(ones_bf[:], 1.0), writes=[ones_b])

        S.op("dve", lambda e: e.memset(vecs_in[0][:], 0.0), writes=[vin_b[0]])
        S.op("dve", lambda e: e.memset(vecs_in[1][:], 0.0), writes=[vin_b[1]])
        S.op("sp", lambda e: e.dma_start(out=vecs_in[0][0:72, :], in_=bmod_d[0].rearrange("(c p) -> c p", p=128)), writes=[vin_b[0]], dma=True)
        S.op("sp", lambda e: e.dma_start(out=vecs_in[0][72:120, :], in_=normg_d.rearrange("l i (k p) -> (l i k) p", p=128)), writes=[vin_b[0]], dma=True)
        S.op("sp", lambda e: e.dma_start(out=vecs_in[1][0:72, :], in_=bmod_d[1].rearrange("(c p) -> c p", p=128)), writes=[vin_b[1]], dma=True)
        S.op("sp", lambda e: e.dma_start(out=vecs_in[1][72:88, :], in_=cond_d.rearrange("c (k p) -> (c k) p", p=128)), writes=[vin_b[1]], dma=True)
        S.op("sp", lambda e: e.dma_start(out=vecs_in[1][88:96, :], in_=finalg_d.rearrange("(k p) -> k p", p=128)), writes=[vin_b[1]], dma=True)
        for i in range(2):
            pt, pb = next_ps()
            S.op("pe", lambda e, i=i, pt=pt: e.transpose(pt[:, 0:128], vecs_in[i][:], ident[:]), reads=[vin_b[i], ident_b], writes=[pb])
            S.op("dve", lambda e, i=i, pt=pt: e.tensor_copy(out=vecs[i][:], in_=pt[:, 0:128]), reads=[pb], writes=[vecs_b[i]])

        def bmodT(l):
            return vecs[l][:, 0:72]

        def normgT(l, i):
            o = 72 + (l * 3 + i) * 8
            return vecs[0][:, o:o + 8]

        finalgT = vecs[1][:, 88:96]
        S.op("act", lambda e: e.activation(out=condT[:].rearrange("p k c -> p c k"), in_=vecs[1][:, 72:88].rearrange("p (c k) -> p c k", c=2), func=AF.Silu),
             reads=[vecs_b[1]], writes=[condT_b])

        S.phase = "load_x"
        for tb in range(NTB):
            sg, sgb = stg[tb % 2], stg_b[tb % 2]
            S.op("sp", lambda e, sg=sg, tb=tb: e.dma_start(out=sg[:], in_=x_d[tb * 128:(tb + 1) * 128, :]), writes=[sgb], dma=True)
            tti = 0 if tb < 2 else (1 if tb < 6 else 2)
            for half in range(2):
                pt, pb = next_ps()
                for q in range(4):
                    k = half * 4 + q
                    S.op("pe", lambda e, pt=pt, sg=sg, k=k, q=q: e.transpose(pt[:, q * 128:(q + 1) * 128], sg[:, k * 128:(k + 1) * 128], ident[:]),
                         reads=[sgb, ident_b], writes=[pb])
                eng = "act" if half == 0 else "dve"
                if eng == "act":
                    S.op("act", lambda e, pt=pt, half=half, tb=tb: e.copy(out=xres[:, half * 4:half * 4 + 4, tb * 128:(tb + 1) * 128], in_=pt[:, :].rearrange("p (a b) -> p a b", a=4)),
                         reads=[pb], writes=[xres_b[tti]])
                else:
                    S.op("dve", lambda e, pt=pt, half=half, tb=tb: e.tensor_copy(out=xres[:, half * 4:half * 4 + 4, tb * 128:(tb + 1) * 128], in_=pt[:, :].rearrange("p (a b) -> p a b", a=4)),
                         reads=[pb], writes=[xres_b[tti]])

        S.phase = "modulation"
        for l in range(NLAYERS):
            for jb in range(9):
                sl, slb = next_slot()
                S.op("pool", lambda e, sl=sl, l=l, jb=jb: e.dma_start(out=sl[:, :].rearrange("p (k n) -> p k n", k=8),
                                                                     in_=wmod_d[l].rearrange("(k p) n -> p k n", p=128)[:, :, jb * 1024:(jb + 1) * 1024]),
                     writes=[slb], dma=True)
                pt, pb = next_ps()
                for cc in range(8):
                    for k in range(8):
                        S.op("pe", lambda e, pt=pt, sl=sl, cc=cc, k=k: e.matmul(pt[:, 2 * cc:2 * cc + 2], lhsT=sl[:, k * 1024 + cc * 128:k * 1024 + (cc + 1) * 128],
                                                                                 rhs=condT[:, k, :], start=(k == 0), stop=(k == 7)),
                             reads=[slb, condT_b], writes=[pb])
                S.op("dve", lambda e, pt=pt, l=l, jb=jb: e.tensor_tensor(out=mod[:, l, jb * 8:(jb + 1) * 8, :], in0=pt[:, 0:16].rearrange("p (a c) -> p a c", c=2),
                                                                         in1=bmodT(l)[:, jb * 8:(jb + 1) * 8].unsqueeze(2).broadcast_to([128, 8, 2]), op=ALU.add),
                     reads=[pb, vecs_b[l]], writes=[mod_b])
            for i in range(3):
                S.op("dve", lambda e, l=l, i=i: e.tensor_scalar(out=modA[:, l, i], in0=mod[:, l, (3 * i + 1) * 8:(3 * i + 2) * 8, :], scalar1=1.0, scalar2=None, op0=ALU.add),
                     reads=[mod_b], writes=[modA_b])
                S.op("dve", lambda e, l=l, i=i: e.tensor_tensor(out=modA[:, l, i], in0=modA[:, l, i], in1=normgT(l, i).unsqueeze(2).broadcast_to([128, 8, 2]), op=ALU.mult),
                     reads=[modA_b, vecs_b[0]], writes=[modA_b])
                S.op("dve", lambda e, l=l, i=i: e.tensor_scalar(out=modG[:, l, i], in0=mod[:, l, (3 * i + 2) * 8:(3 * i + 3) * 8, :], scalar1=(1.0 if i == 1 else 0.5), scalar2=None, op0=ALU.mult),
                     reads=[mod_b], writes=[modA_b])

        def modB(l, i, k, c):
            return mod[:, l, (3 * i) * 8 + k, c:c + 1]

        def norm_mod(l, i):
            for ti, (t0, tl, c) in enumerate(TT):
                S.op("act", lambda e, t0=t0, tl=tl: e.activation(out=u[:, :, t0:t0 + tl], in_=xres[:, :, t0:t0 + tl], func=AF.Square),
                     reads=[xres_b[ti]], writes=[u_b[ti]])
                pt, pb = next_ps()
                for k in range(8):
                    S.op("pe", lambda e, pt=pt, k=k, t0=t0, tl=tl: e.matmul(pt[:, 0:tl], lhsT=ones_bf[:], rhs=u[:, k, t0:t0 + tl], start=(k == 0), stop=(k == 7)),
                         reads=[ones_b, u_b[ti]], writes=[pb])
                S.op("act", lambda e, pt=pt, t0=t0, tl=tl: e.activation(out=rstd[:, t0:t0 + tl], in_=pt[:, 0:tl], func=AF.Sqrt, scale=1.0 / D, bias=EPS),
                     reads=[pb], writes=[rstd_b])
                S.op("dve", lambda e, t0=t0, tl=tl: e.reciprocal(out=rstd[:, t0:t0 + tl], in_=rstd[:, t0:t0 + tl]), reads=[rstd_b], writes=[rstd_b])
                for k in range(8):
                    tm, tmb = next_tmp()
                    S.op("dve", lambda e, tm=tm, k=k, t0=t0, tl=tl, c=c: e.scalar_tensor_tensor(out=tm[:, 0:tl], in0=xres[:, k, t0:t0 + tl], scalar=modA[:, l, i, k, c:c + 1],
                                                                                               in1=rstd[:, t0:t0 + tl], op0=ALU.mult, op1=ALU.mult),
                         reads=[xres_b[ti], rstd_b, modA_b], writes=[tmb])
                    S.op("act", lambda e, tm=tm, k=k, t0=t0, tl=tl, c=c: e.activation(out=u[:, k, t0:t0 + tl], in_=tm[:, 0:tl], func=AF.Identity, bias=modB(l, i, k, c), scale=1.0),
                         reads=[tmb, mod_b], writes=[u_b[ti]])

        def ffn(l, i):
            gi = 0 if i == 0 else 2
            win = fwin_d[l, i].rearrange("(k p) n -> p k n", p=128)
            wout = fwout_d[l, i].rearrange("(j p) n -> p j n", p=128)
            groups = [(0, 4), (4, 4), (8, 4), (12, 4), (16, 4), (20, 2)]
            halves = [groups[0:3], groups[3:6]]
            for hgroups in halves:
                h0 = hgroups[0][0]
                nh = sum(g[1] for g in hgroups)
                for (j0, nj) in hgroups:
                    sl, slb = next_slot()
                    S.op("pool", lambda e, sl=sl, j0=j0, nj=nj: e.dma_start(out=sl[:, 0:8 * nj * 128].rearrange("p (k n) -> p k n", k=8), in_=win[:, :, j0 * 128:(j0 + nj) * 128]),
                         writes=[slb], dma=True)
                    S.op("pool", lambda e, sl=sl, j0=j0, nj=nj: e.dma_start(out=sl[:, 4096:4096 + 8 * nj * 128].rearrange("p (k n) -> p k n", k=8),
                                                                            in_=win[:, :, DFF + j0 * 128:DFF + (j0 + nj) * 128]),
                         writes=[slb], dma=True)
                    for jj in range(nj):
                        j = j0 + jj
                        for ti, (t0, tl, c) in enumerate(TT):
                            pa, pab = next_ps()
                            pbt, pbb = next_ps()
                            for k in range(8):
                                S.op("pe", lambda e, pa=pa, sl=sl, k=k, jj=jj, nj=nj, t0=t0, tl=tl: e.matmul(pa[:, 0:tl], lhsT=sl[:, k * nj * 128 + jj * 128:k * nj * 128 + (jj + 1) * 128],
                                                                                                           rhs=u[:, k, t0:t0 + tl], start=(k == 0), stop=(k == 7)),
                                     reads=[slb, u_b[ti]], writes=[pab])
                            for k in range(8):
                                S.op("pe", lambda e, pbt=pbt, sl=sl, k=k, jj=jj, nj=nj, t0=t0, tl=tl: e.matmul(pbt[:, 0:tl], lhsT=sl[:, 4096 + k * nj * 128 + jj * 128:4096 + k * nj * 128 + (jj + 1) * 128],
                                                                                                             rhs=u[:, k, t0:t0 + tl], start=(k == 0), stop=(k == 7)),
                                     reads=[slb, u_b[ti]], writes=[pbb])
                            tm, tmb = next_tmp()
                            S.op("act", lambda e, pa=pa, tm=tm, tl=tl: e.activation(out=tm[:, 0:tl], in_=pa[:, 0:tl], func=AF.Silu), reads=[pab], writes=[tmb])
                            S.op("dve", lambda e, pbt=pbt, tm=tm, j=j, h0=h0, t0=t0, tl=tl: e.tensor_tensor(out=hid[:, j - h0, t0:t0 + tl], in0=tm[:, 0:tl], in1=pbt[:, 0:tl], op=ALU.mult),
                                 reads=[tmb, pbb], writes=[hid_b[j - h0][ti]])
                slots = []
                jj0 = 0
                while jj0 < nh:
                    n = min(8, nh - jj0)
                    sl, slb = next_slot()
                    S.op("pool", lambda e, sl=sl, jj0=jj0, n=n, h0=h0: e.dma_start(out=sl[:, 0:n * 1024].rearrange("p (j n) -> p j n", j=n), in_=wout[:, h0 + jj0:h0 + jj0 + n, :]),
                         writes=[slb], dma=True)
                    slots.append((sl, slb, jj0, n))
                    jj0 += n
                for f in range(8):
                    for ti, (t0, tl, c) in enumerate(TT):
                        pt, pb = next_ps()
                        for (sl, slb, jj0, n) in slots:
                            for q in range(n):
                                jj = jj0 + q
                                S.op("pe", lambda e, pt=pt, sl=sl, q=q, f=f, jj=jj, t0=t0, tl=tl, nh=nh: e.matmul(pt[:, 0:tl], lhsT=sl[:, q * 1024 + f * 128:q * 1024 + (f + 1) * 128],
                                                                                                                rhs=hid[:, jj, t0:t0 + tl], start=(jj == 0), stop=(jj == nh - 1)),
                                     reads=[slb, hid_b[jj][ti]], writes=[pb])
                        S.op("dve", lambda e, pt=pt, f=f, t0=t0, tl=tl, c=c: e.scalar_tensor_tensor(out=xres[:, f, t0:t0 + tl], in0=pt[:, 0:tl], scalar=modG[:, l, gi, f, c:c + 1],
                                                                                                   in1=xres[:, f, t0:t0 + tl], op0=ALU.mult, op1=ALU.add),
                             reads=[pb, modA_b, xres_b[ti]], writes=[xres_b[ti]])

        S.phase = "consts"
        S.op("sp", lambda e: e.dma_start(out=ropec[:], in_=cos_d.rearrange("(b p) f -> p b f", p=128)), writes=[rope_b], dma=True)
        S.op("sp", lambda e: e.dma_start(out=ropes[:], in_=sin_d.rearrange("(b p) f -> p b f", p=128)), writes=[rope_b], dma=True)
        for l in range(2):
            for hh in range(6):
                S.op("sp", lambda e, l=l, hh=hh: e.dma_start(out=gq[:, l, hh, :], in_=qkg_d[l, (0 if hh < 4 else 1):(1 if hh < 4 else 2), :].broadcast_to([128, 64])),
                     writes=[gq_b], dma=True)
        S.op("sp", lambda e: e.dma_start(out=ctxbias[:], in_=ctxbias_d), writes=[misc_b], dma=True)
        S.op("sp", lambda e: e.dma_start(out=esink[:], in_=sink_d.rearrange("l h -> (l h)").rearrange("(o n) -> o n", o=1).broadcast_to([64, 8])), writes=[misc_b], dma=True)
        S.op("act", lambda e: e.activation(out=esink[:], in_=esink[:], func=AF.Exp), reads=[misc_b], writes=[misc_b])
        S.op("sp", lambda e: e.dma_start(out=esink128[:], in_=sink_d.rearrange("l h -> (l h)").rearrange("(o n) -> o n", o=1).broadcast_to([128, 8])), writes=[misc_b], dma=True)
        S.op("act", lambda e: e.activation(out=esink128[:], in_=esink128[:], func=AF.Exp), reads=[misc_b], writes=[misc_b])
        S.op("dve", lambda e: e.memset(ones_f[:], 1.0), writes=[misc_b])
        S.op("dve", lambda e: e.memset(vin64[:], 0.0), writes=[vec64_b])
        S.op("sp", lambda e: e.dma_start(out=vin64[0:32, :], in_=mixg_d.rearrange("l (c p) -> (l c) p", p=64)), writes=[vec64_b], dma=True)
        S.op("sp", lambda e: e.dma_start(out=vin64[32:36, :], in_=hfr_d.rearrange("l i d -> (l i) d")), writes=[vec64_b], dma=True)
        S.op("sp", lambda e: e.dma_start(out=vin64[36:38, :], in_=hb1_d), writes=[vec64_b], dma=True)
        S.op("sp", lambda e: e.dma_start(out=vin64[38:40, :], in_=hb2_d), writes=[vec64_b], dma=True)
        pt, pb = next_ps()
        S.op("pe", lambda e, pt=pt: e.transpose(pt[0:64, 0:128], vin64[:], ident[:]), reads=[vec64_b, ident_b], writes=[pb])
        S.op("dve", lambda e, pt=pt: e.tensor_copy(out=vec64[:], in_=pt[0:64, 0:128]), reads=[pb], writes=[vec64_b])

        def mixgain64(l, piece):
            return vec64[:, l * 16 + piece:l * 16 + piece + 1]

        _ei = [0]

        def next_e():
            i = _ei[0] % 4
            _ei[0] += 1
            return etile[i], etile_b[i]

        PS_O, PS_D = 6, 7

        def wout_partial(l, pieces, ksz, wslot, wslot_b, ysrc):
            for f in range(8):
                for ti, (t0, tl, c) in enumerate(TT):
                    pt, pb = next_ps(6)
                    for pi in range(pieces):
                        yap, ybufs = ysrc(pi, t0, tl)
                        S.op("pe", lambda e, pt=pt, pi=pi, f=f, tl=tl, yap=yap: e.matmul(pt[:, 0:tl], lhsT=wslot[0:ksz, pi * 1024 + f * 128:pi * 1024 + (f + 1) * 128], rhs=yap,
                                                                                         start=(pi == 0), stop=(pi == pieces - 1)),
                             reads=[wslot_b] + ybufs, writes=[pb])
                    S.op("dve", lambda e, pt=pt, f=f, t0=t0, tl=tl, c=c: e.scalar_tensor_tensor(out=xres[:, f, t0:t0 + tl], in0=pt[:, 0:tl], scalar=modG[:, l, 1, f, c:c + 1],
                                                                                               in1=xres[:, f, t0:t0 + tl], op0=ALU.mult, op1=ALU.add),
                         reads=[pb, modA_b, xres_b[ti]], writes=[xres_b[ti]])

        def attention_group(l, grp):
            c0 = 0 if grp == 0 else 1280
            wv = win_d[l].rearrange("(k p) n -> p k n", p=128)
            fence()
            if grp == 0 or not STAGES.get("A", True):
                S.op("sp", lambda e: e.dma_start(out=amask[:], in_=amask_d.rearrange("a p n -> p a n")), writes=[amask_b], dma=True)
            sl, slb = next_slot()
            S.op("pool", lambda e, sl=sl: e.dma_start(out=sl[:, 0:4096].rearrange("p (k n) -> p k n", k=8), in_=wv[:, :, c0:c0 + 512]), writes=[slb], dma=True)
            for g_ in range(2):
                S.op("pool", lambda e, g_=g_: e.dma_start(out=ctxv[:, :, g_, 0:64], in_=ctx_d[2 * grp + 1, l].rearrange("(b p) f -> p b f", p=128)[:, :, g_ * 64:(g_ + 1) * 64]),
                     reads=[u_b[0]], writes=[ctxv_b], dma=True)
            S.op("dve", lambda e: e.memset(ctxv[:, :, :, 64:65], 1.0), writes=[ctxv_b])
            S.op("dve", lambda e: e.memset(vtok[:, :, :, 64:65], 1.0), writes=[vtok_b])
            sg, sgb = stg[0], stg_b[0]
            S.op("sp", lambda e, sg=sg: e.dma_start(out=sg[:, 0:512].rearrange("p (b f) -> p b f", b=4), in_=ctx_d[2 * grp, l].rearrange("(b p) f -> p b f", p=128)),
                 reads=[u_b[0]], writes=[sgb], dma=True)
            for g in range(2):
                pt, pb = next_ps(6)
                for b in range(4):
                    S.op("pe", lambda e, pt=pt, sg=sg, b=b, g=g: e.transpose(pt[0:64, b * 128:(b + 1) * 128], sg[:, b * 128 + g * 64:b * 128 + (g + 1) * 64], ident[:]),
                         reads=[sgb, ident_b], writes=[pb])
                S.op("act", lambda e, pt=pt, g=g: e.copy(out=ctxkT[:, g, :], in_=pt[0:64, :]), reads=[pb], writes=[ctxkT_b])
            for tb in range(NTB):
                tti = 0 if tb < 2 else (1 if tb < 6 else 2)
                pp, ppb = next_ps(6)
                for k in range(8):
                    S.op("pe", lambda e, pp=pp, sl=sl, k=k, tb=tb: e.matmul(pp[:, :], lhsT=u[:, k, tb * 128:(tb + 1) * 128], rhs=sl[:, k * 512:(k + 1) * 512], start=(k == 0), stop=(k == 7)),
                         reads=[slb, u_b[tti]], writes=[ppb])
                kv, kvb = kvst[tb % 2], kvst_b[tb % 2]
                qn, qnb = qkn[tb % 2], qkn_b[tb % 2]
                qr, qrb = qkr[tb % 2], qkr_b[tb % 2]
                S.op("act", lambda e, pp=pp, tb=tb: e.copy(out=vtok[:, tb, :, 0:64], in_=pp[:, 384:512].rearrange("p (g f) -> p g f", g=2)), reads=[ppb], writes=[vtok_b])
                S.op("act", lambda e, pp=pp, kv=kv: e.copy(out=kv[:, 1, :], in_=pp[:, 384:512]), reads=[ppb], writes=[kvb])
                if grp == 0:
                    S.op("act", lambda e, pp=pp, qn=qn: e.copy(out=qn[:, :], in_=pp[:, 0:384]), reads=[ppb], writes=[qnb])
                else:
                    S.op("act", lambda e, pp=pp, qr=qr: e.activation(out=qr[:, :], in_=pp[:, 0:384], func=AF.Square), reads=[ppb], writes=[qrb])
                    S.op("dve", lambda e, qr=qr: e.reduce_sum(out=ssq[:, 0:6], in_=qr[:, :].rearrange("p (h d) -> p h d", h=6), axis=AX.X), reads=[qrb], writes=[ssq_b])
                    S.op("act", lambda e: e.activation(out=ssq[:, 0:6], in_=ssq[:, 0:6], func=AF.Sqrt, scale=1.0 / 64, bias=EPS), reads=[ssq_b], writes=[ssq_b])
                    S.op("dve", lambda e: e.reciprocal(out=ssq[:, 0:6], in_=ssq[:, 0:6]), reads=[ssq_b], writes=[ssq_b])
                    S.op("dve", lambda e, pp=pp, qn=qn: e.tensor_tensor(out=qn[:, :].rearrange("p (h d) -> p h d", h=6), in0=pp[:, 0:384].rearrange("p (h d) -> p h d", h=6),
                                                                        in1=ssq[:, 0:6].unsqueeze(2).broadcast_to([128, 6, 64]), op=ALU.mult),
                         reads=[ppb, ssq_b], writes=[qnb])
                    S.op("dve", lambda e, qn=qn: e.tensor_tensor(out=qn[:, :], in0=qn[:, :], in1=gq[:, l].rearrange("p h d -> p (h d)"), op=ALU.mult), reads=[qnb, gq_b], writes=[qnb])
                S.op("dve", lambda e, qn=qn, kv=kv: e.tensor_copy(out=kv[:, 0, :], in_=qn[:, 256:384]), reads=[qnb], writes=[kvb])
                S.op("sp", lambda e, kv=kv, tb=tb: e.dma_start(out=kv_d[2 * grp:2 * grp + 2, l, tb * 128:(tb + 1) * 128, :].rearrange("a t f -> t a f"), in_=kv[:]),
                     reads=[kvb], dma=True, is_out=True)
                x1 = qn[:, :].rearrange("p (h two d) -> p h two d", h=6, two=2)[:, :, 0, :]
                x2 = qn[:, :].rearrange("p (h two d) -> p h two d", h=6, two=2)[:, :, 1, :]
                o1 = qr[:, :].rearrange("p (h two d) -> p h two d", h=6, two=2)[:, :, 0, :]
                o2 = qr[:, :].rearrange("p (h two d) -> p h two d", h=6, two=2)[:, :, 1, :]
                cc = ropec[:, tb, :].unsqueeze(1).broadcast_to([128, 6, 32])
                ss = ropes[:, tb, :].unsqueeze(1).broadcast_to([128, 6, 32])
                rt = rtmp[:, :].rearrange("p (h d) -> p h d", h=6)
                S.op("dve", lambda e, o1=o1, x1=x1, cc=cc: e.tensor_tensor(out=o1, in0=x1, in1=cc, op=ALU.mult), reads=[qnb, rope_b], writes=[qrb])
                S.op("dve", lambda e, rt=rt, x2=x2, ss=ss: e.tensor_tensor(out=rt, in0=x2, in1=ss, op=ALU.mult), reads=[qnb, rope_b], writes=[rtmp_b])
                S.op("dve", lambda e, o1=o1, rt=rt: e.tensor_tensor(out=o1, in0=o1, in1=rt, op=ALU.subtract), reads=[qrb, rtmp_b], writes=[qrb])
                S.op("dve", lambda e, o2=o2, x2=x2, cc=cc: e.tensor_tensor(out=o2, in0=x2, in1=cc, op=ALU.mult), reads=[qnb, rope_b], writes=[qrb])
                S.op("dve", lambda e, rt=rt, x1=x1, ss=ss: e.tensor_tensor(out=rt, in0=x1, in1=ss, op=ALU.mult), reads=[qnb, rope_b], writes=[rtmp_b])
                S.op("dve", lambda e, o2=o2, rt=rt: e.tensor_tensor(out=o2, in0=o2, in1=rt, op=ALU.add), reads=[qrb, rtmp_b], writes=[qrb])
                pq, pqb = next_ps(6)
                for h in range(4):
                    S.op("pe", lambda e, pq=pq, qr=qr, h=h: e.transpose(pq[0:64, h * 128:(h + 1) * 128], qr[:, h * 64:(h + 1) * 64], ident[:]), reads=[qrb, ident_b], writes=[pqb])
                S.op("act", lambda e, pq=pq, tb=tb: e.copy(out=qT[:, :, tb * 128:(tb + 1) * 128], in_=pq[0:64, :].rearrange("p (h t) -> p h t", h=4)), reads=[pqb], writes=[qT_b])
                pk, pkb = next_ps(6)
                for g in range(2):
                    S.op("pe", lambda e, pk=pk, qr=qr, g=g: e.transpose(pk[0:64, g * 128:(g + 1) * 128], qr[:, 256 + g * 64:256 + (g + 1) * 64], ident[:]), reads=[qrb, ident_b], writes=[pkb])
                S.op("act", lambda e, pk=pk, tb=tb: e.copy(out=kT[:, :, tb * 128:(tb + 1) * 128], in_=pk[0:64, 0:256].rearrange("p (h t) -> p h t", h=2)), reads=[pkb], writes=[kT_b])
            for h in range(4):
                g = h // 2
                segs = [(0, 256, [("loc", 0, None), ("loc", 128, None)]),
                        (256, 512, None), (768, 512, None)]
                for (q0, nq, keys) in segs:
                    if keys is None:
                        keys = [("ctx", b, None) for b in range(4)]
                        for j in range(8):
                            off = 896 - 128 * j + (q0 - 256)
                            keys.append(("loc", 256 + 128 * j, amask[:, 2 * grp + (j % 2), off:off + nq]))
                    po, pob = ps[PS_O], ps_b[PS_O]
                    pd, pdb = ps[PS_D], ps_b[PS_D]
                    nk = len(keys)
                    def stage1(ki, keys=keys, nq=nq, h=h, g=g, q0=q0):
                        kind, kpos, mk = keys[ki]
                        pss, pssb = next_ps(6)
                        et, etb = next_e()
                        if kind == "ctx":
                            S.op("pe", lambda e, pss=pss, kpos=kpos: e.matmul(pss[:, 0:nq], lhsT=ctxkT[:, g, kpos * 128:(kpos + 1) * 128], rhs=qT[:, h, q0:q0 + nq], start=True, stop=True),
                                 reads=[ctxkT_b, qT_b], writes=[pssb])
                            S.op("act", lambda e, pss=pss, et=et: e.activation(out=et[:, 0:nq], in_=pss[:, 0:nq], func=AF.Exp, scale=0.125, bias=ctxbias[:, 0:1]),
                                 reads=[pssb, misc_b], writes=[etb])
                            vap = ctxv[:, kpos, g, :]
                            vb = ctxv_b
                        else:
                            S.op("pe", lambda e, pss=pss, kpos=kpos: e.matmul(pss[:, 0:nq], lhsT=kT[:, g, kpos:kpos + 128], rhs=qT[:, h, q0:q0 + nq], start=True, stop=True),
                                 reads=[kT_b, qT_b], writes=[pssb])
                            S.op("act", lambda e, pss=pss, et=et: e.activation(out=et[:, 0:nq], in_=pss[:, 0:nq], func=AF.Exp, scale=0.125), reads=[pssb], writes=[etb])
                            if mk is not None:
                                S.op("dve", lambda e, et=et, mk=mk: e.tensor_tensor(out=et[:, 0:nq], in0=et[:, 0:nq], in1=mk, op=ALU.mult), reads=[etb, amask_b], writes=[etb])
                            vap = vtok[:, kpos // 128, g, :]
                            vb = vtok_b
                        return et, etb, vap, vb

                    def stage2(ki, st1, nq=nq, nk=nk):
                        et, etb, vap, vb = st1
                        S.op("pe", lambda e, vap=vap, et=et: e.matmul(po[0:65, 0:nq], lhsT=vap, rhs=et[:, 0:nq], start=(ki == 0), stop=(ki == nk - 1)),
                             reads=[vb, etb], writes=[pob])

                    pend = [stage1(0)]
                    if nk > 1:
                        pend.append(stage1(1))
                    for ki in range(nk):
                        cur_ = pend.pop(0)
                        if ki + 2 < nk:
                            pend.append(stage1(ki + 2))
                        stage2(ki, cur_)
                    if grp == 0:
                        S.op("dve", lambda e, nq=nq, h=h: e.tensor_scalar(out=rrow[64:65, 0:nq], in0=po[64:65, 0:nq], scalar1=esink128[64:65, l * 4 + h:l * 4 + h + 1], scalar2=None, op0=ALU.add),
                             reads=[pob, misc_b], writes=[rrow_b])
                        S.op("dve", lambda e, nq=nq: e.reciprocal(out=rrow[64:65, 0:nq], in_=rrow[64:65, 0:nq]), reads=[rrow_b], writes=[rrow_b])
                    else:
                        S.op("dve", lambda e, nq=nq: e.reciprocal(out=rrow[64:65, 0:nq], in_=po[64:65, 0:nq]), reads=[pob], writes=[rrow_b])
                    S.op("pe", lambda e, nq=nq: e.matmul(pd[0:64, 0:nq], lhsT=ones_f[64:65, 0:64], rhs=rrow[64:65, 0:nq], start=True, stop=True), reads=[misc_b, rrow_b], writes=[pdb])
                    S.op("act", lambda e, nq=nq: e.copy(out=rden[:, 0:nq], in_=pd[0:64, 0:nq]), reads=[pdb], writes=[rden_b])
                    S.op("dve", lambda e, h=h, q0=q0, nq=nq: e.tensor_tensor(out=oT[:, h, q0:q0 + nq], in0=po[0:64, 0:nq], in1=rden[:, 0:nq], op=ALU.mult),
                         reads=[pob, rden_b], writes=[oT_b])
            wsl, wslb = next_slot()
            S.op("pool", lambda e, wsl=wsl: e.dma_start(out=wsl[0:64, 0:4096].rearrange("p (h n) -> p h n", h=4),
                                                        in_=wout_d[l, (0 if grp == 0 else 512):(256 if grp == 0 else 768), :].rearrange("(h p) n -> p h n", p=64)),
                 writes=[wslb], dma=True)
            for ti, (t0, tl, c) in enumerate(TT):
                pt, pb = next_ps(6)
                for h in range(4):
                    et, etb = next_e()
                    S.op("act", lambda e, et=et, h=h, t0=t0, tl=tl: e.activation(out=et[0:64, 0:tl], in_=oT[:, h, t0:t0 + tl], func=AF.Square), reads=[oT_b], writes=[etb])
                    S.op("pe", lambda e, pt=pt, et=et, h=h, tl=tl: e.matmul(pt[0:64, 0:tl], lhsT=ones_bf[0:64, 0:64], rhs=et[0:64, 0:tl], start=(h == 0), stop=(h == 3)),
                         reads=[ones_b, etb], writes=[pb])
                S.op("act", lambda e, pt=pt, tl=tl: e.activation(out=rden[:, 0:tl], in_=pt[0:64, 0:tl], func=AF.Sqrt, scale=1.0 / 256, bias=EPS), reads=[pb], writes=[rden_b])
                S.op("dve", lambda e, tl=tl: e.reciprocal(out=rden[:, 0:tl], in_=rden[:, 0:tl]), reads=[rden_b], writes=[rden_b])
                for h in range(4):
                    piece = (0 if grp == 0 else 8) + h
                    S.op("dve", lambda e, h=h, t0=t0, tl=tl, piece=piece: e.scalar_tensor_tensor(out=oT[:, h, t0:t0 + tl], in0=oT[:, h, t0:t0 + tl], scalar=mixgain64(l, piece),
                                                                                                in1=rden[:, 0:tl], op0=ALU.mult, op1=ALU.mult),
                         reads=[oT_b, rden_b, vec64_b], writes=[oT_b])
            wout_partial(l, 4, 64, wsl, wslb, lambda pi, t0, tl: (oT[:, pi, t0:t0 + tl], [oT_b]))

        dbg_d = dout("dbg", [8, 128, T]) if DEBUG else None

        def dump(idx, ap, bufs, n=T, np_=128):
            if not DEBUG:
                return
            S.op("pool", lambda e: e.dma_start(out=dbg_d[idx, 0:np_, 0:n], in_=ap), reads=list(bufs), dma=True, is_out=True)

        ARENA_BUFS += [gqk_b, gke_b, gqT_b, gkT_b, gktok_b, gvtok_b, gconst_b, gS_b, geb_b, godT_b, ggdT_b, amask_b]

        def fence():
            S.op("dve", lambda e: e.memset(fdummy[:], 0.0), reads=ARENA_BUFS, writes=ARENA_BUFS)

        S.op("dve", lambda e: e.memset(vin3[:], 0.0), writes=[vec3_b])
        S.op("sp", lambda e: e.dma_start(out=vin3[0:36, :], in_=hcw_d.rearrange("l t (c p) -> (l t c) p", p=128)), writes=[vec3_b], dma=True)
        S.op("sp", lambda e: e.dma_start(out=vin3[36:48, :], in_=hcb_d.rearrange("l (c p) -> (l c) p", p=128)), writes=[vec3_b], dma=True)
        S.op("sp", lambda e: e.dma_start(out=vin3[48:56, :], in_=hbias_d.rearrange("l o (c p) -> (l o c) p", p=128)), writes=[vec3_b], dma=True)
        S.op("sp", lambda e: e.dma_start(out=vin3[56:72, :], in_=mixg_d.rearrange("l (c p) -> (l c) p", p=128)), writes=[vec3_b], dma=True)
        pt, pb = next_ps()
        S.op("pe", lambda e, pt=pt: e.transpose(pt[:, 0:128], vin3[:], ident[:]), reads=[vec3_b, ident_b], writes=[pb])
        S.op("dve", lambda e, pt=pt: e.tensor_copy(out=vec3[:], in_=pt[:, 0:128]), reads=[pb], writes=[vec3_b])

        def hcw(l, tap, fc):
            o = (l * 3 + tap) * 6 + fc
            return vec3[:, o:o + 1]

        def hcb(l, fc):
            o = 36 + l * 6 + fc
            return vec3[:, o:o + 1]

        def hbias(l, o_, cc):
            o = 48 + (l * 2 + o_) * 2 + cc
            return vec3[:, o:o + 1]

        def mixgain128(l, chunk):
            o = 56 + l * 8 + chunk
            return vec3[:, o:o + 1]

        S.op("sp", lambda e: e.dma_start(out=hyflag[:], in_=hflag_d), writes=[hysc_b], dma=True)
        for l in range(2):
            for fc in range(6):
                S.op("dve", lambda e, l=l, fc=fc: e.tensor_tensor(out=hysc[:, l, fc, 0:1], in0=hcw(l, 0, fc), in1=hyflag[:, 0:1], op=ALU.mult), reads=[vec3_b, hysc_b], writes=[hysc_b])
                S.op("dve", lambda e, l=l, fc=fc: e.tensor_tensor(out=hysc[:, l, fc, 1:2], in0=hcw(l, 2, fc), in1=hyflag[:, 0:1], op=ALU.mult), reads=[vec3_b, hysc_b], writes=[hysc_b])
                S.op("dve", lambda e, l=l, fc=fc: e.tensor_tensor(out=hysc[:, l, fc, 2:3], in0=hysc[:, l, fc, 0:1], in1=hcw(l, 0, fc), op=ALU.subtract), reads=[vec3_b, hysc_b], writes=[hysc_b])
                S.op("dve", lambda e, l=l, fc=fc: e.tensor_tensor(out=hysc[:, l, fc, 3:4], in0=hysc[:, l, fc, 1:2], in1=hcw(l, 2, fc), op=ALU.subtract), reads=[vec3_b, hysc_b], writes=[hysc_b])
            for i in range(2):
                S.op("dve", lambda e, l=l, i=i: e.tensor_tensor(out=hyfb[:, l * 2 + i:l * 2 + i + 1], in0=vec64[:, 32 + l * 2 + i:33 + l * 2 + i],
                                                                  in1=vec64[:, 36 + i * 2 + l:37 + i * 2 + l], op=ALU.mult), reads=[vec64_b], writes=[hysc_b])

        PI = math.pi

        def hyena_group(l):
            fence()
            wv = win_d[l].rearrange("(k p) n -> p k n", p=128)
            wsl, wslb = ring[0], ring_b[0]
            S.op("pool", lambda e: e.dma_start(out=w3b[:], in_=hw3_d[l]), reads=[hyw_b], writes=[hyw_b], dma=True)
            S.op("sp", lambda e: e.dma_start(out=hyw12[0:33, 0:64], in_=hw1_d[l]), writes=[hyw_b], dma=True)
            S.op("sp", lambda e: e.dma_start(out=hyw12[:, 64:128], in_=hw2_d[l]), writes=[hyw_b], dma=True)
            for ti, (t0, tl, c) in enumerate(TT):
                zt, ztb = next_tmp()
                S.op("sp", lambda e, zt=zt, t0=t0, tl=tl: e.dma_start(out=zt[0:33, 0:tl], in_=hz_d[:, t0:t0 + tl]), writes=[ztb], dma=True)
                cur, curb = zt, ztb
                for i in range(2):
                    pt, pb = next_ps()
                    kk = 33 if i == 0 else 64
                    wap = hyw12[0:33, 0:64] if i == 0 else hyw12[:, 64:128]
                    S.op("pe", lambda e, pt=pt, wap=wap, cur=cur, kk=kk, tl=tl: e.matmul(pt[0:64, 0:tl], lhsT=wap, rhs=cur[0:kk, 0:tl], start=True, stop=True), reads=[hyw_b, curb], writes=[pb])
                    a1, a1b = next_tmp()
                    S.op("dve", lambda e, pt=pt, a1=a1, i=i, tl=tl: e.tensor_scalar(out=a1[0:64, 0:tl], in0=pt[0:64, 0:tl], scalar1=vec64[:, 32 + l * 2 + i:33 + l * 2 + i],
                                                                                   scalar2=hyfb[:, l * 2 + i:l * 2 + i + 1], op0=ALU.mult, op1=ALU.add),
                         reads=[pb, vec64_b, hysc_b], writes=[a1b])
                    S.op("dve", lambda e, a1=a1, tl=tl: e.tensor_scalar(out=rden[:, 0:tl], in0=a1[0:64, 0:tl], scalar1=1.0 / (2.0 * PI), scalar2=12582912.0, op0=ALU.mult, op1=ALU.add), reads=[a1b], writes=[rden_b])
                    S.op("dve", lambda e, tl=tl: e.tensor_scalar(out=rden[:, 0:tl], in0=rden[:, 0:tl], scalar1=12582912.0, scalar2=None, op0=ALU.subtract), reads=[rden_b], writes=[rden_b])
                    S.op("dve", lambda e, a1=a1, tl=tl: e.scalar_tensor_tensor(out=a1[0:64, 0:tl], in0=rden[:, 0:tl], scalar=-2.0 * PI, in1=a1[0:64, 0:tl], op0=ALU.mult, op1=ALU.add), reads=[rden_b, a1b], writes=[a1b])
                    if i == 0:
                        S.op("act", lambda e, a1=a1, tl=tl: e.activation(out=a1[0:64, 0:tl], in_=a1[0:64, 0:tl], func=AF.Sin), reads=[a1b], writes=[a1b])
                        cur, curb = a1, a1b
                    else:
                        S.op("act", lambda e, a1=a1, t0=t0, tl=tl: e.activation(out=hyh2[:, t0:t0 + tl], in_=a1[0:64, 0:tl], func=AF.Sin), reads=[a1b], writes=[hyh2_b])
            for cc in range(2):
                for part in range(3):
                    S.op("pool", lambda e, part=part, cc=cc: e.dma_start(out=wsl[:, part * 1024:(part + 1) * 1024].rearrange("p (k n) -> p k n", k=8),
                                                                        in_=wv[:, :, 512 + (part * 2 + cc) * 128:512 + (part * 2 + cc + 1) * 128]), writes=[wslb], dma=True)
                for part in range(3):
                    fc = part * 2 + cc
                    for ti, (t0, tl, c) in enumerate(TT):
                        pt, pb = next_ps()
                        for k in range(8):
                            S.op("pe", lambda e, pt=pt, wsl=wsl, k=k, part=part, t0=t0, tl=tl: e.matmul(pt[:, 0:tl], lhsT=wsl[:, part * 1024 + k * 128:part * 1024 + (k + 1) * 128], rhs=u[:, k, t0:t0 + tl],
                                                                                                  start=(k == 0), stop=(k == 7)),
                                 reads=[wslb, u_b[ti]], writes=[pb])
                        S.op("act", lambda e, pt=pt, t0=t0, tl=tl: e.copy(out=rstd[:, t0:t0 + tl], in_=pt[:, 0:tl]), reads=[pb], writes=[rstd_b])
                    for ti, (t0, tl, c) in enumerate(TT):
                        ac, acb = next_tmp()
                        S.op("act", lambda e, ac=ac, t0=t0, tl=tl, fc=fc: e.activation(out=ac[:, 0:tl], in_=rstd[:, t0:t0 + tl], func=AF.Identity, scale=hcw(l, 1, fc), bias=hcb(l, fc)),
                             reads=[rstd_b, vec3_b], writes=[acb])
                        S.op("dve", lambda e, ac=ac, t0=t0, tl=tl, fc=fc: e.scalar_tensor_tensor(out=ac[:, 1:tl], in0=rstd[:, t0:t0 + tl - 1], scalar=hcw(l, 0, fc), in1=ac[:, 1:tl], op0=ALU.mult, op1=ALU.add),
                             reads=[rstd_b, vec3_b, acb], writes=[acb])
                        S.op("dve", lambda e, ac=ac, t0=t0, tl=tl, fc=fc: e.scalar_tensor_tensor(out=ac[:, 0:tl - 1], in0=rstd[:, t0 + 1:t0 + tl], scalar=hcw(l, 2, fc), in1=ac[:, 0:tl - 1], op0=ALU.mult, op1=ALU.add),
                             reads=[rstd_b, vec3_b, acb], writes=[acb])
                        fix = []
                        if t0 == 256:
                            fix = [(512 - t0, 511, 2), (511 - t0, 512, 3), (767 - t0, 768, 1)]
                        elif t0 == 768:
                            fix = [(0, 767, 0), (1024 - t0, 1023, 2), (1023 - t0, 1024, 3)]
                        for (col, src, si) in fix:
                            S.op("dve", lambda e, ac=ac, col=col, src=src, si=si, fc=fc: e.scalar_tensor_tensor(out=ac[:, col:col + 1], in0=rstd[:, src:src + 1], scalar=hysc[:, l, fc, si:si + 1],
                                                                                                              in1=ac[:, col:col + 1], op0=ALU.mult, op1=ALU.add),
                                 reads=[rstd_b, hysc_b, acb], writes=[acb])
                        if part == 0:
                            S.op("act", lambda e, ac=ac, t0=t0, tl=tl: e.copy(out=hyF[:, t0:t0 + tl], in_=ac[:, 0:tl]), reads=[acb], writes=[hyF_b])
                        elif part == 1:
                            S.op("act", lambda e, ac=ac, t0=t0, tl=tl: e.copy(out=hyx1[:, t0:t0 + tl], in_=ac[:, 0:tl]), reads=[acb], writes=[hyx1_b])
                        else:
                            S.op("act", lambda e, ac=ac, t0=t0, tl=tl: e.copy(out=hyx2[:, t0:t0 + tl], in_=ac[:, 0:tl]), reads=[acb], writes=[hyx2_b])
                if l == 0 and cc == 0:
                    dump(0, hyF[:, :], [hyF_b])
                    dump(1, hyx1[:, :], [hyx1_b])
                    dump(2, hyh2[:, :], [hyh2_b], np_=64)
                for o_ in range(2):
                    for tb0 in range(0, NTB, 4):
                        nb = min(4, NTB - tb0)
                        pt, pb = next_ps()
                        for q in range(nb):
                            tb = tb0 + q
                            S.op("pe", lambda e, pt=pt, q=q, tb=tb: e.transpose(pt[:, q * 128:(q + 1) * 128], hyF[:, tb * 128:(tb + 1) * 128], ident[:]), reads=[hyF_b, ident_b], writes=[pb])
                        S.op("act", lambda e, pt=pt, tb0=tb0, nb=nb: e.copy(out=hyvtok[:, tb0:tb0 + nb, :], in_=pt[:, 0:nb * 128].rearrange("p (b c) -> p b c", b=nb)), reads=[pb], writes=[hyvtok_b])
                    for jb in range(NTB):
                        dc, dcb = next_tmp()
                        S.op("sp", lambda e, dc=dc, jb=jb, cc=cc: e.dma_start(out=dc[:, 0:256].rearrange("p (d c) -> p d c", d=2), in_=hdec_d[jb * 128:(jb + 1) * 128, :, cc * 128:(cc + 1) * 128]),
                             writes=[dcb], dma=True)
                        pt, pb = next_ps()
                        for dr in range(2):
                            cb0 = dr * 512 + o_ * 256 + cc * 128
                            S.op("pe", lambda e, pt=pt, dr=dr, cb0=cb0, jb=jb: e.matmul(pt[:, dr * 128:(dr + 1) * 128], lhsT=hyh2[:, jb * 128:(jb + 1) * 128], rhs=w3b[:, cb0:cb0 + 128], start=True, stop=True),
                                 reads=[hyh2_b, hyw_b], writes=[pb])
                        S.op("dve", lambda e, pt=pt, dc=dc: e.tensor_tensor(out=dc[:, 0:256], in0=pt[:, 0:256], in1=dc[:, 0:256], op=ALU.mult), reads=[pb, dcb], writes=[dcb])
                        S.op("dve", lambda e, dc=dc, jb=jb: e.tensor_tensor(out=hyhp[:, jb, :], in0=dc[:, 0:128], in1=dc[:, 128:256], op=ALU.add), reads=[dcb], writes=[hyh_b])
                        S.op("dve", lambda e, dc=dc, jb=jb: e.tensor_tensor(out=hyhm[:, jb, :], in0=dc[:, 0:128], in1=dc[:, 128:256], op=ALU.subtract), reads=[dcb], writes=[hyh_b])
                    if l == 0 and cc == 0 and o_ == 0:
                        dump(3, arena[:, 6400:7680], [hyh_b])
                        dump(7, arena[:, 5120:6400], [hyvtok_b])
                    psl, pslb = ring[0], ring_b[0]
                    fws = None
                    for pair in list(range(2, 10)) + [0, 1]:
                        if pair == 0:
                            S.op("pool", lambda e, psl=psl: e.dma_start(out=psl[:, 0:1024].rearrange("p (t r) -> p t r", t=2), in_=hfwp_d.rearrange("(t p) r -> p t r", p=128)), writes=[pslb], dma=True)
                            S.op("pool", lambda e, psl=psl: e.dma_start(out=psl[:, 1024:2048].rearrange("p (r t) -> p r t", r=4), in_=hivp_d.rearrange("(r p) t -> p r t", p=128)), writes=[pslb], dma=True)
                        if pair < 2:
                            rc_re, rc_im = pair, 2 + pair
                            ntc, tb_base = 2, 0
                            lre = lambda tc, rc: psl[:, tc * 512 + rc * 128:tc * 512 + (rc + 1) * 128]
                            fb_ = pslb
                            yi = (rc_re, rc_im)
                        else:
                            pp_ = pair - 2
                            sgrp, a_ = pp_ // 2, pp_ % 2
                            rc_re, rc_im = 4 * sgrp + a_, 4 * sgrp + 2 + a_
                            if pp_ % 4 == 0:
                                half = pp_ // 4
                                fws, fwsb = ring[1 + half], ring_b[1 + half]
                                S.op("pool", lambda e, fws=fws, half=half: e.dma_start(out=fws[:, :].rearrange("p (t r) -> p t r", t=8),
                                                                                      in_=hfwg_d.rearrange("(t p) r -> p t r", p=128)[:, :, half * 1024:(half + 1) * 1024]), writes=[fwsb], dma=True)
                            ntc, tb_base = 8, 2
                            lre = lambda tc, rc, fws=fws: fws[:, tc * 1024 + (rc % 8) * 128:tc * 1024 + (rc % 8 + 1) * 128]
                            fb_ = fwsb
                            yi = (4 + rc_re, 4 + rc_im)
                        pu, pub = next_ps()
                        pk, pkb = next_ps()
                        for qi, rc in enumerate((rc_re, rc_im)):
                            for tc in range(ntc):
                                S.op("pe", lambda e, pu=pu, qi=qi, tc=tc, rc=rc, lre=lre, tb_base=tb_base, ntc=ntc: e.matmul(pu[:, qi * 128:(qi + 1) * 128], lhsT=lre(tc, rc), rhs=hyvtok[:, tb_base + tc, :],
                                                                                                                          start=(tc == 0), stop=(tc == ntc - 1)),
                                     reads=[fb_, hyvtok_b], writes=[pub])
                        for qi, rc in enumerate((rc_re, rc_im)):
                            hsrc = hyhp if qi == 0 else hyhm
                            for tc in range(ntc):
                                S.op("pe", lambda e, pk=pk, qi=qi, tc=tc, rc=rc, lre=lre, tb_base=tb_base, ntc=ntc, hsrc=hsrc: e.matmul(pk[:, qi * 128:(qi + 1) * 128], lhsT=lre(tc, rc), rhs=hsrc[:, tb_base + tc, :],
                                                                                                                                     start=(tc == 0), stop=(tc == ntc - 1)),
                                     reads=[fb_, hyh_b], writes=[pkb])
                        ut, utb = next_tmp()
                        S.op("act", lambda e, ut=ut, pu=pu: e.copy(out=ut[:, 0:256], in_=pu[:, 0:256]), reads=[pub], writes=[utb])
                        S.op("dve", lambda e, ut=ut, pk=pk: e.tensor_tensor(out=ut[:, 256:384], in0=ut[:, 0:128], in1=pk[:, 0:128], op=ALU.mult), reads=[utb, pkb], writes=[utb])
                        S.op("dve", lambda e, ut=ut, pk=pk: e.tensor_tensor(out=ut[:, 384:512], in0=ut[:, 128:256], in1=pk[:, 128:256], op=ALU.mult), reads=[utb, pkb], writes=[utb])
                        S.op("dve", lambda e, ut=ut, yi=yi: e.tensor_tensor(out=hyY[:, yi[0], :], in0=ut[:, 256:384], in1=ut[:, 384:512], op=ALU.subtract), reads=[utb], writes=[hyY_b])
                        S.op("dve", lambda e, ut=ut, pk=pk: e.tensor_tensor(out=ut[:, 256:384], in0=ut[:, 0:128], in1=pk[:, 128:256], op=ALU.mult), reads=[utb, pkb], writes=[utb])
                        S.op("dve", lambda e, ut=ut, pk=pk: e.tensor_tensor(out=ut[:, 384:512], in0=ut[:, 128:256], in1=pk[:, 0:128], op=ALU.mult), reads=[utb, pkb], writes=[utb])
                        S.op("dve", lambda e, ut=ut, yi=yi: e.tensor_tensor(out=hyY[:, yi[1], :], in0=ut[:, 256:384], in1=ut[:, 384:512], op=ALU.add), reads=[utb], writes=[hyY_b])
                    if l == 0 and cc == 0 and o_ == 0:
                        dump(4, arena[:, 8960:10240], [hyY_b])
                    ivs = []
                    for half in range(2):
                        isl, islb = ring[1 + half], ring_b[1 + half]
                        S.op("pool", lambda e, isl=isl, half=half: e.dma_start(out=isl[:, :].rearrange("p (r t) -> p r t", r=8),
                                                                              in_=hivg_d.rearrange("(r p) t -> p r t", p=128)[:, half * 8:(half + 1) * 8, :]), writes=[islb], dma=True)
                        ivs.append((isl, islb))
                    for ti, (t0, tl, c) in enumerate(TT):
                        pt, pb = next_ps()
                        if ti == 0:
                            for rc in range(4):
                                S.op("pe", lambda e, pt=pt, rc=rc: e.matmul(pt[:, 0:256], lhsT=hyY[:, rc, :], rhs=psl[:, 1024 + rc * 256:1024 + (rc + 1) * 256], start=(rc == 0), stop=(rc == 3)),
                                     reads=[hyY_b, pslb], writes=[pb])
                        else:
                            for rc in range(16):
                                isl, islb = ivs[rc // 8]
                                S.op("pe", lambda e, pt=pt, rc=rc, isl=isl, t0=t0: e.matmul(pt[:, 0:512], lhsT=hyY[:, 4 + rc, :], rhs=isl[:, (rc % 8) * 1024 + (t0 - 256):(rc % 8) * 1024 + (t0 - 256) + 512],
                                                                                          start=(rc == 0), stop=(rc == 15)),
                                     reads=[hyY_b, islb], writes=[pb])
                        if o_ == 0:
                            S.op("dve", lambda e, pt=pt, t0=t0, tl=tl, cc=cc: e.scalar_tensor_tensor(out=hyF[:, t0:t0 + tl], in0=hyF[:, t0:t0 + tl], scalar=hbias(l, 0, cc), in1=pt[:, 0:tl], op0=ALU.mult, op1=ALU.add),
                                 reads=[pb, vec3_b, hyF_b], writes=[hyF_b])
                            S.op("dve", lambda e, t0=t0, tl=tl: e.tensor_tensor(out=hyF[:, t0:t0 + tl], in0=hyF[:, t0:t0 + tl], in1=hyx1[:, t0:t0 + tl], op=ALU.mult), reads=[hyF_b, hyx1_b], writes=[hyF_b])
                            if l == 0 and cc == 0 and ti == 2:
                                dump(5, hyF[:, :], [hyF_b])
                        else:
                            tm, tmb = next_tmp()
                            dst = hyob0 if cc == 0 else hyx2
                            dstb = hyob0_b if cc == 0 else hyx2_b
                            S.op("dve", lambda e, pt=pt, tm=tm, t0=t0, tl=tl, cc=cc: e.scalar_tensor_tensor(out=tm[:, 0:tl], in0=hyF[:, t0:t0 + tl], scalar=hbias(l, 1, cc), in1=pt[:, 0:tl], op0=ALU.mult, op1=ALU.add),
                                 reads=[pb, vec3_b, hyF_b], writes=[tmb])
                            S.op("dve", lambda e, tm=tm, dst=dst, t0=t0, tl=tl: e.tensor_tensor(out=dst[:, t0:t0 + tl], in0=tm[:, 0:tl], in1=hyx2[:, t0:t0 + tl], op=ALU.mult), reads=[tmb, hyx2_b], writes=[dstb, hyx2_b])
            if l == 0:
                dump(6, hyob0[:, :], [hyob0_b])
            osrc = [hyob0, hyx2]
            osb = [hyob0_b, hyx2_b]
            wo, wob = ring[0], ring_b[0]
            _ri[0] = 1
            S.op("pool", lambda e, wo=wo: e.dma_start(out=wo[:, 0:2048].rearrange("p (h n) -> p h n", h=2), in_=wout_d[l, 256:512, :].rearrange("(h p) n -> p h n", p=128)), writes=[wob], dma=True)
            for ti, (t0, tl, c) in enumerate(TT):
                pt, pb = next_ps()
                for cc in range(2):
                    et, etb = next_e()
                    S.op("act", lambda e, et=et, cc=cc, t0=t0, tl=tl: e.activation(out=et[:, 0:tl], in_=osrc[cc][:, t0:t0 + tl], func=AF.Square), reads=[osb[cc]], writes=[etb])
                    S.op("pe", lambda e, pt=pt, et=et, cc=cc, tl=tl: e.matmul(pt[:, 0:tl], lhsT=ones_bf[:], rhs=et[:, 0:tl], start=(cc == 0), stop=(cc == 1)), reads=[ones_b, etb], writes=[pb])
                tm, tmb = next_tmp()
                S.op("act", lambda e, pt=pt, tm=tm, tl=tl: e.activation(out=tm[:, 0:tl], in_=pt[:, 0:tl], func=AF.Sqrt, scale=1.0 / 256, bias=EPS), reads=[pb], writes=[tmb])
                S.op("dve", lambda e, tm=tm, tl=tl: e.reciprocal(out=tm[:, 0:tl], in_=tm[:, 0:tl]), reads=[tmb], writes=[tmb])
                for cc in range(2):
                    S.op("dve", lambda e, tm=tm, cc=cc, t0=t0, tl=tl: e.scalar_tensor_tensor(out=osrc[cc][:, t0:t0 + tl], in0=osrc[cc][:, t0:t0 + tl], scalar=mixgain128(l, 2 + cc), in1=tm[:, 0:tl],
                                                                                            op0=ALU.mult, op1=ALU.mult),
                         reads=[osb[cc], tmb, vec3_b], writes=[osb[cc]])
            wout_partial(l, 2, 128, wo, wob, lambda pi, t0, tl: (osrc[pi][:, t0:t0 + tl], [osb[pi]]))

        def gla_group(l):
            fence()
            wv = win_d[l].rearrange("(k p) n -> p k n", p=128)
            wsl, wslb = ring[0], ring_b[0]
            S.op("pool", lambda e: e.dma_start(out=wsl[:, 0:6400].rearrange("p (k n) -> p k n", k=8), in_=wv[:, :, 1792:2592]), writes=[wslb], dma=True)
            S.op("sp", lambda e: e.dma_start(out=gtri, in_=tri_d.rearrange("a p c -> p a c")), writes=[gconst_b], dma=True)
            S.op("pool", lambda e: e.dma_start(out=ggw.rearrange("p (z c) -> p z c", z=2), in_=gw_d[l].rearrange("z r c -> r z c")), writes=[gconst_b], dma=True)
            S.op("pool", lambda e: e.dma_start(out=ggb, in_=gb_d[l].rearrange("z c -> (z c)").rearrange("(o n) -> o n", o=1)), writes=[gconst_b], dma=True)

            def wcol(k, c, n):
                return wsl[:, k * 800 + c:k * 800 + c + n]
            for ti, (t0, tl, c) in enumerate(TT if GLA_PART >= 2 else []):
                for (dst, dstb, cb, m, idx) in ((gqT, gqT_b, 0, 64, 0), (gqT, gqT_b, 64, 64, 1), (gkT, gkT_b, 128, 64, 0), (gkT, gkT_b, 192, 64, 1),
                                                (ggdT, ggdT_b, 512, 16, 0), (ggdT, ggdT_b, 528, 16, 1)):
                    pt, pb = next_ps(6)
                    for k in range(8):
                        S.op("pe", lambda e, pt=pt, k=k, cb=cb, m=m, t0=t0, tl=tl: e.matmul(pt[0:m, 0:tl], lhsT=wcol(k, cb, m), rhs=u[:, k, t0:t0 + tl], start=(k == 0), stop=(k == 7)),
                             reads=[wslb, u_b[ti]], writes=[pb])
                    S.op("act", lambda e, pt=pt, dst=dst, m=m, idx=idx, t0=t0, tl=tl: e.copy(out=dst[0:m, idx, t0:t0 + tl], in_=pt[0:m, 0:tl]), reads=[pb], writes=[dstb])
            for tb in range(NTB if GLA_PART >= 3 else 0):
                tti = 0 if tb < 2 else (1 if tb < 6 else 2)
                pt, pb = next_ps(6)
                for k in range(8):
                    S.op("pe", lambda e, pt=pt, k=k, tb=tb: e.matmul(pt[:, 0:384], lhsT=u[:, k, tb * 128:(tb + 1) * 128], rhs=wcol(k, 128, 384), start=(k == 0), stop=(k == 7)),
                         reads=[wslb, u_b[tti]], writes=[pb])
                if GLA_VAR == 1:
                    continue
                S.op("act", lambda e, pt=pt, tb=tb: e.copy(out=gktok[:, tb, :], in_=pt[:, 0:128]), reads=[pb], writes=[gktok_b])
                if GLA_VAR == 2:
                    continue
                if GLA_VAR == 3:
                    S.op("act", lambda e, pt=pt, tb=tb: e.copy(out=gvtok[:, tb, :], in_=pt[:, 128:384]), reads=[pb], writes=[gvtok_b])
                    continue
                S.op("act", lambda e, pt=pt, tb=tb: e.copy(out=gvtok[:, tb, :], in_=pt[:, 128:384]), reads=[pb], writes=[gvtok_b])
            SCALE = 32.0 ** -0.5
            gqk2 = [gqk, arena[0:64, 12928:13440]]
            gke2 = [gke, arena[:, 13440:13568]]
            geb2 = [geb, arena[0:64, 13568:14592].bitcast(F32).rearrange("p (a t) -> p a t", a=4)]
            gqk2_b = [gqk_b, Buf()]
            gke2_b = [gke_b, Buf()]
            geb2_b = [geb_b, Buf()]
            ARENA_BUFS.extend([gqk2_b[1], gke2_b[1], geb2_b[1]])

            def gla_block(z, tb):
                gebz, gebz_b = geb2[z], geb2_b[z]
                slot = 0 if tb < 2 else 1 + (tb - 2) // 2
                first = (tb % 2 == 0) if z == 0 else (tb % 2 == 1)
                last = not first
                if first:
                    for p in range(2):
                        zi = z * 2 + p
                        if tb in (0, 1):
                            S.op("dve", lambda e, zi=zi: e.memset(gS[:, zi, :], 0.0), writes=[gS_b])
                        elif (z == 0 and tb == 2) or (z == 1 and tb == 9):
                            S.op("sp", lambda e, zi=zi, p=p, z=z: e.dma_start(out=gS[:, zi, :], in_=gs0_d[l, z, 2 * p:2 * p + 2].rearrange("h d v -> (h d) v")), writes=[gS_b], dma=True)
                        else:
                            S.op("dve", lambda e, zi=zi: e.tensor_scalar(out=gS[:, zi, :], in0=gS[:, zi, :], scalar1=hyflag[0:64, 0:1], scalar2=None, op0=ALU.mult), reads=[gS_b, hysc_b], writes=[gS_b])
                        S.op("act", lambda e, zi=zi: e.copy(out=gSb[:, zi, :], in_=gS[:, zi, :]), reads=[gS_b], writes=[gS_b])
                yield
                pl, plb = next_ps(4)
                S.op("pe", lambda e, pl=pl, tb=tb, z=z: e.matmul(pl[:, 0:128], lhsT=ggdT[:, z, tb * 128:(tb + 1) * 128], rhs=ggw[:, z * 128:(z + 1) * 128], start=True, stop=False),
                     reads=[ggdT_b, gconst_b], writes=[plb])
                S.op("pe", lambda e, pl=pl, z=z: e.matmul(pl[:, 0:128], lhsT=ones_bf[0:1, 0:128], rhs=ggb[:, z * 128:(z + 1) * 128], start=False, stop=True),
                     reads=[ones_b, gconst_b], writes=[plb])
                yield
                gp, gpb = tmp[z], tmp_b[z]
                S.op("act", lambda e, pl=pl, gp=gp: e.activation(out=gp[:, 0:128], in_=pl[:, 0:128], func=AF.Exp, scale=-1.0), reads=[plb], writes=[gpb])
                S.op("act", lambda e, gp=gp: e.activation(out=gp[:, 0:128], in_=gp[:, 0:128], func=AF.Ln, bias=1.0), reads=[gpb], writes=[gpb])
                yield
                for p in range(2):
                    pc, pcb = next_ps(4)
                    S.op("pe", lambda e, pc=pc, gp=gp, p=p, z=z: e.matmul(pc[0:64, 0:128], lhsT=gp[:, p * 64:(p + 1) * 64], rhs=gtri[:, z, :], start=True, stop=True),
                         reads=[gpb, gconst_b], writes=[pcb])
                    S.op("act", lambda e, pc=pc, p=p: e.activation(out=gebz[:, 2 * p, :], in_=pc[0:64, 0:128], func=AF.Exp, scale=-1.0 / 16), reads=[pcb], writes=[gebz_b])
                    S.op("act", lambda e, pc=pc, p=p: e.activation(out=gebz[:, 2 * p + 1, :], in_=pc[0:64, 0:128], func=AF.Exp, scale=1.0 / 16), reads=[pcb], writes=[gebz_b])
                yield
                qk, qkb = gqk2[z], gqk2_b[z]
                for p in range(2):
                    S.op("dve", lambda e, qk=qk, p=p, tb=tb: e.scalar_tensor_tensor(out=qk[0:64, p * 128:(p + 1) * 128], in0=gqT[:, p, tb * 128:(tb + 1) * 128], scalar=SCALE, in1=gebz[:, 2 * p, :],
                                                                                   op0=ALU.mult, op1=ALU.mult), reads=[gqT_b, gebz_b], writes=[qkb])
                    S.op("dve", lambda e, qk=qk, p=p, tb=tb: e.tensor_tensor(out=qk[0:64, 256 + p * 128:256 + (p + 1) * 128], in0=gkT[:, p, tb * 128:(tb + 1) * 128], in1=gebz[:, 2 * p + 1, :], op=ALU.mult),
                         reads=[gkT_b, gebz_b], writes=[qkb])
                yield
                pf, pfb = next_ps(4)
                S.op("pe", lambda e, pf=pf, gp=gp, z=z: e.matmul(pf[:, 0:128], lhsT=gtri[:, 2 + z, :], rhs=gp[:, 0:128], start=True, stop=True), reads=[gpb, gconst_b], writes=[pfb])
                S.op("act", lambda e, pf=pf, gp=gp: e.activation(out=gp[:, 128:256], in_=pf[:, 0:128], func=AF.Exp, scale=-1.0 / 16), reads=[pfb], writes=[gpb])
                yield
                ke, keb = gke2[z], gke2_b[z]
                S.op("dve", lambda e, ke=ke, gp=gp, tb=tb: e.tensor_tensor(out=ke[:, 0:128], in0=gktok[:, tb, :], in1=gp[:, 128:256], op=ALU.mult), reads=[gktok_b, gpb], writes=[keb])
                yield
                pcs = [(ps[6 - 2 * z], ps_b[6 - 2 * z]), (ps[7 - 2 * z], ps_b[7 - 2 * z])]
                for h in range(4):
                    p, sidx = h // 2, h % 2
                    zi = z * 2 + p
                    pa, pab = next_ps(4)
                    S.op("pe", lambda e, pa=pa, qk=qk, p=p, sidx=sidx: e.matmul(pa[:, 0:128], lhsT=qk[32 * sidx:32 * sidx + 32, 256 + p * 128:256 + (p + 1) * 128],
                                                                               rhs=qk[32 * sidx:32 * sidx + 32, p * 128:(p + 1) * 128], start=True, stop=True),
                         reads=[qkb], writes=[pab])
                    yield
                    am, amb = next_e()
                    S.op("dve", lambda e, pa=pa, am=am, z=z: e.tensor_tensor(out=am[:, 0:128], in0=pa[:, 0:128], in1=gtri[:, z, :], op=ALU.mult), reads=[pab, gconst_b], writes=[amb])
                    yield
                    po, pob = next_ps(4)
                    S.op("pe", lambda e, po=po, am=am, h=h, tb=tb: e.matmul(po[0:64, 0:128], lhsT=gvtok[:, tb, h * 64:(h + 1) * 64], rhs=am[:, 0:128], start=True, stop=False),
                         reads=[gvtok_b, amb], writes=[pob])
                    S.op("pe", lambda e, po=po, qk=qk, zi=zi, p=p, sidx=sidx: e.matmul(po[0:64, 0:128], lhsT=gSb[32 * sidx:32 * sidx + 32, zi, :], rhs=qk[32 * sidx:32 * sidx + 32, p * 128:(p + 1) * 128],
                                                                                      start=False, stop=True),
                         reads=[gS_b, qkb], writes=[pob])
                    yield
                    if (z == 0 and tb <= 4) or (z == 1 and tb >= 5):
                        S.op("act", lambda e, po=po, h=h, tb=tb: e.copy(out=godT[:, h, tb * 128:(tb + 1) * 128], in_=po[0:64, 0:128]), reads=[pob], writes=[godT_b])
                    else:
                        S.op("dve", lambda e, po=po, h=h, tb=tb: e.tensor_tensor(out=godT[:, h, tb * 128:(tb + 1) * 128], in0=godT[:, h, tb * 128:(tb + 1) * 128], in1=po[0:64, 0:128], op=ALU.add),
                             reads=[pob, godT_b], writes=[godT_b])
                    if sidx == 1:
                        pcx, pcxb = pcs[p]
                        S.op("pe", lambda e, pcx=pcx, ke=ke, p=p, tb=tb: e.matmul(pcx[0:64, 0:128], lhsT=ke[:, p * 64:(p + 1) * 64], rhs=gvtok[:, tb, p * 128:(p + 1) * 128], start=True, stop=True),
                             reads=[keb, gvtok_b], writes=[pcxb])
                yield
                for p in range(2):
                    zi = z * 2 + p
                    pcx, pcxb = pcs[p]
                    dcol = 127 if z == 0 else 0
                    for sx in range(2):
                        S.op("dve", lambda e, pcx=pcx, zi=zi, p=p, dcol=dcol, sx=sx: e.scalar_tensor_tensor(out=gS[32 * sx:32 * sx + 32, zi, :], in0=gS[32 * sx:32 * sx + 32, zi, :],
                                                                                                           scalar=gebz[32 * sx:32 * sx + 32, 2 * p, dcol:dcol + 1], in1=pcx[32 * sx:32 * sx + 32, 64 * sx:64 * sx + 64],
                                                                                                           op0=ALU.mult, op1=ALU.add), reads=[gS_b, gebz_b, pcxb], writes=[gS_b])
                    S.op("act", lambda e, zi=zi: e.copy(out=gSb[:, zi, :], in_=gS[:, zi, :]), reads=[gS_b], writes=[gS_b])
                    if last:
                        S.op("sp", lambda e, zi=zi, p=p, z=z, slot=slot: e.dma_start(out=gout_d[l, z, slot, 2 * p:2 * p + 2].rearrange("h d v -> (h d) v"), in_=gS[:, zi, :]),
                             reads=[gS_b], dma=True, is_out=True)

            for step in range(NTB):
                gens = [gla_block(0, step), gla_block(1, NTB - 1 - step)]
                while gens:
                    for g_ in list(gens):
                        try:
                            next(g_)
                        except StopIteration:
                            gens.remove(g_)
            if l == 0:
                dump(0, arena[0:64, 0:1280], [gqT_b], np_=64)
                dump(1, arena[:, 5120:6400], [gktok_b])
                dump(2, a2[0:64, 0:1280], [godT_b], np_=64)
                dump(3, a2[0:64, 3840:5120], [godT_b], np_=64)
                dump(6, arena[:, 6400:7680], [gvtok_b])
                dump(4, arena[0:64, 11264:12288].bitcast(F32), [geb_b], n=512, np_=64)
                dump(5, arena[0:64, 10496:11008].bitcast(F32), [gS_b], n=256, np_=64)
            wo, wob = ring[1], ring_b[1]
            S.op("pool", lambda e: e.dma_start(out=wo[0:64, 0:4096].rearrange("p (h n) -> p h n", h=4), in_=wout_d[l, 768:1024, :].rearrange("(h p) n -> p h n", p=64)), writes=[wob], dma=True)
            if GLA_PART < 4:
                return
            for ti, (t0, tl, c) in enumerate(TT):
                for h in range(4):
                    et, etb = next_e()
                    S.op("act", lambda e, et=et, h=h, t0=t0, tl=tl: e.activation(out=et[0:64, 0:tl], in_=godT[:, h, t0:t0 + tl], func=AF.Square), reads=[godT_b], writes=[etb])
                    pt, pb = next_ps(6)
                    S.op("pe", lambda e, pt=pt, et=et, tl=tl: e.matmul(pt[0:64, 0:tl], lhsT=ones_bf[0:64, 0:64], rhs=et[0:64, 0:tl], start=True, stop=True), reads=[ones_b, etb], writes=[pb])
                    S.op("act", lambda e, pt=pt, tl=tl: e.activation(out=rden[:, 0:tl], in_=pt[0:64, 0:tl], func=AF.Sqrt, scale=1.0 / 64, bias=EPS), reads=[pb], writes=[rden_b])
                    S.op("dve", lambda e, tl=tl: e.reciprocal(out=rden[:, 0:tl], in_=rden[:, 0:tl]), reads=[rden_b], writes=[rden_b])
                    S.op("dve", lambda e, h=h, t0=t0, tl=tl: e.scalar_tensor_tensor(out=godT[:, h, t0:t0 + tl], in0=godT[:, h, t0:t0 + tl], scalar=mixgain64(l, 12 + h), in1=rden[:, 0:tl], op0=ALU.mult, op1=ALU.mult),
                         reads=[godT_b, rden_b, vec64_b], writes=[godT_b])
                    pr, prb = next_ps(6)
                    for k in range(8):
                        S.op("pe", lambda e, pr=pr, k=k, h=h, t0=t0, tl=tl: e.matmul(pr[0:64, 0:tl], lhsT=wcol(k, 544 + h * 64, 64), rhs=u[:, k, t0:t0 + tl], start=(k == 0), stop=(k == 7)),
                             reads=[wslb, u_b[ti]], writes=[prb])
                    tm, tmb = next_tmp()
                    S.op("act", lambda e, pr=pr, tm=tm, tl=tl: e.activation(out=tm[0:64, 0:tl], in_=pr[0:64, 0:tl], func=AF.Silu), reads=[prb], writes=[tmb])
                    S.op("dve", lambda e, tm=tm, h=h, t0=t0, tl=tl: e.tensor_tensor(out=godT[:, h, t0:t0 + tl], in0=godT[:, h, t0:t0 + tl], in1=tm[0:64, 0:tl], op=ALU.mult), reads=[godT_b, tmb], writes=[godT_b])
            _ri[0] = 2
            wout_partial(l, 4, 64, wo, wob, lambda pi, t0, tl: (godT[:, pi, t0:t0 + tl], [godT_b]))

        for l in range(NLAYERS):
            if STAGES["ffn1"]:
                S.phase = "L%d_ffn1" % l
                norm_mod(l, 0)
                ffn(l, 0)
            if STAGES["mixer"]:
                S.phase = "L%d_norm2" % l
                norm_mod(l, 1)
                if STAGES.get("A", True):
                    S.phase = "L%d_attnA" % l
                    attention_group(l, 0)
                if STAGES.get("C", True):
                    S.phase = "L%d_attnC" % l
                    attention_group(l, 1)
                if STAGES.get("B", True):
                    S.phase = "L%d_hyena" % l
                    hyena_group(l)
                if STAGES.get("D", True):
                    S.phase = "L%d_gla" % l
                    gla_group(l)
            if STAGES["ffn2"]:
                S.phase = "L%d_ffn2" % l
                norm_mod(l, 2)
                ffn(l, 1)
        S.phase = "final"

        for ti, (t0, tl, c) in enumerate(TT):
            S.op("act", lambda e, t0=t0, tl=tl: e.activation(out=u[:, :, t0:t0 + tl], in_=xres[:, :, t0:t0 + tl], func=AF.Square), reads=[xres_b[ti]], writes=[u_b[ti]])
            pt, pb = next_ps()
            for k in range(8):
                S.op("pe", lambda e, pt=pt, k=k, t0=t0, tl=tl: e.matmul(pt[:, 0:tl], lhsT=ones_bf[:], rhs=u[:, k, t0:t0 + tl], start=(k == 0), stop=(k == 7)),
                     reads=[ones_b, u_b[ti]], writes=[pb])
            S.op("act", lambda e, pt=pt, t0=t0, tl=tl: e.activation(out=rstd[:, t0:t0 + tl], in_=pt[:, 0:tl], func=AF.Sqrt, scale=1.0 / D, bias=EPS), reads=[pb], writes=[rstd_b])
            S.op("dve", lambda e, t0=t0, tl=tl: e.reciprocal(out=rstd[:, t0:t0 + tl], in_=rstd[:, t0:t0 + tl]), reads=[rstd_b], writes=[rstd_b])
            for k in range(8):
                S.op("dve", lambda e, k=k, t0=t0, tl=tl: e.scalar_tensor_tensor(out=xres[:, k, t0:t0 + tl], in0=xres[:, k, t0:t0 + tl], scalar=finalgT[:, k:k + 1],
                                                                               in1=rstd[:, t0:t0 + tl], op0=ALU.mult, op1=ALU.mult),
                     reads=[xres_b[ti], rstd_b, vecs_b[1]], writes=[xres_b[ti]])
        for tb in range(NTB):
            sg, sgb = stg[tb % 2], stg_b[tb % 2]
            tti = 0 if tb < 2 else (1 if tb < 6 else 2)
            for half in range(2):
                pt, pb = next_ps()
                for q in range(4):
                    k = half * 4 + q
                    S.op("pe", lambda e, pt=pt, k=k, q=q, tb=tb: e.transpose(pt[:, q * 128:(q + 1) * 128], xres[:, k, tb * 128:(tb + 1) * 128], ident[:]),
                         reads=[xres_b[tti], ident_b], writes=[pb])
                if half == 0:
                    S.op("act", lambda e, pt=pt, sg=sg, half=half: e.copy(out=sg[:, half * 512:(half + 1) * 512], in_=pt[:, :]), reads=[pb], writes=[sgb])
                else:
                    S.op("dve", lambda e, pt=pt, sg=sg, half=half: e.tensor_copy(out=sg[:, half * 512:(half + 1) * 512], in_=pt[:, :]), reads=[pb], writes=[sgb])
            S.op("sp", lambda e, sg=sg, tb=tb: e.dma_start(out=y_d[tb * 128:(tb + 1) * 128, :], in_=sg[:]), reads=[sgb], dma=True, is_out=True)

        S.emit(st)
    return nc


N_CORES = 8


def core_tokens(c):
    if c < 2:
        return [30 + c], c
    base = 5 * (c - 2)
    return [base + i for i in range(5)], None


def rope_tables():
    rows = 1024 // 64
    r = np.repeat(np.arange(rows, dtype=np.float32), 64)
    col = np.tile(np.arange(64, dtype=np.float32), rows)
    nf = 16
    inv = (10000.0 ** (-np.arange(nf, dtype=np.float32) / nf)).astype(np.float32)
    ang = np.concatenate([r[:, None] * inv, col[:, None] * inv], axis=-1).astype(np.float32)
    return np.cos(ang).astype(np.float32), np.sin(ang).astype(np.float32)


def attn_masks(is_sample):
    m = np.zeros((4, 128, 1920), np.float32)
    a = np.arange(128)[:, None]
    x = np.arange(1920)[None, :]
    if is_sample:
        band = (np.abs(x - 896 - a) <= 128).astype(np.float32)
        m[0] = band
        m[1] = band
        m[2] = 1.0
        m[3] = 1.0
    else:
        ev = ((x >= 896) & (x < 1152)).astype(np.float32) * np.ones((128, 1), np.float32)
        od = ((x >= 768) & (x < 1024)).astype(np.float32) * np.ones((128, 1), np.float32)
        m[0] = ev
        m[1] = od
        m[2] = ev
        m[3] = od
    return m.astype(NPBF16)


def hy_tables(L):
    t = np.linspace(0.0, 1.0, L, dtype=np.float32)[:, None]
    w = ((2.0 * math.pi / L) * np.arange(L, dtype=np.float32)[:, None]).astype(np.float32)
    bands = np.linspace(1e-4, 15, 16, dtype=np.float32)[None, :]
    z = np.concatenate([t, np.cos(bands * w), -np.sin(bands * w)], axis=-1).astype(np.float32)
    deltas = np.linspace(math.log(1e-2) / 1.5, math.log(1e-2) / 0.3, 256, dtype=np.float32)
    dec = np.exp(-t * np.abs(deltas)).astype(np.float32)
    dec2 = np.stack([dec, dec], 1)
    dec2[0, 1] = 0.0
    r = np.arange(2 * L)
    f = 256 * (r // 512) + (r % 256)
    is_im = (r % 512) >= 256
    th = np.pi * (f[None, :] + 0.5) * np.arange(L)[:, None].astype(np.float64) / L
    FW = np.where(is_im[None, :], -np.sin(th), np.cos(th))
    IV = FW.T / L
    return z, dec2, FW.astype(np.float32), IV.astype(np.float32)


def blockdiag4(m):
    a, b = m.shape
    o = np.zeros((4 * a, 4 * b), m.dtype)
    for i in range(4):
        o[i * a:(i + 1) * a, i * b:(i + 1) * b] = m
    return o


_NC_CACHE = {}
SHARED_KEYS = ["w_mod", "b_mod", "norm_g", "ffn_w_in", "ffn_w_out", "final_g", "w_in", "w_out", "mix_g", "swa_sink", "qk_norm_g",
               "hy_conv_w", "hy_conv_b", "hy_w1", "hy_b1", "hy_w2", "hy_b2", "hy_w3", "hy_freq", "hy_bias", "gla_gate_w", "gla_gate_b"]


def kernel(**inp):
    f32 = np.float32
    x_prompt = np.asarray(inp["x_prompt"], f32)
    x_sample = np.asarray(inp["x_sample"], f32)
    c = np.asarray(inp["c"], f32)
    c_ctx = np.asarray(inp["c_ctx"], f32)
    if "nc" not in _NC_CACHE:
        _NC_CACHE["nc"] = build_program()
    nc = _NC_CACHE["nc"]
    shared = {k: np.ascontiguousarray(inp[k], f32) for k in SHARED_KEYS}
    shared["ident_in"] = np.eye(128, dtype=f32)
    cos_s, sin_s = rope_tables()
    caches = [np.asarray(inp[k], f32) for k in ("cache_swa_k", "cache_swa_v", "cache_gqa_k", "cache_gqa_v")]
    z_p, dec_p, fw_p, iv_p = hy_tables(256)
    z_s, dec_s, fw_s, iv_s = hy_tables(1024)
    shared["hy_fw_p"] = fw_p.astype(NPBF16)
    r_ = np.arange(128)[:, None]
    c_ = np.arange(128)[None, :]
    shared["tri_in"] = np.stack([r_ <= c_, r_ >= c_, r_ > c_, r_ < c_], 0).astype(f32)
    state_gla = np.asarray(inp["state_gla"], f32)
    shared["hy_iv_p"] = iv_p.astype(NPBF16)
    hy_prompt = dict(hy_zT=np.ascontiguousarray(np.concatenate([z_p] * 5, 0).T), hy_dec=np.ascontiguousarray(np.concatenate([dec_p] * 5, 0)),
                     hy_fw_g=blockdiag4(fw_p).astype(NPBF16), hy_iv_g=blockdiag4(iv_p).astype(NPBF16), hy_flag=np.zeros((128, 1), f32))
    hy_sample = dict(hy_zT=np.ascontiguousarray(np.concatenate([z_p, z_s], 0).T), hy_dec=np.ascontiguousarray(np.concatenate([dec_p, dec_s], 0)),
                     hy_fw_g=fw_s.astype(NPBF16), hy_iv_g=iv_s.astype(NPBF16), hy_flag=np.ones((128, 1), f32))
    in_maps = []
    for core in range(N_CORES):
        pids, sid = core_tokens(core)
        m = dict(shared)
        cos = np.ones((T, 32), f32)
        sin = np.zeros((T, 32), f32)
        if sid is None:
            xs = np.concatenate([x_prompt[p] for p in pids], 0)
            cond = np.stack([c_ctx, c_ctx], 0)
            m["ctx_kv"] = np.zeros((4, 2, 512, 128), f32)
            m["ctx_bias"] = np.full((128, 1), -30000.0, f32)
            m["gla_s0"] = np.zeros((2, 2, 4, 32, 64), f32)
        else:
            xs = np.concatenate([x_prompt[pids[0]], x_sample[sid]], 0)
            cond = np.stack([c_ctx, c[sid]], 0)
            cos[256:] = cos_s
            sin[256:] = sin_s
            m["ctx_kv"] = np.ascontiguousarray(np.stack([cc[sid].reshape(2, 512, 128) for cc in caches], 0), f32)
            m["ctx_bias"] = np.zeros((128, 1), f32)
            m["gla_s0"] = np.ascontiguousarray(state_gla[sid], f32)
        m["attn_mask"] = attn_masks(sid is not None)
        m.update(hy_sample if sid is not None else hy_prompt)
        m["rope_cos"] = cos
        m["rope_sin"] = sin
        m["x_in"] = np.ascontiguousarray(xs, f32)
        m["cond_in"] = np.ascontiguousarray(cond, f32)
        in_maps.append(m)
    if ONE_CORE:
        res = run_bass_kernel_spmd(nc, in_maps[2:3], core_ids=[0])
        LAST["outs"] = res.results
        return None
    res = run_bass_kernel_spmd(nc, in_maps, core_ids=list(range(N_CORES)))
    outs = res.results
    LAST["outs"] = outs
    B, SEQ = x_prompt.shape[0], x_prompt.shape[1]
    y_prompt = np.zeros((B, SEQ, D), f32)
    y_sample = np.zeros(x_sample.shape, f32)
    kvs = [np.zeros((B, 2, SEQ, 2, 64), f32) for _ in range(4)]
    new_state = np.zeros((B, 2, 2, 4, 32, 64), f32)
    for core in range(N_CORES):
        pids, sid = core_tokens(core)
        y = np.asarray(outs[core]["y"], f32)
        kvo = np.asarray(outs[core]["kv_out"], f32)
        gso = np.asarray(outs[core]["gla_out"], f32)
        if sid is not None:
            y_sample[sid] = y[256:]
        for i, p in enumerate(pids):
            y_prompt[p] = y[i * 256:(i + 1) * 256]
            for a in range(4):
                kvs[a][p] = kvo[a, :, i * 256:(i + 1) * 256, :].reshape(2, SEQ, 2, 64)
            new_state[p] = gso[:, :, i]
    return (y_prompt, y_sample, kvs[0], kvs[1], kvs[2], kvs[3], new_state)
```

```python
import math
from contextlib import ExitStack
import numpy as np
import ml_dtypes
import concourse.bass as bass
import concourse.mybir as mybir
from concourse.bass_utils import run_bass_kernel_spmd

F32 = mybir.dt.float32
BF16 = mybir.dt.bfloat16
AF = mybir.ActivationFunctionType
ALU = mybir.AluOpType
AX = mybir.AxisListType
NPBF16 = ml_dtypes.bfloat16

STAGES = {"ffn1": True, "mixer": True, "ffn2": True, "A": True, "C": True, "B": True, "D": True}
NLAYERS = 2
DEBUG = False
PROFILE_SCOPES = False
PROFILE_ENGINE = "pe"
GLA_STEPS = 99
GLA_PART = 9
GLA_VAR = 0
ONE_CORE = False
LAST = {}

ENGS = ("pe", "act", "dve", "pool", "sp")
DMA_NSEM = {"sp": 12, "act": 4, "pool": 12}


class Buf:
    __slots__ = ("name", "w", "r")

    def __init__(self, name=""):
        self.name = name
        self.w = None
        self.r = []


class Op:
    __slots__ = ("eng", "idx", "fn", "deps", "signal", "dma", "dsem", "dval", "cnt", "phase")

    def __init__(self, eng, idx, fn, dma):
        self.eng = eng
        self.idx = idx
        self.fn = fn
        self.deps = []
        self.signal = False
        self.dma = dma
        self.dsem = None
        self.dval = 0
        self.cnt = 0


class Sched:
    def __init__(self, nc, same_engine_sync=True):
        self.nc = nc
        self.ops = {e: [] for e in ENGS}
        self.ndma = {e: 0 for e in DMA_NSEM}
        self.dma_ops = {e: [] for e in DMA_NSEM}
        self.same = same_engine_sync
        self.out_dmas = []
        self.phase = None

    def op(self, eng, fn, reads=(), writes=(), dma=False, is_out=False):
        lst = self.ops[eng]
        o = Op(eng, len(lst), fn, dma)
        o.phase = self.phase
        deps = {}
        for b in reads:
            if b.w is not None:
                deps[id(b.w)] = b.w
        for b in writes:
            if b.w is not None:
                deps[id(b.w)] = b.w
            for r in b.r:
                deps[id(r)] = r
        if dma:
            j = self.ndma[eng]
            k = DMA_NSEM[eng]
            o.dsem = (eng, j % k)
            o.dval = 16 * (j // k + 1)
            if j >= k:
                p = self.dma_ops[eng][j - k]
                deps[id(p)] = p
            self.ndma[eng] += 1
            self.dma_ops[eng].append(o)
            if is_out:
                self.out_dmas.append(o)
        best = {}
        for d in deps.values():
            if d is o:
                continue
            if d.dma:
                o.deps.append(d)
                continue
            if d.eng == eng and (eng in ("pe", "sp") or not self.same):
                continue
            if d.eng not in best or best[d.eng].idx < d.idx:
                best[d.eng] = d
        for d in best.values():
            o.deps.append(d)
            d.signal = True
        for b in reads:
            if not dma:
                b.r = [r for r in b.r if r.dma or r.eng != eng]
            b.r.append(o)
        for b in writes:
            b.w = o
            b.r = []
        lst.append(o)
        return o

    def emit(self, stack):
        nc = self.nc
        CH = 2000
        fin = Op("sp", len(self.ops["sp"]), None, False)
        fin.deps = list(self.out_dmas)
        fin.phase = None
        self.ops["sp"].append(fin)
        for e in ENGS:
            c = 0
            for o in self.ops[e]:
                if o.signal:
                    c += 1
                o.cnt = c
        esem = {}
        for e in ENGS:
            n = (self.ops[e][-1].cnt if self.ops[e] else 0)
            for i in range(max(1, (n + CH - 1) // CH)):
                esem[(e, i)] = stack.enter_context(nc.semaphore("es_%s%d" % (e, i)))
        dsem = {}
        for e, k in DMA_NSEM.items():
            for i in range(k):
                dsem[(e, i)] = stack.enter_context(nc.semaphore("ds_%s%d" % (e, i)))
        block = stack.enter_context(nc.Block())

        def run(e, engine):
            known = {}
            kn_eng = {}
            cur = [None, None]

            def set_phase(ph):
                if not PROFILE_SCOPES or ph == cur[0] or e != PROFILE_ENGINE:
                    return
                if cur[1] is not None:
                    cur[1].__exit__(None, None, None)
                    cur[1] = None
                cur[0] = ph
                if ph is not None:
                    cur[1] = nc.named_scope(ph)
                    cur[1].__enter__()

            for o in self.ops[e] + [None]:
                if o is None:
                    set_phase(None)
                    break
                set_phase(o.phase)
                need = {}
                for d in o.deps:
                    if d.dma:
                        key, val = ("d",) + d.dsem, d.dval
                        if known.get(key, 0) >= val:
                            continue
                    else:
                        if kn_eng.get(d.eng, 0) >= d.cnt:
                            continue
                        key, val = ("e", d.eng, (d.cnt - 1) // CH), (d.cnt - 1) % CH + 1
                        kn_eng[d.eng] = d.cnt
                    if need.get(key, 0) < val:
                        need[key] = val
                for key, val in need.items():
                    s = dsem[key[1:]] if key[0] == "d" else esem[key[1:]]
                    engine.wait_ge(s, val)
                    if key[0] == "d":
                        known[key] = val
                if o.fn is None:
                    continue
                ins = o.fn(engine)
                if o.dma:
                    ins.then_inc(dsem[o.dsem], 16)
                elif o.signal:
                    ins.then_inc(esem[(e, (o.cnt - 1) // CH)], 1)

        @block.tensor
        def _(eng):
            run("pe", eng)

        @block.scalar
        def _(eng):
            run("act", eng)

        @block.vector
        def _(eng):
            run("dve", eng)

        @block.gpsimd
        def _(eng):
            run("pool", eng)

        @block.sync
        def _(eng):
            run("sp", eng)


D = 1024
T = 1280
TT = [(0, 256, 0), (256, 512, 1), (768, 512, 1)]
NTB = T // 128
DFF = 2816
NHC = DFF // 128
EPS = 1e-6
RING_SLOTS = 3
SLOT_ELEMS = 8192


class Prog:
    pass


def build_program():
    nc = bass.Bass("TRN2", target_bir_lowering=False)
    P = Prog()
    st = ExitStack()
    with st:
        S = Sched(nc)

        def din(name, shape, dt=F32):
            return nc.dram_tensor(name, list(shape), dt, kind="ExternalInput").ap()

        def dout(name, shape, dt=F32):
            return nc.dram_tensor(name, list(shape), dt, kind="ExternalOutput").ap()

        _n = [0]

        def sb(shape, dt, name=None):
            _n[0] += 1
            return st.enter_context(nc.sbuf_tensor(name or ("t%d" % _n[0]), list(shape), dt))

        x_d = din("x_in", [T, D])
        cond_d = din("cond_in", [2, D])
        wmod_d = din("w_mod", [2, D, 9 * D])
        bmod_d = din("b_mod", [2, 9 * D])
        normg_d = din("norm_g", [2, 3, D])
        fwin_d = din("ffn_w_in", [2, 2, D, 2 * DFF])
        fwout_d = din("ffn_w_out", [2, 2, DFF, D])
        finalg_d = din("final_g", [D])
        ident_d = din("ident_in", [128, 128])
        y_d = dout("y", [T, D])
        win_d = din("w_in", [2, D, 2592])
        wout_d = din("w_out", [2, D, D])
        mixg_d = din("mix_g", [2, D])
        sink_d = din("swa_sink", [2, 4])
        qkg_d = din("qk_norm_g", [2, 2, 64])
        cos_d = din("rope_cos", [T, 32])
        sin_d = din("rope_sin", [T, 32])
        ctx_d = din("ctx_kv", [4, 2, 512, 128])
        ctxbias_d = din("ctx_bias", [128, 1])
        amask_d = din("attn_mask", [4, 128, 1920], BF16)
        kv_d = dout("kv_out", [4, 2, T, 128])
        hcw_d = din("hy_conv_w", [2, 3, 768])
        hcb_d = din("hy_conv_b", [2, 768])
        hw1_d = din("hy_w1", [2, 33, 64])
        hb1_d = din("hy_b1", [2, 64])
        hw2_d = din("hy_w2", [2, 64, 64])
        hb2_d = din("hy_b2", [2, 64])
        hw3_d = din("hy_w3", [2, 64, 1024])
        hfr_d = din("hy_freq", [2, 2, 64])
        hbias_d = din("hy_bias", [2, 2, 256])
        hz_d = din("hy_zT", [33, T])
        hdec_d = din("hy_dec", [T, 2, 256])
        hfwg_d = din("hy_fw_g", [1024, 2048], BF16)
        hivg_d = din("hy_iv_g", [2048, 1024], BF16)
        hfwp_d = din("hy_fw_p", [256, 512], BF16)
        hivp_d = din("hy_iv_p", [512, 256], BF16)
        hflag_d = din("hy_flag", [128, 1])
        gw_d = din("gla_gate_w", [2, 2, 16, 128])
        gb_d = din("gla_gate_b", [2, 2, 128])
        gs0_d = din("gla_s0", [2, 2, 4, 32, 64])
        tri_d = din("tri_in", [4, 128, 128])
        gout_d = dout("gla_out", [2, 2, 5, 4, 32, 64])

        xres = sb([128, 8, T], F32, "xres")
        u = sb([128, 8, T], BF16, "u")
        hid = sb([128, 12, T], BF16, "hid")
        ring = [sb([128, SLOT_ELEMS], BF16, "ring%d" % i) for i in range(RING_SLOTS)]
        ring_b = [Buf("ring%d" % i) for i in range(RING_SLOTS)]
        stg = [sb([128, D], F32, "stg%d" % i) for i in range(2)]
        stg_b = [Buf() for _ in range(2)]
        tmp = [sb([128, 512], F32, "tmp%d" % i) for i in range(3)]
        tmp_b = [Buf() for _ in range(3)]
        rstd = sb([128, T], F32, "rstd")
        rstd_b = Buf()
        ident = sb([128, 128], F32, "ident")
        ident_b = Buf()
        ones_bf = sb([128, 128], BF16, "ones")
        ones_b = Buf()
        vecs_in = [sb([128, 128], F32, "vin%d" % i) for i in range(2)]
        vecs = [sb([128, 128], F32, "vec%d" % i) for i in range(2)]
        vecs_b = [Buf() for _ in range(2)]
        vin_b = [Buf() for _ in range(2)]
        condT = sb([128, 8, 2], BF16, "condT")
        condT_b = Buf()
        mod = sb([128, 2, 72, 2], F32, "mod")
        mod_b = Buf()
        modA = sb([128, 2, 3, 8, 2], F32, "modA")
        modG = sb([128, 2, 3, 8, 2], F32, "modG")
        modA_b = Buf()
        xres_b = [Buf("xres%d" % i) for i in range(3)]
        u_b = [Buf("u%d" % i) for i in range(3)]
        hid_b = [[Buf() for _ in range(3)] for _ in range(12)]

        arena = hid[:].rearrange("p a b -> p (a b)")
        qT = arena[0:64, 0:5120].rearrange("p (h t) -> p h t", h=4)
        kT = arena[0:64, 5120:7680].rearrange("p (h t) -> p h t", h=2)
        vtok = arena[:, 7680:8980].rearrange("p (b g f) -> p b g f", b=NTB, g=2)
        oT = arena[0:64, 8980:14100].rearrange("p (h t) -> p h t", h=4)
        qT_b, kT_b, vtok_b, oT_b = Buf("qT"), Buf("kT"), Buf("vtok"), Buf("oT")
        amask = sb([128, 4, 1920], BF16, "amask")
        amask_b = Buf()
        ropec = sb([128, NTB, 32], F32, "ropec")
        ropes = sb([128, NTB, 32], F32, "ropes")
        rope_b = Buf()
        gq = sb([128, 2, 6, 64], F32, "gq")
        gq_b = Buf()
        ctxbias = sb([128, 1], F32, "ctxbias")
        esink = sb([64, 8], F32, "esink")
        misc_b = Buf()
        ctxkT = sb([64, 2, 512], BF16, "ctxkT")
        ctxv = sb([128, 4, 2, 65], BF16, "ctxv")
        esink128 = sb([128, 8], F32, "esink128")
        ones_f = sb([128, 64], F32, "ones_f")
        rrow = sb([128, 512], F32, "rrow")
        rrow_b = Buf()
        ctxkT_b, ctxv_b = Buf(), Buf()
        kvst = [sb([128, 2, 128], F32, "kvst%d" % i) for i in range(2)]
        kvst_b = [Buf() for _ in range(2)]
        qkn = [sb([128, 384], F32, "qkn%d" % i) for i in range(2)]
        qkn_b = [Buf() for _ in range(2)]
        qkr = [sb([128, 384], F32, "qkr%d" % i) for i in range(2)]
        qkr_b = [Buf() for _ in range(2)]
        rtmp2 = [sb([128, 192], F32, "rtmp%d" % i) for i in range(2)]
        rtmp2_b = [Buf() for _ in range(2)]
        ssq2 = [sb([128, 8], F32, "ssq%d" % i) for i in range(2)]
        ssq2_b = [Buf() for _ in range(2)]
        etile = [sb([128, 512], BF16, "etile%d" % i) for i in range(4)]
        etile_b = [Buf() for _ in range(4)]
        rden = sb([64, 512], F32, "rden")
        rden_b = Buf()
        vin64 = sb([128, 64], F32, "vin64")
        vec64 = sb([64, 128], F32, "vec64")
        vec64_b = Buf()

        hyF = arena[:, 0:2560].bitcast(F32)
        hyx1 = arena[:, 2560:3840]
        hyx2 = arena[:, 3840:5120]
        hyvtok = arena[:, 5120:6400].rearrange("p (b c) -> p b c", b=NTB)
        hyhp = arena[:, 6400:7680].rearrange("p (b c) -> p b c", b=NTB)
        hyhm = arena[:, 7680:8960].rearrange("p (b c) -> p b c", b=NTB)
        hyY = arena[:, 8960:11520].rearrange("p (r c) -> p r c", r=20)
        hyob0 = arena[:, 11520:12800]
        hyh2 = arena[0:64, 12800:14080]
        hyF_b, hyx1_b, hyx2_b, hyvtok_b, hyh_b, hyY_b, hyob0_b, hyh2_b = (Buf() for _ in range(8))
        ARENA_BUFS = [qT_b, kT_b, vtok_b, oT_b, hyF_b, hyx1_b, hyx2_b, hyvtok_b, hyh_b, hyY_b, hyob0_b, hyh2_b]
        gqT = arena[0:64, 0:2560].rearrange("p (a t) -> p a t", a=2)
        gkT = arena[0:64, 2560:5120].rearrange("p (a t) -> p a t", a=2)
        gktok = arena[:, 5120:6400].rearrange("p (b c) -> p b c", b=NTB)
        gvtok = arena[:, 6400:8960].rearrange("p (b c) -> p b c", b=NTB)
        gtri = arena[:, 8960:9984].bitcast(F32).rearrange("p (a c) -> p a c", a=4)
        ggw = arena[0:16, 9984:10240]
        ggb = arena[0:1, 10240:10496]
        gS = arena[0:64, 10496:11008].bitcast(F32).rearrange("p (a v) -> p a v", a=4)
        gSb = arena[0:64, 11008:11264].rearrange("p (a v) -> p a v", a=4)
        geb = arena[0:64, 11264:12288].bitcast(F32).rearrange("p (a t) -> p a t", a=4)
        gqk = arena[0:64, 12288:12800]
        gke = arena[:, 12800:12928]
        gqk_b, gke_b = Buf(), Buf()
        a2 = amask[:].rearrange("p a n -> p (a n)")
        godT = a2[0:64, 0:5120].rearrange("p (h t) -> p h t", h=4)
        ggdT = a2[0:16, 5120:7680].rearrange("p (z t) -> p z t", z=2)
        gqT_b, gkT_b, gktok_b, gvtok_b, gconst_b, gS_b, geb_b, godT_b, ggdT_b = (Buf() for _ in range(9))
        fdummy = sb([128, 1], F32, "fdummy")
        vin3 = sb([128, 128], F32, "vin3")
        vec3 = sb([128, 128], F32, "vec3")
        vec3_b = Buf()
        w3b = sb([64, 1024], BF16, "w3b")
        hyw12 = sb([64, 128], F32, "hyw12")
        hyw_b = Buf()
        hysc = sb([128, 2, 6, 4], F32, "hysc")
        hyfb = sb([64, 4], F32, "hyfb")
        hyflag = sb([128, 1], F32, "hyflag")
        hysc_b = Buf()

        ps = [st.enter_context(nc.psum_tensor("ps%d" % i, [128, 512], F32)) for i in range(8)]
        ps_b = [Buf("ps%d" % i) for i in range(8)]
        _pi = [0]

        def next_ps(n=8):
            i = _pi[0] % n
            _pi[0] += 1
            return ps[i], ps_b[i]

        _ri = [0]

        def next_slot():
            i = _ri[0] % RING_SLOTS
            _ri[0] += 1
            return ring[i], ring_b[i]

        _ti = [0]

        def next_tmp():
            i = _ti[0] % 3
            _ti[0] += 1
            return tmp[i], tmp_b[i]

        S.op("sp", lambda e: e.dma_start(out=ident[:], in_=ident_d), writes=[ident_b], dma=True)
        S.op("dve", lambda e: e.memset(ones_bf[:], 1.0), writes=[ones_b])

        S.op("dve", lambda e: e.memset(vecs_in[0][:], 0.0), writes=[vin_b[0]])
        S.op("dve", lambda e: e.memset(vecs_in[1][:], 0.0), writes=[vin_b[1]])
        S.op("sp", lambda e: e.dma_start(out=vecs_in[0][0:72, :], in_=bmod_d[0].rearrange("(c p) -> c p", p=128)), writes=[vin_b[0]], dma=True)
        S.op("sp", lambda e: e.dma_start(out=vecs_in[0][72:120, :], in_=normg_d.rearrange("l i (k p) -> (l i k) p", p=128)), writes=[vin_b[0]], dma=True)
        S.op("sp", lambda e: e.dma_start(out=vecs_in[1][0:72, :], in_=bmod_d[1].rearrange("(c p) -> c p", p=128)), writes=[vin_b[1]], dma=True)
        S.op("sp", lambda e: e.dma_start(out=vecs_in[1][72:88, :], in_=cond_d.rearrange("c (k p) -> (c k) p", p=128)), writes=[vin_b[1]], dma=True)
        S.op("sp", lambda e: e.dma_start(out=vecs_in[1][88:96, :], in_=finalg_d.rearrange("(k p) -> k p", p=128)), writes=[vin_b[1]], dma=True)
        for i in range(2):
            pt, pb = next_ps()
            S.op("pe", lambda e, i=i, pt=pt: e.transpose(pt[:, 0:128], vecs_in[i][:], ident[:]), reads=[vin_b[i], ident_b], writes=[pb])
            S.op("dve", lambda e, i=i, pt=pt: e.tensor_copy(out=vecs[i][:], in_=pt[:, 0:128]), reads=[pb], writes=[vecs_b[i]])

        def bmodT(l):
            return vecs[l][:, 0:72]

        def normgT(l, i):
            o = 72 + (l * 3 + i) * 8
            return vecs[0][:, o:o + 8]

        finalgT = vecs[1][:, 88:96]
        S.op("act", lambda e: e.activation(out=condT[:].rearrange("p k c -> p c k"), in_=vecs[1][:, 72:88].rearrange("p (c k) -> p c k", c=2), func=AF.Silu),
             reads=[vecs_b[1]], writes=[condT_b])

        S.phase = "load_x"
        for tb in range(NTB):
            sg, sgb = stg[tb % 2], stg_b[tb % 2]
            S.op("sp", lambda e, sg=sg, tb=tb: e.dma_start(out=sg[:], in_=x_d[tb * 128:(tb + 1) * 128, :]), writes=[sgb], dma=True)
            tti = 0 if tb < 2 else (1 if tb < 6 else 2)
            for half in range(2):
                pt, pb = next_ps()
                for q in range(4):
                    k = half * 4 + q
                    S.op("pe", lambda e, pt=pt, sg=sg, k=k, q=q: e.transpose(pt[:, q * 128:(q + 1) * 128], sg[:, k * 128:(k + 1) * 128], ident[:]),
                         reads=[sgb, ident_b], writes=[pb])
                eng = "act" if half == 0 else "dve"
                if eng == "act":
                    S.op("act", lambda e, pt=pt, half=half, tb=tb: e.copy(out=xres[:, half * 4:half * 4 + 4, tb * 128:(tb + 1) * 128], in_=pt[:, :].rearrange("p (a b) -> p a b", a=4)),
                         reads=[pb], writes=[xres_b[tti]])
                else:
                    S.op("dve", lambda e, pt=pt, half=half, tb=tb: e.tensor_copy(out=xres[:, half * 4:half * 4 + 4, tb * 128:(tb + 1) * 128], in_=pt[:, :].rearrange("p (a b) -> p a b", a=4)),
                         reads=[pb], writes=[xres_b[tti]])

        S.phase = "modulation"
        for l in range(NLAYERS):
            for jb in range(9):
                sl, slb = next_slot()
                S.op("pool", lambda e, sl=sl, l=l, jb=jb: e.dma_start(out=sl[:, :].rearrange("p (k n) -> p k n", k=8),
                                                                     in_=wmod_d[l].rearrange("(k p) n -> p k n", p=128)[:, :, jb * 1024:(jb + 1) * 1024]),
                     writes=[slb], dma=True)
                pt, pb = next_ps()
                for cc in range(8):
                    for k in range(8):
                        S.op("pe", lambda e, pt=pt, sl=sl, cc=cc, k=k: e.matmul(pt[:, 2 * cc:2 * cc + 2], lhsT=sl[:, k * 1024 + cc * 128:k * 1024 + (cc + 1) * 128],
                                                                                 rhs=condT[:, k, :], start=(k == 0), stop=(k == 7)),
                             reads=[slb, condT_b], writes=[pb])
                S.op("dve", lambda e, pt=pt, l=l, jb=jb: e.tensor_tensor(out=mod[:, l, jb * 8:(jb + 1) * 8, :], in0=pt[:, 0:16].rearrange("p (a c) -> p a c", c=2),
                                                                         in1=bmodT(l)[:, jb * 8:(jb + 1) * 8].unsqueeze(2).broadcast_to([128, 8, 2]), op=ALU.add),
                     reads=[pb, vecs_b[l]], writes=[mod_b])
            for i in range(3):
                S.op("dve", lambda e, l=l, i=i: e.tensor_scalar(out=modA[:, l, i], in0=mod[:, l, (3 * i + 1) * 8:(3 * i + 2) * 8, :], scalar1=1.0, scalar2=None, op0=ALU.add),
                     reads=[mod_b], writes=[modA_b])
                S.op("dve", lambda e, l=l, i=i: e.tensor_tensor(out=modA[:, l, i], in0=modA[:, l, i], in1=normgT(l, i).unsqueeze(2).broadcast_to([128, 8, 2]), op=ALU.mult),
                     reads=[modA_b, vecs_b[0]], writes=[modA_b])
                S.op("dve", lambda e, l=l, i=i: e.tensor_scalar(out=modG[:, l, i], in0=mod[:, l, (3 * i + 2) * 8:(3 * i + 3) * 8, :], scalar1=(1.0 if i == 1 else 0.5), scalar2=None, op0=ALU.mult),
                     reads=[mod_b], writes=[modA_b])

        def modB(l, i, k, c):
            return mod[:, l, (3 * i) * 8 + k, c:c + 1]

        def norm_mod(l, i):
            for ti, (t0, tl, c) in enumerate(TT):
                S.op("act", lambda e, t0=t0, tl=tl: e.activation(out=u[:, :, t0:t0 + tl], in_=xres[:, :, t0:t0 + tl], func=AF.Square),
                     reads=[xres_b[ti]], writes=[u_b[ti]])
                pt, pb = next_ps()
                for k in range(8):
                    S.op("pe", lambda e, pt=pt, k=k, t0=t0, tl=tl: e.matmul(pt[:, 0:tl], lhsT=ones_bf[:], rhs=u[:, k, t0:t0 + tl], start=(k == 0), stop=(k == 7)),
                         reads=[ones_b, u_b[ti]], writes=[pb])
                S.op("act", lambda e, pt=pt, t0=t0, tl=tl: e.activation(out=rstd[:, t0:t0 + tl], in_=pt[:, 0:tl], func=AF.Sqrt, scale=1.0 / D, bias=EPS),
                     reads=[pb], writes=[rstd_b])
                S.op("dve", lambda e, t0=t0, tl=tl: e.reciprocal(out=rstd[:, t0:t0 + tl], in_=rstd[:, t0:t0 + tl]), reads=[rstd_b], writes=[rstd_b])
                for k in range(8):
                    tm, tmb = next_tmp()
                    S.op("dve", lambda e, tm=tm, k=k, t0=t0, tl=tl, c=c: e.scalar_tensor_tensor(out=tm[:, 0:tl], in0=xres[:, k, t0:t0 + tl], scalar=modA[:, l, i, k, c:c + 1],
                                                                                               in1=rstd[:, t0:t0 + tl], op0=ALU.mult, op1=ALU.mult),
                         reads=[xres_b[ti], rstd_b, modA_b], writes=[tmb])
                    S.op("act", lambda e, tm=tm, k=k, t0=t0, tl=tl, c=c: e.activation(out=u[:, k, t0:t0 + tl], in_=tm[:, 0:tl], func=AF.Identity, bias=modB(l, i, k, c), scale=1.0),
                         reads=[tmb, mod_b], writes=[u_b[ti]])

        def ffn(l, i):
            gi = 0 if i == 0 else 2
            win = fwin_d[l, i].rearrange("(k p) n -> p k n", p=128)
            wout = fwout_d[l, i].rearrange("(j p) n -> p j n", p=128)
            groups = [(0, 4), (4, 4), (8, 4), (12, 4), (16, 4), (20, 2)]
            halves = [groups[0:3], groups[3:6]]
            for hgroups in halves:
                h0 = hgroups[0][0]
                nh = sum(g[1] for g in hgroups)
                for (j0, nj) in hgroups:
                    sl, slb = next_slot()
                    S.op("pool", lambda e, sl=sl, j0=j0, nj=nj: e.dma_start(out=sl[:, 0:8 * nj * 128].rearrange("p (k n) -> p k n", k=8), in_=win[:, :, j0 * 128:(j0 + nj) * 128]),
                         writes=[slb], dma=True)
                    S.op("pool", lambda e, sl=sl, j0=j0, nj=nj: e.dma_start(out=sl[:, 4096:4096 + 8 * nj * 128].rearrange("p (k n) -> p k n", k=8),
                                                                            in_=win[:, :, DFF + j0 * 128:DFF + (j0 + nj) * 128]),
                         writes=[slb], dma=True)
                    for jj in range(nj):
                        j = j0 + jj
                        for ti, (t0, tl, c) in enumerate(TT):
                            pa, pab = next_ps()
                            pbt, pbb = next_ps()
                            for k in range(8):
                                S.op("pe", lambda e, pa=pa, sl=sl, k=k, jj=jj, nj=nj, t0=t0, tl=tl: e.matmul(pa[:, 0:tl], lhsT=sl[:, k * nj * 128 + jj * 128:k * nj * 128 + (jj + 1) * 128],
                                                                                                           rhs=u[:, k, t0:t0 + tl], start=(k == 0), stop=(k == 7)),
                                     reads=[slb, u_b[ti]], writes=[pab])
                            for k in range(8):
                                S.op("pe", lambda e, pbt=pbt, sl=sl, k=k, jj=jj, nj=nj, t0=t0, tl=tl: e.matmul(pbt[:, 0:tl], lhsT=sl[:, 4096 + k * nj * 128 + jj * 128:4096 + k * nj * 128 + (jj + 1) * 128],
                                                                                                             rhs=u[:, k, t0:t0 + tl], start=(k == 0), stop=(k == 7)),
                                     reads=[slb, u_b[ti]], writes=[pbb])
                            tm, tmb = next_tmp()
                            S.op("act", lambda e, pa=pa, tm=tm, tl=tl: e.activation(out=tm[:, 0:tl], in_=pa[:, 0:tl], func=AF.Silu), reads=[pab], writes=[tmb])
                            S.op("dve", lambda e, pbt=pbt, tm=tm, j=j, h0=h0, t0=t0, tl=tl: e.tensor_tensor(out=hid[:, j - h0, t0:t0 + tl], in0=tm[:, 0:tl], in1=pbt[:, 0:tl], op=ALU.mult),
                                 reads=[tmb, pbb], writes=[hid_b[j - h0][ti]])
                slots = []
                jj0 = 0
                while jj0 < nh:
                    n = min(8, nh - jj0)
                    sl, slb = next_slot()
                    S.op("pool", lambda e, sl=sl, jj0=jj0, n=n, h0=h0: e.dma_start(out=sl[:, 0:n * 1024].rearrange("p (j n) -> p j n", j=n), in_=wout[:, h0 + jj0:h0 + jj0 + n, :]),
                         writes=[slb], dma=True)
                    slots.append((sl, slb, jj0, n))
                    jj0 += n
                for f in range(8):
                    for ti, (t0, tl, c) in enumerate(TT):
                        pt, pb = next_ps()
                        for (sl, slb, jj0, n) in slots:
                            for q in range(n):
                                jj = jj0 + q
                                S.op("pe", lambda e, pt=pt, sl=sl, q=q, f=f, jj=jj, t0=t0, tl=tl, nh=nh: e.matmul(pt[:, 0:tl], lhsT=sl[:, q * 1024 + f * 128:q * 1024 + (f + 1) * 128],
                                                                                                                rhs=hid[:, jj, t0:t0 + tl], start=(jj == 0), stop=(jj == nh - 1)),
                                     reads=[slb, hid_b[jj][ti]], writes=[pb])
                        S.op("dve", lambda e, pt=pt, f=f, t0=t0, tl=tl, c=c: e.scalar_tensor_tensor(out=xres[:, f, t0:t0 + tl], in0=pt[:, 0:tl], scalar=modG[:, l, gi, f, c:c + 1],
                                                                                                   in1=xres[:, f, t0:t0 + tl], op0=ALU.mult, op1=ALU.add),
                             reads=[pb, modA_b, xres_b[ti]], writes=[xres_b[ti]])

        S.phase = "consts"
        S.op("sp", lambda e: e.dma_start(out=ropec[:], in_=cos_d.rearrange("(b p) f -> p b f", p=128)), writes=[rope_b], dma=True)
        S.op("sp", lambda e: e.dma_start(out=ropes[:], in_=sin_d.rearrange("(b p) f -> p b f", p=128)), writes=[rope_b], dma=True)
        for l in range(2):
            for hh in range(6):
                S.op("sp", lambda e, l=l, hh=hh: e.dma_start(out=gq[:, l, hh, :], in_=qkg_d[l, (0 if hh < 4 else 1):(1 if hh < 4 else 2), :].broadcast_to([128, 64])),
                     writes=[gq_b], dma=True)
        S.op("sp", lambda e: e.dma_start(out=ctxbias[:], in_=ctxbias_d), writes=[misc_b], dma=True)
        S.op("sp", lambda e: e.dma_start(out=esink[:], in_=sink_d.rearrange("l h -> (l h)").rearrange("(o n) -> o n", o=1).broadcast_to([64, 8])), writes=[misc_b], dma=True)
        S.op("act", lambda e: e.activation(out=esink[:], in_=esink[:], func=AF.Exp), reads=[misc_b], writes=[misc_b])
        S.op("sp", lambda e: e.dma_start(out=esink128[:], in_=sink_d.rearrange("l h -> (l h)").rearrange("(o n) -> o n", o=1).broadcast_to([128, 8])), writes=[misc_b], dma=True)
        S.op("act", lambda e: e.activation(out=esink128[:], in_=esink128[:], func=AF.Exp), reads=[misc_b], writes=[misc_b])
        S.op("dve", lambda e: e.memset(ones_f[:], 1.0), writes=[misc_b])
        S.op("dve", lambda e: e.memset(vin64[:], 0.0), writes=[vec64_b])
        S.op("sp", lambda e: e.dma_start(out=vin64[0:32, :], in_=mixg_d.rearrange("l (c p) -> (l c) p", p=64)), writes=[vec64_b], dma=True)
        S.op("sp", lambda e: e.dma_start(out=vin64[32:36, :], in_=hfr_d.rearrange("l i d -> (l i) d")), writes=[vec64_b], dma=True)
        S.op("sp", lambda e: e.dma_start(out=vin64[36:38, :], in_=hb1_d), writes=[vec64_b], dma=True)
        S.op("sp", lambda e: e.dma_start(out=vin64[38:40, :], in_=hb2_d), writes=[vec64_b], dma=True)
        pt, pb = next_ps()
        S.op("pe", lambda e, pt=pt: e.transpose(pt[0:64, 0:128], vin64[:], ident[:]), reads=[vec64_b, ident_b], writes=[pb])
        S.op("dve", lambda e, pt=pt: e.tensor_copy(out=vec64[:], in_=pt[0:64, 0:128]), reads=[pb], writes=[vec64_b])

        def mixgain64(l, piece):
            return vec64[:, l * 16 + piece:l * 16 + piece + 1]

        _ei = [0]

        def next_e():
            i = _ei[0] % 4
            _ei[0] += 1
            return etile[i], etile_b[i]

        PS_O, PS_D = 6, 7

        def wout_partial(l, pieces, ksz, wslot, wslot_b, ysrc):
            for f in range(8):
                for ti, (t0, tl, c) in enumerate(TT):
                    pt, pb = next_ps(6)
                    for pi in range(pieces):
                        yap, ybufs = ysrc(pi, t0, tl)
                        S.op("pe", lambda e, pt=pt, pi=pi, f=f, tl=tl, yap=yap: e.matmul(pt[:, 0:tl], lhsT=wslot[0:ksz, pi * 1024 + f * 128:pi * 1024 + (f + 1) * 128], rhs=yap,
                                                                                         start=(pi == 0), stop=(pi == pieces - 1)),
                             reads=[wslot_b] + ybufs, writes=[pb])
                    S.op("dve", lambda e, pt=pt, f=f, t0=t0, tl=tl, c=c: e.scalar_tensor_tensor(out=xres[:, f, t0:t0 + tl], in0=pt[:, 0:tl], scalar=modG[:, l, 1, f, c:c + 1],
                                                                                               in1=xres[:, f, t0:t0 + tl], op0=ALU.mult, op1=ALU.add),
                         reads=[pb, modA_b, xres_b[ti]], writes=[xres_b[ti]])

        def attention_group(l, grp):
            c0 = 0 if grp == 0 else 1280
            wv = win_d[l].rearrange("(k p) n -> p k n", p=128)
            fence()
            if grp == 0 or not STAGES.get("A", True):
                S.op("sp", lambda e: e.dma_start(out=amask[:], in_=amask_d.rearrange("a p n -> p a n")), writes=[amask_b], dma=True)
            sl, slb = next_slot()
            S.op("pool", lambda e, sl=sl: e.dma_start(out=sl[:, 0:4096].rearrange("p (k n) -> p k n", k=8), in_=wv[:, :, c0:c0 + 512]), writes=[slb], dma=True)
            for g_ in range(2):
                S.op("pool", lambda e, g_=g_: e.dma_start(out=ctxv[:, :, g_, 0:64], in_=ctx_d[2 * grp + 1, l].rearrange("(b p) f -> p b f", p=128)[:, :, g_ * 64:(g_ + 1) * 64]),
                     reads=[u_b[0]], writes=[ctxv_b], dma=True)
            S.op("dve", lambda e: e.memset(ctxv[:, :, :, 64:65], 1.0), writes=[ctxv_b])
            S.op("dve", lambda e: e.memset(vtok[:, :, :, 64:65], 1.0), writes=[vtok_b])
            sg, sgb = stg[0], stg_b[0]
            S.op("sp", lambda e, sg=sg: e.dma_start(out=sg[:, 0:512].rearrange("p (b f) -> p b f", b=4), in_=ctx_d[2 * grp, l].rearrange("(b p) f -> p b f", p=128)),
                 reads=[u_b[0]], writes=[sgb], dma=True)
            for g in range(2):
                pt, pb = next_ps(6)
                for b in range(4):
                    S.op("pe", lambda e, pt=pt, sg=sg, b=b, g=g: e.transpose(pt[0:64, b * 128:(b + 1) * 128], sg[:, b * 128 + g * 64:b * 128 + (g + 1) * 64], ident[:]),
                         reads=[sgb, ident_b], writes=[pb])
                S.op("act", lambda e, pt=pt, g=g: e.copy(out=ctxkT[:, g, :], in_=pt[0:64, :]), reads=[pb], writes=[ctxkT_b])
            def prep_block(tb):
                tti = 0 if tb < 2 else (1 if tb < 6 else 2)
                sq_, sqb_ = ssq2[tb % 2], ssq2_b[tb % 2]
                pp, ppb = next_ps(6)
                for k in range(8):
                    S.op("pe", lambda e, pp=pp, sl=sl, k=k, tb=tb: e.matmul(pp[:, :], lhsT=u[:, k, tb * 128:(tb + 1) * 128], rhs=sl[:, k * 512:(k + 1) * 512], start=(k == 0), stop=(k == 7)),
                         reads=[slb, u_b[tti]], writes=[ppb])
                yield
                kv, kvb = kvst[tb % 2], kvst_b[tb % 2]
                qn, qnb = qkn[tb % 2], qkn_b[tb % 2]
                qr, qrb = qkr[tb % 2], qkr_b[tb % 2]
                S.op("act", lambda e, pp=pp, tb=tb: e.copy(out=vtok[:, tb, :, 0:64], in_=pp[:, 384:512].rearrange("p (g f) -> p g f", g=2)), reads=[ppb], writes=[vtok_b])
                S.op("act", lambda e, pp=pp, kv=kv: e.copy(out=kv[:, 1, :], in_=pp[:, 384:512]), reads=[ppb], writes=[kvb])
                if grp == 0:
                    S.op("act", lambda e, pp=pp, qn=qn: e.copy(out=qn[:, :], in_=pp[:, 0:384]), reads=[ppb], writes=[qnb])
                else:
                    S.op("act", lambda e, pp=pp, qr=qr: e.activation(out=qr[:, :], in_=pp[:, 0:384], func=AF.Square), reads=[ppb], writes=[qrb])
                    yield
                    S.op("dve", lambda e, qr=qr: e.reduce_sum(out=sq_[:, 0:6], in_=qr[:, :].rearrange("p (h d) -> p h d", h=6), axis=AX.X), reads=[qrb], writes=[sqb_])
                    yield
                    S.op("act", lambda e: e.activation(out=sq_[:, 0:6], in_=sq_[:, 0:6], func=AF.Sqrt, scale=1.0 / 64, bias=EPS), reads=[sqb_], writes=[sqb_])
                    yield
                    S.op("dve", lambda e: e.reciprocal(out=sq_[:, 0:6], in_=sq_[:, 0:6]), reads=[sqb_], writes=[sqb_])
                    S.op("dve", lambda e, pp=pp, qn=qn: e.tensor_tensor(out=qn[:, :].rearrange("p (h d) -> p h d", h=6), in0=pp[:, 0:384].rearrange("p (h d) -> p h d", h=6),
                                                                        in1=sq_[:, 0:6].unsqueeze(2).broadcast_to([128, 6, 64]), op=ALU.mult),
                         reads=[ppb, sqb_], writes=[qnb])
                    S.op("dve", lambda e, qn=qn: e.tensor_tensor(out=qn[:, :], in0=qn[:, :], in1=gq[:, l].rearrange("p h d -> p (h d)"), op=ALU.mult), reads=[qnb, gq_b], writes=[qnb])
                yield
                S.op("dve", lambda e, qn=qn, kv=kv: e.tensor_copy(out=kv[:, 0, :], in_=qn[:, 256:384]), reads=[qnb], writes=[kvb])
                S.op("sp", lambda e, kv=kv, tb=tb: e.dma_start(out=kv_d[2 * grp:2 * grp + 2, l, tb * 128:(tb + 1) * 128, :].rearrange("a t f -> t a f"), in_=kv[:]),
                     reads=[kvb], dma=True, is_out=True)
                x1 = qn[:, :].rearrange("p (h two d) -> p h two d", h=6, two=2)[:, :, 0, :]
                x2 = qn[:, :].rearrange("p (h two d) -> p h two d", h=6, two=2)[:, :, 1, :]
                o1 = qr[:, :].rearrange("p (h two d) -> p h two d", h=6, two=2)[:, :, 0, :]
                o2 = qr[:, :].rearrange("p (h two d) -> p h two d", h=6, two=2)[:, :, 1, :]
                cc = ropec[:, tb, :].unsqueeze(1).broadcast_to([128, 6, 32])
                ss = ropes[:, tb, :].unsqueeze(1).broadcast_to([128, 6, 32])
                rt = rtmp2[tb % 2][:, :].rearrange("p (h d) -> p h d", h=6)
                rtb = rtmp2_b[tb % 2]
                sq_, sqb_ = ssq2[tb % 2], ssq2_b[tb % 2]
                yield
                S.op("dve", lambda e, o1=o1, x1=x1, cc=cc: e.tensor_tensor(out=o1, in0=x1, in1=cc, op=ALU.mult), reads=[qnb, rope_b], writes=[qrb])
                S.op("dve", lambda e, rt=rt, x2=x2, ss=ss: e.tensor_tensor(out=rt, in0=x2, in1=ss, op=ALU.mult), reads=[qnb, rope_b], writes=[rtb])
                yield
                S.op("dve", lambda e, o1=o1, rt=rt: e.tensor_tensor(out=o1, in0=o1, in1=rt, op=ALU.subtract), reads=[qrb, rtb], writes=[qrb])
                yield
                S.op("dve", lambda e, o2=o2, x2=x2, cc=cc: e.tensor_tensor(out=o2, in0=x2, in1=cc, op=ALU.mult), reads=[qnb, rope_b], writes=[qrb])
                S.op("dve", lambda e, rt=rt, x1=x1, ss=ss: e.tensor_tensor(out=rt, in0=x1, in1=ss, op=ALU.mult), reads=[qnb, rope_b], writes=[rtb])
                yield
                S.op("dve", lambda e, o2=o2, rt=rt: e.tensor_tensor(out=o2, in0=o2, in1=rt, op=ALU.add), reads=[qrb, rtb], writes=[qrb])
                yield
                pq, pqb = next_ps(6)
                for h in range(4):
                    S.op("pe", lambda e, pq=pq, qr=qr, h=h: e.transpose(pq[0:64, h * 128:(h + 1) * 128], qr[:, h * 64:(h + 1) * 64], ident[:]), reads=[qrb, ident_b], writes=[pqb])
                yield
                S.op("act", lambda e, pq=pq, tb=tb: e.copy(out=qT[:, :, tb * 128:(tb + 1) * 128], in_=pq[0:64, :].rearrange("p (h t) -> p h t", h=4)), reads=[pqb], writes=[qT_b])
                yield
                pk, pkb = next_ps(6)
                for g in range(2):
                    S.op("pe", lambda e, pk=pk, qr=qr, g=g: e.transpose(pk[0:64, g * 128:(g + 1) * 128], qr[:, 256 + g * 64:256 + (g + 1) * 64], ident[:]), reads=[qrb, ident_b], writes=[pkb])
                yield
                S.op("act", lambda e, pk=pk, tb=tb: e.copy(out=kT[:, :, tb * 128:(tb + 1) * 128], in_=pk[0:64, 0:256].rearrange("p (h t) -> p h t", h=2)), reads=[pkb], writes=[kT_b])

            for tb0 in range(0, NTB, 2):
                gens = [prep_block(tb0), prep_block(tb0 + 1)]
                while gens:
                    for g_ in list(gens):
                        try:
                            next(g_)
                        except StopIteration:
                            gens.remove(g_)
            for h in range(4):
                g = h // 2
                segs = [(0, 256, [("loc", 0, None), ("loc", 128, None)]),
                        (256, 512, None), (768, 512, None)]
                for (q0, nq, keys) in segs:
                    if keys is None:
                        keys = [("ctx", b, None) for b in range(4)]
                        for j in range(8):
                            off = 896 - 128 * j + (q0 - 256)
                            keys.append(("loc", 256 + 128 * j, amask[:, 2 * grp + (j % 2), off:off + nq]))
                    po, pob = ps[PS_O], ps_b[PS_O]
                    pd, pdb = ps[PS_D], ps_b[PS_D]
                    nk = len(keys)
                    def stage1(ki, keys=keys, nq=nq, h=h, g=g, q0=q0):
                        kind, kpos, mk = keys[ki]
                        pss, pssb = next_ps(6)
                        et, etb = next_e()
                        if kind == "ctx":
                            S.op("pe", lambda e, pss=pss, kpos=kpos: e.matmul(pss[:, 0:nq], lhsT=ctxkT[:, g, kpos * 128:(kpos + 1) * 128], rhs=qT[:, h, q0:q0 + nq], start=True, stop=True),
                                 reads=[ctxkT_b, qT_b], writes=[pssb])
                            S.op("act", lambda e, pss=pss, et=et: e.activation(out=et[:, 0:nq], in_=pss[:, 0:nq], func=AF.Exp, scale=0.125, bias=ctxbias[:, 0:1]),
                                 reads=[pssb, misc_b], writes=[etb])
                            vap = ctxv[:, kpos, g, :]
                            vb = ctxv_b
                        else:
                            S.op("pe", lambda e, pss=pss, kpos=kpos: e.matmul(pss[:, 0:nq], lhsT=kT[:, g, kpos:kpos + 128], rhs=qT[:, h, q0:q0 + nq], start=True, stop=True),
                                 reads=[kT_b, qT_b], writes=[pssb])
                            S.op("act", lambda e, pss=pss, et=et: e.activation(out=et[:, 0:nq], in_=pss[:, 0:nq], func=AF.Exp, scale=0.125), reads=[pssb], writes=[etb])
                            if mk is not None:
                                S.op("dve", lambda e, et=et, mk=mk: e.tensor_tensor(out=et[:, 0:nq], in0=et[:, 0:nq], in1=mk, op=ALU.mult), reads=[etb, amask_b], writes=[etb])
                            vap = vtok[:, kpos // 128, g, :]
                            vb = vtok_b
                        return et, etb, vap, vb

                    def stage2(ki, st1, nq=nq, nk=nk):
                        et, etb, vap, vb = st1
                        S.op("pe", lambda e, vap=vap, et=et: e.matmul(po[0:65, 0:nq], lhsT=vap, rhs=et[:, 0:nq], start=(ki == 0), stop=(ki == nk - 1)),
                             reads=[vb, etb], writes=[pob])

                    pend = [stage1(0)]
                    if nk > 1:
                        pend.append(stage1(1))
                    for ki in range(nk):
                        cur_ = pend.pop(0)
                        if ki + 2 < nk:
                            pend.append(stage1(ki + 2))
                        stage2(ki, cur_)
                    if grp == 0:
                        S.op("dve", lambda e, nq=nq, h=h: e.tensor_scalar(out=rrow[64:65, 0:nq], in0=po[64:65, 0:nq], scalar1=esink128[64:65, l * 4 + h:l * 4 + h + 1], scalar2=None, op0=ALU.add),
                             reads=[pob, misc_b], writes=[rrow_b])
                        S.op("dve", lambda e, nq=nq: e.reciprocal(out=rrow[64:65, 0:nq], in_=rrow[64:65, 0:nq]), reads=[rrow_b], writes=[rrow_b])
                    else:
                        S.op("dve", lambda e, nq=nq: e.reciprocal(out=rrow[64:65, 0:nq], in_=po[64:65, 0:nq]), reads=[pob], writes=[rrow_b])
                    S.op("pe", lambda e, nq=nq: e.matmul(pd[0:64, 0:nq], lhsT=ones_f[64:65, 0:64], rhs=rrow[64:65, 0:nq], start=True, stop=True), reads=[misc_b, rrow_b], writes=[pdb])
                    S.op("act", lambda e, nq=nq: e.copy(out=rden[:, 0:nq], in_=pd[0:64, 0:nq]), reads=[pdb], writes=[rden_b])
                    S.op("dve", lambda e, h=h, q0=q0, nq=nq: e.tensor_tensor(out=oT[:, h, q0:q0 + nq], in0=po[0:64, 0:nq], in1=rden[:, 0:nq], op=ALU.mult),
                         reads=[pob, rden_b], writes=[oT_b])
            wsl, wslb = next_slot()
            S.op("pool", lambda e, wsl=wsl: e.dma_start(out=wsl[0:64, 0:4096].rearrange("p (h n) -> p h n", h=4),
                                                        in_=wout_d[l, (0 if grp == 0 else 512):(256 if grp == 0 else 768), :].rearrange("(h p) n -> p h n", p=64)),
                 writes=[wslb], dma=True)
            for ti, (t0, tl, c) in enumerate(TT):
                pt, pb = next_ps(6)
                for h in range(4):
                    et, etb = next_e()
                    S.op("act", lambda e, et=et, h=h, t0=t0, tl=tl: e.activation(out=et[0:64, 0:tl], in_=oT[:, h, t0:t0 + tl], func=AF.Square), reads=[oT_b], writes=[etb])
                    S.op("pe", lambda e, pt=pt, et=et, h=h, tl=tl: e.matmul(pt[0:64, 0:tl], lhsT=ones_bf[0:64, 0:64], rhs=et[0:64, 0:tl], start=(h == 0), stop=(h == 3)),
                         reads=[ones_b, etb], writes=[pb])
                S.op("act", lambda e, pt=pt, tl=tl: e.activation(out=rden[:, 0:tl], in_=pt[0:64, 0:tl], func=AF.Sqrt, scale=1.0 / 256, bias=EPS), reads=[pb], writes=[rden_b])
                S.op("dve", lambda e, tl=tl: e.reciprocal(out=rden[:, 0:tl], in_=rden[:, 0:tl]), reads=[rden_b], writes=[rden_b])
                for h in range(4):
                    piece = (0 if grp == 0 else 8) + h
                    S.op("dve", lambda e, h=h, t0=t0, tl=tl, piece=piece: e.scalar_tensor_tensor(out=oT[:, h, t0:t0 + tl], in0=oT[:, h, t0:t0 + tl], scalar=mixgain64(l, piece),
                                                                                                in1=rden[:, 0:tl], op0=ALU.mult, op1=ALU.mult),
                         reads=[oT_b, rden_b, vec64_b], writes=[oT_b])
            wout_partial(l, 4, 64, wsl, wslb, lambda pi, t0, tl: (oT[:, pi, t0:t0 + tl], [oT_b]))

        dbg_d = dout("dbg", [8, 128, T]) if DEBUG else None

        def dump(idx, ap, bufs, n=T, np_=128):
            if not DEBUG:
                return
            S.op("pool", lambda e: e.dma_start(out=dbg_d[idx, 0:np_, 0:n], in_=ap), reads=list(bufs), dma=True, is_out=True)

        ARENA_BUFS += [gqk_b, gke_b, gqT_b, gkT_b, gktok_b, gvtok_b, gconst_b, gS_b, geb_b, godT_b, ggdT_b, amask_b]

        def fence():
            S.op("dve", lambda e: e.memset(fdummy[:], 0.0), reads=ARENA_BUFS, writes=ARENA_BUFS)

        S.op("dve", lambda e: e.memset(vin3[:], 0.0), writes=[vec3_b])
        S.op("sp", lambda e: e.dma_start(out=vin3[0:36, :], in_=hcw_d.rearrange("l t (c p) -> (l t c) p", p=128)), writes=[vec3_b], dma=True)
        S.op("sp", lambda e: e.dma_start(out=vin3[36:48, :], in_=hcb_d.rearrange("l (c p) -> (l c) p", p=128)), writes=[vec3_b], dma=True)
        S.op("sp", lambda e: e.dma_start(out=vin3[48:56, :], in_=hbias_d.rearrange("l o (c p) -> (l o c) p", p=128)), writes=[vec3_b], dma=True)
        S.op("sp", lambda e: e.dma_start(out=vin3[56:72, :], in_=mixg_d.rearrange("l (c p) -> (l c) p", p=128)), writes=[vec3_b], dma=True)
        pt, pb = next_ps()
        S.op("pe", lambda e, pt=pt: e.transpose(pt[:, 0:128], vin3[:], ident[:]), reads=[vec3_b, ident_b], writes=[pb])
        S.op("dve", lambda e, pt=pt: e.tensor_copy(out=vec3[:], in_=pt[:, 0:128]), reads=[pb], writes=[vec3_b])

        def hcw(l, tap, fc):
            o = (l * 3 + tap) * 6 + fc
            return vec3[:, o:o + 1]

        def hcb(l, fc):
            o = 36 + l * 6 + fc
            return vec3[:, o:o + 1]

        def hbias(l, o_, cc):
            o = 48 + (l * 2 + o_) * 2 + cc
            return vec3[:, o:o + 1]

        def mixgain128(l, chunk):
            o = 56 + l * 8 + chunk
            return vec3[:, o:o + 1]

        S.op("sp", lambda e: e.dma_start(out=hyflag[:], in_=hflag_d), writes=[hysc_b], dma=True)
        for l in range(2):
            for fc in range(6):
                S.op("dve", lambda e, l=l, fc=fc: e.tensor_tensor(out=hysc[:, l, fc, 0:1], in0=hcw(l, 0, fc), in1=hyflag[:, 0:1], op=ALU.mult), reads=[vec3_b, hysc_b], writes=[hysc_b])
                S.op("dve", lambda e, l=l, fc=fc: e.tensor_tensor(out=hysc[:, l, fc, 1:2], in0=hcw(l, 2, fc), in1=hyflag[:, 0:1], op=ALU.mult), reads=[vec3_b, hysc_b], writes=[hysc_b])
                S.op("dve", lambda e, l=l, fc=fc: e.tensor_tensor(out=hysc[:, l, fc, 2:3], in0=hysc[:, l, fc, 0:1], in1=hcw(l, 0, fc), op=ALU.subtract), reads=[vec3_b, hysc_b], writes=[hysc_b])
                S.op("dve", lambda e, l=l, fc=fc: e.tensor_tensor(out=hysc[:, l, fc, 3:4], in0=hysc[:, l, fc, 1:2], in1=hcw(l, 2, fc), op=ALU.subtract), reads=[vec3_b, hysc_b], writes=[hysc_b])
            for i in range(2):
                S.op("dve", lambda e, l=l, i=i: e.tensor_tensor(out=hyfb[:, l * 2 + i:l * 2 + i + 1], in0=vec64[:, 32 + l * 2 + i:33 + l * 2 + i],
                                                                  in1=vec64[:, 36 + i * 2 + l:37 + i * 2 + l], op=ALU.mult), reads=[vec64_b], writes=[hysc_b])

        PI = math.pi

        def hyena_group(l):
            fence()
            wv = win_d[l].rearrange("(k p) n -> p k n", p=128)
            wsl, wslb = ring[0], ring_b[0]
            S.op("pool", lambda e: e.dma_start(out=w3b[:], in_=hw3_d[l]), reads=[hyw_b], writes=[hyw_b], dma=True)
            S.op("sp", lambda e: e.dma_start(out=hyw12[0:33, 0:64], in_=hw1_d[l]), writes=[hyw_b], dma=True)
            S.op("sp", lambda e: e.dma_start(out=hyw12[:, 64:128], in_=hw2_d[l]), writes=[hyw_b], dma=True)
            for ti, (t0, tl, c) in enumerate(TT):
                zt, ztb = next_tmp()
                S.op("sp", lambda e, zt=zt, t0=t0, tl=tl: e.dma_start(out=zt[0:33, 0:tl], in_=hz_d[:, t0:t0 + tl]), writes=[ztb], dma=True)
                cur, curb = zt, ztb
                for i in range(2):
                    pt, pb = next_ps()
                    kk = 33 if i == 0 else 64
                    wap = hyw12[0:33, 0:64] if i == 0 else hyw12[:, 64:128]
                    S.op("pe", lambda e, pt=pt, wap=wap, cur=cur, kk=kk, tl=tl: e.matmul(pt[0:64, 0:tl], lhsT=wap, rhs=cur[0:kk, 0:tl], start=True, stop=True), reads=[hyw_b, curb], writes=[pb])
                    a1, a1b = next_tmp()
                    S.op("dve", lambda e, pt=pt, a1=a1, i=i, tl=tl: e.tensor_scalar(out=a1[0:64, 0:tl], in0=pt[0:64, 0:tl], scalar1=vec64[:, 32 + l * 2 + i:33 + l * 2 + i],
                                                                                   scalar2=hyfb[:, l * 2 + i:l * 2 + i + 1], op0=ALU.mult, op1=ALU.add),
                         reads=[pb, vec64_b, hysc_b], writes=[a1b])
                    S.op("dve", lambda e, a1=a1, tl=tl: e.tensor_scalar(out=rden[:, 0:tl], in0=a1[0:64, 0:tl], scalar1=1.0 / (2.0 * PI), scalar2=12582912.0, op0=ALU.mult, op1=ALU.add), reads=[a1b], writes=[rden_b])
                    S.op("dve", lambda e, tl=tl: e.tensor_scalar(out=rden[:, 0:tl], in0=rden[:, 0:tl], scalar1=12582912.0, scalar2=None, op0=ALU.subtract), reads=[rden_b], writes=[rden_b])
                    S.op("dve", lambda e, a1=a1, tl=tl: e.scalar_tensor_tensor(out=a1[0:64, 0:tl], in0=rden[:, 0:tl], scalar=-2.0 * PI, in1=a1[0:64, 0:tl], op0=ALU.mult, op1=ALU.add), reads=[rden_b, a1b], writes=[a1b])
                    if i == 0:
                        S.op("act", lambda e, a1=a1, tl=tl: e.activation(out=a1[0:64, 0:tl], in_=a1[0:64, 0:tl], func=AF.Sin), reads=[a1b], writes=[a1b])
                        cur, curb = a1, a1b
                    else:
                        S.op("act", lambda e, a1=a1, t0=t0, tl=tl: e.activation(out=hyh2[:, t0:t0 + tl], in_=a1[0:64, 0:tl], func=AF.Sin), reads=[a1b], writes=[hyh2_b])
            for cc in range(2):
                for part in range(3):
                    S.op("pool", lambda e, part=part, cc=cc: e.dma_start(out=wsl[:, part * 1024:(part + 1) * 1024].rearrange("p (k n) -> p k n", k=8),
                                                                        in_=wv[:, :, 512 + (part * 2 + cc) * 128:512 + (part * 2 + cc + 1) * 128]), writes=[wslb], dma=True)
                for part in range(3):
                    fc = part * 2 + cc
                    for ti, (t0, tl, c) in enumerate(TT):
                        pt, pb = next_ps()
                        for k in range(8):
                            S.op("pe", lambda e, pt=pt, wsl=wsl, k=k, part=part, t0=t0, tl=tl: e.matmul(pt[:, 0:tl], lhsT=wsl[:, part * 1024 + k * 128:part * 1024 + (k + 1) * 128], rhs=u[:, k, t0:t0 + tl],
                                                                                                  start=(k == 0), stop=(k == 7)),
                                 reads=[wslb, u_b[ti]], writes=[pb])
                        S.op("act", lambda e, pt=pt, t0=t0, tl=tl: e.copy(out=rstd[:, t0:t0 + tl], in_=pt[:, 0:tl]), reads=[pb], writes=[rstd_b])
                    for ti, (t0, tl, c) in enumerate(TT):
                        ac, acb = next_tmp()
                        S.op("act", lambda e, ac=ac, t0=t0, tl=tl, fc=fc: e.activation(out=ac[:, 0:tl], in_=rstd[:, t0:t0 + tl], func=AF.Identity, scale=hcw(l, 1, fc), bias=hcb(l, fc)),
                             reads=[rstd_b, vec3_b], writes=[acb])
                        S.op("dve", lambda e, ac=ac, t0=t0, tl=tl, fc=fc: e.scalar_tensor_tensor(out=ac[:, 1:tl], in0=rstd[:, t0:t0 + tl - 1], scalar=hcw(l, 0, fc), in1=ac[:, 1:tl], op0=ALU.mult, op1=ALU.add),
                             reads=[rstd_b, vec3_b, acb], writes=[acb])
                        S.op("dve", lambda e, ac=ac, t0=t0, tl=tl, fc=fc: e.scalar_tensor_tensor(out=ac[:, 0:tl - 1], in0=rstd[:, t0 + 1:t0 + tl], scalar=hcw(l, 2, fc), in1=ac[:, 0:tl - 1], op0=ALU.mult, op1=ALU.add),
                             reads=[rstd_b, vec3_b, acb], writes=[acb])
                        fix = []
                        if t0 == 256:
                            fix = [(512 - t0, 511, 2), (511 - t0, 512, 3), (767 - t0, 768, 1)]
                        elif t0 == 768:
                            fix = [(0, 767, 0), (1024 - t0, 1023, 2), (1023 - t0, 1024, 3)]
                        for (col, src, si) in fix:
                            S.op("dve", lambda e, ac=ac, col=col, src=src, si=si, fc=fc: e.scalar_tensor_tensor(out=ac[:, col:col + 1], in0=rstd[:, src:src + 1], scalar=hysc[:, l, fc, si:si + 1],
                                                                                                              in1=ac[:, col:col + 1], op0=ALU.mult, op1=ALU.add),
                                 reads=[rstd_b, hysc_b, acb], writes=[acb])
                        if part == 0:
                            S.op("act", lambda e, ac=ac, t0=t0, tl=tl: e.copy(out=hyF[:, t0:t0 + tl], in_=ac[:, 0:tl]), reads=[acb], writes=[hyF_b])
                        elif part == 1:
                            S.op("act", lambda e, ac=ac, t0=t0, tl=tl: e.copy(out=hyx1[:, t0:t0 + tl], in_=ac[:, 0:tl]), reads=[acb], writes=[hyx1_b])
                        else:
                            S.op("act", lambda e, ac=ac, t0=t0, tl=tl: e.copy(out=hyx2[:, t0:t0 + tl], in_=ac[:, 0:tl]), reads=[acb], writes=[hyx2_b])
                if l == 0 and cc == 0:
                    dump(0, hyF[:, :], [hyF_b])
                    dump(1, hyx1[:, :], [hyx1_b])
                    dump(2, hyh2[:, :], [hyh2_b], np_=64)
                for o_ in range(2):
                    for tb0 in range(0, NTB, 4):
                        nb = min(4, NTB - tb0)
                        pt, pb = next_ps()
                        for q in range(nb):
                            tb = tb0 + q
                            S.op("pe", lambda e, pt=pt, q=q, tb=tb: e.transpose(pt[:, q * 128:(q + 1) * 128], hyF[:, tb * 128:(tb + 1) * 128], ident[:]), reads=[hyF_b, ident_b], writes=[pb])
                        S.op("act", lambda e, pt=pt, tb0=tb0, nb=nb: e.copy(out=hyvtok[:, tb0:tb0 + nb, :], in_=pt[:, 0:nb * 128].rearrange("p (b c) -> p b c", b=nb)), reads=[pb], writes=[hyvtok_b])
                    for jb in range(NTB):
                        dc, dcb = next_tmp()
                        S.op("sp", lambda e, dc=dc, jb=jb, cc=cc: e.dma_start(out=dc[:, 0:256].rearrange("p (d c) -> p d c", d=2), in_=hdec_d[jb * 128:(jb + 1) * 128, :, cc * 128:(cc + 1) * 128]),
                             writes=[dcb], dma=True)
                        pt, pb = next_ps()
                        for dr in range(2):
                            cb0 = dr * 512 + o_ * 256 + cc * 128
                            S.op("pe", lambda e, pt=pt, dr=dr, cb0=cb0, jb=jb: e.matmul(pt[:, dr * 128:(dr + 1) * 128], lhsT=hyh2[:, jb * 128:(jb + 1) * 128], rhs=w3b[:, cb0:cb0 + 128], start=True, stop=True),
                                 reads=[hyh2_b, hyw_b], writes=[pb])
                        S.op("dve", lambda e, pt=pt, dc=dc: e.tensor_tensor(out=dc[:, 0:256], in0=pt[:, 0:256], in1=dc[:, 0:256], op=ALU.mult), reads=[pb, dcb], writes=[dcb])
                        S.op("dve", lambda e, dc=dc, jb=jb: e.tensor_tensor(out=hyhp[:, jb, :], in0=dc[:, 0:128], in1=dc[:, 128:256], op=ALU.add), reads=[dcb], writes=[hyh_b])
                        S.op("dve", lambda e, dc=dc, jb=jb: e.tensor_tensor(out=hyhm[:, jb, :], in0=dc[:, 0:128], in1=dc[:, 128:256], op=ALU.subtract), reads=[dcb], writes=[hyh_b])
                    if l == 0 and cc == 0 and o_ == 0:
                        dump(3, arena[:, 6400:7680], [hyh_b])
                        dump(7, arena[:, 5120:6400], [hyvtok_b])
                    psl, pslb = ring[0], ring_b[0]
                    fws = None
                    for pair in list(range(2, 10)) + [0, 1]:
                        if pair == 0:
                            S.op("pool", lambda e, psl=psl: e.dma_start(out=psl[:, 0:1024].rearrange("p (t r) -> p t r", t=2), in_=hfwp_d.rearrange("(t p) r -> p t r", p=128)), writes=[pslb], dma=True)
                            S.op("pool", lambda e, psl=psl: e.dma_start(out=psl[:, 1024:2048].rearrange("p (r t) -> p r t", r=4), in_=hivp_d.rearrange("(r p) t -> p r t", p=128)), writes=[pslb], dma=True)
                        if pair < 2:
                            rc_re, rc_im = pair, 2 + pair
                            ntc, tb_base = 2, 0
                            lre = lambda tc, rc: psl[:, tc * 512 + rc * 128:tc * 512 + (rc + 1) * 128]
                            fb_ = pslb
                            yi = (rc_re, rc_im)
                        else:
                            pp_ = pair - 2
                            sgrp, a_ = pp_ // 2, pp_ % 2
                            rc_re, rc_im = 4 * sgrp + a_, 4 * sgrp + 2 + a_
                            if pp_ % 4 == 0:
                                half = pp_ // 4
                                fws, fwsb = ring[1 + half], ring_b[1 + half]
                                S.op("pool", lambda e, fws=fws, half=half: e.dma_start(out=fws[:, :].rearrange("p (t r) -> p t r", t=8),
                                                                                      in_=hfwg_d.rearrange("(t p) r -> p t r", p=128)[:, :, half * 1024:(half + 1) * 1024]), writes=[fwsb], dma=True)
                            ntc, tb_base = 8, 2
                            lre = lambda tc, rc, fws=fws: fws[:, tc * 1024 + (rc % 8) * 128:tc * 1024 + (rc % 8 + 1) * 128]
                            fb_ = fwsb
                            yi = (4 + rc_re, 4 + rc_im)
                        pu, pub = next_ps()
                        pk, pkb = next_ps()
                        for qi, rc in enumerate((rc_re, rc_im)):
                            for tc in range(ntc):
                                S.op("pe", lambda e, pu=pu, qi=qi, tc=tc, rc=rc, lre=lre, tb_base=tb_base, ntc=ntc: e.matmul(pu[:, qi * 128:(qi + 1) * 128], lhsT=lre(tc, rc), rhs=hyvtok[:, tb_base + tc, :],
                                                                                                                          start=(tc == 0), stop=(tc == ntc - 1)),
                                     reads=[fb_, hyvtok_b], writes=[pub])
                        for qi, rc in enumerate((rc_re, rc_im)):
                            hsrc = hyhp if qi == 0 else hyhm
                            for tc in range(ntc):
                                S.op("pe", lambda e, pk=pk, qi=qi, tc=tc, rc=rc, lre=lre, tb_base=tb_base, ntc=ntc, hsrc=hsrc: e.matmul(pk[:, qi * 128:(qi + 1) * 128], lhsT=lre(tc, rc), rhs=hsrc[:, tb_base + tc, :],
                                                                                                                                     start=(tc == 0), stop=(tc == ntc - 1)),
                                     reads=[fb_, hyh_b], writes=[pkb])
                        ut, utb = next_tmp()
                        S.op("act", lambda e, ut=ut, pu=pu: e.copy(out=ut[:, 0:256], in_=pu[:, 0:256]), reads=[pub], writes=[utb])
                        S.op("dve", lambda e, ut=ut, pk=pk: e.tensor_tensor(out=ut[:, 256:384], in0=ut[:, 0:128], in1=pk[:, 0:128], op=ALU.mult), reads=[utb, pkb], writes=[utb])
                        S.op("dve", lambda e, ut=ut, pk=pk: e.tensor_tensor(out=ut[:, 384:512], in0=ut[:, 128:256], in1=pk[:, 128:256], op=ALU.mult), reads=[utb, pkb], writes=[utb])
                        S.op("dve", lambda e, ut=ut, yi=yi: e.tensor_tensor(out=hyY[:, yi[0], :], in0=ut[:, 256:384], in1=ut[:, 384:512], op=ALU.subtract), reads=[utb], writes=[hyY_b])
                        S.op("dve", lambda e, ut=ut, pk=pk: e.tensor_tensor(out=ut[:, 256:384], in0=ut[:, 0:128], in1=pk[:, 128:256], op=ALU.mult), reads=[utb, pkb], writes=[utb])
                        S.op("dve", lambda e, ut=ut, pk=pk: e.tensor_tensor(out=ut[:, 384:512], in0=ut[:, 128:256], in1=pk[:, 0:128], op=ALU.mult), reads=[utb, pkb], writes=[utb])
                        S.op("dve", lambda e, ut=ut, yi=yi: e.tensor_tensor(out=hyY[:, yi[1], :], in0=ut[:, 256:384], in1=ut[:, 384:512], op=ALU.add), reads=[utb], writes=[hyY_b])
                    if l == 0 and cc == 0 and o_ == 0:
                        dump(4, arena[:, 8960:10240], [hyY_b])
                    ivs = []
                    for half in range(2):
                        isl, islb = ring[1 + half], ring_b[1 + half]
                        S.op("pool", lambda e, isl=isl, half=half: e.dma_start(out=isl[:, :].rearrange("p (r t) -> p r t", r=8),
                                                                              in_=hivg_d.rearrange("(r p) t -> p r t", p=128)[:, half * 8:(half + 1) * 8, :]), writes=[islb], dma=True)
                        ivs.append((isl, islb))
                    for ti, (t0, tl, c) in enumerate(TT):
                        pt, pb = next_ps()
                        if ti == 0:
                            for rc in range(4):
                                S.op("pe", lambda e, pt=pt, rc=rc: e.matmul(pt[:, 0:256], lhsT=hyY[:, rc, :], rhs=psl[:, 1024 + rc * 256:1024 + (rc + 1) * 256], start=(rc == 0), stop=(rc == 3)),
                                     reads=[hyY_b, pslb], writes=[pb])
                        else:
                            for rc in range(16):
                                isl, islb = ivs[rc // 8]
                                S.op("pe", lambda e, pt=pt, rc=rc, isl=isl, t0=t0: e.matmul(pt[:, 0:512], lhsT=hyY[:, 4 + rc, :], rhs=isl[:, (rc % 8) * 1024 + (t0 - 256):(rc % 8) * 1024 + (t0 - 256) + 512],
                                                                                          start=(rc == 0), stop=(rc == 15)),
                                     reads=[hyY_b, islb], writes=[pb])
                        if o_ == 0:
                            S.op("dve", lambda e, pt=pt, t0=t0, tl=tl, cc=cc: e.scalar_tensor_tensor(out=hyF[:, t0:t0 + tl], in0=hyF[:, t0:t0 + tl], scalar=hbias(l, 0, cc), in1=pt[:, 0:tl], op0=ALU.mult, op1=ALU.add),
                                 reads=[pb, vec3_b, hyF_b], writes=[hyF_b])
                            S.op("dve", lambda e, t0=t0, tl=tl: e.tensor_tensor(out=hyF[:, t0:t0 + tl], in0=hyF[:, t0:t0 + tl], in1=hyx1[:, t0:t0 + tl], op=ALU.mult), reads=[hyF_b, hyx1_b], writes=[hyF_b])
                            if l == 0 and cc == 0 and ti == 2:
                                dump(5, hyF[:, :], [hyF_b])
                        else:
                            tm, tmb = next_tmp()
                            dst = hyob0 if cc == 0 else hyx2
                            dstb = hyob0_b if cc == 0 else hyx2_b
                            S.op("dve", lambda e, pt=pt, tm=tm, t0=t0, tl=tl, cc=cc: e.scalar_tensor_tensor(out=tm[:, 0:tl], in0=hyF[:, t0:t0 + tl], scalar=hbias(l, 1, cc), in1=pt[:, 0:tl], op0=ALU.mult, op1=ALU.add),
                                 reads=[pb, vec3_b, hyF_b], writes=[tmb])
                            S.op("dve", lambda e, tm=tm, dst=dst, t0=t0, tl=tl: e.tensor_tensor(out=dst[:, t0:t0 + tl], in0=tm[:, 0:tl], in1=hyx2[:, t0:t0 + tl], op=ALU.mult), reads=[tmb, hyx2_b], writes=[dstb, hyx2_b])
            if l == 0:
                dump(6, hyob0[:, :], [hyob0_b])
            osrc = [hyob0, hyx2]
            osb = [hyob0_b, hyx2_b]
            wo, wob = ring[0], ring_b[0]
            _ri[0] = 1
            S.op("pool", lambda e, wo=wo: e.dma_start(out=wo[:, 0:2048].rearrange("p (h n) -> p h n", h=2), in_=wout_d[l, 256:512, :].rearrange("(h p) n -> p h n", p=128)), writes=[wob], dma=True)
            for ti, (t0, tl, c) in enumerate(TT):
                pt, pb = next_ps()
                for cc in range(2):
                    et, etb = next_e()
                    S.op("act", lambda e, et=et, cc=cc, t0=t0, tl=tl: e.activation(out=et[:, 0:tl], in_=osrc[cc][:, t0:t0 + tl], func=AF.Square), reads=[osb[cc]], writes=[etb])
                    S.op("pe", lambda e, pt=pt, et=et, cc=cc, tl=tl: e.matmul(pt[:, 0:tl], lhsT=ones_bf[:], rhs=et[:, 0:tl], start=(cc == 0), stop=(cc == 1)), reads=[ones_b, etb], writes=[pb])
                tm, tmb = next_tmp()
                S.op("act", lambda e, pt=pt, tm=tm, tl=tl: e.activation(out=tm[:, 0:tl], in_=pt[:, 0:tl], func=AF.Sqrt, scale=1.0 / 256, bias=EPS), reads=[pb], writes=[tmb])
                S.op("dve", lambda e, tm=tm, tl=tl: e.reciprocal(out=tm[:, 0:tl], in_=tm[:, 0:tl]), reads=[tmb], writes=[tmb])
                for cc in range(2):
                    S.op("dve", lambda e, tm=tm, cc=cc, t0=t0, tl=tl: e.scalar_tensor_tensor(out=osrc[cc][:, t0:t0 + tl], in0=osrc[cc][:, t0:t0 + tl], scalar=mixgain128(l, 2 + cc), in1=tm[:, 0:tl],
                                                                                            op0=ALU.mult, op1=ALU.mult),
                         reads=[osb[cc], tmb, vec3_b], writes=[osb[cc]])
            wout_partial(l, 2, 128, wo, wob, lambda pi, t0, tl: (osrc[pi][:, t0:t0 + tl], [osb[pi]]))

        def gla_group(l):
            fence()
            wv = win_d[l].rearrange("(k p) n -> p k n", p=128)
            wsl, wslb = ring[0], ring_b[0]
            S.op("pool", lambda e: e.dma_start(out=wsl[:, 0:6400].rearrange("p (k n) -> p k n", k=8), in_=wv[:, :, 1792:2592]), writes=[wslb], dma=True)
            S.op("sp", lambda e: e.dma_start(out=gtri, in_=tri_d.rearrange("a p c -> p a c")), writes=[gconst_b], dma=True)
            S.op("pool", lambda e: e.dma_start(out=ggw.rearrange("p (z c) -> p z c", z=2), in_=gw_d[l].rearrange("z r c -> r z c")), writes=[gconst_b], dma=True)
            S.op("pool", lambda e: e.dma_start(out=ggb, in_=gb_d[l].rearrange("z c -> (z c)").rearrange("(o n) -> o n", o=1)), writes=[gconst_b], dma=True)

            def wcol(k, c, n):
                return wsl[:, k * 800 + c:k * 800 + c + n]
            for ti, (t0, tl, c) in enumerate(TT if GLA_PART >= 2 else []):
                for (dst, dstb, cb, m, idx) in ((gqT, gqT_b, 0, 64, 0), (gqT, gqT_b, 64, 64, 1), (gkT, gkT_b, 128, 64, 0), (gkT, gkT_b, 192, 64, 1),
                                                (ggdT, ggdT_b, 512, 16, 0), (ggdT, ggdT_b, 528, 16, 1)):
                    pt, pb = next_ps(6)
                    for k in range(8):
                        S.op("pe", lambda e, pt=pt, k=k, cb=cb, m=m, t0=t0, tl=tl: e.matmul(pt[0:m, 0:tl], lhsT=wcol(k, cb, m), rhs=u[:, k, t0:t0 + tl], start=(k == 0), stop=(k == 7)),
                             reads=[wslb, u_b[ti]], writes=[pb])
                    S.op("act", lambda e, pt=pt, dst=dst, m=m, idx=idx, t0=t0, tl=tl: e.copy(out=dst[0:m, idx, t0:t0 + tl], in_=pt[0:m, 0:tl]), reads=[pb], writes=[dstb])
            for tb in range(NTB if GLA_PART >= 3 else 0):
                tti = 0 if tb < 2 else (1 if tb < 6 else 2)
                pt, pb = next_ps(6)
                for k in range(8):
                    S.op("pe", lambda e, pt=pt, k=k, tb=tb: e.matmul(pt[:, 0:384], lhsT=u[:, k, tb * 128:(tb + 1) * 128], rhs=wcol(k, 128, 384), start=(k == 0), stop=(k == 7)),
                         reads=[wslb, u_b[tti]], writes=[pb])
                if GLA_VAR == 1:
                    continue
                S.op("act", lambda e, pt=pt, tb=tb: e.copy(out=gktok[:, tb, :], in_=pt[:, 0:128]), reads=[pb], writes=[gktok_b])
                if GLA_VAR == 2:
                    continue
                if GLA_VAR == 3:
                    S.op("act", lambda e, pt=pt, tb=tb: e.copy(out=gvtok[:, tb, :], in_=pt[:, 128:384]), reads=[pb], writes=[gvtok_b])
                    continue
                S.op("act", lambda e, pt=pt, tb=tb: e.copy(out=gvtok[:, tb, :], in_=pt[:, 128:384]), reads=[pb], writes=[gvtok_b])
            SCALE = 32.0 ** -0.5
            gqk2 = [gqk, arena[0:64, 12928:13440]]
            gke2 = [gke, arena[:, 13440:13568]]
            geb2 = [geb, arena[0:64, 13568:14592].bitcast(F32).rearrange("p (a t) -> p a t", a=4)]
            gqk2_b = [gqk_b, Buf()]
            gke2_b = [gke_b, Buf()]
            geb2_b = [geb_b, Buf()]
            ARENA_BUFS.extend([gqk2_b[1], gke2_b[1], geb2_b[1]])

            def gla_block(z, tb):
                gebz, gebz_b = geb2[z], geb2_b[z]
                slot = 0 if tb < 2 else 1 + (tb - 2) // 2
                first = (tb % 2 == 0) if z == 0 else (tb % 2 == 1)
                last = not first
                if first:
                    for p in range(2):
                        zi = z * 2 + p
                        if tb in (0, 1):
                            S.op("dve", lambda e, zi=zi: e.memset(gS[:, zi, :], 0.0), writes=[gS_b])
                        elif (z == 0 and tb == 2) or (z == 1 and tb == 9):
                            S.op("sp", lambda e, zi=zi, p=p, z=z: e.dma_start(out=gS[:, zi, :], in_=gs0_d[l, z, 2 * p:2 * p + 2].rearrange("h d v -> (h d) v")), writes=[gS_b], dma=True)
                        else:
                            S.op("dve", lambda e, zi=zi: e.tensor_scalar(out=gS[:, zi, :], in0=gS[:, zi, :], scalar1=hyflag[0:64, 0:1], scalar2=None, op0=ALU.mult), reads=[gS_b, hysc_b], writes=[gS_b])
                        S.op("act", lambda e, zi=zi: e.copy(out=gSb[:, zi, :], in_=gS[:, zi, :]), reads=[gS_b], writes=[gS_b])
                yield
                pl, plb = next_ps(4)
                S.op("pe", lambda e, pl=pl, tb=tb, z=z: e.matmul(pl[:, 0:128], lhsT=ggdT[:, z, tb * 128:(tb + 1) * 128], rhs=ggw[:, z * 128:(z + 1) * 128], start=True, stop=False),
                     reads=[ggdT_b, gconst_b], writes=[plb])
                S.op("pe", lambda e, pl=pl, z=z: e.matmul(pl[:, 0:128], lhsT=ones_bf[0:1, 0:128], rhs=ggb[:, z * 128:(z + 1) * 128], start=False, stop=True),
                     reads=[ones_b, gconst_b], writes=[plb])
                yield
                gp, gpb = tmp[z], tmp_b[z]
                S.op("act", lambda e, pl=pl, gp=gp: e.activation(out=gp[:, 0:128], in_=pl[:, 0:128], func=AF.Exp, scale=-1.0), reads=[plb], writes=[gpb])
                S.op("act", lambda e, gp=gp: e.activation(out=gp[:, 0:128], in_=gp[:, 0:128], func=AF.Ln, bias=1.0), reads=[gpb], writes=[gpb])
                yield
                for p in range(2):
                    pc, pcb = next_ps(4)
                    S.op("pe", lambda e, pc=pc, gp=gp, p=p, z=z: e.matmul(pc[0:64, 0:128], lhsT=gp[:, p * 64:(p + 1) * 64], rhs=gtri[:, z, :], start=True, stop=True),
                         reads=[gpb, gconst_b], writes=[pcb])
                    S.op("act", lambda e, pc=pc, p=p: e.activation(out=gebz[:, 2 * p, :], in_=pc[0:64, 0:128], func=AF.Exp, scale=-1.0 / 16), reads=[pcb], writes=[gebz_b])
                    S.op("act", lambda e, pc=pc, p=p: e.activation(out=gebz[:, 2 * p + 1, :], in_=pc[0:64, 0:128], func=AF.Exp, scale=1.0 / 16), reads=[pcb], writes=[gebz_b])
                yield
                qk, qkb = gqk2[z], gqk2_b[z]
                for p in range(2):
                    S.op("dve", lambda e, qk=qk, p=p, tb=tb: e.scalar_tensor_tensor(out=qk[0:64, p * 128:(p + 1) * 128], in0=gqT[:, p, tb * 128:(tb + 1) * 128], scalar=SCALE, in1=gebz[:, 2 * p, :],
                                                                                   op0=ALU.mult, op1=ALU.mult), reads=[gqT_b, gebz_b], writes=[qkb])
                    S.op("dve", lambda e, qk=qk, p=p, tb=tb: e.tensor_tensor(out=qk[0:64, 256 + p * 128:256 + (p + 1) * 128], in0=gkT[:, p, tb * 128:(tb + 1) * 128], in1=gebz[:, 2 * p + 1, :], op=ALU.mult),
                         reads=[gkT_b, gebz_b], writes=[qkb])
                yield
                pf, pfb = next_ps(4)
                S.op("pe", lambda e, pf=pf, gp=gp, z=z: e.matmul(pf[:, 0:128], lhsT=gtri[:, 2 + z, :], rhs=gp[:, 0:128], start=True, stop=True), reads=[gpb, gconst_b], writes=[pfb])
                S.op("act", lambda e, pf=pf, gp=gp: e.activation(out=gp[:, 128:256], in_=pf[:, 0:128], func=AF.Exp, scale=-1.0 / 16), reads=[pfb], writes=[gpb])
                yield
                ke, keb = gke2[z], gke2_b[z]
                S.op("dve", lambda e, ke=ke, gp=gp, tb=tb: e.tensor_tensor(out=ke[:, 0:128], in0=gktok[:, tb, :], in1=gp[:, 128:256], op=ALU.mult), reads=[gktok_b, gpb], writes=[keb])
                yield
                pcs = [(ps[6 - 2 * z], ps_b[6 - 2 * z]), (ps[7 - 2 * z], ps_b[7 - 2 * z])]
                for h in range(4):
                    p, sidx = h // 2, h % 2
                    zi = z * 2 + p
                    pa, pab = next_ps(4)
                    S.op("pe", lambda e, pa=pa, qk=qk, p=p, sidx=sidx: e.matmul(pa[:, 0:128], lhsT=qk[32 * sidx:32 * sidx + 32, 256 + p * 128:256 + (p + 1) * 128],
                                                                               rhs=qk[32 * sidx:32 * sidx + 32, p * 128:(p + 1) * 128], start=True, stop=True),
                         reads=[qkb], writes=[pab])
                    yield
                    am, amb = next_e()
                    S.op("dve", lambda e, pa=pa, am=am, z=z: e.tensor_tensor(out=am[:, 0:128], in0=pa[:, 0:128], in1=gtri[:, z, :], op=ALU.mult), reads=[pab, gconst_b], writes=[amb])
                    yield
                    po, pob = next_ps(4)
                    S.op("pe", lambda e, po=po, am=am, h=h, tb=tb: e.matmul(po[0:64, 0:128], lhsT=gvtok[:, tb, h * 64:(h + 1) * 64], rhs=am[:, 0:128], start=True, stop=False),
                         reads=[gvtok_b, amb], writes=[pob])
                    S.op("pe", lambda e, po=po, qk=qk, zi=zi, p=p, sidx=sidx: e.matmul(po[0:64, 0:128], lhsT=gSb[32 * sidx:32 * sidx + 32, zi, :], rhs=qk[32 * sidx:32 * sidx + 32, p * 128:(p + 1) * 128],
                                                                                      start=False, stop=True),
                         reads=[gS_b, qkb], writes=[pob])
                    yield
                    if (z == 0 and tb <= 4) or (z == 1 and tb >= 5):
                        S.op("act", lambda e, po=po, h=h, tb=tb: e.copy(out=godT[:, h, tb * 128:(tb + 1) * 128], in_=po[0:64, 0:128]), reads=[pob], writes=[godT_b])
                    else:
                        S.op("dve", lambda e, po=po, h=h, tb=tb: e.tensor_tensor(out=godT[:, h, tb * 128:(tb + 1) * 128], in0=godT[:, h, tb * 128:(tb + 1) * 128], in1=po[0:64, 0:128], op=ALU.add),
                             reads=[pob, godT_b], writes=[godT_b])
                    if sidx == 1:
                        pcx, pcxb = pcs[p]
                        S.op("pe", lambda e, pcx=pcx, ke=ke, p=p, tb=tb: e.matmul(pcx[0:64, 0:128], lhsT=ke[:, p * 64:(p + 1) * 64], rhs=gvtok[:, tb, p * 128:(p + 1) * 128], start=True, stop=True),
                             reads=[keb, gvtok_b], writes=[pcxb])
                yield
                for p in range(2):
                    zi = z * 2 + p
                    pcx, pcxb = pcs[p]
                    dcol = 127 if z == 0 else 0
                    for sx in range(2):
                        S.op("dve", lambda e, pcx=pcx, zi=zi, p=p, dcol=dcol, sx=sx: e.scalar_tensor_tensor(out=gS[32 * sx:32 * sx + 32, zi, :], in0=gS[32 * sx:32 * sx + 32, zi, :],
                                                                                                           scalar=gebz[32 * sx:32 * sx + 32, 2 * p, dcol:dcol + 1], in1=pcx[32 * sx:32 * sx + 32, 64 * sx:64 * sx + 64],
                                                                                                           op0=ALU.mult, op1=ALU.add), reads=[gS_b, gebz_b, pcxb], writes=[gS_b])
                    S.op("act", lambda e, zi=zi: e.copy(out=gSb[:, zi, :], in_=gS[:, zi, :]), reads=[gS_b], writes=[gS_b])
                    if last:
                        S.op("sp", lambda e, zi=zi, p=p, z=z, slot=slot: e.dma_start(out=gout_d[l, z, slot, 2 * p:2 * p + 2].rearrange("h d v -> (h d) v"), in_=gS[:, zi, :]),
                             reads=[gS_b], dma=True, is_out=True)

            for step in range(NTB):
                gens = [gla_block(0, step), gla_block(1, NTB - 1 - step)]
                while gens:
                    for g_ in list(gens):
                        try:
                            next(g_)
                        except StopIteration:
                            gens.remove(g_)
            if l == 0:
                dump(0, arena[0:64, 0:1280], [gqT_b], np_=64)
                dump(1, arena[:, 5120:6400], [gktok_b])
                dump(2, a2[0:64, 0:1280], [godT_b], np_=64)
                dump(3, a2[0:64, 3840:5120], [godT_b], np_=64)
                dump(6, arena[:, 6400:7680], [gvtok_b])
                dump(4, arena[0:64, 11264:12288].bitcast(F32), [geb_b], n=512, np_=64)
                dump(5, arena[0:64, 10496:11008].bitcast(F32), [gS_b], n=256, np_=64)
            wo, wob = ring[1], ring_b[1]
            S.op("pool", lambda e: e.dma_start(out=wo[0:64, 0:4096].rearrange("p (h n) -> p h n", h=4), in_=wout_d[l, 768:1024, :].rearrange("(h p) n -> p h n", p=64)), writes=[wob], dma=True)
            if GLA_PART < 4:
                return
            for ti, (t0, tl, c) in enumerate(TT):
                for h in range(4):
                    et, etb = next_e()
                    S.op("act", lambda e, et=et, h=h, t0=t0, tl=tl: e.activation(out=et[0:64, 0:tl], in_=godT[:, h, t0:t0 + tl], func=AF.Square), reads=[godT_b], writes=[etb])
                    pt, pb = next_ps(6)
                    S.op("pe", lambda e, pt=pt, et=et, tl=tl: e.matmul(pt[0:64, 0:tl], lhsT=ones_bf[0:64, 0:64], rhs=et[0:64, 0:tl], start=True, stop=True), reads=[ones_b, etb], writes=[pb])
                    S.op("act", lambda e, pt=pt, tl=tl: e.activation(out=rden[:, 0:tl], in_=pt[0:64, 0:tl], func=AF.Sqrt, scale=1.0 / 64, bias=EPS), reads=[pb], writes=[rden_b])
                    S.op("dve", lambda e, tl=tl: e.reciprocal(out=rden[:, 0:tl], in_=rden[:, 0:tl]), reads=[rden_b], writes=[rden_b])
                    S.op("dve", lambda e, h=h, t0=t0, tl=tl: e.scalar_tensor_tensor(out=godT[:, h, t0:t0 + tl], in0=godT[:, h, t0:t0 + tl], scalar=mixgain64(l, 12 + h), in1=rden[:, 0:tl], op0=ALU.mult, op1=ALU.mult),
                         reads=[godT_b, rden_b, vec64_b], writes=[godT_b])
                    pr, prb = next_ps(6)
                    for k in range(8):
                        S.op("pe", lambda e, pr=pr, k=k, h=h, t0=t0, tl=tl: e.matmul(pr[0:64, 0:tl], lhsT=wcol(k, 544 + h * 64, 64), rhs=u[:, k, t0:t0 + tl], start=(k == 0), stop=(k == 7)),
                             reads=[wslb, u_b[ti]], writes=[prb])
                    tm, tmb = next_tmp()
                    S.op("act", lambda e, pr=pr, tm=tm, tl=tl: e.activation(out=tm[0:64, 0:tl], in_=pr[0:64, 0:tl], func=AF.Silu), reads=[prb], writes=[tmb])
                    S.op("dve", lambda e, tm=tm, h=h, t0=t0, tl=tl: e.tensor_tensor(out=godT[:, h, t0:t0 + tl], in0=godT[:, h, t0:t0 + tl], in1=tm[0:64, 0:tl], op=ALU.mult), reads=[godT_b, tmb], writes=[godT_b])
            _ri[0] = 2
            wout_partial(l, 4, 64, wo, wob, lambda pi, t0, tl: (godT[:, pi, t0:t0 + tl], [godT_b]))

        for l in range(NLAYERS):
            if STAGES["ffn1"]:
                S.phase = "L%d_ffn1" % l
                norm_mod(l, 0)
                ffn(l, 0)
            if STAGES["mixer"]:
                S.phase = "L%d_norm2" % l
                norm_mod(l, 1)
                if STAGES.get("A", True):
                    S.phase = "L%d_attnA" % l
                    attention_group(l, 0)
                if STAGES.get("C", True):
                    S.phase = "L%d_attnC" % l
                    attention_group(l, 1)
                if STAGES.get("B", True):
                    S.phase = "L%d_hyena" % l
                    hyena_group(l)
                if STAGES.get("D", True):
                    S.phase = "L%d_gla" % l
                    gla_group(l)
            if STAGES["ffn2"]:
                S.phase = "L%d_ffn2" % l
                norm_mod(l, 2)
                ffn(l, 1)
        S.phase = "final"

        for ti, (t0, tl, c) in enumerate(TT):
            S.op("act", lambda e, t0=t0, tl=tl: e.activation(out=u[:, :, t0:t0 + tl], in_=xres[:, :, t0:t0 + tl], func=AF.Square), reads=[xres_b[ti]], writes=[u_b[ti]])
            pt, pb = next_ps()
            for k in range(8):
                S.op("pe", lambda e, pt=pt, k=k, t0=t0, tl=tl: e.matmul(pt[:, 0:tl], lhsT=ones_bf[:], rhs=u[:, k, t0:t0 + tl], start=(k == 0), stop=(k == 7)),
                     reads=[ones_b, u_b[ti]], writes=[pb])
            S.op("act", lambda e, pt=pt, t0=t0, tl=tl: e.activation(out=rstd[:, t0:t0 + tl], in_=pt[:, 0:tl], func=AF.Sqrt, scale=1.0 / D, bias=EPS), reads=[pb], writes=[rstd_b])
            S.op("dve", lambda e, t0=t0, tl=tl: e.reciprocal(out=rstd[:, t0:t0 + tl], in_=rstd[:, t0:t0 + tl]), reads=[rstd_b], writes=[rstd_b])
            for k in range(8):
                S.op("dve", lambda e, k=k, t0=t0, tl=tl: e.scalar_tensor_tensor(out=xres[:, k, t0:t0 + tl], in0=xres[:, k, t0:t0 + tl], scalar=finalgT[:, k:k + 1],
                                                                               in1=rstd[:, t0:t0 + tl], op0=ALU.mult, op1=ALU.mult),
                     reads=[xres_b[ti], rstd_b, vecs_b[1]], writes=[xres_b[ti]])
        for tb in range(NTB):
            sg, sgb = stg[tb % 2], stg_b[tb % 2]
            tti = 0 if tb < 2 else (1 if tb < 6 else 2)
            for half in range(2):
                pt, pb = next_ps()
                for q in range(4):
                    k = half * 4 + q
                    S.op("pe", lambda e, pt=pt, k=k, q=q, tb=tb: e.transpose(pt[:, q * 128:(q + 1) * 128], xres[:, k, tb * 128:(tb + 1) * 128], ident[:]),
                         reads=[xres_b[tti], ident_b], writes=[pb])
                if half == 0:
                    S.op("act", lambda e, pt=pt, sg=sg, half=half: e.copy(out=sg[:, half * 512:(half + 1) * 512], in_=pt[:, :]), reads=[pb], writes=[sgb])
                else:
                    S.op("dve", lambda e, pt=pt, sg=sg, half=half: e.tensor_copy(out=sg[:, half * 512:(half + 1) * 512], in_=pt[:, :]), reads=[pb], writes=[sgb])
            S.op("sp", lambda e, sg=sg, tb=tb: e.dma_start(out=y_d[tb * 128:(tb + 1) * 128, :], in_=sg[:]), reads=[sgb], dma=True, is_out=True)

        S.emit(st)
    return nc


N_CORES = 8


def core_tokens(c):
    if c < 2:
        return [30 + c], c
    base = 5 * (c - 2)
    return [base + i for i in range(5)], None


def rope_tables():
    rows = 1024 // 64
    r = np.repeat(np.arange(rows, dtype=np.float32), 64)
    col = np.tile(np.arange(64, dtype=np.float32), rows)
    nf = 16
    inv = (10000.0 ** (-np.arange(nf, dtype=np.float32) / nf)).astype(np.float32)
    ang = np.concatenate([r[:, None] * inv, col[:, None] * inv], axis=-1).astype(np.float32)
    return np.cos(ang).astype(np.float32), np.sin(ang).astype(np.float32)


def attn_masks(is_sample):
    m = np.zeros((4, 128, 1920), np.float32)
    a = np.arange(128)[:, None]
    x = np.arange(1920)[None, :]
    if is_sample:
        band = (np.abs(x - 896 - a) <= 128).astype(np.float32)
        m[0] = band
        m[1] = band
        m[2] = 1.0
        m[3] = 1.0
    else:
        ev = ((x >= 896) & (x < 1152)).astype(np.float32) * np.ones((128, 1), np.float32)
        od = ((x >= 768) & (x < 1024)).astype(np.float32) * np.ones((128, 1), np.float32)
        m[0] = ev
        m[1] = od
        m[2] = ev
        m[3] = od
    return m.astype(NPBF16)


def hy_tables(L):
    t = np.linspace(0.0, 1.0, L, dtype=np.float32)[:, None]
    w = ((2.0 * math.pi / L) * np.arange(L, dtype=np.float32)[:, None]).astype(np.float32)
    bands = np.linspace(1e-4, 15, 16, dtype=np.float32)[None, :]
    z = np.concatenate([t, np.cos(bands * w), -np.sin(bands * w)], axis=-1).astype(np.float32)
    deltas = np.linspace(math.log(1e-2) / 1.5, math.log(1e-2) / 0.3, 256, dtype=np.float32)
    dec = np.exp(-t * np.abs(deltas)).astype(np.float32)
    dec2 = np.stack([dec, dec], 1)
    dec2[0, 1] = 0.0
    r = np.arange(2 * L)
    f = 256 * (r // 512) + (r % 256)
    is_im = (r % 512) >= 256
    th = np.pi * (f[None, :] + 0.5) * np.arange(L)[:, None].astype(np.float64) / L
    FW = np.where(is_im[None, :], -np.sin(th), np.cos(th))
    IV = FW.T / L
    return z, dec2, FW.astype(np.float32), IV.astype(np.float32)


def blockdiag4(m):
    a, b = m.shape
    o = np.zeros((4 * a, 4 * b), m.dtype)
    for i in range(4):
        o[i * a:(i + 1) * a, i * b:(i + 1) * b] = m
    return o


_NC_CACHE = {}
SHARED_KEYS = ["w_mod", "b_mod", "norm_g", "ffn_w_in", "ffn_w_out", "final_g", "w_in", "w_out", "mix_g", "swa_sink", "qk_norm_g",
               "hy_conv_w", "hy_conv_b", "hy_w1", "hy_b1", "hy_w2", "hy_b2", "hy_w3", "hy_freq", "hy_bias", "gla_gate_w", "gla_gate_b"]


def kernel(**inp):
    f32 = np.float32
    x_prompt = np.asarray(inp["x_prompt"], f32)
    x_sample = np.asarray(inp["x_sample"], f32)
    c = np.asarray(inp["c"], f32)
    c_ctx = np.asarray(inp["c_ctx"], f32)
    if "nc" not in _NC_CACHE:
        _NC_CACHE["nc"] = build_program()
    nc = _NC_CACHE["nc"]
    shared = {k: np.ascontiguousarray(inp[k], f32) for k in SHARED_KEYS}
    shared["ident_in"] = np.eye(128, dtype=f32)
    cos_s, sin_s = rope_tables()
    caches = [np.asarray(inp[k], f32) for k in ("cache_swa_k", "cache_swa_v", "cache_gqa_k", "cache_gqa_v")]
    z_p, dec_p, fw_p, iv_p = hy_tables(256)
    z_s, dec_s, fw_s, iv_s = hy_tables(1024)
    shared["hy_fw_p"] = fw_p.astype(NPBF16)
    r_ = np.arange(128)[:, None]
    c_ = np.arange(128)[None, :]
    shared["tri_in"] = np.stack([r_ <= c_, r_ >= c_, r_ > c_, r_ < c_], 0).astype(f32)
    state_gla = np.asarray(inp["state_gla"], f32)
    shared["hy_iv_p"] = iv_p.astype(NPBF16)
    hy_prompt = dict(hy_zT=np.ascontiguousarray(np.concatenate([z_p] * 5, 0).T), hy_dec=np.ascontiguousarray(np.concatenate([dec_p] * 5, 0)),
                     hy_fw_g=blockdiag4(fw_p).astype(NPBF16), hy_iv_g=blockdiag4(iv_p).astype(NPBF16), hy_flag=np.zeros((128, 1), f32))
    hy_sample = dict(hy_zT=np.ascontiguousarray(np.concatenate([z_p, z_s], 0).T), hy_dec=np.ascontiguousarray(np.concatenate([dec_p, dec_s], 0)),
                     hy_fw_g=fw_s.astype(NPBF16), hy_iv_g=iv_s.astype(NPBF16), hy_flag=np.ones((128, 1), f32))
    in_maps = []
    for core in range(N_CORES):
        pids, sid = core_tokens(core)
        m = dict(shared)
        cos = np.ones((T, 32), f32)
        sin = np.zeros((T, 32), f32)
        if sid is None:
            xs = np.concatenate([x_prompt[p] for p in pids], 0)
            cond = np.stack([c_ctx, c_ctx], 0)
            m["ctx_kv"] = np.zeros((4, 2, 512, 128), f32)
            m["ctx_bias"] = np.full((128, 1), -30000.0, f32)
            m["gla_s0"] = np.zeros((2, 2, 4, 32, 64), f32)
        else:
            xs = np.concatenate([x_prompt[pids[0]], x_sample[sid]], 0)
            cond = np.stack([c_ctx, c[sid]], 0)
            cos[256:] = cos_s
            sin[256:] = sin_s
            m["ctx_kv"] = np.ascontiguousarray(np.stack([cc[sid].reshape(2, 512, 128) for cc in caches], 0), f32)
            m["ctx_bias"] = np.zeros((128, 1), f32)
            m["gla_s0"] = np.ascontiguousarray(state_gla[sid], f32)
        m["attn_mask"] = attn_masks(sid is not None)
        m.update(hy_sample if sid is not None else hy_prompt)
        m["rope_cos"] = cos
        m["rope_sin"] = sin
        m["x_in"] = np.ascontiguousarray(xs, f32)
        m["cond_in"] = np.ascontiguousarray(cond, f32)
        in_maps.append(m)
    if ONE_CORE:
        res = run_bass_kernel_spmd(nc, in_maps[2:3], core_ids=[0])
        LAST["outs"] = res.results
        return None
    res = run_bass_kernel_spmd(nc, in_maps, core_ids=list(range(N_CORES)))
    outs = res.results
    LAST["outs"] = outs
    B, SEQ = x_prompt.shape[0], x_prompt.shape[1]
    y_prompt = np.zeros((B, SEQ, D), f32)
    y_sample = np.zeros(x_sample.shape, f32)
    kvs = [np.zeros((B, 2, SEQ, 2, 64), f32) for _ in range(4)]
    new_state = np.zeros((B, 2, 2, 4, 32, 64), f32)
    for core in range(N_CORES):
        pids, sid = core_tokens(core)
        y = np.asarray(outs[core]["y"], f32)
        kvo = np.asarray(outs[core]["kv_out"], f32)
        gso = np.asarray(outs[core]["gla_out"], f32)
        if sid is not None:
            y_sample[sid] = y[256:]
        for i, p in enumerate(pids):
            y_prompt[p] = y[i * 256:(i + 1) * 256]
            for a in range(4):
                kvs[a][p] = kvo[a, :, i * 256:(i + 1) * 256, :].reshape(2, SEQ, 2, 64)
            new_state[p] = gso[:, :, i]
    return (y_prompt, y_sample, kvs[0], kvs[1], kvs[2], kvs[3], new_state)
```

```python
import math
from contextlib import ExitStack
import numpy as np
import ml_dtypes
import concourse.bass as bass
import concourse.mybir as mybir
from concourse.bass_utils import run_bass_kernel_spmd

F32 = mybir.dt.float32
BF16 = mybir.dt.bfloat16
AF = mybir.ActivationFunctionType
ALU = mybir.AluOpType
AX = mybir.AxisListType
NPBF16 = ml_dtypes.bfloat16

STAGES = {"ffn1": True, "mixer": True, "ffn2": True, "A": True, "C": True, "B": True, "D": True}
NLAYERS = 2
DEBUG = False
PROFILE_SCOPES = False
PROFILE_ENGINE = "pe"
GLA_STEPS = 99
GLA_PART = 9
GLA_VAR = 0
ONE_CORE = False
LAST = {}

ENGS = ("pe", "act", "dve", "pool", "sp")
DMA_NSEM = {"sp": 12, "act": 4, "pool": 12}


class Buf:
    __slots__ = ("name", "w", "r")

    def __init__(self, name=""):
        self.name = name
        self.w = None
        self.r = []


class Op:
    __slots__ = ("eng", "idx", "fn", "deps", "signal", "dma", "dsem", "dval", "cnt", "phase")

    def __init__(self, eng, idx, fn, dma):
        self.eng = eng
        self.idx = idx
        self.fn = fn
        self.deps = []
        self.signal = False
        self.dma = dma
        self.dsem = None
        self.dval = 0
        self.cnt = 0


class Sched:
    def __init__(self, nc, same_engine_sync=True):
        self.nc = nc
        self.ops = {e: [] for e in ENGS}
        self.ndma = {e: 0 for e in DMA_NSEM}
        self.dma_ops = {e: [] for e in DMA_NSEM}
        self.same = same_engine_sync
        self.out_dmas = []
        self.phase = None

    def op(self, eng, fn, reads=(), writes=(), dma=False, is_out=False):
        lst = self.ops[eng]
        o = Op(eng, len(lst), fn, dma)
        o.phase = self.phase
        deps = {}
        for b in reads:
            if b.w is not None:
                deps[id(b.w)] = b.w
        for b in writes:
            if b.w is not None:
                deps[id(b.w)] = b.w
            for r in b.r:
                deps[id(r)] = r
        if dma:
            j = self.ndma[eng]
            k = DMA_NSEM[eng]
            o.dsem = (eng, j % k)
            o.dval = 16 * (j // k + 1)
            if j >= k:
                p = self.dma_ops[eng][j - k]
                deps[id(p)] = p
            self.ndma[eng] += 1
            self.dma_ops[eng].append(o)
            if is_out:
                self.out_dmas.append(o)
        best = {}
        for d in deps.values():
            if d is o:
                continue
            if d.dma:
                o.deps.append(d)
                continue
            if d.eng == eng and (eng in ("pe", "sp") or not self.same):
                continue
            if d.eng not in best or best[d.eng].idx < d.idx:
                best[d.eng] = d
        for d in best.values():
            o.deps.append(d)
            d.signal = True
        for b in reads:
            if not dma:
                b.r = [r for r in b.r if r.dma or r.eng != eng]
            b.r.append(o)
        for b in writes:
            b.w = o
            b.r = []
        lst.append(o)
        return o

    def emit(self, stack):
        nc = self.nc
        CH = 2000
        fin = Op("sp", len(self.ops["sp"]), None, False)
        fin.deps = list(self.out_dmas)
        fin.phase = None
        self.ops["sp"].append(fin)
        for e in ENGS:
            c = 0
            for o in self.ops[e]:
                if o.signal:
                    c += 1
                o.cnt = c
        esem = {}
        for e in ENGS:
            n = (self.ops[e][-1].cnt if self.ops[e] else 0)
            for i in range(max(1, (n + CH - 1) // CH)):
                esem[(e, i)] = stack.enter_context(nc.semaphore("es_%s%d" % (e, i)))
        dsem = {}
        for e, k in DMA_NSEM.items():
            for i in range(k):
                dsem[(e, i)] = stack.enter_context(nc.semaphore("ds_%s%d" % (e, i)))
        block = stack.enter_context(nc.Block())

        def run(e, engine):
            known = {}
            kn_eng = {}
            cur = [None, None]

            def set_phase(ph):
                if not PROFILE_SCOPES or ph == cur[0] or e != PROFILE_ENGINE:
                    return
                if cur[1] is not None:
                    cur[1].__exit__(None, None, None)
                    cur[1] = None
                cur[0] = ph
                if ph is not None:
                    cur[1] = nc.named_scope(ph)
                    cur[1].__enter__()

            for o in self.ops[e] + [None]:
                if o is None:
                    set_phase(None)
                    break
                set_phase(o.phase)
                need = {}
                for d in o.deps:
                    if d.dma:
                        key, val = ("d",) + d.dsem, d.dval
                        if known.get(key, 0) >= val:
                            continue
                    else:
                        if kn_eng.get(d.eng, 0) >= d.cnt:
                            continue
                        key, val = ("e", d.eng, (d.cnt - 1) // CH), (d.cnt - 1) % CH + 1
                        kn_eng[d.eng] = d.cnt
                    if need.get(key, 0) < val:
                        need[key] = val
                for key, val in need.items():
                    s = dsem[key[1:]] if key[0] == "d" else esem[key[1:]]
                    engine.wait_ge(s, val)
                    if key[0] == "d":
                        known[key] = val
                if o.fn is None:
                    continue
                ins = o.fn(engine)
                if o.dma:
                    ins.then_inc(dsem[o.dsem], 16)
                elif o.signal:
                    ins.then_inc(esem[(e, (o.cnt - 1) // CH)], 1)

        @block.tensor
        def _(eng):
            run("pe", eng)

        @block.scalar
        def _(eng):
            run("act", eng)

        @block.vector
        def _(eng):
            run("dve", eng)

        @block.gpsimd
        def _(eng):
            run("pool", eng)

        @block.sync
        def _(eng):
            run("sp", eng)


D = 1024
T = 1280
TT = [(0, 256, 0), (256, 512, 1), (768, 512, 1)]
NTB = T // 128
DFF = 2816
NHC = DFF // 128
EPS = 1e-6
RING_SLOTS = 3
SLOT_ELEMS = 8192


class Prog:
    pass


def build_program():
    nc = bass.Bass("TRN2", target_bir_lowering=False)
    P = Prog()
    st = ExitStack()
    with st:
        S = Sched(nc)

        def din(name, shape, dt=F32):
            return nc.dram_tensor(name, list(shape), dt, kind="ExternalInput").ap()

        def dout(name, shape, dt=F32):
            return nc.dram_tensor(name, list(shape), dt, kind="ExternalOutput").ap()

        _n = [0]

        def sb(shape, dt, name=None):
            _n[0] += 1
            return st.enter_context(nc.sbuf_tensor(name or ("t%d" % _n[0]), list(shape), dt))

        x_d = din("x_in", [T, D])
        cond_d = din("cond_in", [2, D])
        wmod_d = din("w_mod", [2, D, 9 * D])
        bmod_d = din("b_mod", [2, 9 * D])
        normg_d = din("norm_g", [2, 3, D])
        fwin_d = din("ffn_w_in", [2, 2, D, 2 * DFF])
        fwout_d = din("ffn_w_out", [2, 2, DFF, D])
        finalg_d = din("final_g", [D])
        ident_d = din("ident_in", [128, 128])
        y_d = dout("y", [T, D])
        win_d = din("w_in", [2, D, 2592])
        wout_d = din("w_out", [2, D, D])
        mixg_d = din("mix_g", [2, D])
        sink_d = din("swa_sink", [2, 4])
        qkg_d = din("qk_norm_g", [2, 2, 64])
        cos_d = din("rope_cos", [T, 32])
        sin_d = din("rope_sin", [T, 32])
        ctx_d = din("ctx_kv", [4, 2, 512, 128])
        ctxbias_d = din("ctx_bias", [128, 1])
        amask_d = din("attn_mask", [4, 128, 1920], BF16)
        kv_d = dout("kv_out", [4, 2, T, 128])
        hcw_d = din("hy_conv_w", [2, 3, 768])
        hcb_d = din("hy_conv_b", [2, 768])
        hw1_d = din("hy_w1", [2, 33, 64])
        hb1_d = din("hy_b1", [2, 64])
        hw2_d = din("hy_w2", [2, 64, 64])
        hb2_d = din("hy_b2", [2, 64])
        hw3_d = din("hy_w3", [2, 64, 1024])
        hfr_d = din("hy_freq", [2, 2, 64])
        hbias_d = din("hy_bias", [2, 2, 256])
        hz_d = din("hy_zT", [33, T])
        hdec_d = din("hy_dec", [T, 2, 256])
        hfwg_d = din("hy_fw_g", [1024, 2048], BF16)
        hivg_d = din("hy_iv_g", [2048, 1024], BF16)
        hfwp_d = din("hy_fw_p", [256, 512], BF16)
        hivp_d = din("hy_iv_p", [512, 256], BF16)
        hflag_d = din("hy_flag", [128, 1])
        gw_d = din("gla_gate_w", [2, 2, 16, 128])
        gb_d = din("gla_gate_b", [2, 2, 128])
        gs0_d = din("gla_s0", [2, 2, 4, 32, 64])
        tri_d = din("tri_in", [4, 128, 128])
        gout_d = dout("gla_out", [2, 2, 5, 4, 32, 64])

        xres = sb([128, 8, T], F32, "xres")
        u = sb([128, 8, T], BF16, "u")
        hid = sb([128, 12, T], BF16, "hid")
        ring = [sb([128, SLOT_ELEMS], BF16, "ring%d" % i) for i in range(RING_SLOTS)]
        ring_b = [Buf("ring%d" % i) for i in range(RING_SLOTS)]
        stg = [sb([128, D], F32, "stg%d" % i) for i in range(2)]
        stg_b = [Buf() for _ in range(2)]
        tmp = [sb([128, 512], F32, "tmp%d" % i) for i in range(3)]
        tmp_b = [Buf() for _ in range(3)]
        rstd = sb([128, T], F32, "rstd")
        rstd_b = Buf()
        ident = sb([128, 128], F32, "ident")
        ident_b = Buf()
        ones_bf = sb([128, 128], BF16, "ones")
        ones_b = Buf()
        vecs_in = [sb([128, 128], F32, "vin%d" % i) for i in range(2)]
        vecs = [sb([128, 128], F32, "vec%d" % i) for i in range(2)]
        vecs_b = [Buf() for _ in range(2)]
        vin_b = [Buf() for _ in range(2)]
        condT = sb([128, 8, 2], BF16, "condT")
        condT_b = Buf()
        mod = sb([128, 2, 72, 2], F32, "mod")
        mod_b = Buf()
        modA = sb([128, 2, 3, 8, 2], F32, "modA")
        modG = sb([128, 2, 3, 8, 2], F32, "modG")
        modA_b = Buf()
        xres_b = [Buf("xres%d" % i) for i in range(3)]
        u_b = [Buf("u%d" % i) for i in range(3)]
        hid_b = [[Buf() for _ in range(3)] for _ in range(12)]

        arena = hid[:].rearrange("p a b -> p (a b)")
        qT = arena[0:64, 0:5120].rearrange("p (h t) -> p h t", h=4)
        kT = arena[0:64, 5120:7680].rearrange("p (h t) -> p h t", h=2)
        vtok = arena[:, 7680:8980].rearrange("p (b g f) -> p b g f", b=NTB, g=2)
        oT = arena[0:64, 8980:14100].rearrange("p (h t) -> p h t", h=4)
        qT_b, kT_b, vtok_b, oT_b = Buf("qT"), Buf("kT"), Buf("vtok"), Buf("oT")
        amask = sb([128, 4, 1920], BF16, "amask")
        amask_b = Buf()
        ropec = sb([128, NTB, 32], F32, "ropec")
        ropes = sb([128, NTB, 32], F32, "ropes")
        rope_b = Buf()
        gq = sb([128, 2, 6, 64], F32, "gq")
        gq_b = Buf()
        ctxbias = sb([128, 1], F32, "ctxbias")
        esink = sb([64, 8], F32, "esink")
        misc_b = Buf()
        ctxkT = sb([64, 2, 512], BF16, "ctxkT")
        ctxv = sb([128, 4, 2, 65], BF16, "ctxv")
        esink128 = sb([128, 8], F32, "esink128")
        ones_f = sb([128, 64], F32, "ones_f")
        rrow = sb([128, 512], F32, "rrow")
        rrow_b = Buf()
        ctxkT_b, ctxv_b = Buf(), Buf()
        kvst = [sb([128, 2, 128], F32, "kvst%d" % i) for i in range(2)]
        kvst_b = [Buf() for _ in range(2)]
        qkn = [sb([128, 384], F32, "qkn%d" % i) for i in range(2)]
        qkn_b = [Buf() for _ in range(2)]
        qkr = [sb([128, 384], F32, "qkr%d" % i) for i in range(2)]
        qkr_b = [Buf() for _ in range(2)]
        rtmp2 = [sb([128, 192], F32, "rtmp%d" % i) for i in range(2)]
        rtmp2_b = [Buf() for _ in range(2)]
        ssq2 = [sb([128, 8], F32, "ssq%d" % i) for i in range(2)]
        ssq2_b = [Buf() for _ in range(2)]
        etile = [sb([128, 512], BF16, "etile%d" % i) for i in range(4)]
        etile_b = [Buf() for _ in range(4)]
        rden = sb([64, 512], F32, "rden")
        rden_b = Buf()
        vin64 = sb([128, 64], F32, "vin64")
        vec64 = sb([64, 128], F32, "vec64")
        vec64_b = Buf()

        hyF = arena[:, 0:2560].bitcast(F32)
        hyx1 = arena[:, 2560:3840]
        hyx2 = arena[:, 3840:5120]
        hyvtok = arena[:, 5120:6400].rearrange("p (b c) -> p b c", b=NTB)
        hyhp = arena[:, 6400:7680].rearrange("p (b c) -> p b c", b=NTB)
        hyhm = arena[:, 7680:8960].rearrange("p (b c) -> p b c", b=NTB)
        hyY = arena[:, 8960:11520].rearrange("p (r c) -> p r c", r=20)
        hyob0 = arena[:, 11520:12800]
        hyh2 = arena[0:64, 12800:14080]
        hyF_b, hyx1_b, hyx2_b, hyvtok_b, hyh_b, hyY_b, hyob0_b, hyh2_b = (Buf() for _ in range(8))
        ARENA_BUFS = [qT_b, kT_b, vtok_b, oT_b, hyF_b, hyx1_b, hyx2_b, hyvtok_b, hyh_b, hyY_b, hyob0_b, hyh2_b]
        gqT = arena[0:64, 0:2560].rearrange("p (a t) -> p a t", a=2)
        gkT = arena[0:64, 2560:5120].rearrange("p (a t) -> p a t", a=2)
        gktok = arena[:, 5120:6400].rearrange("p (b c) -> p b c", b=NTB)
        gvtok = arena[:, 6400:8960].rearrange("p (b c) -> p b c", b=NTB)
        gtri = arena[:, 8960:9984].bitcast(F32).rearrange("p (a c) -> p a c", a=4)
        ggw = arena[0:16, 9984:10240]
        ggb = arena[0:1, 10240:10496]
        gS = arena[0:64, 10496:11008].bitcast(F32).rearrange("p (a v) -> p a v", a=4)
        gSb = arena[0:64, 11008:11264].rearrange("p (a v) -> p a v", a=4)
        geb = arena[0:64, 11264:12288].bitcast(F32).rearrange("p (a t) -> p a t", a=4)
        gqk = arena[0:64, 12288:12800]
        gke = arena[:, 12800:12928]
        gqk_b, gke_b = Buf(), Buf()
        a2 = amask[:].rearrange("p a n -> p (a n)")
        godT = a2[0:64, 0:5120].rearrange("p (h t) -> p h t", h=4)
        ggdT = a2[0:16, 5120:7680].rearrange("p (z t) -> p z t", z=2)
        gqT_b, gkT_b, gktok_b, gvtok_b, gconst_b, gS_b, geb_b, godT_b, ggdT_b = (Buf() for _ in range(9))
        fdummy = sb([128, 1], F32, "fdummy")
        vin3 = sb([128, 128], F32, "vin3")
        vec3 = sb([128, 128], F32, "vec3")
        vec3_b = Buf()
        w3b = sb([64, 1024], BF16, "w3b")
        hyw12 = sb([64, 128], F32, "hyw12")
        hyw_b = Buf()
        hysc = sb([128, 2, 6, 4], F32, "hysc")
        hyfb = sb([64, 4], F32, "hyfb")
        hyflag = sb([128, 1], F32, "hyflag")
        hysc_b = Buf()

        ps = [st.enter_context(nc.psum_tensor("ps%d" % i, [128, 512], F32)) for i in range(8)]
        ps_b = [Buf("ps%d" % i) for i in range(8)]
        _pi = [0]

        def next_ps(n=8):
            i = _pi[0] % n
            _pi[0] += 1
            return ps[i], ps_b[i]

        _ri = [0]

        def next_slot():
            i = _ri[0] % RING_SLOTS
            _ri[0] += 1
            return ring[i], ring_b[i]

        _ti = [0]

        def next_tmp():
            i = _ti[0] % 3
            _ti[0] += 1
            return tmp[i], tmp_b[i]

        S.op("sp", lambda e: e.dma_start(out=ident[:], in_=ident_d), writes=[ident_b], dma=True)
        S.op("dve", lambda e: e.memset(ones_bf[:], 1.0), writes=[ones_b])

        S.op("dve", lambda e: e.memset(vecs_in[0][:], 0.0), writes=[vin_b[0]])
        S.op("dve", lambda e: e.memset(vecs_in[1][:], 0.0), writes=[vin_b[1]])
        S.op("sp", lambda e: e.dma_start(out=vecs_in[0][0:72, :], in_=bmod_d[0].rearrange("(c p) -> c p", p=128)), writes=[vin_b[0]], dma=True)
        S.op("sp", lambda e: e.dma_start(out=vecs_in[0][72:120, :], in_=normg_d.rearrange("l i (k p) -> (l i k) p", p=128)), writes=[vin_b[0]], dma=True)
        S.op("sp", lambda e: e.dma_start(out=vecs_in[1][0:72, :], in_=bmod_d[1].rearrange("(c p) -> c p", p=128)), writes=[vin_b[1]], dma=True)
        S.op("sp", lambda e: e.dma_start(out=vecs_in[1][72:88, :], in_=cond_d.rearrange("c (k p) -> (c k) p", p=128)), writes=[vin_b[1]], dma=True)
        S.op("sp", lambda e: e.dma_start(out=vecs_in[1][88:96, :], in_=finalg_d.rearrange("(k p) -> k p", p=128)), writes=[vin_b[1]], dma=True)
        for i in range(2):
            pt, pb = next_ps()
            S.op("pe", lambda e, i=i, pt=pt: e.transpose(pt[:, 0:128], vecs_in[i][:], ident[:]), reads=[vin_b[i], ident_b], writes=[pb])
            S.op("dve", lambda e, i=i, pt=pt: e.tensor_copy(out=vecs[i][:], in_=pt[:, 0:128]), reads=[pb], writes=[vecs_b[i]])

        def bmodT(l):
            return vecs[l][:, 0:72]

        def normgT(l, i):
            o = 72 + (l * 3 + i) * 8
            return vecs[0][:, o:o + 8]

        finalgT = vecs[1][:, 88:96]
        S.op("act", lambda e: e.activation(out=condT[:].rearrange("p k c -> p c k"), in_=vecs[1][:, 72:88].rearrange("p (c k) -> p c k", c=2), func=AF.Silu),
             reads=[vecs_b[1]], writes=[condT_b])

        S.phase = "load_x"
        for tb in range(NTB):
            sg, sgb = stg[tb % 2], stg_b[tb % 2]
            S.op("sp", lambda e, sg=sg, tb=tb: e.dma_start(out=sg[:], in_=x_d[tb * 128:(tb + 1) * 128, :]), writes=[sgb], dma=True)
            tti = 0 if tb < 2 else (1 if tb < 6 else 2)
            for half in range(2):
                pt, pb = next_ps()
                for q in range(4):
                    k = half * 4 + q
                    S.op("pe", lambda e, pt=pt, sg=sg, k=k, q=q: e.transpose(pt[:, q * 128:(q + 1) * 128], sg[:, k * 128:(k + 1) * 128], ident[:]),
                         reads=[sgb, ident_b], writes=[pb])
                eng = "act" if half == 0 else "dve"
                if eng == "act":
                    S.op("act", lambda e, pt=pt, half=half, tb=tb: e.copy(out=xres[:, half * 4:half * 4 + 4, tb * 128:(tb + 1) * 128], in_=pt[:, :].rearrange("p (a b) -> p a b", a=4)),
                         reads=[pb], writes=[xres_b[tti]])
                else:
                    S.op("dve", lambda e, pt=pt, half=half, tb=tb: e.tensor_copy(out=xres[:, half * 4:half * 4 + 4, tb * 128:(tb + 1) * 128], in_=pt[:, :].rearrange("p (a b) -> p a b", a=4)),
                         reads=[pb], writes=[xres_b[tti]])

        S.phase = "modulation"
        def mod_block(l, jb):
            sl, slb = next_slot()
            S.op("pool", lambda e, sl=sl: e.dma_start(out=sl[:, :].rearrange("p (k n) -> p k n", k=8),
                                                      in_=wmod_d[l].rearrange("(k p) n -> p k n", p=128)[:, :, jb * 1024:(jb + 1) * 1024]),
                 writes=[slb], dma=True)
            pt, pb = next_ps(6)
            for cc in range(8):
                for k in range(8):
                    S.op("pe", lambda e, pt=pt, sl=sl, cc=cc, k=k: e.matmul(pt[:, 2 * cc:2 * cc + 2], lhsT=sl[:, k * 1024 + cc * 128:k * 1024 + (cc + 1) * 128],
                                                                             rhs=condT[:, k, :], start=(k == 0), stop=(k == 7)),
                         reads=[slb, condT_b], writes=[pb])
            S.op("dve", lambda e, pt=pt: e.tensor_tensor(out=mod[:, l, jb * 8:(jb + 1) * 8, :], in0=pt[:, 0:16].rearrange("p (a c) -> p a c", c=2),
                                                         in1=bmodT(l)[:, jb * 8:(jb + 1) * 8].unsqueeze(2).broadcast_to([128, 8, 2]), op=ALU.add),
                 reads=[pb, vecs_b[l]], writes=[mod_b])

        def mod_finish(l):
            for i in range(3):
                S.op("dve", lambda e, i=i: e.tensor_scalar(out=modA[:, l, i], in0=mod[:, l, (3 * i + 1) * 8:(3 * i + 2) * 8, :], scalar1=1.0, scalar2=None, op0=ALU.add),
                     reads=[mod_b], writes=[modA_b])
                S.op("dve", lambda e, i=i: e.tensor_tensor(out=modA[:, l, i], in0=modA[:, l, i], in1=normgT(l, i).unsqueeze(2).broadcast_to([128, 8, 2]), op=ALU.mult),
                     reads=[modA_b, vecs_b[0]], writes=[modA_b])
                S.op("dve", lambda e, i=i: e.tensor_scalar(out=modG[:, l, i], in0=mod[:, l, (3 * i + 2) * 8:(3 * i + 3) * 8, :], scalar1=(1.0 if i == 1 else 0.5), scalar2=None, op0=ALU.mult),
                     reads=[mod_b], writes=[modA_b])

        for jb in range(9):
            mod_block(0, jb)
        mod_finish(0)
        MOD_PENDING = [(1, jb) for jb in range(9)] if NLAYERS > 1 else []

        def mod_hook(n=1):
            for _ in range(n):
                if MOD_PENDING:
                    l_, jb_ = MOD_PENDING.pop(0)
                    mod_block(l_, jb_)
                    if not MOD_PENDING:
                        mod_finish(l_)

        def modB(l, i, k, c):
            return mod[:, l, (3 * i) * 8 + k, c:c + 1]

        def norm_mod(l, i):
            for ti, (t0, tl, c) in enumerate(TT):
                S.op("act", lambda e, t0=t0, tl=tl: e.activation(out=u[:, :, t0:t0 + tl], in_=xres[:, :, t0:t0 + tl], func=AF.Square),
                     reads=[xres_b[ti]], writes=[u_b[ti]])
                pt, pb = next_ps()
                for k in range(8):
                    S.op("pe", lambda e, pt=pt, k=k, t0=t0, tl=tl: e.matmul(pt[:, 0:tl], lhsT=ones_bf[:], rhs=u[:, k, t0:t0 + tl], start=(k == 0), stop=(k == 7)),
                         reads=[ones_b, u_b[ti]], writes=[pb])
                S.op("act", lambda e, pt=pt, t0=t0, tl=tl: e.activation(out=rstd[:, t0:t0 + tl], in_=pt[:, 0:tl], func=AF.Sqrt, scale=1.0 / D, bias=EPS),
                     reads=[pb], writes=[rstd_b])
                S.op("dve", lambda e, t0=t0, tl=tl: e.reciprocal(out=rstd[:, t0:t0 + tl], in_=rstd[:, t0:t0 + tl]), reads=[rstd_b], writes=[rstd_b])
                for k in range(8):
                    tm, tmb = next_tmp()
                    S.op("dve", lambda e, tm=tm, k=k, t0=t0, tl=tl, c=c: e.scalar_tensor_tensor(out=tm[:, 0:tl], in0=xres[:, k, t0:t0 + tl], scalar=modA[:, l, i, k, c:c + 1],
                                                                                               in1=rstd[:, t0:t0 + tl], op0=ALU.mult, op1=ALU.mult),
                         reads=[xres_b[ti], rstd_b, modA_b], writes=[tmb])
                    S.op("act", lambda e, tm=tm, k=k, t0=t0, tl=tl, c=c: e.activation(out=u[:, k, t0:t0 + tl], in_=tm[:, 0:tl], func=AF.Identity, bias=modB(l, i, k, c), scale=1.0),
                         reads=[tmb, mod_b], writes=[u_b[ti]])

        def ffn(l, i):
            gi = 0 if i == 0 else 2
            win = fwin_d[l, i].rearrange("(k p) n -> p k n", p=128)
            wout = fwout_d[l, i].rearrange("(j p) n -> p j n", p=128)
            groups = [(0, 4), (4, 4), (8, 4), (12, 4), (16, 4), (20, 2)]
            halves = [groups[0:3], groups[3:6]]
            for hgroups in halves:
                h0 = hgroups[0][0]
                nh = sum(g[1] for g in hgroups)
                for (j0, nj) in hgroups:
                    sl, slb = next_slot()
                    S.op("pool", lambda e, sl=sl, j0=j0, nj=nj: e.dma_start(out=sl[:, 0:8 * nj * 128].rearrange("p (k n) -> p k n", k=8), in_=win[:, :, j0 * 128:(j0 + nj) * 128]),
                         writes=[slb], dma=True)
                    S.op("pool", lambda e, sl=sl, j0=j0, nj=nj: e.dma_start(out=sl[:, 4096:4096 + 8 * nj * 128].rearrange("p (k n) -> p k n", k=8),
                                                                            in_=win[:, :, DFF + j0 * 128:DFF + (j0 + nj) * 128]),
                         writes=[slb], dma=True)
                    for jj in range(nj):
                        j = j0 + jj
                        for ti, (t0, tl, c) in enumerate(TT):
                            pa, pab = next_ps()
                            pbt, pbb = next_ps()
                            for k in range(8):
                                S.op("pe", lambda e, pa=pa, sl=sl, k=k, jj=jj, nj=nj, t0=t0, tl=tl: e.matmul(pa[:, 0:tl], lhsT=sl[:, k * nj * 128 + jj * 128:k * nj * 128 + (jj + 1) * 128],
                                                                                                           rhs=u[:, k, t0:t0 + tl], start=(k == 0), stop=(k == 7)),
                                     reads=[slb, u_b[ti]], writes=[pab])
                            for k in range(8):
                                S.op("pe", lambda e, pbt=pbt, sl=sl, k=k, jj=jj, nj=nj, t0=t0, tl=tl: e.matmul(pbt[:, 0:tl], lhsT=sl[:, 4096 + k * nj * 128 + jj * 128:4096 + k * nj * 128 + (jj + 1) * 128],
                                                                                                             rhs=u[:, k, t0:t0 + tl], start=(k == 0), stop=(k == 7)),
                                     reads=[slb, u_b[ti]], writes=[pbb])
                            tm, tmb = next_tmp()
                            S.op("act", lambda e, pa=pa, tm=tm, tl=tl: e.activation(out=tm[:, 0:tl], in_=pa[:, 0:tl], func=AF.Silu), reads=[pab], writes=[tmb])
                            S.op("dve", lambda e, pbt=pbt, tm=tm, j=j, h0=h0, t0=t0, tl=tl: e.tensor_tensor(out=hid[:, j - h0, t0:t0 + tl], in0=tm[:, 0:tl], in1=pbt[:, 0:tl], op=ALU.mult),
                                 reads=[tmb, pbb], writes=[hid_b[j - h0][ti]])
                    if l == 0 and i == 0:
                        mod_hook(1)
                if l == 0 and i == 0:
                    mod_hook(1)
                slots = []
                jj0 = 0
                while jj0 < nh:
                    n = min(8, nh - jj0)
                    sl, slb = next_slot()
                    S.op("pool", lambda e, sl=sl, jj0=jj0, n=n, h0=h0: e.dma_start(out=sl[:, 0:n * 1024].rearrange("p (j n) -> p j n", j=n), in_=wout[:, h0 + jj0:h0 + jj0 + n, :]),
                         writes=[slb], dma=True)
                    slots.append((sl, slb, jj0, n))
                    jj0 += n
                for f in range(8):
                    for ti, (t0, tl, c) in enumerate(TT):
                        pt, pb = next_ps()
                        for (sl, slb, jj0, n) in slots:
                            for q in range(n):
                                jj = jj0 + q
                                S.op("pe", lambda e, pt=pt, sl=sl, q=q, f=f, jj=jj, t0=t0, tl=tl, nh=nh: e.matmul(pt[:, 0:tl], lhsT=sl[:, q * 1024 + f * 128:q * 1024 + (f + 1) * 128],
                                                                                                                rhs=hid[:, jj, t0:t0 + tl], start=(jj == 0), stop=(jj == nh - 1)),
                                     reads=[slb, hid_b[jj][ti]], writes=[pb])
                        S.op("dve", lambda e, pt=pt, f=f, t0=t0, tl=tl, c=c: e.scalar_tensor_tensor(out=xres[:, f, t0:t0 + tl], in0=pt[:, 0:tl], scalar=modG[:, l, gi, f, c:c + 1],
                                                                                                   in1=xres[:, f, t0:t0 + tl], op0=ALU.mult, op1=ALU.add),
                             reads=[pb, modA_b, xres_b[ti]], writes=[xres_b[ti]])

        S.phase = "consts"
        S.op("sp", lambda e: e.dma_start(out=ropec[:], in_=cos_d.rearrange("(b p) f -> p b f", p=128)), writes=[rope_b], dma=True)
        S.op("sp", lambda e: e.dma_start(out=ropes[:], in_=sin_d.rearrange("(b p) f -> p b f", p=128)), writes=[rope_b], dma=True)
        for l in range(2):
            for hh in range(6):
                S.op("sp", lambda e, l=l, hh=hh: e.dma_start(out=gq[:, l, hh, :], in_=qkg_d[l, (0 if hh < 4 else 1):(1 if hh < 4 else 2), :].broadcast_to([128, 64])),
                     writes=[gq_b], dma=True)
        S.op("sp", lambda e: e.dma_start(out=ctxbias[:], in_=ctxbias_d), writes=[misc_b], dma=True)
        S.op("sp", lambda e: e.dma_start(out=esink[:], in_=sink_d.rearrange("l h -> (l h)").rearrange("(o n) -> o n", o=1).broadcast_to([64, 8])), writes=[misc_b], dma=True)
        S.op("act", lambda e: e.activation(out=esink[:], in_=esink[:], func=AF.Exp), reads=[misc_b], writes=[misc_b])
        S.op("sp", lambda e: e.dma_start(out=esink128[:], in_=sink_d.rearrange("l h -> (l h)").rearrange("(o n) -> o n", o=1).broadcast_to([128, 8])), writes=[misc_b], dma=True)
        S.op("act", lambda e: e.activation(out=esink128[:], in_=esink128[:], func=AF.Exp), reads=[misc_b], writes=[misc_b])
        S.op("dve", lambda e: e.memset(ones_f[:], 1.0), writes=[misc_b])
        S.op("dve", lambda e: e.memset(vin64[:], 0.0), writes=[vec64_b])
        S.op("sp", lambda e: e.dma_start(out=vin64[0:32, :], in_=mixg_d.rearrange("l (c p) -> (l c) p", p=64)), writes=[vec64_b], dma=True)
        S.op("sp", lambda e: e.dma_start(out=vin64[32:36, :], in_=hfr_d.rearrange("l i d -> (l i) d")), writes=[vec64_b], dma=True)
        S.op("sp", lambda e: e.dma_start(out=vin64[36:38, :], in_=hb1_d), writes=[vec64_b], dma=True)
        S.op("sp", lambda e: e.dma_start(out=vin64[38:40, :], in_=hb2_d), writes=[vec64_b], dma=True)
        pt, pb = next_ps()
        S.op("pe", lambda e, pt=pt: e.transpose(pt[0:64, 0:128], vin64[:], ident[:]), reads=[vec64_b, ident_b], writes=[pb])
        S.op("dve", lambda e, pt=pt: e.tensor_copy(out=vec64[:], in_=pt[0:64, 0:128]), reads=[pb], writes=[vec64_b])

        def mixgain64(l, piece):
            return vec64[:, l * 16 + piece:l * 16 + piece + 1]

        _ei = [0]

        def next_e():
            i = _ei[0] % 4
            _ei[0] += 1
            return etile[i], etile_b[i]

        PS_O, PS_D = 6, 7

        def wout_partial(l, pieces, ksz, wslot, wslot_b, ysrc):
            for f in range(8):
                for ti, (t0, tl, c) in enumerate(TT):
                    pt, pb = next_ps(6)
                    for pi in range(pieces):
                        yap, ybufs = ysrc(pi, t0, tl)
                        S.op("pe", lambda e, pt=pt, pi=pi, f=f, tl=tl, yap=yap: e.matmul(pt[:, 0:tl], lhsT=wslot[0:ksz, pi * 1024 + f * 128:pi * 1024 + (f + 1) * 128], rhs=yap,
                                                                                         start=(pi == 0), stop=(pi == pieces - 1)),
                             reads=[wslot_b] + ybufs, writes=[pb])
                    S.op("dve", lambda e, pt=pt, f=f, t0=t0, tl=tl, c=c: e.scalar_tensor_tensor(out=xres[:, f, t0:t0 + tl], in0=pt[:, 0:tl], scalar=modG[:, l, 1, f, c:c + 1],
                                                                                               in1=xres[:, f, t0:t0 + tl], op0=ALU.mult, op1=ALU.add),
                         reads=[pb, modA_b, xres_b[ti]], writes=[xres_b[ti]])

        def attention_group(l, grp):
            c0 = 0 if grp == 0 else 1280
            wv = win_d[l].rearrange("(k p) n -> p k n", p=128)
            fence()
            if grp == 0 or not STAGES.get("A", True):
                S.op("sp", lambda e: e.dma_start(out=amask[:], in_=amask_d.rearrange("a p n -> p a n")), writes=[amask_b], dma=True)
            sl, slb = next_slot()
            S.op("pool", lambda e, sl=sl: e.dma_start(out=sl[:, 0:4096].rearrange("p (k n) -> p k n", k=8), in_=wv[:, :, c0:c0 + 512]), writes=[slb], dma=True)
            for g_ in range(2):
                S.op("pool", lambda e, g_=g_: e.dma_start(out=ctxv[:, :, g_, 0:64], in_=ctx_d[2 * grp + 1, l].rearrange("(b p) f -> p b f", p=128)[:, :, g_ * 64:(g_ + 1) * 64]),
                     reads=[u_b[0]], writes=[ctxv_b], dma=True)
            S.op("dve", lambda e: e.memset(ctxv[:, :, :, 64:65], 1.0), writes=[ctxv_b])
            S.op("dve", lambda e: e.memset(vtok[:, :, :, 64:65], 1.0), writes=[vtok_b])
            sg, sgb = stg[0], stg_b[0]
            S.op("sp", lambda e, sg=sg: e.dma_start(out=sg[:, 0:512].rearrange("p (b f) -> p b f", b=4), in_=ctx_d[2 * grp, l].rearrange("(b p) f -> p b f", p=128)),
                 reads=[u_b[0]], writes=[sgb], dma=True)
            for g in range(2):
                pt, pb = next_ps(6)
                for b in range(4):
                    S.op("pe", lambda e, pt=pt, sg=sg, b=b, g=g: e.transpose(pt[0:64, b * 128:(b + 1) * 128], sg[:, b * 128 + g * 64:b * 128 + (g + 1) * 64], ident[:]),
                         reads=[sgb, ident_b], writes=[pb])
                S.op("act", lambda e, pt=pt, g=g: e.copy(out=ctxkT[:, g, :], in_=pt[0:64, :]), reads=[pb], writes=[ctxkT_b])
            def prep_block(tb):
                tti = 0 if tb < 2 else (1 if tb < 6 else 2)
                sq_, sqb_ = ssq2[tb % 2], ssq2_b[tb % 2]
                pp, ppb = next_ps(6)
                for k in range(8):
                    S.op("pe", lambda e, pp=pp, sl=sl, k=k, tb=tb: e.matmul(pp[:, :], lhsT=u[:, k, tb * 128:(tb + 1) * 128], rhs=sl[:, k * 512:(k + 1) * 512], start=(k == 0), stop=(k == 7)),
                         reads=[slb, u_b[tti]], writes=[ppb])
                yield
                kv, kvb = kvst[tb % 2], kvst_b[tb % 2]
                qn, qnb = qkn[tb % 2], qkn_b[tb % 2]
                qr, qrb = qkr[tb % 2], qkr_b[tb % 2]
                S.op("act", lambda e, pp=pp, tb=tb: e.copy(out=vtok[:, tb, :, 0:64], in_=pp[:, 384:512].rearrange("p (g f) -> p g f", g=2)), reads=[ppb], writes=[vtok_b])
                S.op("act", lambda e, pp=pp, kv=kv: e.copy(out=kv[:, 1, :], in_=pp[:, 384:512]), reads=[ppb], writes=[kvb])
                if grp == 0:
                    S.op("act", lambda e, pp=pp, qn=qn: e.copy(out=qn[:, :], in_=pp[:, 0:384]), reads=[ppb], writes=[qnb])
                else:
                    S.op("act", lambda e, pp=pp, qr=qr: e.activation(out=qr[:, :], in_=pp[:, 0:384], func=AF.Square), reads=[ppb], writes=[qrb])
                    yield
                    S.op("dve", lambda e, qr=qr: e.reduce_sum(out=sq_[:, 0:6], in_=qr[:, :].rearrange("p (h d) -> p h d", h=6), axis=AX.X), reads=[qrb], writes=[sqb_])
                    yield
                    S.op("act", lambda e: e.activation(out=sq_[:, 0:6], in_=sq_[:, 0:6], func=AF.Sqrt, scale=1.0 / 64, bias=EPS), reads=[sqb_], writes=[sqb_])
                    yield
                    S.op("dve", lambda e: e.reciprocal(out=sq_[:, 0:6], in_=sq_[:, 0:6]), reads=[sqb_], writes=[sqb_])
                    S.op("dve", lambda e, pp=pp, qn=qn: e.tensor_tensor(out=qn[:, :].rearrange("p (h d) -> p h d", h=6), in0=pp[:, 0:384].rearrange("p (h d) -> p h d", h=6),
                                                                        in1=sq_[:, 0:6].unsqueeze(2).broadcast_to([128, 6, 64]), op=ALU.mult),
                         reads=[ppb, sqb_], writes=[qnb])
                    S.op("dve", lambda e, qn=qn: e.tensor_tensor(out=qn[:, :], in0=qn[:, :], in1=gq[:, l].rearrange("p h d -> p (h d)"), op=ALU.mult), reads=[qnb, gq_b], writes=[qnb])
                yield
                S.op("dve", lambda e, qn=qn, kv=kv: e.tensor_copy(out=kv[:, 0, :], in_=qn[:, 256:384]), reads=[qnb], writes=[kvb])
                S.op("sp", lambda e, kv=kv, tb=tb: e.dma_start(out=kv_d[2 * grp:2 * grp + 2, l, tb * 128:(tb + 1) * 128, :].rearrange("a t f -> t a f"), in_=kv[:]),
                     reads=[kvb], dma=True, is_out=True)
                x1 = qn[:, :].rearrange("p (h two d) -> p h two d", h=6, two=2)[:, :, 0, :]
                x2 = qn[:, :].rearrange("p (h two d) -> p h two d", h=6, two=2)[:, :, 1, :]
                o1 = qr[:, :].rearrange("p (h two d) -> p h two d", h=6, two=2)[:, :, 0, :]
                o2 = qr[:, :].rearrange("p (h two d) -> p h two d", h=6, two=2)[:, :, 1, :]
                cc = ropec[:, tb, :].unsqueeze(1).broadcast_to([128, 6, 32])
                ss = ropes[:, tb, :].unsqueeze(1).broadcast_to([128, 6, 32])
                rt = rtmp2[tb % 2][:, :].rearrange("p (h d) -> p h d", h=6)
                rtb = rtmp2_b[tb % 2]
                sq_, sqb_ = ssq2[tb % 2], ssq2_b[tb % 2]
                yield
                S.op("dve", lambda e, o1=o1, x1=x1, cc=cc: e.tensor_tensor(out=o1, in0=x1, in1=cc, op=ALU.mult), reads=[qnb, rope_b], writes=[qrb])
                S.op("dve", lambda e, rt=rt, x2=x2, ss=ss: e.tensor_tensor(out=rt, in0=x2, in1=ss, op=ALU.mult), reads=[qnb, rope_b], writes=[rtb])
                yield
                S.op("dve", lambda e, o1=o1, rt=rt: e.tensor_tensor(out=o1, in0=o1, in1=rt, op=ALU.subtract), reads=[qrb, rtb], writes=[qrb])
                yield
                S.op("dve", lambda e, o2=o2, x2=x2, cc=cc: e.tensor_tensor(out=o2, in0=x2, in1=cc, op=ALU.mult), reads=[qnb, rope_b], writes=[qrb])
                S.op("dve", lambda e, rt=rt, x1=x1, ss=ss: e.tensor_tensor(out=rt, in0=x1, in1=ss, op=ALU.mult), reads=[qnb, rope_b], writes=[rtb])
                yield
                S.op("dve", lambda e, o2=o2, rt=rt: e.tensor_tensor(out=o2, in0=o2, in1=rt, op=ALU.add), reads=[qrb, rtb], writes=[qrb])
                yield
                pq, pqb = next_ps(6)
                for h in range(4):
                    S.op("pe", lambda e, pq=pq, qr=qr, h=h: e.transpose(pq[0:64, h * 128:(h + 1) * 128], qr[:, h * 64:(h + 1) * 64], ident[:]), reads=[qrb, ident_b], writes=[pqb])
                yield
                S.op("act", lambda e, pq=pq, tb=tb: e.copy(out=qT[:, :, tb * 128:(tb + 1) * 128], in_=pq[0:64, :].rearrange("p (h t) -> p h t", h=4)), reads=[pqb], writes=[qT_b])
                yield
                pk, pkb = next_ps(6)
                for g in range(2):
                    S.op("pe", lambda e, pk=pk, qr=qr, g=g: e.transpose(pk[0:64, g * 128:(g + 1) * 128], qr[:, 256 + g * 64:256 + (g + 1) * 64], ident[:]), reads=[qrb, ident_b], writes=[pkb])
                yield
                S.op("act", lambda e, pk=pk, tb=tb: e.copy(out=kT[:, :, tb * 128:(tb + 1) * 128], in_=pk[0:64, 0:256].rearrange("p (h t) -> p h t", h=2)), reads=[pkb], writes=[kT_b])

            for tb0 in range(0, NTB, 2):
                gens = [prep_block(tb0), prep_block(tb0 + 1)]
                while gens:
                    for g_ in list(gens):
                        try:
                            next(g_)
                        except StopIteration:
                            gens.remove(g_)
            for h in range(4):
                g = h // 2
                segs = [(0, 256, [("loc", 0, None), ("loc", 128, None)]),
                        (256, 512, None), (768, 512, None)]
                for (q0, nq, keys) in segs:
                    if keys is None:
                        keys = [("ctx", b, None) for b in range(4)]
                        for j in range(8):
                            off = 896 - 128 * j + (q0 - 256)
                            keys.append(("loc", 256 + 128 * j, amask[:, 2 * grp + (j % 2), off:off + nq]))
                    po, pob = ps[PS_O], ps_b[PS_O]
                    pd, pdb = ps[PS_D], ps_b[PS_D]
                    nk = len(keys)
                    def stage1(ki, keys=keys, nq=nq, h=h, g=g, q0=q0):
                        kind, kpos, mk = keys[ki]
                        pss, pssb = next_ps(6)
                        et, etb = next_e()
                        if kind == "ctx":
                            S.op("pe", lambda e, pss=pss, kpos=kpos: e.matmul(pss[:, 0:nq], lhsT=ctxkT[:, g, kpos * 128:(kpos + 1) * 128], rhs=qT[:, h, q0:q0 + nq], start=True, stop=True),
                                 reads=[ctxkT_b, qT_b], writes=[pssb])
                            S.op("act", lambda e, pss=pss, et=et: e.activation(out=et[:, 0:nq], in_=pss[:, 0:nq], func=AF.Exp, scale=0.125, bias=ctxbias[:, 0:1]),
                                 reads=[pssb, misc_b], writes=[etb])
                            vap = ctxv[:, kpos, g, :]
                            vb = ctxv_b
                        else:
                            S.op("pe", lambda e, pss=pss, kpos=kpos: e.matmul(pss[:, 0:nq], lhsT=kT[:, g, kpos:kpos + 128], rhs=qT[:, h, q0:q0 + nq], start=True, stop=True),
                                 reads=[kT_b, qT_b], writes=[pssb])
                            S.op("act", lambda e, pss=pss, et=et: e.activation(out=et[:, 0:nq], in_=pss[:, 0:nq], func=AF.Exp, scale=0.125), reads=[pssb], writes=[etb])
                            if mk is not None:
                                S.op("dve", lambda e, et=et, mk=mk: e.tensor_tensor(out=et[:, 0:nq], in0=et[:, 0:nq], in1=mk, op=ALU.mult), reads=[etb, amask_b], writes=[etb])
                            vap = vtok[:, kpos // 128, g, :]
                            vb = vtok_b
                        return et, etb, vap, vb

                    def stage2(ki, st1, nq=nq, nk=nk):
                        et, etb, vap, vb = st1
                        S.op("pe", lambda e, vap=vap, et=et: e.matmul(po[0:65, 0:nq], lhsT=vap, rhs=et[:, 0:nq], start=(ki == 0), stop=(ki == nk - 1)),
                             reads=[vb, etb], writes=[pob])

                    pend = [stage1(0)]
                    if nk > 1:
                        pend.append(stage1(1))
                    for ki in range(nk):
                        cur_ = pend.pop(0)
                        if ki + 2 < nk:
                            pend.append(stage1(ki + 2))
                        stage2(ki, cur_)
                    if grp == 0:
                        S.op("dve", lambda e, nq=nq, h=h: e.tensor_scalar(out=rrow[64:65, 0:nq], in0=po[64:65, 0:nq], scalar1=esink128[64:65, l * 4 + h:l * 4 + h + 1], scalar2=None, op0=ALU.add),
                             reads=[pob, misc_b], writes=[rrow_b])
                        S.op("dve", lambda e, nq=nq: e.reciprocal(out=rrow[64:65, 0:nq], in_=rrow[64:65, 0:nq]), reads=[rrow_b], writes=[rrow_b])
                    else:
                        S.op("dve", lambda e, nq=nq: e.reciprocal(out=rrow[64:65, 0:nq], in_=po[64:65, 0:nq]), reads=[pob], writes=[rrow_b])
                    S.op("pe", lambda e, nq=nq: e.matmul(pd[0:64, 0:nq], lhsT=ones_f[64:65, 0:64], rhs=rrow[64:65, 0:nq], start=True, stop=True), reads=[misc_b, rrow_b], writes=[pdb])
                    S.op("act", lambda e, nq=nq: e.copy(out=rden[:, 0:nq], in_=pd[0:64, 0:nq]), reads=[pdb], writes=[rden_b])
                    S.op("dve", lambda e, h=h, q0=q0, nq=nq: e.tensor_tensor(out=oT[:, h, q0:q0 + nq], in0=po[0:64, 0:nq], in1=rden[:, 0:nq], op=ALU.mult),
                         reads=[pob, rden_b], writes=[oT_b])
            wsl, wslb = next_slot()
            S.op("pool", lambda e, wsl=wsl: e.dma_start(out=wsl[0:64, 0:4096].rearrange("p (h n) -> p h n", h=4),
                                                        in_=wout_d[l, (0 if grp == 0 else 512):(256 if grp == 0 else 768), :].rearrange("(h p) n -> p h n", p=64)),
                 writes=[wslb], dma=True)
            for ti, (t0, tl, c) in enumerate(TT):
                pt, pb = next_ps(6)
                for h in range(4):
                    et, etb = next_e()
                    S.op("act", lambda e, et=et, h=h, t0=t0, tl=tl: e.activation(out=et[0:64, 0:tl], in_=oT[:, h, t0:t0 + tl], func=AF.Square), reads=[oT_b], writes=[etb])
                    S.op("pe", lambda e, pt=pt, et=et, h=h, tl=tl: e.matmul(pt[0:64, 0:tl], lhsT=ones_bf[0:64, 0:64], rhs=et[0:64, 0:tl], start=(h == 0), stop=(h == 3)),
                         reads=[ones_b, etb], writes=[pb])
                S.op("act", lambda e, pt=pt, tl=tl: e.activation(out=rden[:, 0:tl], in_=pt[0:64, 0:tl], func=AF.Sqrt, scale=1.0 / 256, bias=EPS), reads=[pb], writes=[rden_b])
                S.op("dve", lambda e, tl=tl: e.reciprocal(out=rden[:, 0:tl], in_=rden[:, 0:tl]), reads=[rden_b], writes=[rden_b])
                for h in range(4):
                    piece = (0 if grp == 0 else 8) + h
                    S.op("dve", lambda e, h=h, t0=t0, tl=tl, piece=piece: e.scalar_tensor_tensor(out=oT[:, h, t0:t0 + tl], in0=oT[:, h, t0:t0 + tl], scalar=mixgain64(l, piece),
                                                                                                in1=rden[:, 0:tl], op0=ALU.mult, op1=ALU.mult),
                         reads=[oT_b, rden_b, vec64_b], writes=[oT_b])
            wout_partial(l, 4, 64, wsl, wslb, lambda pi, t0, tl: (oT[:, pi, t0:t0 + tl], [oT_b]))

        dbg_d = dout("dbg", [8, 128, T]) if DEBUG else None

        def dump(idx, ap, bufs, n=T, np_=128):
            if not DEBUG:
                return
            S.op("pool", lambda e: e.dma_start(out=dbg_d[idx, 0:np_, 0:n], in_=ap), reads=list(bufs), dma=True, is_out=True)

        ARENA_BUFS += [gqk_b, gke_b, gqT_b, gkT_b, gktok_b, gvtok_b, gconst_b, gS_b, geb_b, godT_b, ggdT_b, amask_b]

        def fence():
            S.op("dve", lambda e: e.memset(fdummy[:], 0.0), reads=ARENA_BUFS, writes=ARENA_BUFS)

        S.op("dve", lambda e: e.memset(vin3[:], 0.0), writes=[vec3_b])
        S.op("sp", lambda e: e.dma_start(out=vin3[0:36, :], in_=hcw_d.rearrange("l t (c p) -> (l t c) p", p=128)), writes=[vec3_b], dma=True)
        S.op("sp", lambda e: e.dma_start(out=vin3[36:48, :], in_=hcb_d.rearrange("l (c p) -> (l c) p", p=128)), writes=[vec3_b], dma=True)
        S.op("sp", lambda e: e.dma_start(out=vin3[48:56, :], in_=hbias_d.rearrange("l o (c p) -> (l o c) p", p=128)), writes=[vec3_b], dma=True)
        S.op("sp", lambda e: e.dma_start(out=vin3[56:72, :], in_=mixg_d.rearrange("l (c p) -> (l c) p", p=128)), writes=[vec3_b], dma=True)
        pt, pb = next_ps()
        S.op("pe", lambda e, pt=pt: e.transpose(pt[:, 0:128], vin3[:], ident[:]), reads=[vec3_b, ident_b], writes=[pb])
        S.op("dve", lambda e, pt=pt: e.tensor_copy(out=vec3[:], in_=pt[:, 0:128]), reads=[pb], writes=[vec3_b])

        def hcw(l, tap, fc):
            o = (l * 3 + tap) * 6 + fc
            return vec3[:, o:o + 1]

        def hcb(l, fc):
            o = 36 + l * 6 + fc
            return vec3[:, o:o + 1]

        def hbias(l, o_, cc):
            o = 48 + (l * 2 + o_) * 2 + cc
            return vec3[:, o:o + 1]

        def mixgain128(l, chunk):
            o = 56 + l * 8 + chunk
            return vec3[:, o:o + 1]

        S.op("sp", lambda e: e.dma_start(out=hyflag[:], in_=hflag_d), writes=[hysc_b], dma=True)
        for l in range(2):
            for fc in range(6):
                S.op("dve", lambda e, l=l, fc=fc: e.tensor_tensor(out=hysc[:, l, fc, 0:1], in0=hcw(l, 0, fc), in1=hyflag[:, 0:1], op=ALU.mult), reads=[vec3_b, hysc_b], writes=[hysc_b])
                S.op("dve", lambda e, l=l, fc=fc: e.tensor_tensor(out=hysc[:, l, fc, 1:2], in0=hcw(l, 2, fc), in1=hyflag[:, 0:1], op=ALU.mult), reads=[vec3_b, hysc_b], writes=[hysc_b])
                S.op("dve", lambda e, l=l, fc=fc: e.tensor_tensor(out=hysc[:, l, fc, 2:3], in0=hysc[:, l, fc, 0:1], in1=hcw(l, 0, fc), op=ALU.subtract), reads=[vec3_b, hysc_b], writes=[hysc_b])
                S.op("dve", lambda e, l=l, fc=fc: e.tensor_tensor(out=hysc[:, l, fc, 3:4], in0=hysc[:, l, fc, 1:2], in1=hcw(l, 2, fc), op=ALU.subtract), reads=[vec3_b, hysc_b], writes=[hysc_b])
            for i in range(2):
                S.op("dve", lambda e, l=l, i=i: e.tensor_tensor(out=hyfb[:, l * 2 + i:l * 2 + i + 1], in0=vec64[:, 32 + l * 2 + i:33 + l * 2 + i],
                                                                  in1=vec64[:, 36 + i * 2 + l:37 + i * 2 + l], op=ALU.mult), reads=[vec64_b], writes=[hysc_b])

        PI = math.pi

        def hyena_group(l):
            fence()
            wv = win_d[l].rearrange("(k p) n -> p k n", p=128)
            wsl, wslb = ring[0], ring_b[0]
            S.op("pool", lambda e: e.dma_start(out=w3b[:], in_=hw3_d[l]), reads=[hyw_b], writes=[hyw_b], dma=True)
            S.op("sp", lambda e: e.dma_start(out=hyw12[0:33, 0:64], in_=hw1_d[l]), writes=[hyw_b], dma=True)
            S.op("sp", lambda e: e.dma_start(out=hyw12[:, 64:128], in_=hw2_d[l]), writes=[hyw_b], dma=True)
            for ti, (t0, tl, c) in enumerate(TT):
                zt, ztb = next_tmp()
                S.op("sp", lambda e, zt=zt, t0=t0, tl=tl: e.dma_start(out=zt[0:33, 0:tl], in_=hz_d[:, t0:t0 + tl]), writes=[ztb], dma=True)
                cur, curb = zt, ztb
                for i in range(2):
                    pt, pb = next_ps()
                    kk = 33 if i == 0 else 64
                    wap = hyw12[0:33, 0:64] if i == 0 else hyw12[:, 64:128]
                    S.op("pe", lambda e, pt=pt, wap=wap, cur=cur, kk=kk, tl=tl: e.matmul(pt[0:64, 0:tl], lhsT=wap, rhs=cur[0:kk, 0:tl], start=True, stop=True), reads=[hyw_b, curb], writes=[pb])
                    a1, a1b = next_tmp()
                    S.op("dve", lambda e, pt=pt, a1=a1, i=i, tl=tl: e.tensor_scalar(out=a1[0:64, 0:tl], in0=pt[0:64, 0:tl], scalar1=vec64[:, 32 + l * 2 + i:33 + l * 2 + i],
                                                                                   scalar2=hyfb[:, l * 2 + i:l * 2 + i + 1], op0=ALU.mult, op1=ALU.add),
                         reads=[pb, vec64_b, hysc_b], writes=[a1b])
                    S.op("dve", lambda e, a1=a1, tl=tl: e.tensor_scalar(out=rden[:, 0:tl], in0=a1[0:64, 0:tl], scalar1=1.0 / (2.0 * PI), scalar2=12582912.0, op0=ALU.mult, op1=ALU.add), reads=[a1b], writes=[rden_b])
                    S.op("dve", lambda e, tl=tl: e.tensor_scalar(out=rden[:, 0:tl], in0=rden[:, 0:tl], scalar1=12582912.0, scalar2=None, op0=ALU.subtract), reads=[rden_b], writes=[rden_b])
                    S.op("dve", lambda e, a1=a1, tl=tl: e.scalar_tensor_tensor(out=a1[0:64, 0:tl], in0=rden[:, 0:tl], scalar=-2.0 * PI, in1=a1[0:64, 0:tl], op0=ALU.mult, op1=ALU.add), reads=[rden_b, a1b], writes=[a1b])
                    if i == 0:
                        S.op("act", lambda e, a1=a1, tl=tl: e.activation(out=a1[0:64, 0:tl], in_=a1[0:64, 0:tl], func=AF.Sin), reads=[a1b], writes=[a1b])
                        cur, curb = a1, a1b
                    else:
                        S.op("act", lambda e, a1=a1, t0=t0, tl=tl: e.activation(out=hyh2[:, t0:t0 + tl], in_=a1[0:64, 0:tl], func=AF.Sin), reads=[a1b], writes=[hyh2_b])
            for cc in range(2):
                for part in range(3):
                    S.op("pool", lambda e, part=part, cc=cc: e.dma_start(out=wsl[:, part * 1024:(part + 1) * 1024].rearrange("p (k n) -> p k n", k=8),
                                                                        in_=wv[:, :, 512 + (part * 2 + cc) * 128:512 + (part * 2 + cc + 1) * 128]), writes=[wslb], dma=True)
                for part in range(3):
                    fc = part * 2 + cc
                    for ti, (t0, tl, c) in enumerate(TT):
                        pt, pb = next_ps()
                        for k in range(8):
                            S.op("pe", lambda e, pt=pt, wsl=wsl, k=k, part=part, t0=t0, tl=tl: e.matmul(pt[:, 0:tl], lhsT=wsl[:, part * 1024 + k * 128:part * 1024 + (k + 1) * 128], rhs=u[:, k, t0:t0 + tl],
                                                                                                  start=(k == 0), stop=(k == 7)),
                                 reads=[wslb, u_b[ti]], writes=[pb])
                        S.op("act", lambda e, pt=pt, t0=t0, tl=tl: e.copy(out=rstd[:, t0:t0 + tl], in_=pt[:, 0:tl]), reads=[pb], writes=[rstd_b])
                    for ti, (t0, tl, c) in enumerate(TT):
                        ac, acb = next_tmp()
                        S.op("act", lambda e, ac=ac, t0=t0, tl=tl, fc=fc: e.activation(out=ac[:, 0:tl], in_=rstd[:, t0:t0 + tl], func=AF.Identity, scale=hcw(l, 1, fc), bias=hcb(l, fc)),
                             reads=[rstd_b, vec3_b], writes=[acb])
                        S.op("dve", lambda e, ac=ac, t0=t0, tl=tl, fc=fc: e.scalar_tensor_tensor(out=ac[:, 1:tl], in0=rstd[:, t0:t0 + tl - 1], scalar=hcw(l, 0, fc), in1=ac[:, 1:tl], op0=ALU.mult, op1=ALU.add),
                             reads=[rstd_b, vec3_b, acb], writes=[acb])
                        S.op("dve", lambda e, ac=ac, t0=t0, tl=tl, fc=fc: e.scalar_tensor_tensor(out=ac[:, 0:tl - 1], in0=rstd[:, t0 + 1:t0 + tl], scalar=hcw(l, 2, fc), in1=ac[:, 0:tl - 1], op0=ALU.mult, op1=ALU.add),
                             reads=[rstd_b, vec3_b, acb], writes=[acb])
                        fix = []
                        if t0 == 256:
                            fix = [(512 - t0, 511, 2), (511 - t0, 512, 3), (767 - t0, 768, 1)]
                        elif t0 == 768:
                            fix = [(0, 767, 0), (1024 - t0, 1023, 2), (1023 - t0, 1024, 3)]
                        for (col, src, si) in fix:
                            S.op("dve", lambda e, ac=ac, col=col, src=src, si=si, fc=fc: e.scalar_tensor_tensor(out=ac[:, col:col + 1], in0=rstd[:, src:src + 1], scalar=hysc[:, l, fc, si:si + 1],
                                                                                                              in1=ac[:, col:col + 1], op0=ALU.mult, op1=ALU.add),
                                 reads=[rstd_b, hysc_b, acb], writes=[acb])
                        if part == 0:
                            S.op("act", lambda e, ac=ac, t0=t0, tl=tl: e.copy(out=hyF[:, t0:t0 + tl], in_=ac[:, 0:tl]), reads=[acb], writes=[hyF_b])
                        elif part == 1:
                            S.op("act", lambda e, ac=ac, t0=t0, tl=tl: e.copy(out=hyx1[:, t0:t0 + tl], in_=ac[:, 0:tl]), reads=[acb], writes=[hyx1_b])
                        else:
                            S.op("act", lambda e, ac=ac, t0=t0, tl=tl: e.copy(out=hyx2[:, t0:t0 + tl], in_=ac[:, 0:tl]), reads=[acb], writes=[hyx2_b])
                if l == 0 and cc == 0:
                    dump(0, hyF[:, :], [hyF_b])
                    dump(1, hyx1[:, :], [hyx1_b])
                    dump(2, hyh2[:, :], [hyh2_b], np_=64)
                for o_ in range(2):
                    for tb0 in range(0, NTB, 4):
                        nb = min(4, NTB - tb0)
                        pt, pb = next_ps()
                        for q in range(nb):
                            tb = tb0 + q
                            S.op("pe", lambda e, pt=pt, q=q, tb=tb: e.transpose(pt[:, q * 128:(q + 1) * 128], hyF[:, tb * 128:(tb + 1) * 128], ident[:]), reads=[hyF_b, ident_b], writes=[pb])
                        S.op("act", lambda e, pt=pt, tb0=tb0, nb=nb: e.copy(out=hyvtok[:, tb0:tb0 + nb, :], in_=pt[:, 0:nb * 128].rearrange("p (b c) -> p b c", b=nb)), reads=[pb], writes=[hyvtok_b])
                    for jb in range(NTB):
                        dc, dcb = next_tmp()
                        S.op("sp", lambda e, dc=dc, jb=jb, cc=cc: e.dma_start(out=dc[:, 0:256].rearrange("p (d c) -> p d c", d=2), in_=hdec_d[jb * 128:(jb + 1) * 128, :, cc * 128:(cc + 1) * 128]),
                             writes=[dcb], dma=True)
                        pt, pb = next_ps()
                        for dr in range(2):
                            cb0 = dr * 512 + o_ * 256 + cc * 128
                            S.op("pe", lambda e, pt=pt, dr=dr, cb0=cb0, jb=jb: e.matmul(pt[:, dr * 128:(dr + 1) * 128], lhsT=hyh2[:, jb * 128:(jb + 1) * 128], rhs=w3b[:, cb0:cb0 + 128], start=True, stop=True),
                                 reads=[hyh2_b, hyw_b], writes=[pb])
                        S.op("dve", lambda e, pt=pt, dc=dc: e.tensor_tensor(out=dc[:, 0:256], in0=pt[:, 0:256], in1=dc[:, 0:256], op=ALU.mult), reads=[pb, dcb], writes=[dcb])
                        S.op("dve", lambda e, dc=dc, jb=jb: e.tensor_tensor(out=hyhp[:, jb, :], in0=dc[:, 0:128], in1=dc[:, 128:256], op=ALU.add), reads=[dcb], writes=[hyh_b])
                        S.op("dve", lambda e, dc=dc, jb=jb: e.tensor_tensor(out=hyhm[:, jb, :], in0=dc[:, 0:128], in1=dc[:, 128:256], op=ALU.subtract), reads=[dcb], writes=[hyh_b])
                    if l == 0 and cc == 0 and o_ == 0:
                        dump(3, arena[:, 6400:7680], [hyh_b])
                        dump(7, arena[:, 5120:6400], [hyvtok_b])
                    psl, pslb = ring[0], ring_b[0]
                    fws = None
                    for pair in list(range(2, 10)) + [0, 1]:
                        if pair == 0:
                            S.op("pool", lambda e, psl=psl: e.dma_start(out=psl[:, 0:1024].rearrange("p (t r) -> p t r", t=2), in_=hfwp_d.rearrange("(t p) r -> p t r", p=128)), writes=[pslb], dma=True)
                            S.op("pool", lambda e, psl=psl: e.dma_start(out=psl[:, 1024:2048].rearrange("p (r t) -> p r t", r=4), in_=hivp_d.rearrange("(r p) t -> p r t", p=128)), writes=[pslb], dma=True)
                        if pair < 2:
                            rc_re, rc_im = pair, 2 + pair
                            ntc, tb_base = 2, 0
                            lre = lambda tc, rc: psl[:, tc * 512 + rc * 128:tc * 512 + (rc + 1) * 128]
                            fb_ = pslb
                            yi = (rc_re, rc_im)
                        else:
                            pp_ = pair - 2
                            sgrp, a_ = pp_ // 2, pp_ % 2
                            rc_re, rc_im = 4 * sgrp + a_, 4 * sgrp + 2 + a_
                            if pp_ % 4 == 0:
                                half = pp_ // 4
                                fws, fwsb = ring[1 + half], ring_b[1 + half]
                                S.op("pool", lambda e, fws=fws, half=half: e.dma_start(out=fws[:, :].rearrange("p (t r) -> p t r", t=8),
                                                                                      in_=hfwg_d.rearrange("(t p) r -> p t r", p=128)[:, :, half * 1024:(half + 1) * 1024]), writes=[fwsb], dma=True)
                            ntc, tb_base = 8, 2
                            lre = lambda tc, rc, fws=fws: fws[:, tc * 1024 + (rc % 8) * 128:tc * 1024 + (rc % 8 + 1) * 128]
                            fb_ = fwsb
                            yi = (4 + rc_re, 4 + rc_im)
                        pu, pub = next_ps()
                        pk, pkb = next_ps()
                        for qi, rc in enumerate((rc_re, rc_im)):
                            for tc in range(ntc):
                                S.op("pe", lambda e, pu=pu, qi=qi, tc=tc, rc=rc, lre=lre, tb_base=tb_base, ntc=ntc: e.matmul(pu[:, qi * 128:(qi + 1) * 128], lhsT=lre(tc, rc), rhs=hyvtok[:, tb_base + tc, :],
                                                                                                                          start=(tc == 0), stop=(tc == ntc - 1)),
                                     reads=[fb_, hyvtok_b], writes=[pub])
                        for qi, rc in enumerate((rc_re, rc_im)):
                            hsrc = hyhp if qi == 0 else hyhm
                            for tc in range(ntc):
                                S.op("pe", lambda e, pk=pk, qi=qi, tc=tc, rc=rc, lre=lre, tb_base=tb_base, ntc=ntc, hsrc=hsrc: e.matmul(pk[:, qi * 128:(qi + 1) * 128], lhsT=lre(tc, rc), rhs=hsrc[:, tb_base + tc, :],
                                                                                                                                     start=(tc == 0), stop=(tc == ntc - 1)),
                                     reads=[fb_, hyh_b], writes=[pkb])
                        ut, utb = next_tmp()
                        S.op("act", lambda e, ut=ut, pu=pu: e.copy(out=ut[:, 0:256], in_=pu[:, 0:256]), reads=[pub], writes=[utb])
                        S.op("dve", lambda e, ut=ut, pk=pk: e.tensor_tensor(out=ut[:, 256:384], in0=ut[:, 0:128], in1=pk[:, 0:128], op=ALU.mult), reads=[utb, pkb], writes=[utb])
                        S.op("dve", lambda e, ut=ut, pk=pk: e.tensor_tensor(out=ut[:, 384:512], in0=ut[:, 128:256], in1=pk[:, 128:256], op=ALU.mult), reads=[utb, pkb], writes=[utb])
                        S.op("dve", lambda e, ut=ut, yi=yi: e.tensor_tensor(out=hyY[:, yi[0], :], in0=ut[:, 256:384], in1=ut[:, 384:512], op=ALU.subtract), reads=[utb], writes=[hyY_b])
                        S.op("dve", lambda e, ut=ut, pk=pk: e.tensor_tensor(out=ut[:, 256:384], in0=ut[:, 0:128], in1=pk[:, 128:256], op=ALU.mult), reads=[utb, pkb], writes=[utb])
                        S.op("dve", lambda e, ut=ut, pk=pk: e.tensor_tensor(out=ut[:, 384:512], in0=ut[:, 128:256], in1=pk[:, 0:128], op=ALU.mult), reads=[utb, pkb], writes=[utb])
                        S.op("dve", lambda e, ut=ut, yi=yi: e.tensor_tensor(out=hyY[:, yi[1], :], in0=ut[:, 256:384], in1=ut[:, 384:512], op=ALU.add), reads=[utb], writes=[hyY_b])
                    if l == 0 and cc == 0 and o_ == 0:
                        dump(4, arena[:, 8960:10240], [hyY_b])
                    ivs = []
                    for half in range(2):
                        isl, islb = ring[1 + half], ring_b[1 + half]
                        S.op("pool", lambda e, isl=isl, half=half: e.dma_start(out=isl[:, :].rearrange("p (r t) -> p r t", r=8),
                                                                              in_=hivg_d.rearrange("(r p) t -> p r t", p=128)[:, half * 8:(half + 1) * 8, :]), writes=[islb], dma=True)
                        ivs.append((isl, islb))
                    for ti, (t0, tl, c) in enumerate(TT):
                        pt, pb = next_ps()
                        if ti == 0:
                            for rc in range(4):
                                S.op("pe", lambda e, pt=pt, rc=rc: e.matmul(pt[:, 0:256], lhsT=hyY[:, rc, :], rhs=psl[:, 1024 + rc * 256:1024 + (rc + 1) * 256], start=(rc == 0), stop=(rc == 3)),
                                     reads=[hyY_b, pslb], writes=[pb])
                        else:
                            for rc in range(16):
                                isl, islb = ivs[rc // 8]
                                S.op("pe", lambda e, pt=pt, rc=rc, isl=isl, t0=t0: e.matmul(pt[:, 0:512], lhsT=hyY[:, 4 + rc, :], rhs=isl[:, (rc % 8) * 1024 + (t0 - 256):(rc % 8) * 1024 + (t0 - 256) + 512],
                                                                                          start=(rc == 0), stop=(rc == 15)),
                                     reads=[hyY_b, islb], writes=[pb])
                        if o_ == 0:
                            S.op("dve", lambda e, pt=pt, t0=t0, tl=tl, cc=cc: e.scalar_tensor_tensor(out=hyF[:, t0:t0 + tl], in0=hyF[:, t0:t0 + tl], scalar=hbias(l, 0, cc), in1=pt[:, 0:tl], op0=ALU.mult, op1=ALU.add),
                                 reads=[pb, vec3_b, hyF_b], writes=[hyF_b])
                            S.op("dve", lambda e, t0=t0, tl=tl: e.tensor_tensor(out=hyF[:, t0:t0 + tl], in0=hyF[:, t0:t0 + tl], in1=hyx1[:, t0:t0 + tl], op=ALU.mult), reads=[hyF_b, hyx1_b], writes=[hyF_b])
                            if l == 0 and cc == 0 and ti == 2:
                                dump(5, hyF[:, :], [hyF_b])
                        else:
                            tm, tmb = next_tmp()
                            dst = hyob0 if cc == 0 else hyx2
                            dstb = hyob0_b if cc == 0 else hyx2_b
                            S.op("dve", lambda e, pt=pt, tm=tm, t0=t0, tl=tl, cc=cc: e.scalar_tensor_tensor(out=tm[:, 0:tl], in0=hyF[:, t0:t0 + tl], scalar=hbias(l, 1, cc), in1=pt[:, 0:tl], op0=ALU.mult, op1=ALU.add),
                                 reads=[pb, vec3_b, hyF_b], writes=[tmb])
                            S.op("dve", lambda e, tm=tm, dst=dst, t0=t0, tl=tl: e.tensor_tensor(out=dst[:, t0:t0 + tl], in0=tm[:, 0:tl], in1=hyx2[:, t0:t0 + tl], op=ALU.mult), reads=[tmb, hyx2_b], writes=[dstb, hyx2_b])
            if l == 0:
                dump(6, hyob0[:, :], [hyob0_b])
            osrc = [hyob0, hyx2]
            osb = [hyob0_b, hyx2_b]
            wo, wob = ring[0], ring_b[0]
            _ri[0] = 1
            S.op("pool", lambda e, wo=wo: e.dma_start(out=wo[:, 0:2048].rearrange("p (h n) -> p h n", h=2), in_=wout_d[l, 256:512, :].rearrange("(h p) n -> p h n", p=128)), writes=[wob], dma=True)
            for ti, (t0, tl, c) in enumerate(TT):
                pt, pb = next_ps()
                for cc in range(2):
                    et, etb = next_e()
                    S.op("act", lambda e, et=et, cc=cc, t0=t0, tl=tl: e.activation(out=et[:, 0:tl], in_=osrc[cc][:, t0:t0 + tl], func=AF.Square), reads=[osb[cc]], writes=[etb])
                    S.op("pe", lambda e, pt=pt, et=et, cc=cc, tl=tl: e.matmul(pt[:, 0:tl], lhsT=ones_bf[:], rhs=et[:, 0:tl], start=(cc == 0), stop=(cc == 1)), reads=[ones_b, etb], writes=[pb])
                tm, tmb = next_tmp()
                S.op("act", lambda e, pt=pt, tm=tm, tl=tl: e.activation(out=tm[:, 0:tl], in_=pt[:, 0:tl], func=AF.Sqrt, scale=1.0 / 256, bias=EPS), reads=[pb], writes=[tmb])
                S.op("dve", lambda e, tm=tm, tl=tl: e.reciprocal(out=tm[:, 0:tl], in_=tm[:, 0:tl]), reads=[tmb], writes=[tmb])
                for cc in range(2):
                    S.op("dve", lambda e, tm=tm, cc=cc, t0=t0, tl=tl: e.scalar_tensor_tensor(out=osrc[cc][:, t0:t0 + tl], in0=osrc[cc][:, t0:t0 + tl], scalar=mixgain128(l, 2 + cc), in1=tm[:, 0:tl],
                                                                                            op0=ALU.mult, op1=ALU.mult),
                         reads=[osb[cc], tmb, vec3_b], writes=[osb[cc]])
            wout_partial(l, 2, 128, wo, wob, lambda pi, t0, tl: (osrc[pi][:, t0:t0 + tl], [osb[pi]]))

        def gla_group(l):
            fence()
            wv = win_d[l].rearrange("(k p) n -> p k n", p=128)
            wsl, wslb = ring[0], ring_b[0]
            S.op("pool", lambda e: e.dma_start(out=wsl[:, 0:6400].rearrange("p (k n) -> p k n", k=8), in_=wv[:, :, 1792:2592]), writes=[wslb], dma=True)
            S.op("sp", lambda e: e.dma_start(out=gtri, in_=tri_d.rearrange("a p c -> p a c")), writes=[gconst_b], dma=True)
            S.op("pool", lambda e: e.dma_start(out=ggw.rearrange("p (z c) -> p z c", z=2), in_=gw_d[l].rearrange("z r c -> r z c")), writes=[gconst_b], dma=True)
            S.op("pool", lambda e: e.dma_start(out=ggb, in_=gb_d[l].rearrange("z c -> (z c)").rearrange("(o n) -> o n", o=1)), writes=[gconst_b], dma=True)

            def wcol(k, c, n):
                return wsl[:, k * 800 + c:k * 800 + c + n]
            for ti, (t0, tl, c) in enumerate(TT if GLA_PART >= 2 else []):
                for (dst, dstb, cb, m, idx) in ((gqT, gqT_b, 0, 64, 0), (gqT, gqT_b, 64, 64, 1), (gkT, gkT_b, 128, 64, 0), (gkT, gkT_b, 192, 64, 1),
                                                (ggdT, ggdT_b, 512, 16, 0), (ggdT, ggdT_b, 528, 16, 1)):
                    pt, pb = next_ps(6)
                    for k in range(8):
                        S.op("pe", lambda e, pt=pt, k=k, cb=cb, m=m, t0=t0, tl=tl: e.matmul(pt[0:m, 0:tl], lhsT=wcol(k, cb, m), rhs=u[:, k, t0:t0 + tl], start=(k == 0), stop=(k == 7)),
                             reads=[wslb, u_b[ti]], writes=[pb])
                    S.op("act", lambda e, pt=pt, dst=dst, m=m, idx=idx, t0=t0, tl=tl: e.copy(out=dst[0:m, idx, t0:t0 + tl], in_=pt[0:m, 0:tl]), reads=[pb], writes=[dstb])
            for tb in range(NTB if GLA_PART >= 3 else 0):
                tti = 0 if tb < 2 else (1 if tb < 6 else 2)
                pt, pb = next_ps(6)
                for k in range(8):
                    S.op("pe", lambda e, pt=pt, k=k, tb=tb: e.matmul(pt[:, 0:384], lhsT=u[:, k, tb * 128:(tb + 1) * 128], rhs=wcol(k, 128, 384), start=(k == 0), stop=(k == 7)),
                         reads=[wslb, u_b[tti]], writes=[pb])
                if GLA_VAR == 1:
                    continue
                S.op("act", lambda e, pt=pt, tb=tb: e.copy(out=gktok[:, tb, :], in_=pt[:, 0:128]), reads=[pb], writes=[gktok_b])
                if GLA_VAR == 2:
                    continue
                if GLA_VAR == 3:
                    S.op("act", lambda e, pt=pt, tb=tb: e.copy(out=gvtok[:, tb, :], in_=pt[:, 128:384]), reads=[pb], writes=[gvtok_b])
                    continue
                S.op("act", lambda e, pt=pt, tb=tb: e.copy(out=gvtok[:, tb, :], in_=pt[:, 128:384]), reads=[pb], writes=[gvtok_b])
            SCALE = 32.0 ** -0.5
            gqk2 = [gqk, arena[0:64, 12928:13440]]
            gke2 = [gke, arena[:, 13440:13568]]
            geb2 = [geb, arena[0:64, 13568:14592].bitcast(F32).rearrange("p (a t) -> p a t", a=4)]
            gqk2_b = [gqk_b, Buf()]
            gke2_b = [gke_b, Buf()]
            geb2_b = [geb_b, Buf()]
            ARENA_BUFS.extend([gqk2_b[1], gke2_b[1], geb2_b[1]])

            def gla_block(z, tb):
                gebz, gebz_b = geb2[z], geb2_b[z]
                slot = 0 if tb < 2 else 1 + (tb - 2) // 2
                first = (tb % 2 == 0) if z == 0 else (tb % 2 == 1)
                last = not first
                if first:
                    for p in range(2):
                        zi = z * 2 + p
                        if tb in (0, 1):
                            S.op("dve", lambda e, zi=zi: e.memset(gS[:, zi, :], 0.0), writes=[gS_b])
                        elif (z == 0 and tb == 2) or (z == 1 and tb == 9):
                            S.op("sp", lambda e, zi=zi, p=p, z=z: e.dma_start(out=gS[:, zi, :], in_=gs0_d[l, z, 2 * p:2 * p + 2].rearrange("h d v -> (h d) v")), writes=[gS_b], dma=True)
                        else:
                            S.op("dve", lambda e, zi=zi: e.tensor_scalar(out=gS[:, zi, :], in0=gS[:, zi, :], scalar1=hyflag[0:64, 0:1], scalar2=None, op0=ALU.mult), reads=[gS_b, hysc_b], writes=[gS_b])
                        S.op("act", lambda e, zi=zi: e.copy(out=gSb[:, zi, :], in_=gS[:, zi, :]), reads=[gS_b], writes=[gS_b])
                yield
                pl, plb = next_ps(4)
                S.op("pe", lambda e, pl=pl, tb=tb, z=z: e.matmul(pl[:, 0:128], lhsT=ggdT[:, z, tb * 128:(tb + 1) * 128], rhs=ggw[:, z * 128:(z + 1) * 128], start=True, stop=False),
                     reads=[ggdT_b, gconst_b], writes=[plb])
                S.op("pe", lambda e, pl=pl, z=z: e.matmul(pl[:, 0:128], lhsT=ones_bf[0:1, 0:128], rhs=ggb[:, z * 128:(z + 1) * 128], start=False, stop=True),
                     reads=[ones_b, gconst_b], writes=[plb])
                yield
                gp, gpb = tmp[z], tmp_b[z]
                S.op("act", lambda e, pl=pl, gp=gp: e.activation(out=gp[:, 0:128], in_=pl[:, 0:128], func=AF.Exp, scale=-1.0), reads=[plb], writes=[gpb])
                S.op("act", lambda e, gp=gp: e.activation(out=gp[:, 0:128], in_=gp[:, 0:128], func=AF.Ln, bias=1.0), reads=[gpb], writes=[gpb])
                yield
                for p in range(2):
                    pc, pcb = next_ps(4)
                    S.op("pe", lambda e, pc=pc, gp=gp, p=p, z=z: e.matmul(pc[0:64, 0:128], lhsT=gp[:, p * 64:(p + 1) * 64], rhs=gtri[:, z, :], start=True, stop=True),
                         reads=[gpb, gconst_b], writes=[pcb])
                    S.op("act", lambda e, pc=pc, p=p: e.activation(out=gebz[:, 2 * p, :], in_=pc[0:64, 0:128], func=AF.Exp, scale=-1.0 / 16), reads=[pcb], writes=[gebz_b])
                    S.op("act", lambda e, pc=pc, p=p: e.activation(out=gebz[:, 2 * p + 1, :], in_=pc[0:64, 0:128], func=AF.Exp, scale=1.0 / 16), reads=[pcb], writes=[gebz_b])
                yield
                qk, qkb = gqk2[z], gqk2_b[z]
                for p in range(2):
                    S.op("dve", lambda e, qk=qk, p=p, tb=tb: e.scalar_tensor_tensor(out=qk[0:64, p * 128:(p + 1) * 128], in0=gqT[:, p, tb * 128:(tb + 1) * 128], scalar=SCALE, in1=gebz[:, 2 * p, :],
                                                                                   op0=ALU.mult, op1=ALU.mult), reads=[gqT_b, gebz_b], writes=[qkb])
                    S.op("dve", lambda e, qk=qk, p=p, tb=tb: e.tensor_tensor(out=qk[0:64, 256 + p * 128:256 + (p + 1) * 128], in0=gkT[:, p, tb * 128:(tb + 1) * 128], in1=gebz[:, 2 * p + 1, :], op=ALU.mult),
                         reads=[gkT_b, gebz_b], writes=[qkb])
                yield
                pf, pfb = next_ps(4)
                S.op("pe", lambda e, pf=pf, gp=gp, z=z: e.matmul(pf[:, 0:128], lhsT=gtri[:, 2 + z, :], rhs=gp[:, 0:128], start=True, stop=True), reads=[gpb, gconst_b], writes=[pfb])
                S.op("act", lambda e, pf=pf, gp=gp: e.activation(out=gp[:, 128:256], in_=pf[:, 0:128], func=AF.Exp, scale=-1.0 / 16), reads=[pfb], writes=[gpb])
                yield
                ke, keb = gke2[z], gke2_b[z]
                S.op("dve", lambda e, ke=ke, gp=gp, tb=tb: e.tensor_tensor(out=ke[:, 0:128], in0=gktok[:, tb, :], in1=gp[:, 128:256], op=ALU.mult), reads=[gktok_b, gpb], writes=[keb])
                yield
                pcs = [(ps[6 - 2 * z], ps_b[6 - 2 * z]), (ps[7 - 2 * z], ps_b[7 - 2 * z])]
                for h in range(4):
                    p, sidx = h // 2, h % 2
                    zi = z * 2 + p
                    pa, pab = next_ps(4)
                    S.op("pe", lambda e, pa=pa, qk=qk, p=p, sidx=sidx: e.matmul(pa[:, 0:128], lhsT=qk[32 * sidx:32 * sidx + 32, 256 + p * 128:256 + (p + 1) * 128],
                                                                               rhs=qk[32 * sidx:32 * sidx + 32, p * 128:(p + 1) * 128], start=True, stop=True),
                         reads=[qkb], writes=[pab])
                    yield
                    am, amb = next_e()
                    S.op("dve", lambda e, pa=pa, am=am, z=z: e.tensor_tensor(out=am[:, 0:128], in0=pa[:, 0:128], in1=gtri[:, z, :], op=ALU.mult), reads=[pab, gconst_b], writes=[amb])
                    yield
                    po, pob = next_ps(4)
                    S.op("pe", lambda e, po=po, am=am, h=h, tb=tb: e.matmul(po[0:64, 0:128], lhsT=gvtok[:, tb, h * 64:(h + 1) * 64], rhs=am[:, 0:128], start=True, stop=False),
                         reads=[gvtok_b, amb], writes=[pob])
                    S.op("pe", lambda e, po=po, qk=qk, zi=zi, p=p, sidx=sidx: e.matmul(po[0:64, 0:128], lhsT=gSb[32 * sidx:32 * sidx + 32, zi, :], rhs=qk[32 * sidx:32 * sidx + 32, p * 128:(p + 1) * 128],
                                                                                      start=False, stop=True),
                         reads=[gS_b, qkb], writes=[pob])
                    yield
                    if (z == 0 and tb <= 4) or (z == 1 and tb >= 5):
                        S.op("act", lambda e, po=po, h=h, tb=tb: e.copy(out=godT[:, h, tb * 128:(tb + 1) * 128], in_=po[0:64, 0:128]), reads=[pob], writes=[godT_b])
                    else:
                        S.op("dve", lambda e, po=po, h=h, tb=tb: e.tensor_tensor(out=godT[:, h, tb * 128:(tb + 1) * 128], in0=godT[:, h, tb * 128:(tb + 1) * 128], in1=po[0:64, 0:128], op=ALU.add),
                             reads=[pob, godT_b], writes=[godT_b])
                    if sidx == 1:
                        pcx, pcxb = pcs[p]
                        S.op("pe", lambda e, pcx=pcx, ke=ke, p=p, tb=tb: e.matmul(pcx[0:64, 0:128], lhsT=ke[:, p * 64:(p + 1) * 64], rhs=gvtok[:, tb, p * 128:(p + 1) * 128], start=True, stop=True),
                             reads=[keb, gvtok_b], writes=[pcxb])
                yield
                for p in range(2):
                    zi = z * 2 + p
                    pcx, pcxb = pcs[p]
                    dcol = 127 if z == 0 else 0
                    for sx in range(2):
                        S.op("dve", lambda e, pcx=pcx, zi=zi, p=p, dcol=dcol, sx=sx: e.scalar_tensor_tensor(out=gS[32 * sx:32 * sx + 32, zi, :], in0=gS[32 * sx:32 * sx + 32, zi, :],
                                                                                                           scalar=gebz[32 * sx:32 * sx + 32, 2 * p, dcol:dcol + 1], in1=pcx[32 * sx:32 * sx + 32, 64 * sx:64 * sx + 64],
                                                                                                           op0=ALU.mult, op1=ALU.add), reads=[gS_b, gebz_b, pcxb], writes=[gS_b])
                    S.op("act", lambda e, zi=zi: e.copy(out=gSb[:, zi, :], in_=gS[:, zi, :]), reads=[gS_b], writes=[gS_b])
                    if last:
                        S.op("sp", lambda e, zi=zi, p=p, z=z, slot=slot: e.dma_start(out=gout_d[l, z, slot, 2 * p:2 * p + 2].rearrange("h d v -> (h d) v"), in_=gS[:, zi, :]),
                             reads=[gS_b], dma=True, is_out=True)

            for step in range(NTB):
                gens = [gla_block(0, step), gla_block(1, NTB - 1 - step)]
                while gens:
                    for g_ in list(gens):
                        try:
                            next(g_)
                        except StopIteration:
                            gens.remove(g_)
            if l == 0:
                dump(0, arena[0:64, 0:1280], [gqT_b], np_=64)
                dump(1, arena[:, 5120:6400], [gktok_b])
                dump(2, a2[0:64, 0:1280], [godT_b], np_=64)
                dump(3, a2[0:64, 3840:5120], [godT_b], np_=64)
                dump(6, arena[:, 6400:7680], [gvtok_b])
                dump(4, arena[0:64, 11264:12288].bitcast(F32), [geb_b], n=512, np_=64)
                dump(5, arena[0:64, 10496:11008].bitcast(F32), [gS_b], n=256, np_=64)
            wo, wob = ring[1], ring_b[1]
            S.op("pool", lambda e: e.dma_start(out=wo[0:64, 0:4096].rearrange("p (h n) -> p h n", h=4), in_=wout_d[l, 768:1024, :].rearrange("(h p) n -> p h n", p=64)), writes=[wob], dma=True)
            if GLA_PART < 4:
                return
            for ti, (t0, tl, c) in enumerate(TT):
                for h in range(4):
                    et, etb = next_e()
                    S.op("act", lambda e, et=et, h=h, t0=t0, tl=tl: e.activation(out=et[0:64, 0:tl], in_=godT[:, h, t0:t0 + tl], func=AF.Square), reads=[godT_b], writes=[etb])
                    pt, pb = next_ps(6)
                    S.op("pe", lambda e, pt=pt, et=et, tl=tl: e.matmul(pt[0:64, 0:tl], lhsT=ones_bf[0:64, 0:64], rhs=et[0:64, 0:tl], start=True, stop=True), reads=[ones_b, etb], writes=[pb])
                    S.op("act", lambda e, pt=pt, tl=tl: e.activation(out=rden[:, 0:tl], in_=pt[0:64, 0:tl], func=AF.Sqrt, scale=1.0 / 64, bias=EPS), reads=[pb], writes=[rden_b])
                    S.op("dve", lambda e, tl=tl: e.reciprocal(out=rden[:, 0:tl], in_=rden[:, 0:tl]), reads=[rden_b], writes=[rden_b])
                    S.op("dve", lambda e, h=h, t0=t0, tl=tl: e.scalar_tensor_tensor(out=godT[:, h, t0:t0 + tl], in0=godT[:, h, t0:t0 + tl], scalar=mixgain64(l, 12 + h), in1=rden[:, 0:tl], op0=ALU.mult, op1=ALU.mult),
                         reads=[godT_b, rden_b, vec64_b], writes=[godT_b])
                    pr, prb = next_ps(6)
                    for k in range(8):
                        S.op("pe", lambda e, pr=pr, k=k, h=h, t0=t0, tl=tl: e.matmul(pr[0:64, 0:tl], lhsT=wcol(k, 544 + h * 64, 64), rhs=u[:, k, t0:t0 + tl], start=(k == 0), stop=(k == 7)),
                             reads=[wslb, u_b[ti]], writes=[prb])
                    tm, tmb = next_tmp()
                    S.op("act", lambda e, pr=pr, tm=tm, tl=tl: e.activation(out=tm[0:64, 0:tl], in_=pr[0:64, 0:tl], func=AF.Silu), reads=[prb], writes=[tmb])
                    S.op("dve", lambda e, tm=tm, h=h, t0=t0, tl=tl: e.tensor_tensor(out=godT[:, h, t0:t0 + tl], in0=godT[:, h, t0:t0 + tl], in1=tm[0:64, 0:tl], op=ALU.mult), reads=[godT_b, tmb], writes=[godT_b])
            _ri[0] = 2
            wout_partial(l, 4, 64, wo, wob, lambda pi, t0, tl: (godT[:, pi, t0:t0 + tl], [godT_b]))

        for l in range(NLAYERS):
            if STAGES["ffn1"]:
                S.phase = "L%d_ffn1" % l
                norm_mod(l, 0)
                ffn(l, 0)
            if l == 0:
                mod_hook(9)
            if STAGES["mixer"]:
                S.phase = "L%d_norm2" % l
                norm_mod(l, 1)
                if STAGES.get("A", True):
                    S.phase = "L%d_attnA" % l
                    attention_group(l, 0)
                if STAGES.get("C", True):
                    S.phase = "L%d_attnC" % l
                    attention_group(l, 1)
                if STAGES.get("B", True):
                    S.phase = "L%d_hyena" % l
                    hyena_group(l)
                if STAGES.get("D", True):
                    S.phase = "L%d_gla" % l
                    gla_group(l)
            if STAGES["ffn2"]:
                S.phase = "L%d_ffn2" % l
                norm_mod(l, 2)
                ffn(l, 1)
        S.phase = "final"

        for ti, (t0, tl, c) in enumerate(TT):
            S.op("act", lambda e, t0=t0, tl=tl: e.activation(out=u[:, :, t0:t0 + tl], in_=xres[:, :, t0:t0 + tl], func=AF.Square), reads=[xres_b[ti]], writes=[u_b[ti]])
            pt, pb = next_ps()
            for k in range(8):
                S.op("pe", lambda e, pt=pt, k=k, t0=t0, tl=tl: e.matmul(pt[:, 0:tl], lhsT=ones_bf[:], rhs=u[:, k, t0:t0 + tl], start=(k == 0), stop=(k == 7)),
                     reads=[ones_b, u_b[ti]], writes=[pb])
            S.op("act", lambda e, pt=pt, t0=t0, tl=tl: e.activation(out=rstd[:, t0:t0 + tl], in_=pt[:, 0:tl], func=AF.Sqrt, scale=1.0 / D, bias=EPS), reads=[pb], writes=[rstd_b])
            S.op("dve", lambda e, t0=t0, tl=tl: e.reciprocal(out=rstd[:, t0:t0 + tl], in_=rstd[:, t0:t0 + tl]), reads=[rstd_b], writes=[rstd_b])
            for k in range(8):
                S.op("dve", lambda e, k=k, t0=t0, tl=tl: e.scalar_tensor_tensor(out=xres[:, k, t0:t0 + tl], in0=xres[:, k, t0:t0 + tl], scalar=finalgT[:, k:k + 1],
                                                                               in1=rstd[:, t0:t0 + tl], op0=ALU.mult, op1=ALU.mult),
                     reads=[xres_b[ti], rstd_b, vecs_b[1]], writes=[xres_b[ti]])
        for tb in range(NTB):
            sg, sgb = stg[tb % 2], stg_b[tb % 2]
            tti = 0 if tb < 2 else (1 if tb < 6 else 2)
            for half in range(2):
                pt, pb = next_ps()
                for q in range(4):
                    k = half * 4 + q
                    S.op("pe", lambda e, pt=pt, k=k, q=q, tb=tb: e.transpose(pt[:, q * 128:(q + 1) * 128], xres[:, k, tb * 128:(tb + 1) * 128], ident[:]),
                         reads=[xres_b[tti], ident_b], writes=[pb])
                if half == 0:
                    S.op("act", lambda e, pt=pt, sg=sg, half=half: e.copy(out=sg[:, half * 512:(half + 1) * 512], in_=pt[:, :]), reads=[pb], writes=[sgb])
                else:
                    S.op("dve", lambda e, pt=pt, sg=sg, half=half: e.tensor_copy(out=sg[:, half * 512:(half + 1) * 512], in_=pt[:, :]), reads=[pb], writes=[sgb])
            S.op("sp", lambda e, sg=sg, tb=tb: e.dma_start(out=y_d[tb * 128:(tb + 1) * 128, :], in_=sg[:]), reads=[sgb], dma=True, is_out=True)

        S.emit(st)
    return nc


N_CORES = 8


def core_tokens(c):
    if c < 2:
        return [30 + c], c
    base = 5 * (c - 2)
    return [base + i for i in range(5)], None


def rope_tables():
    rows = 1024 // 64
    r = np.repeat(np.arange(rows, dtype=np.float32), 64)
    col = np.tile(np.arange(64, dtype=np.float32), rows)
    nf = 16
    inv = (10000.0 ** (-np.arange(nf, dtype=np.float32) / nf)).astype(np.float32)
    ang = np.concatenate([r[:, None] * inv, col[:, None] * inv], axis=-1).astype(np.float32)
    return np.cos(ang).astype(np.float32), np.sin(ang).astype(np.float32)


def attn_masks(is_sample):
    m = np.zeros((4, 128, 1920), np.float32)
    a = np.arange(128)[:, None]
    x = np.arange(1920)[None, :]
    if is_sample:
        band = (np.abs(x - 896 - a) <= 128).astype(np.float32)
        m[0] = band
        m[1] = band
        m[2] = 1.0
        m[3] = 1.0
    else:
        ev = ((x >= 896) & (x < 1152)).astype(np.float32) * np.ones((128, 1), np.float32)
        od = ((x >= 768) & (x < 1024)).astype(np.float32) * np.ones((128, 1), np.float32)
        m[0] = ev
        m[1] = od
        m[2] = ev
        m[3] = od
    return m.astype(NPBF16)


def hy_tables(L):
    t = np.linspace(0.0, 1.0, L, dtype=np.float32)[:, None]
    w = ((2.0 * math.pi / L) * np.arange(L, dtype=np.float32)[:, None]).astype(np.float32)
    bands = np.linspace(1e-4, 15, 16, dtype=np.float32)[None, :]
    z = np.concatenate([t, np.cos(bands * w), -np.sin(bands * w)], axis=-1).astype(np.float32)
    deltas = np.linspace(math.log(1e-2) / 1.5, math.log(1e-2) / 0.3, 256, dtype=np.float32)
    dec = np.exp(-t * np.abs(deltas)).astype(np.float32)
    dec2 = np.stack([dec, dec], 1)
    dec2[0, 1] = 0.0
    r = np.arange(2 * L)
    f = 256 * (r // 512) + (r % 256)
    is_im = (r % 512) >= 256
    th = np.pi * (f[None, :] + 0.5) * np.arange(L)[:, None].astype(np.float64) / L
    FW = np.where(is_im[None, :], -np.sin(th), np.cos(th))
    IV = FW.T / L
    return z, dec2, FW.astype(np.float32), IV.astype(np.float32)


def blockdiag4(m):
    a, b = m.shape
    o = np.zeros((4 * a, 4 * b), m.dtype)
    for i in range(4):
        o[i * a:(i + 1) * a, i * b:(i + 1) * b] = m
    return o


_NC_CACHE = {}
SHARED_KEYS = ["w_mod", "b_mod", "norm_g", "ffn_w_in", "ffn_w_out", "final_g", "w_in", "w_out", "mix_g", "swa_sink", "qk_norm_g",
               "hy_conv_w", "hy_conv_b", "hy_w1", "hy_b1", "hy_w2", "hy_b2", "hy_w3", "hy_freq", "hy_bias", "gla_gate_w", "gla_gate_b"]


def kernel(**inp):
    f32 = np.float32
    x_prompt = np.asarray(inp["x_prompt"], f32)
    x_sample = np.asarray(inp["x_sample"], f32)
    c = np.asarray(inp["c"], f32)
    c_ctx = np.asarray(inp["c_ctx"], f32)
    if "nc" not in _NC_CACHE:
        _NC_CACHE["nc"] = build_program()
    nc = _NC_CACHE["nc"]
    shared = {k: np.ascontiguousarray(inp[k], f32) for k in SHARED_KEYS}
    shared["ident_in"] = np.eye(128, dtype=f32)
    cos_s, sin_s = rope_tables()
    caches = [np.asarray(inp[k], f32) for k in ("cache_swa_k", "cache_swa_v", "cache_gqa_k", "cache_gqa_v")]
    z_p, dec_p, fw_p, iv_p = hy_tables(256)
    z_s, dec_s, fw_s, iv_s = hy_tables(1024)
    shared["hy_fw_p"] = fw_p.astype(NPBF16)
    r_ = np.arange(128)[:, None]
    c_ = np.arange(128)[None, :]
    shared["tri_in"] = np.stack([r_ <= c_, r_ >= c_, r_ > c_, r_ < c_], 0).astype(f32)
    state_gla = np.asarray(inp["state_gla"], f32)
    shared["hy_iv_p"] = iv_p.astype(NPBF16)
    hy_prompt = dict(hy_zT=np.ascontiguousarray(np.concatenate([z_p] * 5, 0).T), hy_dec=np.ascontiguousarray(np.concatenate([dec_p] * 5, 0)),
                     hy_fw_g=blockdiag4(fw_p).astype(NPBF16), hy_iv_g=blockdiag4(iv_p).astype(NPBF16), hy_flag=np.zeros((128, 1), f32))
    hy_sample = dict(hy_zT=np.ascontiguousarray(np.concatenate([z_p, z_s], 0).T), hy_dec=np.ascontiguousarray(np.concatenate([dec_p, dec_s], 0)),
                     hy_fw_g=fw_s.astype(NPBF16), hy_iv_g=iv_s.astype(NPBF16), hy_flag=np.ones((128, 1), f32))
    in_maps = []
    for core in range(N_CORES):
        pids, sid = core_tokens(core)
        m = dict(shared)
        cos = np.ones((T, 32), f32)
        sin = np.zeros((T, 32), f32)
        if sid is None:
            xs = np.concatenate([x_prompt[p] for p in pids], 0)
            cond = np.stack([c_ctx, c_ctx], 0)
            m["ctx_kv"] = np.zeros((4, 2, 512, 128), f32)
            m["ctx_bias"] = np.full((128, 1), -30000.0, f32)
            m["gla_s0"] = np.zeros((2, 2, 4, 32, 64), f32)
        else:
            xs = np.concatenate([x_prompt[pids[0]], x_sample[sid]], 0)
            cond = np.stack([c_ctx, c[sid]], 0)
            cos[256:] = cos_s
            sin[256:] = sin_s
            m["ctx_kv"] = np.ascontiguousarray(np.stack([cc[sid].reshape(2, 512, 128) for cc in caches], 0), f32)
            m["ctx_bias"] = np.zeros((128, 1), f32)
            m["gla_s0"] = np.ascontiguousarray(state_gla[sid], f32)
        m["attn_mask"] = attn_masks(sid is not None)
        m.update(hy_sample if sid is not None else hy_prompt)
        m["rope_cos"] = cos
        m["rope_sin"] = sin
        m["x_in"] = np.ascontiguousarray(xs, f32)
        m["cond_in"] = np.ascontiguousarray(cond, f32)
        in_maps.append(m)
    if ONE_CORE:
        res = run_bass_kernel_spmd(nc, in_maps[2:3], core_ids=[0])
        LAST["outs"] = res.results
        return None
    res = run_bass_kernel_spmd(nc, in_maps, core_ids=list(range(N_CORES)))
    outs = res.results
    LAST["outs"] = outs
    B, SEQ = x_prompt.shape[0], x_prompt.shape[1]
    y_prompt = np.zeros((B, SEQ, D), f32)
    y_sample = np.zeros(x_sample.shape, f32)
    kvs = [np.zeros((B, 2, SEQ, 2, 64), f32) for _ in range(4)]
    new_state = np.zeros((B, 2, 2, 4, 32, 64), f32)
    for core in range(N_CORES):
        pids, sid = core_tokens(core)
        y = np.asarray(outs[core]["y"], f32)
        kvo = np.asarray(outs[core]["kv_out"], f32)
        gso = np.asarray(outs[core]["gla_out"], f32)
        if sid is not None:
            y_sample[sid] = y[256:]
        for i, p in enumerate(pids):
            y_prompt[p] = y[i * 256:(i + 1) * 256]
            for a in range(4):
                kvs[a][p] = kvo[a, :, i * 256:(i + 1) * 256, :].reshape(2, SEQ, 2, 64)
            new_state[p] = gso[:, :, i]
    return (y_prompt, y_sample, kvs[0], kvs[1], kvs[2], kvs[3], new_state)
```

```python
import math
from contextlib import ExitStack
import numpy as np
import ml_dtypes
import concourse.bass as bass
import concourse.mybir as mybir
from concourse.bass_utils import run_bass_kernel_spmd

F32 = mybir.dt.float32
BF16 = mybir.dt.bfloat16
AF = mybir.ActivationFunctionType
ALU = mybir.AluOpType
AX = mybir.AxisListType
NPBF16 = ml_dtypes.bfloat16

STAGES = {"ffn1": True, "mixer": True, "ffn2": True, "A": True, "C": True, "B": True, "D": True}
NLAYERS = 2
DEBUG = False
PROFILE_SCOPES = False
PROFILE_ENGINE = "pe"
GLA_STEPS = 99
GLA_PART = 9
GLA_VAR = 0
ONE_CORE = False
LAST = {}

ENGS = ("pe", "act", "dve", "pool", "sp")
DMA_NSEM = {"sp": 12, "act": 4, "pool": 12}


class Buf:
    __slots__ = ("name", "w", "r")

    def __init__(self, name=""):
        self.name = name
        self.w = None
        self.r = []


class Op:
    __slots__ = ("eng", "idx", "fn", "deps", "signal", "dma", "dsem", "dval", "cnt", "phase")

    def __init__(self, eng, idx, fn, dma):
        self.eng = eng
        self.idx = idx
        self.fn = fn
        self.deps = []
        self.signal = False
        self.dma = dma
        self.dsem = None
        self.dval = 0
        self.cnt = 0


class Sched:
    def __init__(self, nc, same_engine_sync=True):
        self.nc = nc
        self.ops = {e: [] for e in ENGS}
        self.ndma = {e: 0 for e in DMA_NSEM}
        self.dma_ops = {e: [] for e in DMA_NSEM}
        self.same = same_engine_sync
        self.out_dmas = []
        self.phase = None

    def op(self, eng, fn, reads=(), writes=(), dma=False, is_out=False):
        lst = self.ops[eng]
        o = Op(eng, len(lst), fn, dma)
        o.phase = self.phase
        deps = {}
        for b in reads:
            if b.w is not None:
                deps[id(b.w)] = b.w
        for b in writes:
            if b.w is not None:
                deps[id(b.w)] = b.w
            for r in b.r:
                deps[id(r)] = r
        if dma:
            j = self.ndma[eng]
            k = DMA_NSEM[eng]
            o.dsem = (eng, j % k)
            o.dval = 16 * (j // k + 1)
            if j >= k:
                p = self.dma_ops[eng][j - k]
                deps[id(p)] = p
            self.ndma[eng] += 1
            self.dma_ops[eng].append(o)
            if is_out:
                self.out_dmas.append(o)
        best = {}
        for d in deps.values():
            if d is o:
                continue
            if d.dma:
                o.deps.append(d)
                continue
            if d.eng == eng and (eng in ("pe", "sp") or not self.same):
                continue
            if d.eng not in best or best[d.eng].idx < d.idx:
                best[d.eng] = d
        for d in best.values():
            o.deps.append(d)
            d.signal = True
        for b in reads:
            if not dma:
                b.r = [r for r in b.r if r.dma or r.eng != eng]
            b.r.append(o)
        for b in writes:
            b.w = o
            b.r = []
        lst.append(o)
        return o

    def emit(self, stack):
        nc = self.nc
        CH = 2000
        fin = Op("sp", len(self.ops["sp"]), None, False)
        fin.deps = list(self.out_dmas)
        fin.phase = None
        self.ops["sp"].append(fin)
        for e in ENGS:
            c = 0
            for o in self.ops[e]:
                if o.signal:
                    c += 1
                o.cnt = c
        esem = {}
        for e in ENGS:
            n = (self.ops[e][-1].cnt if self.ops[e] else 0)
            for i in range(max(1, (n + CH - 1) // CH)):
                esem[(e, i)] = stack.enter_context(nc.semaphore("es_%s%d" % (e, i)))
        dsem = {}
        for e, k in DMA_NSEM.items():
            for i in range(k):
                dsem[(e, i)] = stack.enter_context(nc.semaphore("ds_%s%d" % (e, i)))
        block = stack.enter_context(nc.Block())

        def run(e, engine):
            known = {}
            kn_eng = {}
            cur = [None, None]

            def set_phase(ph):
                if not PROFILE_SCOPES or ph == cur[0] or e != PROFILE_ENGINE:
                    return
                if cur[1] is not None:
                    cur[1].__exit__(None, None, None)
                    cur[1] = None
                cur[0] = ph
                if ph is not None:
                    cur[1] = nc.named_scope(ph)
                    cur[1].__enter__()

            for o in self.ops[e] + [None]:
                if o is None:
                    set_phase(None)
                    break
                set_phase(o.phase)
                need = {}
                for d in o.deps:
                    if d.dma:
                        key, val = ("d",) + d.dsem, d.dval
                        if known.get(key, 0) >= val:
                            continue
                    else:
                        if kn_eng.get(d.eng, 0) >= d.cnt:
                            continue
                        key, val = ("e", d.eng, (d.cnt - 1) // CH), (d.cnt - 1) % CH + 1
                        kn_eng[d.eng] = d.cnt
                    if need.get(key, 0) < val:
                        need[key] = val
                for key, val in need.items():
                    s = dsem[key[1:]] if key[0] == "d" else esem[key[1:]]
                    engine.wait_ge(s, val)
                    if key[0] == "d":
                        known[key] = val
                if o.fn is None:
                    continue
                ins = o.fn(engine)
                if o.dma:
                    ins.then_inc(dsem[o.dsem], 16)
                elif o.signal:
                    ins.then_inc(esem[(e, (o.cnt - 1) // CH)], 1)

        @block.tensor
        def _(eng):
            run("pe", eng)

        @block.scalar
        def _(eng):
            run("act", eng)

        @block.vector
        def _(eng):
            run("dve", eng)

        @block.gpsimd
        def _(eng):
            run("pool", eng)

        @block.sync
        def _(eng):
            run("sp", eng)


D = 1024
T = 1280
TT = [(0, 256, 0), (256, 512, 1), (768, 512, 1)]
NTB = T // 128
DFF = 2816
NHC = DFF // 128
EPS = 1e-6
RING_SLOTS = 3
SLOT_ELEMS = 8192


class Prog:
    pass


def build_program():
    nc = bass.Bass("TRN2", target_bir_lowering=False)
    P = Prog()
    st = ExitStack()
    with st:
        S = Sched(nc)

        def din(name, shape, dt=F32):
            return nc.dram_tensor(name, list(shape), dt, kind="ExternalInput").ap()

        def dout(name, shape, dt=F32):
            return nc.dram_tensor(name, list(shape), dt, kind="ExternalOutput").ap()

        _n = [0]

        def sb(shape, dt, name=None):
            _n[0] += 1
            return st.enter_context(nc.sbuf_tensor(name or ("t%d" % _n[0]), list(shape), dt))

        x_d = din("x_in", [T, D])
        cond_d = din("cond_in", [2, D])
        wmod_d = din("w_mod", [2, D, 9 * D])
        bmod_d = din("b_mod", [2, 9 * D])
        normg_d = din("norm_g", [2, 3, D])
        fwin_d = din("ffn_w_in", [2, 2, D, 2 * DFF])
        fwout_d = din("ffn_w_out", [2, 2, DFF, D])
        finalg_d = din("final_g", [D])
        ident_d = din("ident_in", [128, 128])
        y_d = dout("y", [T, D])
        win_d = din("w_in", [2, D, 2592])
        wout_d = din("w_out", [2, D, D])
        mixg_d = din("mix_g", [2, D])
        sink_d = din("swa_sink", [2, 4])
        qkg_d = din("qk_norm_g", [2, 2, 64])
        cos_d = din("rope_cos", [T, 32])
        sin_d = din("rope_sin", [T, 32])
        ctx_d = din("ctx_kv", [4, 2, 512, 128])
        ctxbias_d = din("ctx_bias", [128, 1])
        amask_d = din("attn_mask", [4, 128, 1920], BF16)
        kv_d = dout("kv_out", [4, 2, T, 128])
        hcw_d = din("hy_conv_w", [2, 3, 768])
        hcb_d = din("hy_conv_b", [2, 768])
        hw1_d = din("hy_w1", [2, 33, 64])
        hb1_d = din("hy_b1", [2, 64])
        hw2_d = din("hy_w2", [2, 64, 64])
        hb2_d = din("hy_b2", [2, 64])
        hw3_d = din("hy_w3", [2, 64, 1024])
        hfr_d = din("hy_freq", [2, 2, 64])
        hbias_d = din("hy_bias", [2, 2, 256])
        hz_d = din("hy_zT", [33, T])
        hdec_d = din("hy_dec", [T, 2, 256])
        hfwg_d = din("hy_fw_g", [1024, 2048], BF16)
        hivg_d = din("hy_iv_g", [2048, 1024], BF16)
        hfwp_d = din("hy_fw_p", [256, 512], BF16)
        hivp_d = din("hy_iv_p", [512, 256], BF16)
        hflag_d = din("hy_flag", [128, 1])
        gw_d = din("gla_gate_w", [2, 2, 16, 128])
        gb_d = din("gla_gate_b", [2, 2, 128])
        gs0_d = din("gla_s0", [2, 2, 4, 32, 64])
        tri_d = din("tri_in", [4, 128, 128])
        gout_d = dout("gla_out", [2, 2, 5, 4, 32, 64])

        xres = sb([128, 8, T], F32, "xres")
        u = sb([128, 8, T], BF16, "u")
        hid = sb([128, 12, T], BF16, "hid")
        ring = [sb([128, SLOT_ELEMS], BF16, "ring%d" % i) for i in range(RING_SLOTS)]
        ring_b = [Buf("ring%d" % i) for i in range(RING_SLOTS)]
        stg = [sb([128, D], F32, "stg%d" % i) for i in range(2)]
        stg_b = [Buf() for _ in range(2)]
        tmp = [sb([128, 512], F32, "tmp%d" % i) for i in range(3)]
        tmp_b = [Buf() for _ in range(3)]
        rstd = sb([128, T], F32, "rstd")
        rstd_b = Buf()
        ident = sb([128, 128], F32, "ident")
        ident_b = Buf()
        ones_bf = sb([128, 128], BF16, "ones")
        ones_b = Buf()
        vecs_in = [sb([128, 128], F32, "vin%d" % i) for i in range(2)]
        vecs = [sb([128, 128], F32, "vec%d" % i) for i in range(2)]
        vecs_b = [Buf() for _ in range(2)]
        vin_b = [Buf() for _ in range(2)]
        condT = sb([128, 8, 2], BF16, "condT")
        condT_b = Buf()
        mod = sb([128, 2, 72, 2], F32, "mod")
        mod_b = Buf()
        modA = sb([128, 2, 3, 8, 2], F32, "modA")
        modG = sb([128, 2, 3, 8, 2], F32, "modG")
        modA_b = Buf()
        xres_b = [Buf("xres%d" % i) for i in range(3)]
        u_b = [Buf("u%d" % i) for i in range(3)]
        hid_b = [[Buf() for _ in range(3)] for _ in range(12)]

        arena = hid[:].rearrange("p a b -> p (a b)")
        qT = arena[0:64, 0:5120].rearrange("p (h t) -> p h t", h=4)
        kT = arena[0:64, 5120:7680].rearrange("p (h t) -> p h t", h=2)
        vtok = arena[:, 7680:8980].rearrange("p (b g f) -> p b g f", b=NTB, g=2)
        oT = arena[0:64, 8980:14100].rearrange("p (h t) -> p h t", h=4)
        qT_b, kT_b, vtok_b, oT_b = Buf("qT"), Buf("kT"), Buf("vtok"), Buf("oT")
        amask = sb([128, 4, 1920], BF16, "amask")
        amask_b = Buf()
        ropec = sb([128, NTB, 32], F32, "ropec")
        ropes = sb([128, NTB, 32], F32, "ropes")
        rope_b = Buf()
        gq = sb([128, 2, 6, 64], F32, "gq")
        gq_b = Buf()
        ctxbias = sb([128, 1], F32, "ctxbias")
        esink = sb([64, 8], F32, "esink")
        misc_b = Buf()
        ctxkT = sb([64, 2, 512], BF16, "ctxkT")
        ctxv = sb([128, 4, 2, 65], BF16, "ctxv")
        esink128 = sb([128, 8], F32, "esink128")
        ones_f = sb([128, 64], F32, "ones_f")
        rrow = sb([128, 512], F32, "rrow")
        rrow_b = Buf()
        ctxkT_b, ctxv_b = Buf(), Buf()
        kvst = [sb([128, 2, 128], F32, "kvst%d" % i) for i in range(2)]
        kvst_b = [Buf() for _ in range(2)]
        qkn = [sb([128, 384], F32, "qkn%d" % i) for i in range(2)]
        qkn_b = [Buf() for _ in range(2)]
        qkr = [sb([128, 384], F32, "qkr%d" % i) for i in range(2)]
        qkr_b = [Buf() for _ in range(2)]
        rtmp2 = [sb([128, 192], F32, "rtmp%d" % i) for i in range(2)]
        rtmp2_b = [Buf() for _ in range(2)]
        ssq2 = [sb([128, 8], F32, "ssq%d" % i) for i in range(2)]
        ssq2_b = [Buf() for _ in range(2)]
        etile = [sb([128, 512], BF16, "etile%d" % i) for i in range(4)]
        etile_b = [Buf() for _ in range(4)]
        rden = sb([64, 512], F32, "rden")
        rden_b = Buf()
        vin64 = sb([128, 64], F32, "vin64")
        vec64 = sb([64, 128], F32, "vec64")
        vec64_b = Buf()

        hyF = arena[:, 0:2560].bitcast(F32)
        hyx1 = arena[:, 2560:3840]
        hyx2 = arena[:, 3840:5120]
        hyvtok = arena[:, 5120:6400].rearrange("p (b c) -> p b c", b=NTB)
        hyhp = arena[:, 6400:7680].rearrange("p (b c) -> p b c", b=NTB)
        hyhm = arena[:, 7680:8960].rearrange("p (b c) -> p b c", b=NTB)
        hyY = arena[:, 8960:11520].rearrange("p (r c) -> p r c", r=20)
        hyob0 = arena[:, 11520:12800]
        hyh2 = arena[0:64, 12800:14080]
        hyF_b, hyx1_b, hyx2_b, hyvtok_b, hyh_b, hyY_b, hyob0_b, hyh2_b = (Buf() for _ in range(8))
        ARENA_BUFS = [qT_b, kT_b, vtok_b, oT_b, hyF_b, hyx1_b, hyx2_b, hyvtok_b, hyh_b, hyY_b, hyob0_b, hyh2_b]
        gqT = arena[0:64, 0:2560].rearrange("p (a t) -> p a t", a=2)
        gkT = arena[0:64, 2560:5120].rearrange("p (a t) -> p a t", a=2)
        gktok = arena[:, 5120:6400].rearrange("p (b c) -> p b c", b=NTB)
        gvtok = arena[:, 6400:8960].rearrange("p (b c) -> p b c", b=NTB)
        gtri = arena[:, 8960:9984].bitcast(F32).rearrange("p (a c) -> p a c", a=4)
        ggw = arena[0:16, 9984:10240]
        ggb = arena[0:1, 10240:10496]
        gS = arena[0:64, 10496:11008].bitcast(F32).rearrange("p (a v) -> p a v", a=4)
        gSb = arena[0:64, 11008:11264].rearrange("p (a v) -> p a v", a=4)
        geb = arena[0:64, 11264:12288].bitcast(F32).rearrange("p (a t) -> p a t", a=4)
        gqk = arena[0:64, 12288:12800]
        gke = arena[:, 12800:12928]
        gqk_b, gke_b = Buf(), Buf()
        a2 = amask[:].rearrange("p a n -> p (a n)")
        godT = a2[0:64, 0:5120].rearrange("p (h t) -> p h t", h=4)
        ggdT = a2[0:16, 5120:7680].rearrange("p (z t) -> p z t", z=2)
        gqT_b, gkT_b, gktok_b, gvtok_b, gconst_b, gS_b, geb_b, godT_b, ggdT_b = (Buf() for _ in range(9))
        fdummy = sb([128, 1], F32, "fdummy")
        vin3 = sb([128, 128], F32, "vin3")
        vec3 = sb([128, 128], F32, "vec3")
        vec3_b = Buf()
        w3b = sb([64, 1024], BF16, "w3b")
        hyw12 = sb([64, 128], F32, "hyw12")
        hyw_b = Buf()
        hysc = sb([128, 2, 6, 4], F32, "hysc")
        hyfb = sb([64, 4], F32, "hyfb")
        hyflag = sb([128, 1], F32, "hyflag")
        hysc_b = Buf()

        ps = [st.enter_context(nc.psum_tensor("ps%d" % i, [128, 512], F32)) for i in range(8)]
        ps_b = [Buf("ps%d" % i) for i in range(8)]
        _pi = [0]

        def next_ps(n=8):
            i = _pi[0] % n
            _pi[0] += 1
            return ps[i], ps_b[i]

        _ri = [0]

        def next_slot():
            i = _ri[0] % RING_SLOTS
            _ri[0] += 1
            return ring[i], ring_b[i]

        _ti = [0]

        def next_tmp():
            i = _ti[0] % 3
            _ti[0] += 1
            return tmp[i], tmp_b[i]

        S.op("sp", lambda e: e.dma_start(out=ident[:], in_=ident_d), writes=[ident_b], dma=True)
        S.op("dve", lambda e: e.memset(ones_bf[:], 1.0), writes=[ones_b])

        S.op("dve", lambda e: e.memset(vecs_in[0][:], 0.0), writes=[vin_b[0]])
        S.op("dve", lambda e: e.memset(vecs_in[1][:], 0.0), writes=[vin_b[1]])
        S.op("sp", lambda e: e.dma_start(out=vecs_in[0][0:72, :], in_=bmod_d[0].rearrange("(c p) -> c p", p=128)), writes=[vin_b[0]], dma=True)
        S.op("sp", lambda e: e.dma_start(out=vecs_in[0][72:120, :], in_=normg_d.rearrange("l i (k p) -> (l i k) p", p=128)), writes=[vin_b[0]], dma=True)
        S.op("sp", lambda e: e.dma_start(out=vecs_in[1][0:72, :], in_=bmod_d[1].rearrange("(c p) -> c p", p=128)), writes=[vin_b[1]], dma=True)
        S.op("sp", lambda e: e.dma_start(out=vecs_in[1][72:88, :], in_=cond_d.rearrange("c (k p) -> (c k) p", p=128)), writes=[vin_b[1]], dma=True)
        S.op("sp", lambda e: e.dma_start(out=vecs_in[1][88:96, :], in_=finalg_d.rearrange("(k p) -> k p", p=128)), writes=[vin_b[1]], dma=True)
        for i in range(2):
            pt, pb = next_ps()
            S.op("pe", lambda e, i=i, pt=pt: e.transpose(pt[:, 0:128], vecs_in[i][:], ident[:]), reads=[vin_b[i], ident_b], writes=[pb])
            S.op("dve", lambda e, i=i, pt=pt: e.tensor_copy(out=vecs[i][:], in_=pt[:, 0:128]), reads=[pb], writes=[vecs_b[i]])

        def bmodT(l):
            return vecs[l][:, 0:72]

        def normgT(l, i):
            o = 72 + (l * 3 + i) * 8
            return vecs[0][:, o:o + 8]

        finalgT = vecs[1][:, 88:96]
        S.op("act", lambda e: e.activation(out=condT[:].rearrange("p k c -> p c k"), in_=vecs[1][:, 72:88].rearrange("p (c k) -> p c k", c=2), func=AF.Silu),
             reads=[vecs_b[1]], writes=[condT_b])

        S.phase = "load_x"
        for tb in range(NTB):
            sg, sgb = stg[tb % 2], stg_b[tb % 2]
            S.op("sp", lambda e, sg=sg, tb=tb: e.dma_start(out=sg[:], in_=x_d[tb * 128:(tb + 1) * 128, :]), writes=[sgb], dma=True)
            tti = 0 if tb < 2 else (1 if tb < 6 else 2)
            for half in range(2):
                pt, pb = next_ps()
                for q in range(4):
                    k = half * 4 + q
                    S.op("pe", lambda e, pt=pt, sg=sg, k=k, q=q: e.transpose(pt[:, q * 128:(q + 1) * 128], sg[:, k * 128:(k + 1) * 128], ident[:]),
                         reads=[sgb, ident_b], writes=[pb])
                eng = "act" if half == 0 else "dve"
                if eng == "act":
                    S.op("act", lambda e, pt=pt, half=half, tb=tb: e.copy(out=xres[:, half * 4:half * 4 + 4, tb * 128:(tb + 1) * 128], in_=pt[:, :].rearrange("p (a b) -> p a b", a=4)),
                         reads=[pb], writes=[xres_b[tti]])
                else:
                    S.op("dve", lambda e, pt=pt, half=half, tb=tb: e.tensor_copy(out=xres[:, half * 4:half * 4 + 4, tb * 128:(tb + 1) * 128], in_=pt[:, :].rearrange("p (a b) -> p a b", a=4)),
                         reads=[pb], writes=[xres_b[tti]])

        S.phase = "modulation"
        def mod_block(l, jb):
            sl, slb = next_slot()
            S.op("pool", lambda e, sl=sl: e.dma_start(out=sl[:, :].rearrange("p (k n) -> p k n", k=8),
                                                      in_=wmod_d[l].rearrange("(k p) n -> p k n", p=128)[:, :, jb * 1024:(jb + 1) * 1024]),
                 writes=[slb], dma=True)
            pt, pb = next_ps(6)
            for cc in range(8):
                for k in range(8):
                    S.op("pe", lambda e, pt=pt, sl=sl, cc=cc, k=k: e.matmul(pt[:, 2 * cc:2 * cc + 2], lhsT=sl[:, k * 1024 + cc * 128:k * 1024 + (cc + 1) * 128],
                                                                             rhs=condT[:, k, :], start=(k == 0), stop=(k == 7)),
                         reads=[slb, condT_b], writes=[pb])
            S.op("dve", lambda e, pt=pt: e.tensor_tensor(out=mod[:, l, jb * 8:(jb + 1) * 8, :], in0=pt[:, 0:16].rearrange("p (a c) -> p a c", c=2),
                                                         in1=bmodT(l)[:, jb * 8:(jb + 1) * 8].unsqueeze(2).broadcast_to([128, 8, 2]), op=ALU.add),
                 reads=[pb, vecs_b[l]], writes=[mod_b])

        def mod_finish(l):
            for i in range(3):
                S.op("dve", lambda e, i=i: e.tensor_scalar(out=modA[:, l, i], in0=mod[:, l, (3 * i + 1) * 8:(3 * i + 2) * 8, :], scalar1=1.0, scalar2=None, op0=ALU.add),
                     reads=[mod_b], writes=[modA_b])
                S.op("dve", lambda e, i=i: e.tensor_tensor(out=modA[:, l, i], in0=modA[:, l, i], in1=normgT(l, i).unsqueeze(2).broadcast_to([128, 8, 2]), op=ALU.mult),
                     reads=[modA_b, vecs_b[0]], writes=[modA_b])
                S.op("dve", lambda e, i=i: e.tensor_scalar(out=modG[:, l, i], in0=mod[:, l, (3 * i + 2) * 8:(3 * i + 3) * 8, :], scalar1=(1.0 if i == 1 else 0.5), scalar2=None, op0=ALU.mult),
                     reads=[mod_b], writes=[modA_b])

        for jb in range(9):
            mod_block(0, jb)
        mod_finish(0)
        MOD_PENDING = [(1, jb) for jb in range(9)] if NLAYERS > 1 else []

        def mod_hook(n=1):
            for _ in range(n):
                if MOD_PENDING:
                    l_, jb_ = MOD_PENDING.pop(0)
                    mod_block(l_, jb_)
                    if not MOD_PENDING:
                        mod_finish(l_)

        def modB(l, i, k, c):
            return mod[:, l, (3 * i) * 8 + k, c:c + 1]

        def norm_mod(l, i):
            for ti, (t0, tl, c) in enumerate(TT):
                S.op("act", lambda e, t0=t0, tl=tl: e.activation(out=u[:, :, t0:t0 + tl], in_=xres[:, :, t0:t0 + tl], func=AF.Square),
                     reads=[xres_b[ti]], writes=[u_b[ti]])
                pt, pb = next_ps()
                for k in range(8):
                    S.op("pe", lambda e, pt=pt, k=k, t0=t0, tl=tl: e.matmul(pt[:, 0:tl], lhsT=ones_bf[:], rhs=u[:, k, t0:t0 + tl], start=(k == 0), stop=(k == 7)),
                         reads=[ones_b, u_b[ti]], writes=[pb])
                S.op("act", lambda e, pt=pt, t0=t0, tl=tl: e.activation(out=rstd[:, t0:t0 + tl], in_=pt[:, 0:tl], func=AF.Sqrt, scale=1.0 / D, bias=EPS),
                     reads=[pb], writes=[rstd_b])
                S.op("dve", lambda e, t0=t0, tl=tl: e.reciprocal(out=rstd[:, t0:t0 + tl], in_=rstd[:, t0:t0 + tl]), reads=[rstd_b], writes=[rstd_b])
                for k in range(8):
                    tm, tmb = next_tmp()
                    S.op("dve", lambda e, tm=tm, k=k, t0=t0, tl=tl, c=c: e.scalar_tensor_tensor(out=tm[:, 0:tl], in0=xres[:, k, t0:t0 + tl], scalar=modA[:, l, i, k, c:c + 1],
                                                                                               in1=rstd[:, t0:t0 + tl], op0=ALU.mult, op1=ALU.mult),
                         reads=[xres_b[ti], rstd_b, modA_b], writes=[tmb])
                    S.op("act", lambda e, tm=tm, k=k, t0=t0, tl=tl, c=c: e.activation(out=u[:, k, t0:t0 + tl], in_=tm[:, 0:tl], func=AF.Identity, bias=modB(l, i, k, c), scale=1.0),
                         reads=[tmb, mod_b], writes=[u_b[ti]])

        def ffn(l, i):
            gi = 0 if i == 0 else 2
            win = fwin_d[l, i].rearrange("(k p) n -> p k n", p=128)
            wout = fwout_d[l, i].rearrange("(j p) n -> p j n", p=128)
            groups = [(0, 4), (4, 4), (8, 4), (12, 4), (16, 4), (20, 2)]
            halves = [groups[0:3], groups[3:6]]
            for hgroups in halves:
                h0 = hgroups[0][0]
                nh = sum(g[1] for g in hgroups)
                for (j0, nj) in hgroups:
                    sl, slb = next_slot()
                    S.op("pool", lambda e, sl=sl, j0=j0, nj=nj: e.dma_start(out=sl[:, 0:8 * nj * 128].rearrange("p (k n) -> p k n", k=8), in_=win[:, :, j0 * 128:(j0 + nj) * 128]),
                         writes=[slb], dma=True)
                    S.op("pool", lambda e, sl=sl, j0=j0, nj=nj: e.dma_start(out=sl[:, 4096:4096 + 8 * nj * 128].rearrange("p (k n) -> p k n", k=8),
                                                                            in_=win[:, :, DFF + j0 * 128:DFF + (j0 + nj) * 128]),
                         writes=[slb], dma=True)
                    for jj in range(nj):
                        j = j0 + jj
                        for ti, (t0, tl, c) in enumerate(TT):
                            pa, pab = next_ps()
                            pbt, pbb = next_ps()
                            for k in range(8):
                                S.op("pe", lambda e, pa=pa, sl=sl, k=k, jj=jj, nj=nj, t0=t0, tl=tl: e.matmul(pa[:, 0:tl], lhsT=sl[:, k * nj * 128 + jj * 128:k * nj * 128 + (jj + 1) * 128],
                                                                                                           rhs=u[:, k, t0:t0 + tl], start=(k == 0), stop=(k == 7)),
                                     reads=[slb, u_b[ti]], writes=[pab])
                            for k in range(8):
                                S.op("pe", lambda e, pbt=pbt, sl=sl, k=k, jj=jj, nj=nj, t0=t0, tl=tl: e.matmul(pbt[:, 0:tl], lhsT=sl[:, 4096 + k * nj * 128 + jj * 128:4096 + k * nj * 128 + (jj + 1) * 128],
                                                                                                             rhs=u[:, k, t0:t0 + tl], start=(k == 0), stop=(k == 7)),
                                     reads=[slb, u_b[ti]], writes=[pbb])
                            tm, tmb = next_tmp()
                            S.op("act", lambda e, pa=pa, tm=tm, tl=tl: e.activation(out=tm[:, 0:tl], in_=pa[:, 0:tl], func=AF.Silu), reads=[pab], writes=[tmb])
                            S.op("dve", lambda e, pbt=pbt, tm=tm, j=j, h0=h0, t0=t0, tl=tl: e.tensor_tensor(out=hid[:, j - h0, t0:t0 + tl], in0=tm[:, 0:tl], in1=pbt[:, 0:tl], op=ALU.mult),
                                 reads=[tmb, pbb], writes=[hid_b[j - h0][ti]])
                    if l == 0 and i == 0:
                        mod_hook(1)
                if l == 0 and i == 0:
                    mod_hook(1)
                slots = []
                jj0 = 0
                while jj0 < nh:
                    n = min(8, nh - jj0)
                    sl, slb = next_slot()
                    S.op("pool", lambda e, sl=sl, jj0=jj0, n=n, h0=h0: e.dma_start(out=sl[:, 0:n * 1024].rearrange("p (j n) -> p j n", j=n), in_=wout[:, h0 + jj0:h0 + jj0 + n, :]),
                         writes=[slb], dma=True)
                    slots.append((sl, slb, jj0, n))
                    jj0 += n
                for f in range(8):
                    for ti, (t0, tl, c) in enumerate(TT):
                        pt, pb = next_ps()
                        for (sl, slb, jj0, n) in slots:
                            for q in range(n):
                                jj = jj0 + q
                                S.op("pe", lambda e, pt=pt, sl=sl, q=q, f=f, jj=jj, t0=t0, tl=tl, nh=nh: e.matmul(pt[:, 0:tl], lhsT=sl[:, q * 1024 + f * 128:q * 1024 + (f + 1) * 128],
                                                                                                                rhs=hid[:, jj, t0:t0 + tl], start=(jj == 0), stop=(jj == nh - 1)),
                                     reads=[slb, hid_b[jj][ti]], writes=[pb])
                        S.op("dve", lambda e, pt=pt, f=f, t0=t0, tl=tl, c=c: e.scalar_tensor_tensor(out=xres[:, f, t0:t0 + tl], in0=pt[:, 0:tl], scalar=modG[:, l, gi, f, c:c + 1],
                                                                                                   in1=xres[:, f, t0:t0 + tl], op0=ALU.mult, op1=ALU.add),
                             reads=[pb, modA_b, xres_b[ti]], writes=[xres_b[ti]])

        S.phase = "consts"
        S.op("sp", lambda e: e.dma_start(out=ropec[:], in_=cos_d.rearrange("(b p) f -> p b f", p=128)), writes=[rope_b], dma=True)
        S.op("sp", lambda e: e.dma_start(out=ropes[:], in_=sin_d.rearrange("(b p) f -> p b f", p=128)), writes=[rope_b], dma=True)
        for l in range(2):
            for hh in range(6):
                S.op("sp", lambda e, l=l, hh=hh: e.dma_start(out=gq[:, l, hh, :], in_=qkg_d[l, (0 if hh < 4 else 1):(1 if hh < 4 else 2), :].broadcast_to([128, 64])),
                     writes=[gq_b], dma=True)
        S.op("sp", lambda e: e.dma_start(out=ctxbias[:], in_=ctxbias_d), writes=[misc_b], dma=True)
        S.op("sp", lambda e: e.dma_start(out=esink[:], in_=sink_d.rearrange("l h -> (l h)").rearrange("(o n) -> o n", o=1).broadcast_to([64, 8])), writes=[misc_b], dma=True)
        S.op("act", lambda e: e.activation(out=esink[:], in_=esink[:], func=AF.Exp), reads=[misc_b], writes=[misc_b])
        S.op("sp", lambda e: e.dma_start(out=esink128[:], in_=sink_d.rearrange("l h -> (l h)").rearrange("(o n) -> o n", o=1).broadcast_to([128, 8])), writes=[misc_b], dma=True)
        S.op("act", lambda e: e.activation(out=esink128[:], in_=esink128[:], func=AF.Exp), reads=[misc_b], writes=[misc_b])
        S.op("dve", lambda e: e.memset(ones_f[:], 1.0), writes=[misc_b])
        S.op("dve", lambda e: e.memset(vin64[:], 0.0), writes=[vec64_b])
        S.op("sp", lambda e: e.dma_start(out=vin64[0:32, :], in_=mixg_d.rearrange("l (c p) -> (l c) p", p=64)), writes=[vec64_b], dma=True)
        S.op("sp", lambda e: e.dma_start(out=vin64[32:36, :], in_=hfr_d.rearrange("l i d -> (l i) d")), writes=[vec64_b], dma=True)
        S.op("sp", lambda e: e.dma_start(out=vin64[36:38, :], in_=hb1_d), writes=[vec64_b], dma=True)
        S.op("sp", lambda e: e.dma_start(out=vin64[38:40, :], in_=hb2_d), writes=[vec64_b], dma=True)
        pt, pb = next_ps()
        S.op("pe", lambda e, pt=pt: e.transpose(pt[0:64, 0:128], vin64[:], ident[:]), reads=[vec64_b, ident_b], writes=[pb])
        S.op("dve", lambda e, pt=pt: e.tensor_copy(out=vec64[:], in_=pt[0:64, 0:128]), reads=[pb], writes=[vec64_b])

        def mixgain64(l, piece):
            return vec64[:, l * 16 + piece:l * 16 + piece + 1]

        _ei = [0]

        def next_e():
            i = _ei[0] % 4
            _ei[0] += 1
            return etile[i], etile_b[i]

        PS_O, PS_D = 6, 7

        def wout_partial(l, pieces, ksz, wslot, wslot_b, ysrc):
            for f in range(8):
                for ti, (t0, tl, c) in enumerate(TT):
                    pt, pb = next_ps(6)
                    for pi in range(pieces):
                        yap, ybufs = ysrc(pi, t0, tl)
                        S.op("pe", lambda e, pt=pt, pi=pi, f=f, tl=tl, yap=yap: e.matmul(pt[:, 0:tl], lhsT=wslot[0:ksz, pi * 1024 + f * 128:pi * 1024 + (f + 1) * 128], rhs=yap,
                                                                                         start=(pi == 0), stop=(pi == pieces - 1)),
                             reads=[wslot_b] + ybufs, writes=[pb])
                    S.op("dve", lambda e, pt=pt, f=f, t0=t0, tl=tl, c=c: e.scalar_tensor_tensor(out=xres[:, f, t0:t0 + tl], in0=pt[:, 0:tl], scalar=modG[:, l, 1, f, c:c + 1],
                                                                                               in1=xres[:, f, t0:t0 + tl], op0=ALU.mult, op1=ALU.add),
                         reads=[pb, modA_b, xres_b[ti]], writes=[xres_b[ti]])

        def attention_group(l, grp):
            c0 = 0 if grp == 0 else 1280
            wv = win_d[l].rearrange("(k p) n -> p k n", p=128)
            fence()
            if grp == 0 or not STAGES.get("A", True):
                S.op("sp", lambda e: e.dma_start(out=amask[:], in_=amask_d.rearrange("a p n -> p a n")), writes=[amask_b], dma=True)
            sl, slb = next_slot()
            S.op("pool", lambda e, sl=sl: e.dma_start(out=sl[:, 0:4096].rearrange("p (k n) -> p k n", k=8), in_=wv[:, :, c0:c0 + 512]), writes=[slb], dma=True)
            for g_ in range(2):
                S.op("pool", lambda e, g_=g_: e.dma_start(out=ctxv[:, :, g_, 0:64], in_=ctx_d[2 * grp + 1, l].rearrange("(b p) f -> p b f", p=128)[:, :, g_ * 64:(g_ + 1) * 64]),
                     reads=[u_b[0]], writes=[ctxv_b], dma=True)
            S.op("dve", lambda e: e.memset(ctxv[:, :, :, 64:65], 1.0), writes=[ctxv_b])
            S.op("dve", lambda e: e.memset(vtok[:, :, :, 64:65], 1.0), writes=[vtok_b])
            sg, sgb = stg[0], stg_b[0]
            S.op("sp", lambda e, sg=sg: e.dma_start(out=sg[:, 0:512].rearrange("p (b f) -> p b f", b=4), in_=ctx_d[2 * grp, l].rearrange("(b p) f -> p b f", p=128)),
                 reads=[u_b[0]], writes=[sgb], dma=True)
            for g in range(2):
                pt, pb = next_ps(6)
                for b in range(4):
                    S.op("pe", lambda e, pt=pt, sg=sg, b=b, g=g: e.transpose(pt[0:64, b * 128:(b + 1) * 128], sg[:, b * 128 + g * 64:b * 128 + (g + 1) * 64], ident[:]),
                         reads=[sgb, ident_b], writes=[pb])
                S.op("act", lambda e, pt=pt, g=g: e.copy(out=ctxkT[:, g, :], in_=pt[0:64, :]), reads=[pb], writes=[ctxkT_b])
            def prep_block(tb):
                tti = 0 if tb < 2 else (1 if tb < 6 else 2)
                sq_, sqb_ = ssq2[tb % 2], ssq2_b[tb % 2]
                pp, ppb = next_ps(6)
                for k in range(8):
                    S.op("pe", lambda e, pp=pp, sl=sl, k=k, tb=tb: e.matmul(pp[:, :], lhsT=u[:, k, tb * 128:(tb + 1) * 128], rhs=sl[:, k * 512:(k + 1) * 512], start=(k == 0), stop=(k == 7)),
                         reads=[slb, u_b[tti]], writes=[ppb])
                yield
                kv, kvb = kvst[tb % 2], kvst_b[tb % 2]
                qn, qnb = qkn[tb % 2], qkn_b[tb % 2]
                qr, qrb = qkr[tb % 2], qkr_b[tb % 2]
                S.op("act", lambda e, pp=pp, tb=tb: e.copy(out=vtok[:, tb, :, 0:64], in_=pp[:, 384:512].rearrange("p (g f) -> p g f", g=2)), reads=[ppb], writes=[vtok_b])
                S.op("act", lambda e, pp=pp, kv=kv: e.copy(out=kv[:, 1, :], in_=pp[:, 384:512]), reads=[ppb], writes=[kvb])
                if grp == 0:
                    S.op("act", lambda e, pp=pp, qn=qn: e.copy(out=qn[:, :], in_=pp[:, 0:384]), reads=[ppb], writes=[qnb])
                else:
                    S.op("act", lambda e, pp=pp, qr=qr: e.activation(out=qr[:, :], in_=pp[:, 0:384], func=AF.Square), reads=[ppb], writes=[qrb])
                    yield
                    S.op("dve", lambda e, qr=qr: e.reduce_sum(out=sq_[:, 0:6], in_=qr[:, :].rearrange("p (h d) -> p h d", h=6), axis=AX.X), reads=[qrb], writes=[sqb_])
                    yield
                    S.op("act", lambda e: e.activation(out=sq_[:, 0:6], in_=sq_[:, 0:6], func=AF.Sqrt, scale=1.0 / 64, bias=EPS), reads=[sqb_], writes=[sqb_])
                    yield
                    S.op("dve", lambda e: e.reciprocal(out=sq_[:, 0:6], in_=sq_[:, 0:6]), reads=[sqb_], writes=[sqb_])
                    S.op("dve", lambda e, pp=pp, qn=qn: e.tensor_tensor(out=qn[:, :].rearrange("p (h d) -> p h d", h=6), in0=pp[:, 0:384].rearrange("p (h d) -> p h d", h=6),
                                                                        in1=sq_[:, 0:6].unsqueeze(2).broadcast_to([128, 6, 64]), op=ALU.mult),
                         reads=[ppb, sqb_], writes=[qnb])
                    S.op("dve", lambda e, qn=qn: e.tensor_tensor(out=qn[:, :], in0=qn[:, :], in1=gq[:, l].rearrange("p h d -> p (h d)"), op=ALU.mult), reads=[qnb, gq_b], writes=[qnb])
                yield
                S.op("dve", lambda e, qn=qn, kv=kv: e.tensor_copy(out=kv[:, 0, :], in_=qn[:, 256:384]), reads=[qnb], writes=[kvb])
                S.op("sp", lambda e, kv=kv, tb=tb: e.dma_start(out=kv_d[2 * grp:2 * grp + 2, l, tb * 128:(tb + 1) * 128, :].rearrange("a t f -> t a f"), in_=kv[:]),
                     reads=[kvb], dma=True, is_out=True)
                x1 = qn[:, :].rearrange("p (h two d) -> p h two d", h=6, two=2)[:, :, 0, :]
                x2 = qn[:, :].rearrange("p (h two d) -> p h two d", h=6, two=2)[:, :, 1, :]
                o1 = qr[:, :].rearrange("p (h two d) -> p h two d", h=6, two=2)[:, :, 0, :]
                o2 = qr[:, :].rearrange("p (h two d) -> p h two d", h=6, two=2)[:, :, 1, :]
                cc = ropec[:, tb, :].unsqueeze(1).broadcast_to([128, 6, 32])
                ss = ropes[:, tb, :].unsqueeze(1).broadcast_to([128, 6, 32])
                rt = rtmp2[tb % 2][:, :].rearrange("p (h d) -> p h d", h=6)
                rtb = rtmp2_b[tb % 2]
                sq_, sqb_ = ssq2[tb % 2], ssq2_b[tb % 2]
                yield
                S.op("dve", lambda e, o1=o1, x1=x1, cc=cc: e.tensor_tensor(out=o1, in0=x1, in1=cc, op=ALU.mult), reads=[qnb, rope_b], writes=[qrb])
                S.op("dve", lambda e, rt=rt, x2=x2, ss=ss: e.tensor_tensor(out=rt, in0=x2, in1=ss, op=ALU.mult), reads=[qnb, rope_b], writes=[rtb])
                yield
                S.op("dve", lambda e, o1=o1, rt=rt: e.tensor_tensor(out=o1, in0=o1, in1=rt, op=ALU.subtract), reads=[qrb, rtb], writes=[qrb])
                yield
                S.op("dve", lambda e, o2=o2, x2=x2, cc=cc: e.tensor_tensor(out=o2, in0=x2, in1=cc, op=ALU.mult), reads=[qnb, rope_b], writes=[qrb])
                S.op("dve", lambda e, rt=rt, x1=x1, ss=ss: e.tensor_tensor(out=rt, in0=x1, in1=ss, op=ALU.mult), reads=[qnb, rope_b], writes=[rtb])
                yield
                S.op("dve", lambda e, o2=o2, rt=rt: e.tensor_tensor(out=o2, in0=o2, in1=rt, op=ALU.add), reads=[qrb, rtb], writes=[qrb])
                yield
                pq, pqb = next_ps(6)
                for h in range(4):
                    S.op("pe", lambda e, pq=pq, qr=qr, h=h: e.transpose(pq[0:64, h * 128:(h + 1) * 128], qr[:, h * 64:(h + 1) * 64], ident[:]), reads=[qrb, ident_b], writes=[pqb])
                yield
                S.op("act", lambda e, pq=pq, tb=tb: e.copy(out=qT[:, :, tb * 128:(tb + 1) * 128], in_=pq[0:64, :].rearrange("p (h t) -> p h t", h=4)), reads=[pqb], writes=[qT_b])
                yield
                pk, pkb = next_ps(6)
                for g in range(2):
                    S.op("pe", lambda e, pk=pk, qr=qr, g=g: e.transpose(pk[0:64, g * 128:(g + 1) * 128], qr[:, 256 + g * 64:256 + (g + 1) * 64], ident[:]), reads=[qrb, ident_b], writes=[pkb])
                yield
                S.op("act", lambda e, pk=pk, tb=tb: e.copy(out=kT[:, :, tb * 128:(tb + 1) * 128], in_=pk[0:64, 0:256].rearrange("p (h t) -> p h t", h=2)), reads=[pkb], writes=[kT_b])

            for tb0 in range(0, NTB, 2):
                gens = [prep_block(tb0), prep_block(tb0 + 1)]
                while gens:
                    for g_ in list(gens):
                        try:
                            next(g_)
                        except StopIteration:
                            gens.remove(g_)
            for h in range(4):
                g = h // 2
                segs = [(0, 256, [("loc", 0, None), ("loc", 128, None)]),
                        (256, 512, None), (768, 512, None)]
                for (q0, nq, keys) in segs:
                    if keys is None:
                        keys = [("ctx", b, None) for b in range(4)]
                        for j in range(8):
                            off = 896 - 128 * j + (q0 - 256)
                            keys.append(("loc", 256 + 128 * j, amask[:, 2 * grp + (j % 2), off:off + nq]))
                    po, pob = ps[PS_O], ps_b[PS_O]
                    pd, pdb = ps[PS_D], ps_b[PS_D]
                    nk = len(keys)
                    def stage1(ki, keys=keys, nq=nq, h=h, g=g, q0=q0):
                        kind, kpos, mk = keys[ki]
                        pss, pssb = next_ps(6)
                        et, etb = next_e()
                        if kind == "ctx":
                            S.op("pe", lambda e, pss=pss, kpos=kpos: e.matmul(pss[:, 0:nq], lhsT=ctxkT[:, g, kpos * 128:(kpos + 1) * 128], rhs=qT[:, h, q0:q0 + nq], start=True, stop=True),
                                 reads=[ctxkT_b, qT_b], writes=[pssb])
                            S.op("act", lambda e, pss=pss, et=et: e.activation(out=et[:, 0:nq], in_=pss[:, 0:nq], func=AF.Exp, scale=0.125, bias=ctxbias[:, 0:1]),
                                 reads=[pssb, misc_b], writes=[etb])
                            vap = ctxv[:, kpos, g, :]
                            vb = ctxv_b
                        else:
                            S.op("pe", lambda e, pss=pss, kpos=kpos: e.matmul(pss[:, 0:nq], lhsT=kT[:, g, kpos:kpos + 128], rhs=qT[:, h, q0:q0 + nq], start=True, stop=True),
                                 reads=[kT_b, qT_b], writes=[pssb])
                            S.op("act", lambda e, pss=pss, et=et: e.activation(out=et[:, 0:nq], in_=pss[:, 0:nq], func=AF.Exp, scale=0.125), reads=[pssb], writes=[etb])
                            if mk is not None:
                                S.op("dve", lambda e, et=et, mk=mk: e.tensor_tensor(out=et[:, 0:nq], in0=et[:, 0:nq], in1=mk, op=ALU.mult), reads=[etb, amask_b], writes=[etb])
                            vap = vtok[:, kpos // 128, g, :]
                            vb = vtok_b
                        return et, etb, vap, vb

                    def stage2(ki, st1, nq=nq, nk=nk):
                        et, etb, vap, vb = st1
                        S.op("pe", lambda e, vap=vap, et=et: e.matmul(po[0:65, 0:nq], lhsT=vap, rhs=et[:, 0:nq], start=(ki == 0), stop=(ki == nk - 1)),
                             reads=[vb, etb], writes=[pob])

                    pend = [stage1(0)]
                    if nk > 1:
                        pend.append(stage1(1))
                    for ki in range(nk):
                        cur_ = pend.pop(0)
                        if ki + 2 < nk:
                            pend.append(stage1(ki + 2))
                        stage2(ki, cur_)
                    if grp == 0:
                        S.op("dve", lambda e, nq=nq, h=h: e.tensor_scalar(out=rrow[64:65, 0:nq], in0=po[64:65, 0:nq], scalar1=esink128[64:65, l * 4 + h:l * 4 + h + 1], scalar2=None, op0=ALU.add),
                             reads=[pob, misc_b], writes=[rrow_b])
                        S.op("dve", lambda e, nq=nq: e.reciprocal(out=rrow[64:65, 0:nq], in_=rrow[64:65, 0:nq]), reads=[rrow_b], writes=[rrow_b])
                    else:
                        S.op("dve", lambda e, nq=nq: e.reciprocal(out=rrow[64:65, 0:nq], in_=po[64:65, 0:nq]), reads=[pob], writes=[rrow_b])
                    S.op("pe", lambda e, nq=nq: e.matmul(pd[0:64, 0:nq], lhsT=ones_f[64:65, 0:64], rhs=rrow[64:65, 0:nq], start=True, stop=True), reads=[misc_b, rrow_b], writes=[pdb])
                    S.op("act", lambda e, nq=nq: e.copy(out=rden[:, 0:nq], in_=pd[0:64, 0:nq]), reads=[pdb], writes=[rden_b])
                    S.op("dve", lambda e, h=h, q0=q0, nq=nq: e.tensor_tensor(out=oT[:, h, q0:q0 + nq], in0=po[0:64, 0:nq], in1=rden[:, 0:nq], op=ALU.mult),
                         reads=[pob, rden_b], writes=[oT_b])
            wsl, wslb = next_slot()
            S.op("pool", lambda e, wsl=wsl: e.dma_start(out=wsl[0:64, 0:4096].rearrange("p (h n) -> p h n", h=4),
                                                        in_=wout_d[l, (0 if grp == 0 else 512):(256 if grp == 0 else 768), :].rearrange("(h p) n -> p h n", p=64)),
                 writes=[wslb], dma=True)
            for ti, (t0, tl, c) in enumerate(TT):
                pt, pb = next_ps(6)
                for h in range(4):
                    et, etb = next_e()
                    S.op("act", lambda e, et=et, h=h, t0=t0, tl=tl: e.activation(out=et[0:64, 0:tl], in_=oT[:, h, t0:t0 + tl], func=AF.Square), reads=[oT_b], writes=[etb])
                    S.op("pe", lambda e, pt=pt, et=et, h=h, tl=tl: e.matmul(pt[0:64, 0:tl], lhsT=ones_bf[0:64, 0:64], rhs=et[0:64, 0:tl], start=(h == 0), stop=(h == 3)),
                         reads=[ones_b, etb], writes=[pb])
                S.op("act", lambda e, pt=pt, tl=tl: e.activation(out=rden[:, 0:tl], in_=pt[0:64, 0:tl], func=AF.Sqrt, scale=1.0 / 256, bias=EPS), reads=[pb], writes=[rden_b])
                S.op("dve", lambda e, tl=tl: e.reciprocal(out=rden[:, 0:tl], in_=rden[:, 0:tl]), reads=[rden_b], writes=[rden_b])
                for h in range(4):
                    piece = (0 if grp == 0 else 8) + h
                    S.op("dve", lambda e, h=h, t0=t0, tl=tl, piece=piece: e.scalar_tensor_tensor(out=oT[:, h, t0:t0 + tl], in0=oT[:, h, t0:t0 + tl], scalar=mixgain64(l, piece),
                                                                                                in1=rden[:, 0:tl], op0=ALU.mult, op1=ALU.mult),
                         reads=[oT_b, rden_b, vec64_b], writes=[oT_b])
            wout_partial(l, 4, 64, wsl, wslb, lambda pi, t0, tl: (oT[:, pi, t0:t0 + tl], [oT_b]))

        dbg_d = dout("dbg", [8, 128, T]) if DEBUG else None

        def dump(idx, ap, bufs, n=T, np_=128):
            if not DEBUG:
                return
            S.op("pool", lambda e: e.dma_start(out=dbg_d[idx, 0:np_, 0:n], in_=ap), reads=list(bufs), dma=True, is_out=True)

        ARENA_BUFS += [gqk_b, gke_b, gqT_b, gkT_b, gktok_b, gvtok_b, gconst_b, gS_b, geb_b, godT_b, ggdT_b, amask_b]

        def fence():
            S.op("dve", lambda e: e.memset(fdummy[:], 0.0), reads=ARENA_BUFS, writes=ARENA_BUFS)

        S.op("dve", lambda e: e.memset(vin3[:], 0.0), writes=[vec3_b])
        S.op("sp", lambda e: e.dma_start(out=vin3[0:36, :], in_=hcw_d.rearrange("l t (c p) -> (l t c) p", p=128)), writes=[vec3_b], dma=True)
        S.op("sp", lambda e: e.dma_start(out=vin3[36:48, :], in_=hcb_d.rearrange("l (c p) -> (l c) p", p=128)), writes=[vec3_b], dma=True)
        S.op("sp", lambda e: e.dma_start(out=vin3[48:56, :], in_=hbias_d.rearrange("l o (c p) -> (l o c) p", p=128)), writes=[vec3_b], dma=True)
        S.op("sp", lambda e: e.dma_start(out=vin3[56:72, :], in_=mixg_d.rearrange("l (c p) -> (l c) p", p=128)), writes=[vec3_b], dma=True)
        pt, pb = next_ps()
        S.op("pe", lambda e, pt=pt: e.transpose(pt[:, 0:128], vin3[:], ident[:]), reads=[vec3_b, ident_b], writes=[pb])
        S.op("dve", lambda e, pt=pt: e.tensor_copy(out=vec3[:], in_=pt[:, 0:128]), reads=[pb], writes=[vec3_b])

        def hcw(l, tap, fc):
            o = (l * 3 + tap) * 6 + fc
            return vec3[:, o:o + 1]

        def hcb(l, fc):
            o = 36 + l * 6 + fc
            return vec3[:, o:o + 1]

        def hbias(l, o_, cc):
            o = 48 + (l * 2 + o_) * 2 + cc
            return vec3[:, o:o + 1]

        def mixgain128(l, chunk):
            o = 56 + l * 8 + chunk
            return vec3[:, o:o + 1]

        S.op("sp", lambda e: e.dma_start(out=hyflag[:], in_=hflag_d), writes=[hysc_b], dma=True)
        for l in range(2):
            for fc in range(6):
                S.op("dve", lambda e, l=l, fc=fc: e.tensor_tensor(out=hysc[:, l, fc, 0:1], in0=hcw(l, 0, fc), in1=hyflag[:, 0:1], op=ALU.mult), reads=[vec3_b, hysc_b], writes=[hysc_b])
                S.op("dve", lambda e, l=l, fc=fc: e.tensor_tensor(out=hysc[:, l, fc, 1:2], in0=hcw(l, 2, fc), in1=hyflag[:, 0:1], op=ALU.mult), reads=[vec3_b, hysc_b], writes=[hysc_b])
                S.op("dve", lambda e, l=l, fc=fc: e.tensor_tensor(out=hysc[:, l, fc, 2:3], in0=hysc[:, l, fc, 0:1], in1=hcw(l, 0, fc), op=ALU.subtract), reads=[vec3_b, hysc_b], writes=[hysc_b])
                S.op("dve", lambda e, l=l, fc=fc: e.tensor_tensor(out=hysc[:, l, fc, 3:4], in0=hysc[:, l, fc, 1:2], in1=hcw(l, 2, fc), op=ALU.subtract), reads=[vec3_b, hysc_b], writes=[hysc_b])
            for i in range(2):
                S.op("dve", lambda e, l=l, i=i: e.tensor_tensor(out=hyfb[:, l * 2 + i:l * 2 + i + 1], in0=vec64[:, 32 + l * 2 + i:33 + l * 2 + i],
                                                                  in1=vec64[:, 36 + i * 2 + l:37 + i * 2 + l], op=ALU.mult), reads=[vec64_b], writes=[hysc_b])

        PI = math.pi

        def hyena_group(l):
            fence()
            wv = win_d[l].rearrange("(k p) n -> p k n", p=128)
            wsl, wslb = ring[0], ring_b[0]
            S.op("pool", lambda e: e.dma_start(out=w3b[:], in_=hw3_d[l]), reads=[hyw_b], writes=[hyw_b], dma=True)
            S.op("sp", lambda e: e.dma_start(out=hyw12[0:33, 0:64], in_=hw1_d[l]), writes=[hyw_b], dma=True)
            S.op("sp", lambda e: e.dma_start(out=hyw12[:, 64:128], in_=hw2_d[l]), writes=[hyw_b], dma=True)
            for ti, (t0, tl, c) in enumerate(TT):
                zt, ztb = next_tmp()
                S.op("sp", lambda e, zt=zt, t0=t0, tl=tl: e.dma_start(out=zt[0:33, 0:tl], in_=hz_d[:, t0:t0 + tl]), writes=[ztb], dma=True)
                cur, curb = zt, ztb
                for i in range(2):
                    pt, pb = next_ps()
                    kk = 33 if i == 0 else 64
                    wap = hyw12[0:33, 0:64] if i == 0 else hyw12[:, 64:128]
                    S.op("pe", lambda e, pt=pt, wap=wap, cur=cur, kk=kk, tl=tl: e.matmul(pt[0:64, 0:tl], lhsT=wap, rhs=cur[0:kk, 0:tl], start=True, stop=True), reads=[hyw_b, curb], writes=[pb])
                    a1, a1b = next_tmp()
                    S.op("dve", lambda e, pt=pt, a1=a1, i=i, tl=tl: e.tensor_scalar(out=a1[0:64, 0:tl], in0=pt[0:64, 0:tl], scalar1=vec64[:, 32 + l * 2 + i:33 + l * 2 + i],
                                                                                   scalar2=hyfb[:, l * 2 + i:l * 2 + i + 1], op0=ALU.mult, op1=ALU.add),
                         reads=[pb, vec64_b, hysc_b], writes=[a1b])
                    S.op("dve", lambda e, a1=a1, tl=tl: e.tensor_scalar(out=rden[:, 0:tl], in0=a1[0:64, 0:tl], scalar1=1.0 / (2.0 * PI), scalar2=12582912.0, op0=ALU.mult, op1=ALU.add), reads=[a1b], writes=[rden_b])
                    S.op("dve", lambda e, tl=tl: e.tensor_scalar(out=rden[:, 0:tl], in0=rden[:, 0:tl], scalar1=12582912.0, scalar2=None, op0=ALU.subtract), reads=[rden_b], writes=[rden_b])
                    S.op("dve", lambda e, a1=a1, tl=tl: e.scalar_tensor_tensor(out=a1[0:64, 0:tl], in0=rden[:, 0:tl], scalar=-2.0 * PI, in1=a1[0:64, 0:tl], op0=ALU.mult, op1=ALU.add), reads=[rden_b, a1b], writes=[a1b])
                    if i == 0:
                        S.op("act", lambda e, a1=a1, tl=tl: e.activation(out=a1[0:64, 0:tl], in_=a1[0:64, 0:tl], func=AF.Sin), reads=[a1b], writes=[a1b])
                        cur, curb = a1, a1b
                    else:
                        S.op("act", lambda e, a1=a1, t0=t0, tl=tl: e.activation(out=hyh2[:, t0:t0 + tl], in_=a1[0:64, 0:tl], func=AF.Sin), reads=[a1b], writes=[hyh2_b])
            for cc in range(2):
                for part in range(3):
                    S.op("pool", lambda e, part=part, cc=cc: e.dma_start(out=wsl[:, part * 1024:(part + 1) * 1024].rearrange("p (k n) -> p k n", k=8),
                                                                        in_=wv[:, :, 512 + (part * 2 + cc) * 128:512 + (part * 2 + cc + 1) * 128]), writes=[wslb], dma=True)
                for part in range(3):
                    fc = part * 2 + cc
                    pts = []
                    for ti, (t0, tl, c) in enumerate(TT):
                        pt, pb = next_ps()
                        for k in range(8):
                            S.op("pe", lambda e, pt=pt, wsl=wsl, k=k, part=part, t0=t0, tl=tl: e.matmul(pt[:, 0:tl], lhsT=wsl[:, part * 1024 + k * 128:part * 1024 + (k + 1) * 128], rhs=u[:, k, t0:t0 + tl],
                                                                                                      start=(k == 0), stop=(k == 7)),
                                 reads=[wslb, u_b[ti]], writes=[pb])
                        pts.append((pt, pb))

                    def zcol(t):
                        ti_ = 0 if t < 256 else (1 if t < 768 else 2)
                        return pts[ti_][0][:, t - TT[ti_][0]:t - TT[ti_][0] + 1], pts[ti_][1]

                    for ti, (t0, tl, c) in enumerate(TT):
                        pt, pb = pts[ti]
                        ac, acb = next_tmp()
                        S.op("act", lambda e, ac=ac, pt=pt, tl=tl, fc=fc: e.activation(out=ac[:, 0:tl], in_=pt[:, 0:tl], func=AF.Identity, scale=hcw(l, 1, fc), bias=hcb(l, fc)),
                             reads=[pb, vec3_b], writes=[acb])
                        S.op("dve", lambda e, ac=ac, pt=pt, tl=tl, fc=fc: e.scalar_tensor_tensor(out=ac[:, 1:tl], in0=pt[:, 0:tl - 1], scalar=hcw(l, 0, fc), in1=ac[:, 1:tl], op0=ALU.mult, op1=ALU.add),
                             reads=[pb, vec3_b, acb], writes=[acb])
                        S.op("dve", lambda e, ac=ac, pt=pt, tl=tl, fc=fc: e.scalar_tensor_tensor(out=ac[:, 0:tl - 1], in0=pt[:, 1:tl], scalar=hcw(l, 2, fc), in1=ac[:, 0:tl - 1], op0=ALU.mult, op1=ALU.add),
                             reads=[pb, vec3_b, acb], writes=[acb])
                        fix = []
                        if t0 == 256:
                            fix = [(512 - t0, 511, 2), (511 - t0, 512, 3), (767 - t0, 768, 1)]
                        elif t0 == 768:
                            fix = [(0, 767, 0), (1024 - t0, 1023, 2), (1023 - t0, 1024, 3)]
                        for (col, src, si) in fix:
                            zap, zb = zcol(src)
                            S.op("dve", lambda e, ac=ac, col=col, zap=zap, si=si, fc=fc: e.scalar_tensor_tensor(out=ac[:, col:col + 1], in0=zap, scalar=hysc[:, l, fc, si:si + 1],
                                                                                                              in1=ac[:, col:col + 1], op0=ALU.mult, op1=ALU.add),
                                 reads=[zb, hysc_b, acb], writes=[acb])
                        if part == 0:
                            S.op("act", lambda e, ac=ac, t0=t0, tl=tl: e.copy(out=hyF[:, t0:t0 + tl], in_=ac[:, 0:tl]), reads=[acb], writes=[hyF_b])
                        elif part == 1:
                            S.op("act", lambda e, ac=ac, t0=t0, tl=tl: e.copy(out=hyx1[:, t0:t0 + tl], in_=ac[:, 0:tl]), reads=[acb], writes=[hyx1_b])
                        else:
                            S.op("act", lambda e, ac=ac, t0=t0, tl=tl: e.copy(out=hyx2[:, t0:t0 + tl], in_=ac[:, 0:tl]), reads=[acb], writes=[hyx2_b])
                if l == 0 and cc == 0:
                    dump(0, hyF[:, :], [hyF_b])
                    dump(1, hyx1[:, :], [hyx1_b])
                    dump(2, hyh2[:, :], [hyh2_b], np_=64)
                for o_ in range(2):
                    for tb0 in range(0, NTB, 4):
                        nb = min(4, NTB - tb0)
                        pt, pb = next_ps()
                        for q in range(nb):
                            tb = tb0 + q
                            S.op("pe", lambda e, pt=pt, q=q, tb=tb: e.transpose(pt[:, q * 128:(q + 1) * 128], hyF[:, tb * 128:(tb + 1) * 128], ident[:]), reads=[hyF_b, ident_b], writes=[pb])
                        S.op("act", lambda e, pt=pt, tb0=tb0, nb=nb: e.copy(out=hyvtok[:, tb0:tb0 + nb, :], in_=pt[:, 0:nb * 128].rearrange("p (b c) -> p b c", b=nb)), reads=[pb], writes=[hyvtok_b])
                    for jb in range(NTB):
                        dc, dcb = next_tmp()
                        S.op("sp", lambda e, dc=dc, jb=jb, cc=cc: e.dma_start(out=dc[:, 0:256].rearrange("p (d c) -> p d c", d=2), in_=hdec_d[jb * 128:(jb + 1) * 128, :, cc * 128:(cc + 1) * 128]),
                             writes=[dcb], dma=True)
                        pt, pb = next_ps()
                        for dr in range(2):
                            cb0 = dr * 512 + o_ * 256 + cc * 128
                            S.op("pe", lambda e, pt=pt, dr=dr, cb0=cb0, jb=jb: e.matmul(pt[:, dr * 128:(dr + 1) * 128], lhsT=hyh2[:, jb * 128:(jb + 1) * 128], rhs=w3b[:, cb0:cb0 + 128], start=True, stop=True),
                                 reads=[hyh2_b, hyw_b], writes=[pb])
                        S.op("dve", lambda e, pt=pt, dc=dc: e.tensor_tensor(out=dc[:, 0:256], in0=pt[:, 0:256], in1=dc[:, 0:256], op=ALU.mult), reads=[pb, dcb], writes=[dcb])
                        S.op("dve", lambda e, dc=dc, jb=jb: e.tensor_tensor(out=hyhp[:, jb, :], in0=dc[:, 0:128], in1=dc[:, 128:256], op=ALU.add), reads=[dcb], writes=[hyh_b])
                        S.op("dve", lambda e, dc=dc, jb=jb: e.tensor_tensor(out=hyhm[:, jb, :], in0=dc[:, 0:128], in1=dc[:, 128:256], op=ALU.subtract), reads=[dcb], writes=[hyh_b])
                    if l == 0 and cc == 0 and o_ == 0:
                        dump(3, arena[:, 6400:7680], [hyh_b])
                        dump(7, arena[:, 5120:6400], [hyvtok_b])
                    psl, pslb = ring[0], ring_b[0]
                    fws = None
                    for pair in list(range(2, 10)) + [0, 1]:
                        if pair == 0:
                            S.op("pool", lambda e, psl=psl: e.dma_start(out=psl[:, 0:1024].rearrange("p (t r) -> p t r", t=2), in_=hfwp_d.rearrange("(t p) r -> p t r", p=128)), writes=[pslb], dma=True)
                            S.op("pool", lambda e, psl=psl: e.dma_start(out=psl[:, 1024:2048].rearrange("p (r t) -> p r t", r=4), in_=hivp_d.rearrange("(r p) t -> p r t", p=128)), writes=[pslb], dma=True)
                        if pair < 2:
                            rc_re, rc_im = pair, 2 + pair
                            ntc, tb_base = 2, 0
                            lre = lambda tc, rc: psl[:, tc * 512 + rc * 128:tc * 512 + (rc + 1) * 128]
                            fb_ = pslb
                            yi = (rc_re, rc_im)
                        else:
                            pp_ = pair - 2
                            sgrp, a_ = pp_ // 2, pp_ % 2
                            rc_re, rc_im = 4 * sgrp + a_, 4 * sgrp + 2 + a_
                            if pp_ % 4 == 0:
                                half = pp_ // 4
                                fws, fwsb = ring[1 + half], ring_b[1 + half]
                                S.op("pool", lambda e, fws=fws, half=half: e.dma_start(out=fws[:, :].rearrange("p (t r) -> p t r", t=8),
                                                                                      in_=hfwg_d.rearrange("(t p) r -> p t r", p=128)[:, :, half * 1024:(half + 1) * 1024]), writes=[fwsb], dma=True)
                            ntc, tb_base = 8, 2
                            lre = lambda tc, rc, fws=fws: fws[:, tc * 1024 + (rc % 8) * 128:tc * 1024 + (rc % 8 + 1) * 128]
                            fb_ = fwsb
                            yi = (4 + rc_re, 4 + rc_im)
                        pu, pub = next_ps()
                        pk, pkb = next_ps()
                        for qi, rc in enumerate((rc_re, rc_im)):
                            for tc in range(ntc):
                                S.op("pe", lambda e, pu=pu, qi=qi, tc=tc, rc=rc, lre=lre, tb_base=tb_base, ntc=ntc: e.matmul(pu[:, qi * 128:(qi + 1) * 128], lhsT=lre(tc, rc), rhs=hyvtok[:, tb_base + tc, :],
                                                                                                                          start=(tc == 0), stop=(tc == ntc - 1)),
                                     reads=[fb_, hyvtok_b], writes=[pub])
                        for qi, rc in enumerate((rc_re, rc_im)):
                            hsrc = hyhp if qi == 0 else hyhm
                            for tc in range(ntc):
                                S.op("pe", lambda e, pk=pk, qi=qi, tc=tc, rc=rc, lre=lre, tb_base=tb_base, ntc=ntc, hsrc=hsrc: e.matmul(pk[:, qi * 128:(qi + 1) * 128], lhsT=lre(tc, rc), rhs=hsrc[:, tb_base + tc, :],
                                                                                                                                     start=(tc == 0), stop=(tc == ntc - 1)),
                                     reads=[fb_, hyh_b], writes=[pkb])
                        ut, utb = next_tmp()
                        S.op("act", lambda e, ut=ut, pu=pu: e.copy(out=ut[:, 0:256], in_=pu[:, 0:256]), reads=[pub], writes=[utb])
                        S.op("dve", lambda e, ut=ut, pk=pk: e.tensor_tensor(out=ut[:, 256:384], in0=ut[:, 0:128], in1=pk[:, 0:128], op=ALU.mult), reads=[utb, pkb], writes=[utb])
                        S.op("dve", lambda e, ut=ut, pk=pk: e.tensor_tensor(out=ut[:, 384:512], in0=ut[:, 128:256], in1=pk[:, 128:256], op=ALU.mult), reads=[utb, pkb], writes=[utb])
                        S.op("dve", lambda e, ut=ut, yi=yi: e.tensor_tensor(out=hyY[:, yi[0], :], in0=ut[:, 256:384], in1=ut[:, 384:512], op=ALU.subtract), reads=[utb], writes=[hyY_b])
                        S.op("dve", lambda e, ut=ut, pk=pk: e.tensor_tensor(out=ut[:, 256:384], in0=ut[:, 0:128], in1=pk[:, 128:256], op=ALU.mult), reads=[utb, pkb], writes=[utb])
                        S.op("dve", lambda e, ut=ut, pk=pk: e.tensor_tensor(out=ut[:, 384:512], in0=ut[:, 128:256], in1=pk[:, 0:128], op=ALU.mult), reads=[utb, pkb], writes=[utb])
                        S.op("dve", lambda e, ut=ut, yi=yi: e.tensor_tensor(out=hyY[:, yi[1], :], in0=ut[:, 256:384], in1=ut[:, 384:512], op=ALU.add), reads=[utb], writes=[hyY_b])
                    if l == 0 and cc == 0 and o_ == 0:
                        dump(4, arena[:, 8960:10240], [hyY_b])
                    ivs = []
                    for half in range(2):
                        isl, islb = ring[1 + half], ring_b[1 + half]
                        S.op("pool", lambda e, isl=isl, half=half: e.dma_start(out=isl[:, :].rearrange("p (r t) -> p r t", r=8),
                                                                              in_=hivg_d.rearrange("(r p) t -> p r t", p=128)[:, half * 8:(half + 1) * 8, :]), writes=[islb], dma=True)
                        ivs.append((isl, islb))
                    for ti, (t0, tl, c) in enumerate(TT):
                        pt, pb = next_ps()
                        if ti == 0:
                            for rc in range(4):
                                S.op("pe", lambda e, pt=pt, rc=rc: e.matmul(pt[:, 0:256], lhsT=hyY[:, rc, :], rhs=psl[:, 1024 + rc * 256:1024 + (rc + 1) * 256], start=(rc == 0), stop=(rc == 3)),
                                     reads=[hyY_b, pslb], writes=[pb])
                        else:
                            for rc in range(16):
                                isl, islb = ivs[rc // 8]
                                S.op("pe", lambda e, pt=pt, rc=rc, isl=isl, t0=t0: e.matmul(pt[:, 0:512], lhsT=hyY[:, 4 + rc, :], rhs=isl[:, (rc % 8) * 1024 + (t0 - 256):(rc % 8) * 1024 + (t0 - 256) + 512],
                                                                                          start=(rc == 0), stop=(rc == 15)),
                                     reads=[hyY_b, islb], writes=[pb])
                        if o_ == 0:
                            S.op("dve", lambda e, pt=pt, t0=t0, tl=tl, cc=cc: e.scalar_tensor_tensor(out=hyF[:, t0:t0 + tl], in0=hyF[:, t0:t0 + tl], scalar=hbias(l, 0, cc), in1=pt[:, 0:tl], op0=ALU.mult, op1=ALU.add),
                                 reads=[pb, vec3_b, hyF_b], writes=[hyF_b])
                            S.op("dve", lambda e, t0=t0, tl=tl: e.tensor_tensor(out=hyF[:, t0:t0 + tl], in0=hyF[:, t0:t0 + tl], in1=hyx1[:, t0:t0 + tl], op=ALU.mult), reads=[hyF_b, hyx1_b], writes=[hyF_b])
                            if l == 0 and cc == 0 and ti == 2:
                                dump(5, hyF[:, :], [hyF_b])
                        else:
                            tm, tmb = next_tmp()
                            dst = hyob0 if cc == 0 else hyx2
                            dstb = hyob0_b if cc == 0 else hyx2_b
                            S.op("dve", lambda e, pt=pt, tm=tm, t0=t0, tl=tl, cc=cc: e.scalar_tensor_tensor(out=tm[:, 0:tl], in0=hyF[:, t0:t0 + tl], scalar=hbias(l, 1, cc), in1=pt[:, 0:tl], op0=ALU.mult, op1=ALU.add),
                                 reads=[pb, vec3_b, hyF_b], writes=[tmb])
                            S.op("dve", lambda e, tm=tm, dst=dst, t0=t0, tl=tl: e.tensor_tensor(out=dst[:, t0:t0 + tl], in0=tm[:, 0:tl], in1=hyx2[:, t0:t0 + tl], op=ALU.mult), reads=[tmb, hyx2_b], writes=[dstb, hyx2_b])
            if l == 0:
                dump(6, hyob0[:, :], [hyob0_b])
            osrc = [hyob0, hyx2]
            osb = [hyob0_b, hyx2_b]
            wo, wob = ring[0], ring_b[0]
            _ri[0] = 1
            S.op("pool", lambda e, wo=wo: e.dma_start(out=wo[:, 0:2048].rearrange("p (h n) -> p h n", h=2), in_=wout_d[l, 256:512, :].rearrange("(h p) n -> p h n", p=128)), writes=[wob], dma=True)
            for ti, (t0, tl, c) in enumerate(TT):
                pt, pb = next_ps()
                for cc in range(2):
                    et, etb = next_e()
                    S.op("act", lambda e, et=et, cc=cc, t0=t0, tl=tl: e.activation(out=et[:, 0:tl], in_=osrc[cc][:, t0:t0 + tl], func=AF.Square), reads=[osb[cc]], writes=[etb])
                    S.op("pe", lambda e, pt=pt, et=et, cc=cc, tl=tl: e.matmul(pt[:, 0:tl], lhsT=ones_bf[:], rhs=et[:, 0:tl], start=(cc == 0), stop=(cc == 1)), reads=[ones_b, etb], writes=[pb])
                tm, tmb = next_tmp()
                S.op("act", lambda e, pt=pt, tm=tm, tl=tl: e.activation(out=tm[:, 0:tl], in_=pt[:, 0:tl], func=AF.Sqrt, scale=1.0 / 256, bias=EPS), reads=[pb], writes=[tmb])
                S.op("dve", lambda e, tm=tm, tl=tl: e.reciprocal(out=tm[:, 0:tl], in_=tm[:, 0:tl]), reads=[tmb], writes=[tmb])
                for cc in range(2):
                    S.op("dve", lambda e, tm=tm, cc=cc, t0=t0, tl=tl: e.scalar_tensor_tensor(out=osrc[cc][:, t0:t0 + tl], in0=osrc[cc][:, t0:t0 + tl], scalar=mixgain128(l, 2 + cc), in1=tm[:, 0:tl],
                                                                                            op0=ALU.mult, op1=ALU.mult),
                         reads=[osb[cc], tmb, vec3_b], writes=[osb[cc]])
            wout_partial(l, 2, 128, wo, wob, lambda pi, t0, tl: (osrc[pi][:, t0:t0 + tl], [osb[pi]]))

        def gla_group(l):
            fence()
            wv = win_d[l].rearrange("(k p) n -> p k n", p=128)
            wsl, wslb = ring[0], ring_b[0]
            S.op("pool", lambda e: e.dma_start(out=wsl[:, 0:6400].rearrange("p (k n) -> p k n", k=8), in_=wv[:, :, 1792:2592]), writes=[wslb], dma=True)
            S.op("sp", lambda e: e.dma_start(out=gtri, in_=tri_d.rearrange("a p c -> p a c")), writes=[gconst_b], dma=True)
            S.op("pool", lambda e: e.dma_start(out=ggw.rearrange("p (z c) -> p z c", z=2), in_=gw_d[l].rearrange("z r c -> r z c")), writes=[gconst_b], dma=True)
            S.op("pool", lambda e: e.dma_start(out=ggb, in_=gb_d[l].rearrange("z c -> (z c)").rearrange("(o n) -> o n", o=1)), writes=[gconst_b], dma=True)

            def wcol(k, c, n):
                return wsl[:, k * 800 + c:k * 800 + c + n]
            for ti, (t0, tl, c) in enumerate(TT if GLA_PART >= 2 else []):
                for (dst, dstb, cb, m, idx) in ((gqT, gqT_b, 0, 64, 0), (gqT, gqT_b, 64, 64, 1), (gkT, gkT_b, 128, 64, 0), (gkT, gkT_b, 192, 64, 1),
                                                (ggdT, ggdT_b, 512, 16, 0), (ggdT, ggdT_b, 528, 16, 1)):
                    pt, pb = next_ps(6)
                    for k in range(8):
                        S.op("pe", lambda e, pt=pt, k=k, cb=cb, m=m, t0=t0, tl=tl: e.matmul(pt[0:m, 0:tl], lhsT=wcol(k, cb, m), rhs=u[:, k, t0:t0 + tl], start=(k == 0), stop=(k == 7)),
                             reads=[wslb, u_b[ti]], writes=[pb])
                    S.op("act", lambda e, pt=pt, dst=dst, m=m, idx=idx, t0=t0, tl=tl: e.copy(out=dst[0:m, idx, t0:t0 + tl], in_=pt[0:m, 0:tl]), reads=[pb], writes=[dstb])
            for tb in range(NTB if GLA_PART >= 3 else 0):
                tti = 0 if tb < 2 else (1 if tb < 6 else 2)
                pt, pb = next_ps(6)
                for k in range(8):
                    S.op("pe", lambda e, pt=pt, k=k, tb=tb: e.matmul(pt[:, 0:384], lhsT=u[:, k, tb * 128:(tb + 1) * 128], rhs=wcol(k, 128, 384), start=(k == 0), stop=(k == 7)),
                         reads=[wslb, u_b[tti]], writes=[pb])
                if GLA_VAR == 1:
                    continue
                S.op("act", lambda e, pt=pt, tb=tb: e.copy(out=gktok[:, tb, :], in_=pt[:, 0:128]), reads=[pb], writes=[gktok_b])
                if GLA_VAR == 2:
                    continue
                if GLA_VAR == 3:
                    S.op("act", lambda e, pt=pt, tb=tb: e.copy(out=gvtok[:, tb, :], in_=pt[:, 128:384]), reads=[pb], writes=[gvtok_b])
                    continue
                S.op("act", lambda e, pt=pt, tb=tb: e.copy(out=gvtok[:, tb, :], in_=pt[:, 128:384]), reads=[pb], writes=[gvtok_b])
            SCALE = 32.0 ** -0.5
            gqk2 = [gqk, arena[0:64, 12928:13440]]
            gke2 = [gke, arena[:, 13440:13568]]
            geb2 = [geb, arena[0:64, 13568:14592].bitcast(F32).rearrange("p (a t) -> p a t", a=4)]
            gqk2_b = [gqk_b, Buf()]
            gke2_b = [gke_b, Buf()]
            geb2_b = [geb_b, Buf()]
            ARENA_BUFS.extend([gqk2_b[1], gke2_b[1], geb2_b[1]])

            def gla_block(z, tb):
                gebz, gebz_b = geb2[z], geb2_b[z]
                slot = 0 if tb < 2 else 1 + (tb - 2) // 2
                first = (tb % 2 == 0) if z == 0 else (tb % 2 == 1)
                last = not first
                if first:
                    for p in range(2):
                        zi = z * 2 + p
                        if tb in (0, 1):
                            S.op("dve", lambda e, zi=zi: e.memset(gS[:, zi, :], 0.0), writes=[gS_b])
                        elif (z == 0 and tb == 2) or (z == 1 and tb == 9):
                            S.op("sp", lambda e, zi=zi, p=p, z=z: e.dma_start(out=gS[:, zi, :], in_=gs0_d[l, z, 2 * p:2 * p + 2].rearrange("h d v -> (h d) v")), writes=[gS_b], dma=True)
                        else:
                            S.op("dve", lambda e, zi=zi: e.tensor_scalar(out=gS[:, zi, :], in0=gS[:, zi, :], scalar1=hyflag[0:64, 0:1], scalar2=None, op0=ALU.mult), reads=[gS_b, hysc_b], writes=[gS_b])
                        S.op("act", lambda e, zi=zi: e.copy(out=gSb[:, zi, :], in_=gS[:, zi, :]), reads=[gS_b], writes=[gS_b])
                yield
                pl, plb = next_ps(4)
                S.op("pe", lambda e, pl=pl, tb=tb, z=z: e.matmul(pl[:, 0:128], lhsT=ggdT[:, z, tb * 128:(tb + 1) * 128], rhs=ggw[:, z * 128:(z + 1) * 128], start=True, stop=False),
                     reads=[ggdT_b, gconst_b], writes=[plb])
                S.op("pe", lambda e, pl=pl, z=z: e.matmul(pl[:, 0:128], lhsT=ones_bf[0:1, 0:128], rhs=ggb[:, z * 128:(z + 1) * 128], start=False, stop=True),
                     reads=[ones_b, gconst_b], writes=[plb])
                yield
                gp, gpb = tmp[z], tmp_b[z]
                S.op("act", lambda e, pl=pl, gp=gp: e.activation(out=gp[:, 0:128], in_=pl[:, 0:128], func=AF.Exp, scale=-1.0), reads=[plb], writes=[gpb])
                S.op("act", lambda e, gp=gp: e.activation(out=gp[:, 0:128], in_=gp[:, 0:128], func=AF.Ln, bias=1.0), reads=[gpb], writes=[gpb])
                yield
                for p in range(2):
                    pc, pcb = next_ps(4)
                    S.op("pe", lambda e, pc=pc, gp=gp, p=p, z=z: e.matmul(pc[0:64, 0:128], lhsT=gp[:, p * 64:(p + 1) * 64], rhs=gtri[:, z, :], start=True, stop=True),
                         reads=[gpb, gconst_b], writes=[pcb])
                    S.op("act", lambda e, pc=pc, p=p: e.activation(out=gebz[:, 2 * p, :], in_=pc[0:64, 0:128], func=AF.Exp, scale=-1.0 / 16), reads=[pcb], writes=[gebz_b])
                    S.op("act", lambda e, pc=pc, p=p: e.activation(out=gebz[:, 2 * p + 1, :], in_=pc[0:64, 0:128], func=AF.Exp, scale=1.0 / 16), reads=[pcb], writes=[gebz_b])
                yield
                qk, qkb = gqk2[z], gqk2_b[z]
                for p in range(2):
                    S.op("dve", lambda e, qk=qk, p=p, tb=tb: e.scalar_tensor_tensor(out=qk[0:64, p * 128:(p + 1) * 128], in0=gqT[:, p, tb * 128:(tb + 1) * 128], scalar=SCALE, in1=gebz[:, 2 * p, :],
                                                                                   op0=ALU.mult, op1=ALU.mult), reads=[gqT_b, gebz_b], writes=[qkb])
                    S.op("dve", lambda e, qk=qk, p=p, tb=tb: e.tensor_tensor(out=qk[0:64, 256 + p * 128:256 + (p + 1) * 128], in0=gkT[:, p, tb * 128:(tb + 1) * 128], in1=gebz[:, 2 * p + 1, :], op=ALU.mult),
                         reads=[gkT_b, gebz_b], writes=[qkb])
                yield
                pf, pfb = next_ps(4)
                S.op("pe", lambda e, pf=pf, gp=gp, z=z: e.matmul(pf[:, 0:128], lhsT=gtri[:, 2 + z, :], rhs=gp[:, 0:128], start=True, stop=True), reads=[gpb, gconst_b], writes=[pfb])
                S.op("act", lambda e, pf=pf, gp=gp: e.activation(out=gp[:, 128:256], in_=pf[:, 0:128], func=AF.Exp, scale=-1.0 / 16), reads=[pfb], writes=[gpb])
                yield
                ke, keb = gke2[z], gke2_b[z]
                S.op("dve", lambda e, ke=ke, gp=gp, tb=tb: e.tensor_tensor(out=ke[:, 0:128], in0=gktok[:, tb, :], in1=gp[:, 128:256], op=ALU.mult), reads=[gktok_b, gpb], writes=[keb])
                yield
                pcs = [(ps[6 - 2 * z], ps_b[6 - 2 * z]), (ps[7 - 2 * z], ps_b[7 - 2 * z])]
                for h in range(4):
                    p, sidx = h // 2, h % 2
                    zi = z * 2 + p
                    pa, pab = next_ps(4)
                    S.op("pe", lambda e, pa=pa, qk=qk, p=p, sidx=sidx: e.matmul(pa[:, 0:128], lhsT=qk[32 * sidx:32 * sidx + 32, 256 + p * 128:256 + (p + 1) * 128],
                                                                               rhs=qk[32 * sidx:32 * sidx + 32, p * 128:(p + 1) * 128], start=True, stop=True),
                         reads=[qkb], writes=[pab])
                    yield
                    am, amb = next_e()
                    S.op("dve", lambda e, pa=pa, am=am, z=z: e.tensor_tensor(out=am[:, 0:128], in0=pa[:, 0:128], in1=gtri[:, z, :], op=ALU.mult), reads=[pab, gconst_b], writes=[amb])
                    yield
                    po, pob = next_ps(4)
                    S.op("pe", lambda e, po=po, am=am, h=h, tb=tb: e.matmul(po[0:64, 0:128], lhsT=gvtok[:, tb, h * 64:(h + 1) * 64], rhs=am[:, 0:128], start=True, stop=False),
                         reads=[gvtok_b, amb], writes=[pob])
                    S.op("pe", lambda e, po=po, qk=qk, zi=zi, p=p, sidx=sidx: e.matmul(po[0:64, 0:128], lhsT=gSb[32 * sidx:32 * sidx + 32, zi, :], rhs=qk[32 * sidx:32 * sidx + 32, p * 128:(p + 1) * 128],
                                                                                      start=False, stop=True),
                         reads=[gS_b, qkb], writes=[pob])
                    yield
                    if (z == 0 and tb <= 4) or (z == 1 and tb >= 5):
                        S.op("act", lambda e, po=po, h=h, tb=tb: e.copy(out=godT[:, h, tb * 128:(tb + 1) * 128], in_=po[0:64, 0:128]), reads=[pob], writes=[godT_b])
                    else:
                        S.op("dve", lambda e, po=po, h=h, tb=tb: e.tensor_tensor(out=godT[:, h, tb * 128:(tb + 1) * 128], in0=godT[:, h, tb * 128:(tb + 1) * 128], in1=po[0:64, 0:128], op=ALU.add),
                             reads=[pob, godT_b], writes=[godT_b])
                    if sidx == 1:
                        pcx, pcxb = pcs[p]
                        S.op("pe", lambda e, pcx=pcx, ke=ke, p=p, tb=tb: e.matmul(pcx[0:64, 0:128], lhsT=ke[:, p * 64:(p + 1) * 64], rhs=gvtok[:, tb, p * 128:(p + 1) * 128], start=True, stop=True),
                             reads=[keb, gvtok_b], writes=[pcxb])
                yield
                for p in range(2):
                    zi = z * 2 + p
                    pcx, pcxb = pcs[p]
                    dcol = 127 if z == 0 else 0
                    for sx in range(2):
                        S.op("dve", lambda e, pcx=pcx, zi=zi, p=p, dcol=dcol, sx=sx: e.scalar_tensor_tensor(out=gS[32 * sx:32 * sx + 32, zi, :], in0=gS[32 * sx:32 * sx + 32, zi, :],
                                                                                                           scalar=gebz[32 * sx:32 * sx + 32, 2 * p, dcol:dcol + 1], in1=pcx[32 * sx:32 * sx + 32, 64 * sx:64 * sx + 64],
                                                                                                           op0=ALU.mult, op1=ALU.add), reads=[gS_b, gebz_b, pcxb], writes=[gS_b])
                    S.op("act", lambda e, zi=zi: e.copy(out=gSb[:, zi, :], in_=gS[:, zi, :]), reads=[gS_b], writes=[gS_b])
                    if last:
                        S.op("sp", lambda e, zi=zi, p=p, z=z, slot=slot: e.dma_start(out=gout_d[l, z, slot, 2 * p:2 * p + 2].rearrange("h d v -> (h d) v"), in_=gS[:, zi, :]),
                             reads=[gS_b], dma=True, is_out=True)

            for step in range(NTB):
                gens = [gla_block(0, step), gla_block(1, NTB - 1 - step)]
                while gens:
                    for g_ in list(gens):
                        try:
                            next(g_)
                        except StopIteration:
                            gens.remove(g_)
            if l == 0:
                dump(0, arena[0:64, 0:1280], [gqT_b], np_=64)
                dump(1, arena[:, 5120:6400], [gktok_b])
                dump(2, a2[0:64, 0:1280], [godT_b], np_=64)
                dump(3, a2[0:64, 3840:5120], [godT_b], np_=64)
                dump(6, arena[:, 6400:7680], [gvtok_b])
                dump(4, arena[0:64, 11264:12288].bitcast(F32), [geb_b], n=512, np_=64)
                dump(5, arena[0:64, 10496:11008].bitcast(F32), [gS_b], n=256, np_=64)
            wo, wob = ring[1], ring_b[1]
            S.op("pool", lambda e: e.dma_start(out=wo[0:64, 0:4096].rearrange("p (h n) -> p h n", h=4), in_=wout_d[l, 768:1024, :].rearrange("(h p) n -> p h n", p=64)), writes=[wob], dma=True)
            if GLA_PART < 4:
                return
            for ti, (t0, tl, c) in enumerate(TT):
                for h in range(4):
                    et, etb = next_e()
                    S.op("act", lambda e, et=et, h=h, t0=t0, tl=tl: e.activation(out=et[0:64, 0:tl], in_=godT[:, h, t0:t0 + tl], func=AF.Square), reads=[godT_b], writes=[etb])
                    pt, pb = next_ps(6)
                    S.op("pe", lambda e, pt=pt, et=et, tl=tl: e.matmul(pt[0:64, 0:tl], lhsT=ones_bf[0:64, 0:64], rhs=et[0:64, 0:tl], start=True, stop=True), reads=[ones_b, etb], writes=[pb])
                    S.op("act", lambda e, pt=pt, tl=tl: e.activation(out=rden[:, 0:tl], in_=pt[0:64, 0:tl], func=AF.Sqrt, scale=1.0 / 64, bias=EPS), reads=[pb], writes=[rden_b])
                    S.op("dve", lambda e, tl=tl: e.reciprocal(out=rden[:, 0:tl], in_=rden[:, 0:tl]), reads=[rden_b], writes=[rden_b])
                    S.op("dve", lambda e, h=h, t0=t0, tl=tl: e.scalar_tensor_tensor(out=godT[:, h, t0:t0 + tl], in0=godT[:, h, t0:t0 + tl], scalar=mixgain64(l, 12 + h), in1=rden[:, 0:tl], op0=ALU.mult, op1=ALU.mult),
                         reads=[godT_b, rden_b, vec64_b], writes=[godT_b])
                    pr, prb = next_ps(6)
                    for k in range(8):
                        S.op("pe", lambda e, pr=pr, k=k, h=h, t0=t0, tl=tl: e.matmul(pr[0:64, 0:tl], lhsT=wcol(k, 544 + h * 64, 64), rhs=u[:, k, t0:t0 + tl], start=(k == 0), stop=(k == 7)),
                             reads=[wslb, u_b[ti]], writes=[prb])
                    tm, tmb = next_tmp()
                    S.op("act", lambda e, pr=pr, tm=tm, tl=tl: e.activation(out=tm[0:64, 0:tl], in_=pr[0:64, 0:tl], func=AF.Silu), reads=[prb], writes=[tmb])
                    S.op("dve", lambda e, tm=tm, h=h, t0=t0, tl=tl: e.tensor_tensor(out=godT[:, h, t0:t0 + tl], in0=godT[:, h, t0:t0 + tl], in1=tm[0:64, 0:tl], op=ALU.mult), reads=[godT_b, tmb], writes=[godT_b])
            _ri[0] = 2
            wout_partial(l, 4, 64, wo, wob, lambda pi, t0, tl: (godT[:, pi, t0:t0 + tl], [godT_b]))

        for l in range(NLAYERS):
            if STAGES["ffn1"]:
                S.phase = "L%d_ffn1" % l
                norm_mod(l, 0)
                ffn(l, 0)
            if l == 0:
                mod_hook(9)
            if STAGES["mixer"]:
                S.phase = "L%d_norm2" % l
                norm_mod(l, 1)
                if STAGES.get("A", True):
                    S.phase = "L%d_attnA" % l
                    attention_group(l, 0)
                if STAGES.get("C", True):
                    S.phase = "L%d_attnC" % l
                    attention_group(l, 1)
                if STAGES.get("B", True):
                    S.phase = "L%d_hyena" % l
                    hyena_group(l)
                if STAGES.get("D", True):
                    S.phase = "L%d_gla" % l
                    gla_group(l)
            if STAGES["ffn2"]:
                S.phase = "L%d_ffn2" % l
                norm_mod(l, 2)
                ffn(l, 1)
        S.phase = "final"

        for ti, (t0, tl, c) in enumerate(TT):
            S.op("act", lambda e, t0=t0, tl=tl: e.activation(out=u[:, :, t0:t0 + tl], in_=xres[:, :, t0:t0 + tl], func=AF.Square), reads=[xres_b[ti]], writes=[u_b[ti]])
            pt, pb = next_ps()
            for k in range(8):
                S.op("pe", lambda e, pt=pt, k=k, t0=t0, tl=tl: e.matmul(pt[:, 0:tl], lhsT=ones_bf[:], rhs=u[:, k, t0:t0 + tl], start=(k == 0), stop=(k == 7)),
                     reads=[ones_b, u_b[ti]], writes=[pb])
            S.op("act", lambda e, pt=pt, t0=t0, tl=tl: e.activation(out=rstd[:, t0:t0 + tl], in_=pt[:, 0:tl], func=AF.Sqrt, scale=1.0 / D, bias=EPS), reads=[pb], writes=[rstd_b])
            S.op("dve", lambda e, t0=t0, tl=tl: e.reciprocal(out=rstd[:, t0:t0 + tl], in_=rstd[:, t0:t0 + tl]), reads=[rstd_b], writes=[rstd_b])
            for k in range(8):
                S.op("dve", lambda e, k=k, t0=t0, tl=tl: e.scalar_tensor_tensor(out=xres[:, k, t0:t0 + tl], in0=xres[:, k, t0:t0 + tl], scalar=finalgT[:, k:k + 1],
                                                                               in1=rstd[:, t0:t0 + tl], op0=ALU.mult, op1=ALU.mult),
                     reads=[xres_b[ti], rstd_b, vecs_b[1]], writes=[xres_b[ti]])
        for tb in range(NTB):
            sg, sgb = stg[tb % 2], stg_b[tb % 2]
            tti = 0 if tb < 2 else (1 if tb < 6 else 2)
            for half in range(2):
                pt, pb = next_ps()
                for q in range(4):
                    k = half * 4 + q
                    S.op("pe", lambda e, pt=pt, k=k, q=q, tb=tb: e.transpose(pt[:, q * 128:(q + 1) * 128], xres[:, k, tb * 128:(tb + 1) * 128], ident[:]),
                         reads=[xres_b[tti], ident_b], writes=[pb])
                if half == 0:
                    S.op("act", lambda e, pt=pt, sg=sg, half=half: e.copy(out=sg[:, half * 512:(half + 1) * 512], in_=pt[:, :]), reads=[pb], writes=[sgb])
                else:
                    S.op("dve", lambda e, pt=pt, sg=sg, half=half: e.tensor_copy(out=sg[:, half * 512:(half + 1) * 512], in_=pt[:, :]), reads=[pb], writes=[sgb])
            S.op("sp", lambda e, sg=sg, tb=tb: e.dma_start(out=y_d[tb * 128:(tb + 1) * 128, :], in_=sg[:]), reads=[sgb], dma=True, is_out=True)

        S.emit(st)
    return nc


N_CORES = 8


def core_tokens(c):
    if c < 2:
        return [30 + c], c
    base = 5 * (c - 2)
    return [base + i for i in range(5)], None


def rope_tables():
    rows = 1024 // 64
    r = np.repeat(np.arange(rows, dtype=np.float32), 64)
    col = np.tile(np.arange(64, dtype=np.float32), rows)
    nf = 16
    inv = (10000.0 ** (-np.arange(nf, dtype=np.float32) / nf)).astype(np.float32)
    ang = np.concatenate([r[:, None] * inv, col[:, None] * inv], axis=-1).astype(np.float32)
    return np.cos(ang).astype(np.float32), np.sin(ang).astype(np.float32)


def attn_masks(is_sample):
    m = np.zeros((4, 128, 1920), np.float32)
    a = np.arange(128)[:, None]
    x = np.arange(1920)[None, :]
    if is_sample:
        band = (np.abs(x - 896 - a) <= 128).astype(np.float32)
        m[0] = band
        m[1] = band
        m[2] = 1.0
        m[3] = 1.0
    else:
        ev = ((x >= 896) & (x < 1152)).astype(np.float32) * np.ones((128, 1), np.float32)
        od = ((x >= 768) & (x < 1024)).astype(np.float32) * np.ones((128, 1), np.float32)
        m[0] = ev
        m[1] = od
        m[2] = ev
        m[3] = od
    return m.astype(NPBF16)


def hy_tables(L):
    t = np.linspace(0.0, 1.0, L, dtype=np.float32)[:, None]
    w = ((2.0 * math.pi / L) * np.arange(L, dtype=np.float32)[:, None]).astype(np.float32)
    bands = np.linspace(1e-4, 15, 16, dtype=np.float32)[None, :]
    z = np.concatenate([t, np.cos(bands * w), -np.sin(bands * w)], axis=-1).astype(np.float32)
    deltas = np.linspace(math.log(1e-2) / 1.5, math.log(1e-2) / 0.3, 256, dtype=np.float32)
    dec = np.exp(-t * np.abs(deltas)).astype(np.float32)
    dec2 = np.stack([dec, dec], 1)
    dec2[0, 1] = 0.0
    r = np.arange(2 * L)
    f = 256 * (r // 512) + (r % 256)
    is_im = (r % 512) >= 256
    th = np.pi * (f[None, :] + 0.5) * np.arange(L)[:, None].astype(np.float64) / L
    FW = np.where(is_im[None, :], -np.sin(th), np.cos(th))
    IV = FW.T / L
    return z, dec2, FW.astype(np.float32), IV.astype(np.float32)


def blockdiag4(m):
    a, b = m.shape
    o = np.zeros((4 * a, 4 * b), m.dtype)
    for i in range(4):
        o[i * a:(i + 1) * a, i * b:(i + 1) * b] = m
    return o


_NC_CACHE = {}
SHARED_KEYS = ["w_mod", "b_mod", "norm_g", "ffn_w_in", "ffn_w_out", "final_g", "w_in", "w_out", "mix_g", "swa_sink", "qk_norm_g",
               "hy_conv_w", "hy_conv_b", "hy_w1", "hy_b1", "hy_w2", "hy_b2", "hy_w3", "hy_freq", "hy_bias", "gla_gate_w", "gla_gate_b"]


def kernel(**inp):
    f32 = np.float32
    x_prompt = np.asarray(inp["x_prompt"], f32)
    x_sample = np.asarray(inp["x_sample"], f32)
    c = np.asarray(inp["c"], f32)
    c_ctx = np.asarray(inp["c_ctx"], f32)
    if "nc" not in _NC_CACHE:
        _NC_CACHE["nc"] = build_program()
    nc = _NC_CACHE["nc"]
    shared = {k: np.ascontiguousarray(inp[k], f32) for k in SHARED_KEYS}
    shared["ident_in"] = np.eye(128, dtype=f32)
    cos_s, sin_s = rope_tables()
    caches = [np.asarray(inp[k], f32) for k in ("cache_swa_k", "cache_swa_v", "cache_gqa_k", "cache_gqa_v")]
    z_p, dec_p, fw_p, iv_p = hy_tables(256)
    z_s, dec_s, fw_s, iv_s = hy_tables(1024)
    shared["hy_fw_p"] = fw_p.astype(NPBF16)
    r_ = np.arange(128)[:, None]
    c_ = np.arange(128)[None, :]
    shared["tri_in"] = np.stack([r_ <= c_, r_ >= c_, r_ > c_, r_ < c_], 0).astype(f32)
    state_gla = np.asarray(inp["state_gla"], f32)
    shared["hy_iv_p"] = iv_p.astype(NPBF16)
    hy_prompt = dict(hy_zT=np.ascontiguousarray(np.concatenate([z_p] * 5, 0).T), hy_dec=np.ascontiguousarray(np.concatenate([dec_p] * 5, 0)),
                     hy_fw_g=blockdiag4(fw_p).astype(NPBF16), hy_iv_g=blockdiag4(iv_p).astype(NPBF16), hy_flag=np.zeros((128, 1), f32))
    hy_sample = dict(hy_zT=np.ascontiguousarray(np.concatenate([z_p, z_s], 0).T), hy_dec=np.ascontiguousarray(np.concatenate([dec_p, dec_s], 0)),
                     hy_fw_g=fw_s.astype(NPBF16), hy_iv_g=iv_s.astype(NPBF16), hy_flag=np.ones((128, 1), f32))
    in_maps = []
    for core in range(N_CORES):
        pids, sid = core_tokens(core)
        m = dict(shared)
        cos = np.ones((T, 32), f32)
        sin = np.zeros((T, 32), f32)
        if sid is None:
            xs = np.concatenate([x_prompt[p] for p in pids], 0)
            cond = np.stack([c_ctx, c_ctx], 0)
            m["ctx_kv"] = np.zeros((4, 2, 512, 128), f32)
            m["ctx_bias"] = np.full((128, 1), -30000.0, f32)
            m["gla_s0"] = np.zeros((2, 2, 4, 32, 64), f32)
        else:
            xs = np.concatenate([x_prompt[pids[0]], x_sample[sid]], 0)
            cond = np.stack([c_ctx, c[sid]], 0)
            cos[256:] = cos_s
            sin[256:] = sin_s
            m["ctx_kv"] = np.ascontiguousarray(np.stack([cc[sid].reshape(2, 512, 128) for cc in caches], 0), f32)
            m["ctx_bias"] = np.zeros((128, 1), f32)
            m["gla_s0"] = np.ascontiguousarray(state_gla[sid], f32)
        m["attn_mask"] = attn_masks(sid is not None)
        m.update(hy_sample if sid is not None else hy_prompt)
        m["rope_cos"] = cos
        m["rope_sin"] = sin
        m["x_in"] = np.ascontiguousarray(xs, f32)
        m["cond_in"] = np.ascontiguousarray(cond, f32)
        in_maps.append(m)
    if ONE_CORE:
        res = run_bass_kernel_spmd(nc, in_maps[2:3], core_ids=[0])
        LAST["outs"] = res.results
        return None
    res = run_bass_kernel_spmd(nc, in_maps, core_ids=list(range(N_CORES)))
    outs = res.results
    LAST["outs"] = outs
    B, SEQ = x_prompt.shape[0], x_prompt.shape[1]
    y_prompt = np.zeros((B, SEQ, D), f32)
    y_sample = np.zeros(x_sample.shape, f32)
    kvs = [np.zeros((B, 2, SEQ, 2, 64), f32) for _ in range(4)]
    new_state = np.zeros((B, 2, 2, 4, 32, 64), f32)
    for core in range(N_CORES):
        pids, sid = core_tokens(core)
        y = np.asarray(outs[core]["y"], f32)
        kvo = np.asarray(outs[core]["kv_out"], f32)
        gso = np.asarray(outs[core]["gla_out"], f32)
        if sid is not None:
            y_sample[sid] = y[256:]
        for i, p in enumerate(pids):
            y_prompt[p] = y[i * 256:(i + 1) * 256]
            for a in range(4):
                kvs[a][p] = kvo[a, :, i * 256:(i + 1) * 256, :].reshape(2, SEQ, 2, 64)
            new_state[p] = gso[:, :, i]
    return (y_prompt, y_sample, kvs[0], kvs[1], kvs[2], kvs[3], new_state)
```

```python
import math
from contextlib import ExitStack
import numpy as np
import ml_dtypes
import concourse.bass as bass
import concourse.mybir as mybir
from concourse.bass_utils import run_bass_kernel_spmd

F32 = mybir.dt.float32
BF16 = mybir.dt.bfloat16
AF = mybir.ActivationFunctionType
ALU = mybir.AluOpType
AX = mybir.AxisListType
NPBF16 = ml_dtypes.bfloat16

STAGES = {"ffn1": True, "mixer": True, "ffn2": True, "A": True, "C": True, "B": True, "D": True}
NLAYERS = 2
DEBUG = False
PROFILE_SCOPES = False
PROFILE_ENGINE = "pe"
GLA_STEPS = 99
GLA_PART = 9
GLA_VAR = 0
ONE_CORE = False
LAST = {}

ENGS = ("pe", "act", "dve", "pool", "sp")
DMA_NSEM = {"sp": 12, "act": 4, "pool": 12}


class Buf:
    __slots__ = ("name", "w", "r")

    def __init__(self, name=""):
        self.name = name
        self.w = None
        self.r = []


class Op:
    __slots__ = ("eng", "idx", "fn", "deps", "signal", "dma", "dsem", "dval", "cnt", "phase")

    def __init__(self, eng, idx, fn, dma):
        self.eng = eng
        self.idx = idx
        self.fn = fn
        self.deps = []
        self.signal = False
        self.dma = dma
        self.dsem = None
        self.dval = 0
        self.cnt = 0


class Sched:
    def __init__(self, nc, same_engine_sync=True):
        self.nc = nc
        self.ops = {e: [] for e in ENGS}
        self.ndma = {e: 0 for e in DMA_NSEM}
        self.dma_ops = {e: [] for e in DMA_NSEM}
        self.same = same_engine_sync
        self.out_dmas = []
        self.phase = None

    def op(self, eng, fn, reads=(), writes=(), dma=False, is_out=False):
        lst = self.ops[eng]
        o = Op(eng, len(lst), fn, dma)
        o.phase = self.phase
        deps = {}
        for b in reads:
            if b.w is not None:
                deps[id(b.w)] = b.w
        for b in writes:
            if b.w is not None:
                deps[id(b.w)] = b.w
            for r in b.r:
                deps[id(r)] = r
        if dma:
            j = self.ndma[eng]
            k = DMA_NSEM[eng]
            o.dsem = (eng, j % k)
            o.dval = 16 * (j // k + 1)
            if j >= k:
                p = self.dma_ops[eng][j - k]
                deps[id(p)] = p
            self.ndma[eng] += 1
            self.dma_ops[eng].append(o)
            if is_out:
                self.out_dmas.append(o)
        best = {}
        for d in deps.values():
            if d is o:
                continue
            if d.dma:
                o.deps.append(d)
                continue
            if d.eng == eng and (eng in ("pe", "sp") or not self.same):
                continue
            if d.eng not in best or best[d.eng].idx < d.idx:
                best[d.eng] = d
        for d in best.values():
            o.deps.append(d)
            d.signal = True
        for b in reads:
            if not dma:
                b.r = [r for r in b.r if r.dma or r.eng != eng]
            b.r.append(o)
        for b in writes:
            b.w = o
            b.r = []
        lst.append(o)
        return o

    def emit(self, stack):
        nc = self.nc
        CH = 2000
        fin = Op("sp", len(self.ops["sp"]), None, False)
        fin.deps = list(self.out_dmas)
        fin.phase = None
        self.ops["sp"].append(fin)
        for e in ENGS:
            c = 0
            for o in self.ops[e]:
                if o.signal:
                    c += 1
                o.cnt = c
        esem = {}
        for e in ENGS:
            n = (self.ops[e][-1].cnt if self.ops[e] else 0)
            for i in range(max(1, (n + CH - 1) // CH)):
                esem[(e, i)] = stack.enter_context(nc.semaphore("es_%s%d" % (e, i)))
        dsem = {}
        for e, k in DMA_NSEM.items():
            for i in range(k):
                dsem[(e, i)] = stack.enter_context(nc.semaphore("ds_%s%d" % (e, i)))
        block = stack.enter_context(nc.Block())

        def run(e, engine):
            known = {}
            kn_eng = {}
            cur = [None, None]

            def set_phase(ph):
                if not PROFILE_SCOPES or ph == cur[0] or e != PROFILE_ENGINE:
                    return
                if cur[1] is not None:
                    cur[1].__exit__(None, None, None)
                    cur[1] = None
                cur[0] = ph
                if ph is not None:
                    cur[1] = nc.named_scope(ph)
                    cur[1].__enter__()

            for o in self.ops[e] + [None]:
                if o is None:
                    set_phase(None)
                    break
                set_phase(o.phase)
                need = {}
                for d in o.deps:
                    if d.dma:
                        key, val = ("d",) + d.dsem, d.dval
                        if known.get(key, 0) >= val:
                            continue
                    else:
                        if kn_eng.get(d.eng, 0) >= d.cnt:
                            continue
                        key, val = ("e", d.eng, (d.cnt - 1) // CH), (d.cnt - 1) % CH + 1
                        kn_eng[d.eng] = d.cnt
                    if need.get(key, 0) < val:
                        need[key] = val
                for key, val in need.items():
                    s = dsem[key[1:]] if key[0] == "d" else esem[key[1:]]
                    engine.wait_ge(s, val)
                    if key[0] == "d":
                        known[key] = val
                if o.fn is None:
                    continue
                ins = o.fn(engine)
                if o.dma:
                    ins.then_inc(dsem[o.dsem], 16)
                elif o.signal:
                    ins.then_inc(esem[(e, (o.cnt - 1) // CH)], 1)

        @block.tensor
        def _(eng):
            run("pe", eng)

        @block.scalar
        def _(eng):
            run("act", eng)

        @block.vector
        def _(eng):
            run("dve", eng)

        @block.gpsimd
        def _(eng):
            run("pool", eng)

        @block.sync
        def _(eng):
            run("sp", eng)


D = 1024
T = 1280
TT = [(0, 256, 0), (256, 512, 1), (768, 512, 1)]
NTB = T // 128
DFF = 2816
NHC = DFF // 128
EPS = 1e-6
RING_SLOTS = 3
SLOT_ELEMS = 8192


class Prog:
    pass


def build_program():
    nc = bass.Bass("TRN2", target_bir_lowering=False)
    P = Prog()
    st = ExitStack()
    with st:
        S = Sched(nc)

        def din(name, shape, dt=F32):
            return nc.dram_tensor(name, list(shape), dt, kind="ExternalInput").ap()

        def dout(name, shape, dt=F32):
            return nc.dram_tensor(name, list(shape), dt, kind="ExternalOutput").ap()

        _n = [0]

        def sb(shape, dt, name=None):
            _n[0] += 1
            return st.enter_context(nc.sbuf_tensor(name or ("t%d" % _n[0]), list(shape), dt))

        x_d = din("x_in", [T, D])
        cond_d = din("cond_in", [2, D])
        wmod_d = din("w_mod", [2, D, 9 * D])
        bmod_d = din("b_mod", [2, 9 * D])
        normg_d = din("norm_g", [2, 3, D])
        fwin_d = din("ffn_w_in", [2, 2, D, 2 * DFF])
        fwout_d = din("ffn_w_out", [2, 2, DFF, D])
        finalg_d = din("final_g", [D])
        ident_d = din("ident_in", [128, 128])
        y_d = dout("y", [T, D])
        win_d = din("w_in", [2, D, 2592])
        wout_d = din("w_out", [2, D, D])
        mixg_d = din("mix_g", [2, D])
        sink_d = din("swa_sink", [2, 4])
        qkg_d = din("qk_norm_g", [2, 2, 64])
        cos_d = din("rope_cos", [T, 32])
        sin_d = din("rope_sin", [T, 32])
        ctx_d = din("ctx_kv", [4, 2, 512, 128])
        ctxbias_d = din("ctx_bias", [128, 1])
        amask_d = din("attn_mask", [4, 128, 1920], BF16)
        kv_d = dout("kv_out", [4, 2, T, 128])
        hcw_d = din("hy_conv_w", [2, 3, 768])
        hcb_d = din("hy_conv_b", [2, 768])
        hw1_d = din("hy_w1", [2, 33, 64])
        hb1_d = din("hy_b1", [2, 64])
        hw2_d = din("hy_w2", [2, 64, 64])
        hb2_d = din("hy_b2", [2, 64])
        hw3_d = din("hy_w3", [2, 64, 1024])
        hfr_d = din("hy_freq", [2, 2, 64])
        hbias_d = din("hy_bias", [2, 2, 256])
        hz_d = din("hy_zT", [33, T])
        hdec_d = din("hy_dec", [T, 2, 256])
        hfwg_d = din("hy_fw_g", [1024, 2048], BF16)
        hivg_d = din("hy_iv_g", [2048, 1024], BF16)
        hfwp_d = din("hy_fw_p", [256, 512], BF16)
        hivp_d = din("hy_iv_p", [512, 256], BF16)
        hflag_d = din("hy_flag", [128, 1])
        gw_d = din("gla_gate_w", [2, 2, 16, 128])
        gb_d = din("gla_gate_b", [2, 2, 128])
        gs0_d = din("gla_s0", [2, 2, 4, 32, 64])
        tri_d = din("tri_in", [4, 128, 128])
        gout_d = dout("gla_out", [2, 2, 5, 4, 32, 64])

        xres = sb([128, 8, T], F32, "xres")
        u = sb([128, 8, T], BF16, "u")
        hid = sb([128, 12, T], BF16, "hid")
        ring = [sb([128, SLOT_ELEMS], BF16, "ring%d" % i) for i in range(RING_SLOTS)]
        ring_b = [Buf("ring%d" % i) for i in range(RING_SLOTS)]
        stg = [sb([128, D], F32, "stg%d" % i) for i in range(2)]
        stg_b = [Buf() for _ in range(2)]
        tmp = [sb([128, 512], F32, "tmp%d" % i) for i in range(3)]
        tmp_b = [Buf() for _ in range(3)]
        rstd = sb([128, T], F32, "rstd")
        rstd_b = Buf()
        ident = sb([128, 128], F32, "ident")
        ident_b = Buf()
        ones_bf = sb([128, 128], BF16, "ones")
        ones_b = Buf()
        vecs_in = [sb([128, 128], F32, "vin%d" % i) for i in range(2)]
        vecs = [sb([128, 128], F32, "vec%d" % i) for i in range(2)]
        vecs_b = [Buf() for _ in range(2)]
        vin_b = [Buf() for _ in range(2)]
        condT = sb([128, 8, 2], BF16, "condT")
        condT_b = Buf()
        mod = sb([128, 2, 72, 2], F32, "mod")
        mod_b = Buf()
        modA = sb([128, 2, 3, 8, 2], F32, "modA")
        modG = sb([128, 2, 3, 8, 2], F32, "modG")
        modA_b = Buf()
        xres_b = [Buf("xres%d" % i) for i in range(3)]
        u_b = [Buf("u%d" % i) for i in range(3)]
        hid_b = [[Buf() for _ in range(3)] for _ in range(12)]

        arena = hid[:].rearrange("p a b -> p (a b)")
        qT = arena[0:64, 0:5120].rearrange("p (h t) -> p h t", h=4)
        kT = arena[0:64, 5120:7680].rearrange("p (h t) -> p h t", h=2)
        vtok = arena[:, 7680:8980].rearrange("p (b g f) -> p b g f", b=NTB, g=2)
        oT = arena[0:64, 8980:14100].rearrange("p (h t) -> p h t", h=4)
        qT_b, kT_b, vtok_b, oT_b = Buf("qT"), Buf("kT"), Buf("vtok"), Buf("oT")
        amask = sb([128, 4, 1920], BF16, "amask")
        amask_b = Buf()
        ropec = sb([128, NTB, 32], F32, "ropec")
        ropes = sb([128, NTB, 32], F32, "ropes")
        rope_b = Buf()
        gq = sb([128, 2, 6, 64], F32, "gq")
        gq_b = Buf()
        ctxbias = sb([128, 1], F32, "ctxbias")
        esink = sb([64, 8], F32, "esink")
        misc_b = Buf()
        ctxkT = sb([64, 2, 512], BF16, "ctxkT")
        ctxv = sb([128, 4, 2, 65], BF16, "ctxv")
        esink128 = sb([128, 8], F32, "esink128")
        ones_f = sb([128, 64], F32, "ones_f")
        rrow = sb([128, 512], F32, "rrow")
        rrow_b = Buf()
        ctxkT_b, ctxv_b = Buf(), Buf()
        kvst = [sb([128, 2, 128], F32, "kvst%d" % i) for i in range(2)]
        kvst_b = [Buf() for _ in range(2)]
        qkn = [sb([128, 384], F32, "qkn%d" % i) for i in range(2)]
        qkn_b = [Buf() for _ in range(2)]
        qkr = [sb([128, 384], F32, "qkr%d" % i) for i in range(2)]
        qkr_b = [Buf() for _ in range(2)]
        rtmp2 = [sb([128, 192], F32, "rtmp%d" % i) for i in range(2)]
        rtmp2_b = [Buf() for _ in range(2)]
        ssq2 = [sb([128, 8], F32, "ssq%d" % i) for i in range(2)]
        ssq2_b = [Buf() for _ in range(2)]
        etile = [sb([128, 512], BF16, "etile%d" % i) for i in range(4)]
        etile_b = [Buf() for _ in range(4)]
        rden = sb([64, 512], F32, "rden")
        rden_b = Buf()
        vin64 = sb([128, 64], F32, "vin64")
        vec64 = sb([64, 128], F32, "vec64")
        vec64_b = Buf()

        hyF = arena[:, 0:2560].bitcast(F32)
        hyx1 = arena[:, 2560:3840]
        hyx2 = arena[:, 3840:5120]
        hyvtok = arena[:, 5120:6400].rearrange("p (b c) -> p b c", b=NTB)
        hyhp = arena[:, 6400:7680].rearrange("p (b c) -> p b c", b=NTB)
        hyhm = arena[:, 7680:8960].rearrange("p (b c) -> p b c", b=NTB)
        hyY = arena[:, 8960:11520].rearrange("p (r c) -> p r c", r=20)
        hyob0 = arena[:, 11520:12800]
        hyh2 = arena[0:64, 12800:14080]
        hyF_b, hyx1_b, hyx2_b, hyvtok_b, hyh_b, hyY_b, hyob0_b, hyh2_b = (Buf() for _ in range(8))
        ARENA_BUFS = [qT_b, kT_b, vtok_b, oT_b, hyF_b, hyx1_b, hyx2_b, hyvtok_b, hyh_b, hyY_b, hyob0_b, hyh2_b]
        gqT = arena[0:64, 0:2560].rearrange("p (a t) -> p a t", a=2)
        gkT = arena[0:64, 2560:5120].rearrange("p (a t) -> p a t", a=2)
        gktok = arena[:, 5120:6400].rearrange("p (b c) -> p b c", b=NTB)
        gvtok = arena[:, 6400:8960].rearrange("p (b c) -> p b c", b=NTB)
        gtri = arena[:, 8960:9984].bitcast(F32).rearrange("p (a c) -> p a c", a=4)
        ggw = arena[0:16, 9984:10240]
        ggb = arena[0:1, 10240:10496]
        gS = arena[0:64, 10496:11008].bitcast(F32).rearrange("p (a v) -> p a v", a=4)
        gSb = arena[0:64, 11008:11264].rearrange("p (a v) -> p a v", a=4)
        geb = arena[0:64, 11264:12288].bitcast(F32).rearrange("p (a t) -> p a t", a=4)
        gqk = arena[0:64, 12288:12800]
        gke = arena[:, 12800:12928]
        gqk_b, gke_b = Buf(), Buf()
        a2 = amask[:].rearrange("p a n -> p (a n)")
        godT = a2[0:64, 0:5120].rearrange("p (h t) -> p h t", h=4)
        ggdT = a2[0:16, 5120:7680].rearrange("p (z t) -> p z t", z=2)
        gqT_b, gkT_b, gktok_b, gvtok_b, gconst_b, gS_b, geb_b, godT_b, ggdT_b = (Buf() for _ in range(9))
        fdummy = sb([128, 1], F32, "fdummy")
        vin3 = sb([128, 128], F32, "vin3")
        vec3 = sb([128, 128], F32, "vec3")
        vec3_b = Buf()
        w3b = sb([64, 1024], BF16, "w3b")
        hyw12 = sb([64, 128], F32, "hyw12")
        hyw_b = Buf()
        hysc = sb([128, 2, 6, 4], F32, "hysc")
        hyfb = sb([64, 4], F32, "hyfb")
        hyflag = sb([128, 1], F32, "hyflag")
        hysc_b = Buf()

        ps = [st.enter_context(nc.psum_tensor("ps%d" % i, [128, 512], F32)) for i in range(8)]
        ps_b = [Buf("ps%d" % i) for i in range(8)]
        _pi = [0]

        def next_ps(n=8):
            i = _pi[0] % n
            _pi[0] += 1
            return ps[i], ps_b[i]

        _ri = [0]

        def next_slot():
            i = _ri[0] % RING_SLOTS
            _ri[0] += 1
            return ring[i], ring_b[i]

        _ti = [0]

        def next_tmp():
            i = _ti[0] % 3
            _ti[0] += 1
            return tmp[i], tmp_b[i]

        S.op("sp", lambda e: e.dma_start(out=ident[:], in_=ident_d), writes=[ident_b], dma=True)
        S.op("dve", lambda e: e.memset(ones_bf[:], 1.0), writes=[ones_b])

        S.op("dve", lambda e: e.memset(vecs_in[0][:], 0.0), writes=[vin_b[0]])
        S.op("dve", lambda e: e.memset(vecs_in[1][:], 0.0), writes=[vin_b[1]])
        S.op("sp", lambda e: e.dma_start(out=vecs_in[0][0:72, :], in_=bmod_d[0].rearrange("(c p) -> c p", p=128)), writes=[vin_b[0]], dma=True)
        S.op("sp", lambda e: e.dma_start(out=vecs_in[0][72:120, :], in_=normg_d.rearrange("l i (k p) -> (l i k) p", p=128)), writes=[vin_b[0]], dma=True)
        S.op("sp", lambda e: e.dma_start(out=vecs_in[1][0:72, :], in_=bmod_d[1].rearrange("(c p) -> c p", p=128)), writes=[vin_b[1]], dma=True)
        S.op("sp", lambda e: e.dma_start(out=vecs_in[1][72:88, :], in_=cond_d.rearrange("c (k p) -> (c k) p", p=128)), writes=[vin_b[1]], dma=True)
        S.op("sp", lambda e: e.dma_start(out=vecs_in[1][88:96, :], in_=finalg_d.rearrange("(k p) -> k p", p=128)), writes=[vin_b[1]], dma=True)
        for i in range(2):
            pt, pb = next_ps()
            S.op("pe", lambda e, i=i, pt=pt: e.transpose(pt[:, 0:128], vecs_in[i][:], ident[:]), reads=[vin_b[i], ident_b], writes=[pb])
            S.op("dve", lambda e, i=i, pt=pt: e.tensor_copy(out=vecs[i][:], in_=pt[:, 0:128]), reads=[pb], writes=[vecs_b[i]])

        def bmodT(l):
            return vecs[l][:, 0:72]

        def normgT(l, i):
            o = 72 + (l * 3 + i) * 8
            return vecs[0][:, o:o + 8]

        finalgT = vecs[1][:, 88:96]
        S.op("act", lambda e: e.activation(out=condT[:].rearrange("p k c -> p c k"), in_=vecs[1][:, 72:88].rearrange("p (c k) -> p c k", c=2), func=AF.Silu),
             reads=[vecs_b[1]], writes=[condT_b])

        S.phase = "load_x"
        for tb in range(NTB):
            sg, sgb = stg[tb % 2], stg_b[tb % 2]
            S.op("sp", lambda e, sg=sg, tb=tb: e.dma_start(out=sg[:], in_=x_d[tb * 128:(tb + 1) * 128, :]), writes=[sgb], dma=True)
            tti = 0 if tb < 2 else (1 if tb < 6 else 2)
            for half in range(2):
                pt, pb = next_ps()
                for q in range(4):
                    k = half * 4 + q
                    S.op("pe", lambda e, pt=pt, sg=sg, k=k, q=q: e.transpose(pt[:, q * 128:(q + 1) * 128], sg[:, k * 128:(k + 1) * 128], ident[:]),
                         reads=[sgb, ident_b], writes=[pb])
                eng = "act" if half == 0 else "dve"
                if eng == "act":
                    S.op("act", lambda e, pt=pt, half=half, tb=tb: e.copy(out=xres[:, half * 4:half * 4 + 4, tb * 128:(tb + 1) * 128], in_=pt[:, :].rearrange("p (a b) -> p a b", a=4)),
                         reads=[pb], writes=[xres_b[tti]])
                else:
                    S.op("dve", lambda e, pt=pt, half=half, tb=tb: e.tensor_copy(out=xres[:, half * 4:half * 4 + 4, tb * 128:(tb + 1) * 128], in_=pt[:, :].rearrange("p (a b) -> p a b", a=4)),
                         reads=[pb], writes=[xres_b[tti]])

        S.phase = "modulation"
        def mod_block(l, jb):
            sl, slb = next_slot()
            S.op("pool", lambda e, sl=sl: e.dma_start(out=sl[:, :].rearrange("p (k n) -> p k n", k=8),
                                                      in_=wmod_d[l].rearrange("(k p) n -> p k n", p=128)[:, :, jb * 1024:(jb + 1) * 1024]),
                 writes=[slb], dma=True)
            pt, pb = next_ps(6)
            for cc in range(8):
                for k in range(8):
                    S.op("pe", lambda e, pt=pt, sl=sl, cc=cc, k=k: e.matmul(pt[:, 2 * cc:2 * cc + 2], lhsT=sl[:, k * 1024 + cc * 128:k * 1024 + (cc + 1) * 128],
                                                                             rhs=condT[:, k, :], start=(k == 0), stop=(k == 7)),
                         reads=[slb, condT_b], writes=[pb])
            S.op("dve", lambda e, pt=pt: e.tensor_tensor(out=mod[:, l, jb * 8:(jb + 1) * 8, :], in0=pt[:, 0:16].rearrange("p (a c) -> p a c", c=2),
                                                         in1=bmodT(l)[:, jb * 8:(jb + 1) * 8].unsqueeze(2).broadcast_to([128, 8, 2]), op=ALU.add),
                 reads=[pb, vecs_b[l]], writes=[mod_b])

        def mod_finish(l):
            for i in range(3):
                S.op("dve", lambda e, i=i: e.tensor_scalar(out=modA[:, l, i], in0=mod[:, l, (3 * i + 1) * 8:(3 * i + 2) * 8, :], scalar1=1.0, scalar2=None, op0=ALU.add),
                     reads=[mod_b], writes=[modA_b])
                S.op("dve", lambda e, i=i: e.tensor_tensor(out=modA[:, l, i], in0=modA[:, l, i], in1=normgT(l, i).unsqueeze(2).broadcast_to([128, 8, 2]), op=ALU.mult),
                     reads=[modA_b, vecs_b[0]], writes=[modA_b])
                S.op("dve", lambda e, i=i: e.tensor_scalar(out=modG[:, l, i], in0=mod[:, l, (3 * i + 2) * 8:(3 * i + 3) * 8, :], scalar1=(1.0 if i == 1 else 0.5), scalar2=None, op0=ALU.mult),
                     reads=[mod_b], writes=[modA_b])

        for jb in range(9):
            mod_block(0, jb)
        mod_finish(0)
        MOD_PENDING = [(1, jb) for jb in range(9)] if NLAYERS > 1 else []

        def mod_hook(n=1):
            for _ in range(n):
                if MOD_PENDING:
                    l_, jb_ = MOD_PENDING.pop(0)
                    mod_block(l_, jb_)
                    if not MOD_PENDING:
                        mod_finish(l_)

        def modB(l, i, k, c):
            return mod[:, l, (3 * i) * 8 + k, c:c + 1]

        def norm_mod(l, i):
            for ti, (t0, tl, c) in enumerate(TT):
                S.op("act", lambda e, t0=t0, tl=tl: e.activation(out=u[:, :, t0:t0 + tl], in_=xres[:, :, t0:t0 + tl], func=AF.Square),
                     reads=[xres_b[ti]], writes=[u_b[ti]])
                pt, pb = next_ps()
                for k in range(8):
                    S.op("pe", lambda e, pt=pt, k=k, t0=t0, tl=tl: e.matmul(pt[:, 0:tl], lhsT=ones_bf[:], rhs=u[:, k, t0:t0 + tl], start=(k == 0), stop=(k == 7)),
                         reads=[ones_b, u_b[ti]], writes=[pb])
                S.op("act", lambda e, pt=pt, t0=t0, tl=tl: e.activation(out=rstd[:, t0:t0 + tl], in_=pt[:, 0:tl], func=AF.Sqrt, scale=1.0 / D, bias=EPS),
                     reads=[pb], writes=[rstd_b])
                S.op("dve", lambda e, t0=t0, tl=tl: e.reciprocal(out=rstd[:, t0:t0 + tl], in_=rstd[:, t0:t0 + tl]), reads=[rstd_b], writes=[rstd_b])
                for k in range(8):
                    tm, tmb = next_tmp()
                    S.op("dve", lambda e, tm=tm, k=k, t0=t0, tl=tl, c=c: e.scalar_tensor_tensor(out=tm[:, 0:tl], in0=xres[:, k, t0:t0 + tl], scalar=modA[:, l, i, k, c:c + 1],
                                                                                               in1=rstd[:, t0:t0 + tl], op0=ALU.mult, op1=ALU.mult),
                         reads=[xres_b[ti], rstd_b, modA_b], writes=[tmb])
                    S.op("act", lambda e, tm=tm, k=k, t0=t0, tl=tl, c=c: e.activation(out=u[:, k, t0:t0 + tl], in_=tm[:, 0:tl], func=AF.Identity, bias=modB(l, i, k, c), scale=1.0),
                         reads=[tmb, mod_b], writes=[u_b[ti]])

        def ffn(l, i):
            gi = 0 if i == 0 else 2
            win = fwin_d[l, i].rearrange("(k p) n -> p k n", p=128)
            wout = fwout_d[l, i].rearrange("(j p) n -> p j n", p=128)
            groups = [(0, 4), (4, 4), (8, 4), (12, 4), (16, 4), (20, 2)]
            halves = [groups[0:3], groups[3:6]]
            for hgroups in halves:
                h0 = hgroups[0][0]
                nh = sum(g[1] for g in hgroups)
                for (j0, nj) in hgroups:
                    sl, slb = next_slot()
                    S.op("pool", lambda e, sl=sl, j0=j0, nj=nj: e.dma_start(out=sl[:, 0:8 * nj * 128].rearrange("p (k n) -> p k n", k=8), in_=win[:, :, j0 * 128:(j0 + nj) * 128]),
                         writes=[slb], dma=True)
                    S.op("pool", lambda e, sl=sl, j0=j0, nj=nj: e.dma_start(out=sl[:, 4096:4096 + 8 * nj * 128].rearrange("p (k n) -> p k n", k=8),
                                                                            in_=win[:, :, DFF + j0 * 128:DFF + (j0 + nj) * 128]),
                         writes=[slb], dma=True)
                    for jj in range(nj):
                        j = j0 + jj
                        for ti, (t0, tl, c) in enumerate(TT):
                            pa, pab = next_ps()
                            pbt, pbb = next_ps()
                            for k in range(8):
                                S.op("pe", lambda e, pa=pa, sl=sl, k=k, jj=jj, nj=nj, t0=t0, tl=tl: e.matmul(pa[:, 0:tl], lhsT=sl[:, k * nj * 128 + jj * 128:k * nj * 128 + (jj + 1) * 128],
                                                                                                           rhs=u[:, k, t0:t0 + tl], start=(k == 0), stop=(k == 7)),
                                     reads=[slb, u_b[ti]], writes=[pab])
                            for k in range(8):
                                S.op("pe", lambda e, pbt=pbt, sl=sl, k=k, jj=jj, nj=nj, t0=t0, tl=tl: e.matmul(pbt[:, 0:tl], lhsT=sl[:, 4096 + k * nj * 128 + jj * 128:4096 + k * nj * 128 + (jj + 1) * 128],
                                                                                                             rhs=u[:, k, t0:t0 + tl], start=(k == 0), stop=(k == 7)),
                                     reads=[slb, u_b[ti]], writes=[pbb])
                            tm, tmb = next_tmp()
                            S.op("act", lambda e, pa=pa, tm=tm, tl=tl: e.activation(out=tm[:, 0:tl], in_=pa[:, 0:tl], func=AF.Silu), reads=[pab], writes=[tmb])
                            S.op("dve", lambda e, pbt=pbt, tm=tm, j=j, h0=h0, t0=t0, tl=tl: e.tensor_tensor(out=hid[:, j - h0, t0:t0 + tl], in0=tm[:, 0:tl], in1=pbt[:, 0:tl], op=ALU.mult),
                                 reads=[tmb, pbb], writes=[hid_b[j - h0][ti]])
                    if l == 0 and i == 0:
                        mod_hook(1)
                if l == 0 and i == 0:
                    mod_hook(1)
                slots = []
                jj0 = 0
                while jj0 < nh:
                    n = min(8, nh - jj0)
                    sl, slb = next_slot()
                    S.op("pool", lambda e, sl=sl, jj0=jj0, n=n, h0=h0: e.dma_start(out=sl[:, 0:n * 1024].rearrange("p (j n) -> p j n", j=n), in_=wout[:, h0 + jj0:h0 + jj0 + n, :]),
                         writes=[slb], dma=True)
                    slots.append((sl, slb, jj0, n))
                    jj0 += n
                for f in range(8):
                    for ti, (t0, tl, c) in enumerate(TT):
                        pt, pb = next_ps()
                        for (sl, slb, jj0, n) in slots:
                            for q in range(n):
                                jj = jj0 + q
                                S.op("pe", lambda e, pt=pt, sl=sl, q=q, f=f, jj=jj, t0=t0, tl=tl, nh=nh: e.matmul(pt[:, 0:tl], lhsT=sl[:, q * 1024 + f * 128:q * 1024 + (f + 1) * 128],
                                                                                                                rhs=hid[:, jj, t0:t0 + tl], start=(jj == 0), stop=(jj == nh - 1)),
                                     reads=[slb, hid_b[jj][ti]], writes=[pb])
                        S.op("dve", lambda e, pt=pt, f=f, t0=t0, tl=tl, c=c: e.scalar_tensor_tensor(out=xres[:, f, t0:t0 + tl], in0=pt[:, 0:tl], scalar=modG[:, l, gi, f, c:c + 1],
                                                                                                   in1=xres[:, f, t0:t0 + tl], op0=ALU.mult, op1=ALU.add),
                             reads=[pb, modA_b, xres_b[ti]], writes=[xres_b[ti]])

        S.phase = "consts"
        S.op("sp", lambda e: e.dma_start(out=ropec[:], in_=cos_d.rearrange("(b p) f -> p b f", p=128)), writes=[rope_b], dma=True)
        S.op("sp", lambda e: e.dma_start(out=ropes[:], in_=sin_d.rearrange("(b p) f -> p b f", p=128)), writes=[rope_b], dma=True)
        for l in range(2):
            for hh in range(6):
                S.op("sp", lambda e, l=l, hh=hh: e.dma_start(out=gq[:, l, hh, :], in_=qkg_d[l, (0 if hh < 4 else 1):(1 if hh < 4 else 2), :].broadcast_to([128, 64])),
                     writes=[gq_b], dma=True)
        S.op("sp", lambda e: e.dma_start(out=ctxbias[:], in_=ctxbias_d), writes=[misc_b], dma=True)
        S.op("sp", lambda e: e.dma_start(out=esink[:], in_=sink_d.rearrange("l h -> (l h)").rearrange("(o n) -> o n", o=1).broadcast_to([64, 8])), writes=[misc_b], dma=True)
        S.op("act", lambda e: e.activation(out=esink[:], in_=esink[:], func=AF.Exp), reads=[misc_b], writes=[misc_b])
        S.op("sp", lambda e: e.dma_start(out=esink128[:], in_=sink_d.rearrange("l h -> (l h)").rearrange("(o n) -> o n", o=1).broadcast_to([128, 8])), writes=[misc_b], dma=True)
        S.op("act", lambda e: e.activation(out=esink128[:], in_=esink128[:], func=AF.Exp), reads=[misc_b], writes=[misc_b])
        S.op("dve", lambda e: e.memset(ones_f[:], 1.0), writes=[misc_b])
        S.op("dve", lambda e: e.memset(vin64[:], 0.0), writes=[vec64_b])
        S.op("sp", lambda e: e.dma_start(out=vin64[0:32, :], in_=mixg_d.rearrange("l (c p) -> (l c) p", p=64)), writes=[vec64_b], dma=True)
        S.op("sp", lambda e: e.dma_start(out=vin64[32:36, :], in_=hfr_d.rearrange("l i d -> (l i) d")), writes=[vec64_b], dma=True)
        S.op("sp", lambda e: e.dma_start(out=vin64[36:38, :], in_=hb1_d), writes=[vec64_b], dma=True)
        S.op("sp", lambda e: e.dma_start(out=vin64[38:40, :], in_=hb2_d), writes=[vec64_b], dma=True)
        pt, pb = next_ps()
        S.op("pe", lambda e, pt=pt: e.transpose(pt[0:64, 0:128], vin64[:], ident[:]), reads=[vec64_b, ident_b], writes=[pb])
        S.op("dve", lambda e, pt=pt: e.tensor_copy(out=vec64[:], in_=pt[0:64, 0:128]), reads=[pb], writes=[vec64_b])

        def mixgain64(l, piece):
            return vec64[:, l * 16 + piece:l * 16 + piece + 1]

        _ei = [0]

        def next_e():
            i = _ei[0] % 4
            _ei[0] += 1
            return etile[i], etile_b[i]

        PS_O, PS_D = 6, 7

        def wout_partial(l, pieces, ksz, wslot, wslot_b, ysrc):
            for f in range(8):
                for ti, (t0, tl, c) in enumerate(TT):
                    pt, pb = next_ps(6)
                    for pi in range(pieces):
                        yap, ybufs = ysrc(pi, t0, tl)
                        S.op("pe", lambda e, pt=pt, pi=pi, f=f, tl=tl, yap=yap: e.matmul(pt[:, 0:tl], lhsT=wslot[0:ksz, pi * 1024 + f * 128:pi * 1024 + (f + 1) * 128], rhs=yap,
                                                                                         start=(pi == 0), stop=(pi == pieces - 1)),
                             reads=[wslot_b] + ybufs, writes=[pb])
                    S.op("dve", lambda e, pt=pt, f=f, t0=t0, tl=tl, c=c: e.scalar_tensor_tensor(out=xres[:, f, t0:t0 + tl], in0=pt[:, 0:tl], scalar=modG[:, l, 1, f, c:c + 1],
                                                                                               in1=xres[:, f, t0:t0 + tl], op0=ALU.mult, op1=ALU.add),
                         reads=[pb, modA_b, xres_b[ti]], writes=[xres_b[ti]])

        def attention_group(l, grp):
            c0 = 0 if grp == 0 else 1280
            wv = win_d[l].rearrange("(k p) n -> p k n", p=128)
            fence()
            if grp == 0 or not STAGES.get("A", True):
                S.op("sp", lambda e: e.dma_start(out=amask[:], in_=amask_d.rearrange("a p n -> p a n")), writes=[amask_b], dma=True)
            sl, slb = next_slot()
            S.op("pool", lambda e, sl=sl: e.dma_start(out=sl[:, 0:4096].rearrange("p (k n) -> p k n", k=8), in_=wv[:, :, c0:c0 + 512]), writes=[slb], dma=True)
            for g_ in range(2):
                S.op("pool", lambda e, g_=g_: e.dma_start(out=ctxv[:, :, g_, 0:64], in_=ctx_d[2 * grp + 1, l].rearrange("(b p) f -> p b f", p=128)[:, :, g_ * 64:(g_ + 1) * 64]),
                     reads=[u_b[0]], writes=[ctxv_b], dma=True)
            S.op("dve", lambda e: e.memset(ctxv[:, :, :, 64:65], 1.0), writes=[ctxv_b])
            S.op("dve", lambda e: e.memset(vtok[:, :, :, 64:65], 1.0), writes=[vtok_b])
            sg, sgb = stg[0], stg_b[0]
            S.op("sp", lambda e, sg=sg: e.dma_start(out=sg[:, 0:512].rearrange("p (b f) -> p b f", b=4), in_=ctx_d[2 * grp, l].rearrange("(b p) f -> p b f", p=128)),
                 reads=[u_b[0]], writes=[sgb], dma=True)
            for g in range(2):
                pt, pb = next_ps(6)
                for b in range(4):
                    S.op("pe", lambda e, pt=pt, sg=sg, b=b, g=g: e.transpose(pt[0:64, b * 128:(b + 1) * 128], sg[:, b * 128 + g * 64:b * 128 + (g + 1) * 64], ident[:]),
                         reads=[sgb, ident_b], writes=[pb])
                S.op("act", lambda e, pt=pt, g=g: e.copy(out=ctxkT[:, g, :], in_=pt[0:64, :]), reads=[pb], writes=[ctxkT_b])
            def prep_block(tb):
                tti = 0 if tb < 2 else (1 if tb < 6 else 2)
                sq_, sqb_ = ssq2[tb % 2], ssq2_b[tb % 2]
                pp, ppb = next_ps(6)
                for k in range(8):
                    S.op("pe", lambda e, pp=pp, sl=sl, k=k, tb=tb: e.matmul(pp[:, :], lhsT=u[:, k, tb * 128:(tb + 1) * 128], rhs=sl[:, k * 512:(k + 1) * 512], start=(k == 0), stop=(k == 7)),
                         reads=[slb, u_b[tti]], writes=[ppb])
                yield
                kv, kvb = kvst[tb % 2], kvst_b[tb % 2]
                qn, qnb = qkn[tb % 2], qkn_b[tb % 2]
                qr, qrb = qkr[tb % 2], qkr_b[tb % 2]
                S.op("act", lambda e, pp=pp, tb=tb: e.copy(out=vtok[:, tb, :, 0:64], in_=pp[:, 384:512].rearrange("p (g f) -> p g f", g=2)), reads=[ppb], writes=[vtok_b])
                S.op("act", lambda e, pp=pp, kv=kv: e.copy(out=kv[:, 1, :], in_=pp[:, 384:512]), reads=[ppb], writes=[kvb])
                if grp == 0:
                    S.op("act", lambda e, pp=pp, qn=qn: e.copy(out=qn[:, :], in_=pp[:, 0:384]), reads=[ppb], writes=[qnb])
                else:
                    S.op("act", lambda e, pp=pp, qr=qr: e.activation(out=qr[:, :], in_=pp[:, 0:384], func=AF.Square), reads=[ppb], writes=[qrb])
                    yield
                    S.op("dve", lambda e, qr=qr: e.reduce_sum(out=sq_[:, 0:6], in_=qr[:, :].rearrange("p (h d) -> p h d", h=6), axis=AX.X), reads=[qrb], writes=[sqb_])
                    yield
                    S.op("act", lambda e: e.activation(out=sq_[:, 0:6], in_=sq_[:, 0:6], func=AF.Sqrt, scale=1.0 / 64, bias=EPS), reads=[sqb_], writes=[sqb_])
                    yield
                    S.op("dve", lambda e: e.reciprocal(out=sq_[:, 0:6], in_=sq_[:, 0:6]), reads=[sqb_], writes=[sqb_])
                    S.op("dve", lambda e, pp=pp, qn=qn: e.tensor_tensor(out=qn[:, :].rearrange("p (h d) -> p h d", h=6), in0=pp[:, 0:384].rearrange("p (h d) -> p h d", h=6),
                                                                        in1=sq_[:, 0:6].unsqueeze(2).broadcast_to([128, 6, 64]), op=ALU.mult),
                         reads=[ppb, sqb_], writes=[qnb])
                    S.op("dve", lambda e, qn=qn: e.tensor_tensor(out=qn[:, :], in0=qn[:, :], in1=gq[:, l].rearrange("p h d -> p (h d)"), op=ALU.mult), reads=[qnb, gq_b], writes=[qnb])
                yield
                S.op("dve", lambda e, qn=qn, kv=kv: e.tensor_copy(out=kv[:, 0, :], in_=qn[:, 256:384]), reads=[qnb], writes=[kvb])
                S.op("sp", lambda e, kv=kv, tb=tb: e.dma_start(out=kv_d[2 * grp:2 * grp + 2, l, tb * 128:(tb + 1) * 128, :].rearrange("a t f -> t a f"), in_=kv[:]),
                     reads=[kvb], dma=True, is_out=True)
                x1 = qn[:, :].rearrange("p (h two d) -> p h two d", h=6, two=2)[:, :, 0, :]
                x2 = qn[:, :].rearrange("p (h two d) -> p h two d", h=6, two=2)[:, :, 1, :]
                o1 = qr[:, :].rearrange("p (h two d) -> p h two d", h=6, two=2)[:, :, 0, :]
                o2 = qr[:, :].rearrange("p (h two d) -> p h two d", h=6, two=2)[:, :, 1, :]
                cc = ropec[:, tb, :].unsqueeze(1).broadcast_to([128, 6, 32])
                ss = ropes[:, tb, :].unsqueeze(1).broadcast_to([128, 6, 32])
                rt = rtmp2[tb % 2][:, :].rearrange("p (h d) -> p h d", h=6)
                rtb = rtmp2_b[tb % 2]
                sq_, sqb_ = ssq2[tb % 2], ssq2_b[tb % 2]
                yield
                S.op("dve", lambda e, o1=o1, x1=x1, cc=cc: e.tensor_tensor(out=o1, in0=x1, in1=cc, op=ALU.mult), reads=[qnb, rope_b], writes=[qrb])
                S.op("dve", lambda e, rt=rt, x2=x2, ss=ss: e.tensor_tensor(out=rt, in0=x2, in1=ss, op=ALU.mult), reads=[qnb, rope_b], writes=[rtb])
                yield
                S.op("dve", lambda e, o1=o1, rt=rt: e.tensor_tensor(out=o1, in0=o1, in1=rt, op=ALU.subtract), reads=[qrb, rtb], writes=[qrb])
                yield
                S.op("dve", lambda e, o2=o2, x2=x2, cc=cc: e.tensor_tensor(out=o2, in0=x2, in1=cc, op=ALU.mult), reads=[qnb, rope_b], writes=[qrb])
                S.op("dve", lambda e, rt=rt, x1=x1, ss=ss: e.tensor_tensor(out=rt, in0=x1, in1=ss, op=ALU.mult), reads=[qnb, rope_b], writes=[rtb])
                yield
                S.op("dve", lambda e, o2=o2, rt=rt: e.tensor_tensor(out=o2, in0=o2, in1=rt, op=ALU.add), reads=[qrb, rtb], writes=[qrb])
                yield
                pq, pqb = next_ps(6)
                for h in range(4):
                    S.op("pe", lambda e, pq=pq, qr=qr, h=h: e.transpose(pq[0:64, h * 128:(h + 1) * 128], qr[:, h * 64:(h + 1) * 64], ident[:]), reads=[qrb, ident_b], writes=[pqb])
                yield
                S.op("act", lambda e, pq=pq, tb=tb: e.copy(out=qT[:, :, tb * 128:(tb + 1) * 128], in_=pq[0:64, :].rearrange("p (h t) -> p h t", h=4)), reads=[pqb], writes=[qT_b])
                yield
                pk, pkb = next_ps(6)
                for g in range(2):
                    S.op("pe", lambda e, pk=pk, qr=qr, g=g: e.transpose(pk[0:64, g * 128:(g + 1) * 128], qr[:, 256 + g * 64:256 + (g + 1) * 64], ident[:]), reads=[qrb, ident_b], writes=[pkb])
                yield
                S.op("act", lambda e, pk=pk, tb=tb: e.copy(out=kT[:, :, tb * 128:(tb + 1) * 128], in_=pk[0:64, 0:256].rearrange("p (h t) -> p h t", h=2)), reads=[pkb], writes=[kT_b])

            for tb0 in range(0, NTB, 2):
                gens = [prep_block(tb0), prep_block(tb0 + 1)]
                while gens:
                    for g_ in list(gens):
                        try:
                            next(g_)
                        except StopIteration:
                            gens.remove(g_)
            _segc = [0]
            pending = [None]
            for h in range(4):
                g = h // 2
                segs = [(0, 256, [("loc", 0, None), ("loc", 128, None)]),
                        (256, 512, None), (768, 512, None)]
                for (q0, nq, keys) in segs:
                    if keys is None:
                        keys = [("ctx", b, None) for b in range(4)]
                        for j in range(8):
                            off = 896 - 128 * j + (q0 - 256)
                            keys.append(("loc", 256 + 128 * j, amask[:, 2 * grp + (j % 2), off:off + nq]))
                    po, pob = ps[6 + _segc[0] % 2], ps_b[6 + _segc[0] % 2]
                    _segc[0] += 1
                    nk = len(keys)
                    def stage1(ki, keys=keys, nq=nq, h=h, g=g, q0=q0):
                        kind, kpos, mk = keys[ki]
                        pss, pssb = next_ps(6)
                        et, etb = next_e()
                        if kind == "ctx":
                            S.op("pe", lambda e, pss=pss, kpos=kpos: e.matmul(pss[:, 0:nq], lhsT=ctxkT[:, g, kpos * 128:(kpos + 1) * 128], rhs=qT[:, h, q0:q0 + nq], start=True, stop=True),
                                 reads=[ctxkT_b, qT_b], writes=[pssb])
                            S.op("act", lambda e, pss=pss, et=et: e.activation(out=et[:, 0:nq], in_=pss[:, 0:nq], func=AF.Exp, scale=0.125, bias=ctxbias[:, 0:1]),
                                 reads=[pssb, misc_b], writes=[etb])
                            vap = ctxv[:, kpos, g, :]
                            vb = ctxv_b
                        else:
                            S.op("pe", lambda e, pss=pss, kpos=kpos: e.matmul(pss[:, 0:nq], lhsT=kT[:, g, kpos:kpos + 128], rhs=qT[:, h, q0:q0 + nq], start=True, stop=True),
                                 reads=[kT_b, qT_b], writes=[pssb])
                            S.op("act", lambda e, pss=pss, et=et: e.activation(out=et[:, 0:nq], in_=pss[:, 0:nq], func=AF.Exp, scale=0.125), reads=[pssb], writes=[etb])
                            if mk is not None:
                                S.op("dve", lambda e, et=et, mk=mk: e.tensor_tensor(out=et[:, 0:nq], in0=et[:, 0:nq], in1=mk, op=ALU.mult), reads=[etb, amask_b], writes=[etb])
                            vap = vtok[:, kpos // 128, g, :]
                            vb = vtok_b
                        return et, etb, vap, vb

                    def stage2(ki, st1, nq=nq, nk=nk, po=po, pob=pob):
                        et, etb, vap, vb = st1
                        S.op("pe", lambda e, vap=vap, et=et: e.matmul(po[0:65, 0:nq], lhsT=vap, rhs=et[:, 0:nq], start=(ki == 0), stop=(ki == nk - 1)),
                             reads=[vb, etb], writes=[pob])

                    pend = [stage1(0)]
                    if nk > 1:
                        pend.append(stage1(1))
                    if pending[0] is not None:
                        pending[0]()
                        pending[0] = None
                    for ki in range(nk):
                        cur_ = pend.pop(0)
                        if ki + 2 < nk:
                            pend.append(stage1(ki + 2))
                        stage2(ki, cur_)
                    def epilogue(h=h, q0=q0, nq=nq, po=po, pob=pob):
                        if grp == 0:
                            S.op("dve", lambda e: e.tensor_scalar(out=rrow[64:65, 0:nq], in0=po[64:65, 0:nq], scalar1=esink128[64:65, l * 4 + h:l * 4 + h + 1], scalar2=None, op0=ALU.add),
                                 reads=[pob, misc_b], writes=[rrow_b])
                            S.op("dve", lambda e: e.reciprocal(out=rrow[64:65, 0:nq], in_=rrow[64:65, 0:nq]), reads=[rrow_b], writes=[rrow_b])
                        else:
                            S.op("dve", lambda e: e.reciprocal(out=rrow[64:65, 0:nq], in_=po[64:65, 0:nq]), reads=[pob], writes=[rrow_b])
                        pbk, pbkb = next_ps(6)
                        S.op("pe", lambda e: e.matmul(pbk[0:64, 0:nq], lhsT=ones_f[64:65, 0:64], rhs=rrow[64:65, 0:nq], start=True, stop=True), reads=[misc_b, rrow_b], writes=[pbkb])
                        S.op("act", lambda e: e.copy(out=rden[:, 0:nq], in_=pbk[0:64, 0:nq]), reads=[pbkb], writes=[rden_b])
                        S.op("dve", lambda e: e.tensor_tensor(out=oT[:, h, q0:q0 + nq], in0=po[0:64, 0:nq], in1=rden[:, 0:nq], op=ALU.mult),
                             reads=[pob, rden_b], writes=[oT_b])

                    pending[0] = epilogue
            if pending[0] is not None:
                pending[0]()
                pending[0] = None
            wsl, wslb = next_slot()
            S.op("pool", lambda e, wsl=wsl: e.dma_start(out=wsl[0:64, 0:4096].rearrange("p (h n) -> p h n", h=4),
                                                        in_=wout_d[l, (0 if grp == 0 else 512):(256 if grp == 0 else 768), :].rearrange("(h p) n -> p h n", p=64)),
                 writes=[wslb], dma=True)
            for ti, (t0, tl, c) in enumerate(TT):
                pt, pb = next_ps(6)
                for h in range(4):
                    et, etb = next_e()
                    S.op("act", lambda e, et=et, h=h, t0=t0, tl=tl: e.activation(out=et[0:64, 0:tl], in_=oT[:, h, t0:t0 + tl], func=AF.Square), reads=[oT_b], writes=[etb])
                    S.op("pe", lambda e, pt=pt, et=et, h=h, tl=tl: e.matmul(pt[0:64, 0:tl], lhsT=ones_bf[0:64, 0:64], rhs=et[0:64, 0:tl], start=(h == 0), stop=(h == 3)),
                         reads=[ones_b, etb], writes=[pb])
                S.op("act", lambda e, pt=pt, tl=tl: e.activation(out=rden[:, 0:tl], in_=pt[0:64, 0:tl], func=AF.Sqrt, scale=1.0 / 256, bias=EPS), reads=[pb], writes=[rden_b])
                S.op("dve", lambda e, tl=tl: e.reciprocal(out=rden[:, 0:tl], in_=rden[:, 0:tl]), reads=[rden_b], writes=[rden_b])
                for h in range(4):
                    piece = (0 if grp == 0 else 8) + h
                    S.op("dve", lambda e, h=h, t0=t0, tl=tl, piece=piece: e.scalar_tensor_tensor(out=oT[:, h, t0:t0 + tl], in0=oT[:, h, t0:t0 + tl], scalar=mixgain64(l, piece),
                                                                                                in1=rden[:, 0:tl], op0=ALU.mult, op1=ALU.mult),
                         reads=[oT_b, rden_b, vec64_b], writes=[oT_b])
            wout_partial(l, 4, 64, wsl, wslb, lambda pi, t0, tl: (oT[:, pi, t0:t0 + tl], [oT_b]))

        dbg_d = dout("dbg", [8, 128, T]) if DEBUG else None

        def dump(idx, ap, bufs, n=T, np_=128):
            if not DEBUG:
                return
            S.op("pool", lambda e: e.dma_start(out=dbg_d[idx, 0:np_, 0:n], in_=ap), reads=list(bufs), dma=True, is_out=True)

        ARENA_BUFS += [gqk_b, gke_b, gqT_b, gkT_b, gktok_b, gvtok_b, gconst_b, gS_b, geb_b, godT_b, ggdT_b, amask_b]

        def fence():
            S.op("dve", lambda e: e.memset(fdummy[:], 0.0), reads=ARENA_BUFS, writes=ARENA_BUFS)

        S.op("dve", lambda e: e.memset(vin3[:], 0.0), writes=[vec3_b])
        S.op("sp", lambda e: e.dma_start(out=vin3[0:36, :], in_=hcw_d.rearrange("l t (c p) -> (l t c) p", p=128)), writes=[vec3_b], dma=True)
        S.op("sp", lambda e: e.dma_start(out=vin3[36:48, :], in_=hcb_d.rearrange("l (c p) -> (l c) p", p=128)), writes=[vec3_b], dma=True)
        S.op("sp", lambda e: e.dma_start(out=vin3[48:56, :], in_=hbias_d.rearrange("l o (c p) -> (l o c) p", p=128)), writes=[vec3_b], dma=True)
        S.op("sp", lambda e: e.dma_start(out=vin3[56:72, :], in_=mixg_d.rearrange("l (c p) -> (l c) p", p=128)), writes=[vec3_b], dma=True)
        pt, pb = next_ps()
        S.op("pe", lambda e, pt=pt: e.transpose(pt[:, 0:128], vin3[:], ident[:]), reads=[vec3_b, ident_b], writes=[pb])
        S.op("dve", lambda e, pt=pt: e.tensor_copy(out=vec3[:], in_=pt[:, 0:128]), reads=[pb], writes=[vec3_b])

        def hcw(l, tap, fc):
            o = (l * 3 + tap) * 6 + fc
            return vec3[:, o:o + 1]

        def hcb(l, fc):
            o = 36 + l * 6 + fc
            return vec3[:, o:o + 1]

        def hbias(l, o_, cc):
            o = 48 + (l * 2 + o_) * 2 + cc
            return vec3[:, o:o + 1]

        def mixgain128(l, chunk):
            o = 56 + l * 8 + chunk
            return vec3[:, o:o + 1]

        S.op("sp", lambda e: e.dma_start(out=hyflag[:], in_=hflag_d), writes=[hysc_b], dma=True)
        for l in range(2):
            for fc in range(6):
                S.op("dve", lambda e, l=l, fc=fc: e.tensor_tensor(out=hysc[:, l, fc, 0:1], in0=hcw(l, 0, fc), in1=hyflag[:, 0:1], op=ALU.mult), reads=[vec3_b, hysc_b], writes=[hysc_b])
                S.op("dve", lambda e, l=l, fc=fc: e.tensor_tensor(out=hysc[:, l, fc, 1:2], in0=hcw(l, 2, fc), in1=hyflag[:, 0:1], op=ALU.mult), reads=[vec3_b, hysc_b], writes=[hysc_b])
                S.op("dve", lambda e, l=l, fc=fc: e.tensor_tensor(out=hysc[:, l, fc, 2:3], in0=hysc[:, l, fc, 0:1], in1=hcw(l, 0, fc), op=ALU.subtract), reads=[vec3_b, hysc_b], writes=[hysc_b])
                S.op("dve", lambda e, l=l, fc=fc: e.tensor_tensor(out=hysc[:, l, fc, 3:4], in0=hysc[:, l, fc, 1:2], in1=hcw(l, 2, fc), op=ALU.subtract), reads=[vec3_b, hysc_b], writes=[hysc_b])
            for i in range(2):
                S.op("dve", lambda e, l=l, i=i: e.tensor_tensor(out=hyfb[:, l * 2 + i:l * 2 + i + 1], in0=vec64[:, 32 + l * 2 + i:33 + l * 2 + i],
                                                                  in1=vec64[:, 36 + i * 2 + l:37 + i * 2 + l], op=ALU.mult), reads=[vec64_b], writes=[hysc_b])

        PI = math.pi

        def hyena_group(l):
            fence()
            wv = win_d[l].rearrange("(k p) n -> p k n", p=128)
            wsl, wslb = ring[0], ring_b[0]
            S.op("pool", lambda e: e.dma_start(out=w3b[:], in_=hw3_d[l]), reads=[hyw_b], writes=[hyw_b], dma=True)
            S.op("sp", lambda e: e.dma_start(out=hyw12[0:33, 0:64], in_=hw1_d[l]), writes=[hyw_b], dma=True)
            S.op("sp", lambda e: e.dma_start(out=hyw12[:, 64:128], in_=hw2_d[l]), writes=[hyw_b], dma=True)
            for ti, (t0, tl, c) in enumerate(TT):
                zt, ztb = next_tmp()
                S.op("sp", lambda e, zt=zt, t0=t0, tl=tl: e.dma_start(out=zt[0:33, 0:tl], in_=hz_d[:, t0:t0 + tl]), writes=[ztb], dma=True)
                cur, curb = zt, ztb
                for i in range(2):
                    pt, pb = next_ps()
                    kk = 33 if i == 0 else 64
                    wap = hyw12[0:33, 0:64] if i == 0 else hyw12[:, 64:128]
                    S.op("pe", lambda e, pt=pt, wap=wap, cur=cur, kk=kk, tl=tl: e.matmul(pt[0:64, 0:tl], lhsT=wap, rhs=cur[0:kk, 0:tl], start=True, stop=True), reads=[hyw_b, curb], writes=[pb])
                    a1, a1b = next_tmp()
                    S.op("dve", lambda e, pt=pt, a1=a1, i=i, tl=tl: e.tensor_scalar(out=a1[0:64, 0:tl], in0=pt[0:64, 0:tl], scalar1=vec64[:, 32 + l * 2 + i:33 + l * 2 + i],
                                                                                   scalar2=hyfb[:, l * 2 + i:l * 2 + i + 1], op0=ALU.mult, op1=ALU.add),
                         reads=[pb, vec64_b, hysc_b], writes=[a1b])
                    S.op("dve", lambda e, a1=a1, tl=tl: e.tensor_scalar(out=rden[:, 0:tl], in0=a1[0:64, 0:tl], scalar1=1.0 / (2.0 * PI), scalar2=12582912.0, op0=ALU.mult, op1=ALU.add), reads=[a1b], writes=[rden_b])
                    S.op("dve", lambda e, tl=tl: e.tensor_scalar(out=rden[:, 0:tl], in0=rden[:, 0:tl], scalar1=12582912.0, scalar2=None, op0=ALU.subtract), reads=[rden_b], writes=[rden_b])
                    S.op("dve", lambda e, a1=a1, tl=tl: e.scalar_tensor_tensor(out=a1[0:64, 0:tl], in0=rden[:, 0:tl], scalar=-2.0 * PI, in1=a1[0:64, 0:tl], op0=ALU.mult, op1=ALU.add), reads=[rden_b, a1b], writes=[a1b])
                    if i == 0:
                        S.op("act", lambda e, a1=a1, tl=tl: e.activation(out=a1[0:64, 0:tl], in_=a1[0:64, 0:tl], func=AF.Sin), reads=[a1b], writes=[a1b])
                        cur, curb = a1, a1b
                    else:
                        S.op("act", lambda e, a1=a1, t0=t0, tl=tl: e.activation(out=hyh2[:, t0:t0 + tl], in_=a1[0:64, 0:tl], func=AF.Sin), reads=[a1b], writes=[hyh2_b])
            for cc in range(2):
                for part in range(3):
                    S.op("pool", lambda e, part=part, cc=cc: e.dma_start(out=wsl[:, part * 1024:(part + 1) * 1024].rearrange("p (k n) -> p k n", k=8),
                                                                        in_=wv[:, :, 512 + (part * 2 + cc) * 128:512 + (part * 2 + cc + 1) * 128]), writes=[wslb], dma=True)
                for part in range(3):
                    fc = part * 2 + cc
                    pts = []
                    for ti, (t0, tl, c) in enumerate(TT):
                        pt, pb = next_ps()
                        for k in range(8):
                            S.op("pe", lambda e, pt=pt, wsl=wsl, k=k, part=part, t0=t0, tl=tl: e.matmul(pt[:, 0:tl], lhsT=wsl[:, part * 1024 + k * 128:part * 1024 + (k + 1) * 128], rhs=u[:, k, t0:t0 + tl],
                                                                                                      start=(k == 0), stop=(k == 7)),
                                 reads=[wslb, u_b[ti]], writes=[pb])
                        pts.append((pt, pb))

                    def zcol(t):
                        ti_ = 0 if t < 256 else (1 if t < 768 else 2)
                        return pts[ti_][0][:, t - TT[ti_][0]:t - TT[ti_][0] + 1], pts[ti_][1]

                    for ti, (t0, tl, c) in enumerate(TT):
                        pt, pb = pts[ti]
                        ac, acb = next_tmp()
                        S.op("act", lambda e, ac=ac, pt=pt, tl=tl, fc=fc: e.activation(out=ac[:, 0:tl], in_=pt[:, 0:tl], func=AF.Identity, scale=hcw(l, 1, fc), bias=hcb(l, fc)),
                             reads=[pb, vec3_b], writes=[acb])
                        S.op("dve", lambda e, ac=ac, pt=pt, tl=tl, fc=fc: e.scalar_tensor_tensor(out=ac[:, 1:tl], in0=pt[:, 0:tl - 1], scalar=hcw(l, 0, fc), in1=ac[:, 1:tl], op0=ALU.mult, op1=ALU.add),
                             reads=[pb, vec3_b, acb], writes=[acb])
                        S.op("dve", lambda e, ac=ac, pt=pt, tl=tl, fc=fc: e.scalar_tensor_tensor(out=ac[:, 0:tl - 1], in0=pt[:, 1:tl], scalar=hcw(l, 2, fc), in1=ac[:, 0:tl - 1], op0=ALU.mult, op1=ALU.add),
                             reads=[pb, vec3_b, acb], writes=[acb])
                        fix = []
                        if t0 == 256:
                            fix = [(512 - t0, 511, 2), (511 - t0, 512, 3), (767 - t0, 768, 1)]
                        elif t0 == 768:
                            fix = [(0, 767, 0), (1024 - t0, 1023, 2), (1023 - t0, 1024, 3)]
                        for (col, src, si) in fix:
                            zap, zb = zcol(src)
                            S.op("dve", lambda e, ac=ac, col=col, zap=zap, si=si, fc=fc: e.scalar_tensor_tensor(out=ac[:, col:col + 1], in0=zap, scalar=hysc[:, l, fc, si:si + 1],
                                                                                                              in1=ac[:, col:col + 1], op0=ALU.mult, op1=ALU.add),
                                 reads=[zb, hysc_b, acb], writes=[acb])
                        if part == 0:
                            S.op("act", lambda e, ac=ac, t0=t0, tl=tl: e.copy(out=hyF[:, t0:t0 + tl], in_=ac[:, 0:tl]), reads=[acb], writes=[hyF_b])
                        elif part == 1:
                            S.op("act", lambda e, ac=ac, t0=t0, tl=tl: e.copy(out=hyx1[:, t0:t0 + tl], in_=ac[:, 0:tl]), reads=[acb], writes=[hyx1_b])
                        else:
                            S.op("act", lambda e, ac=ac, t0=t0, tl=tl: e.copy(out=hyx2[:, t0:t0 + tl], in_=ac[:, 0:tl]), reads=[acb], writes=[hyx2_b])
                if l == 0 and cc == 0:
                    dump(0, hyF[:, :], [hyF_b])
                    dump(1, hyx1[:, :], [hyx1_b])
                    dump(2, hyh2[:, :], [hyh2_b], np_=64)
                for o_ in range(2):
                    for tb0 in range(0, NTB, 4):
                        nb = min(4, NTB - tb0)
                        pt, pb = next_ps()
                        for q in range(nb):
                            tb = tb0 + q
                            S.op("pe", lambda e, pt=pt, q=q, tb=tb: e.transpose(pt[:, q * 128:(q + 1) * 128], hyF[:, tb * 128:(tb + 1) * 128], ident[:]), reads=[hyF_b, ident_b], writes=[pb])
                        S.op("act", lambda e, pt=pt, tb0=tb0, nb=nb: e.copy(out=hyvtok[:, tb0:tb0 + nb, :], in_=pt[:, 0:nb * 128].rearrange("p (b c) -> p b c", b=nb)), reads=[pb], writes=[hyvtok_b])
                    for jb in range(NTB):
                        dc, dcb = next_tmp()
                        S.op("sp", lambda e, dc=dc, jb=jb, cc=cc: e.dma_start(out=dc[:, 0:256].rearrange("p (d c) -> p d c", d=2), in_=hdec_d[jb * 128:(jb + 1) * 128, :, cc * 128:(cc + 1) * 128]),
                             writes=[dcb], dma=True)
                        pt, pb = next_ps()
                        for dr in range(2):
                            cb0 = dr * 512 + o_ * 256 + cc * 128
                            S.op("pe", lambda e, pt=pt, dr=dr, cb0=cb0, jb=jb: e.matmul(pt[:, dr * 128:(dr + 1) * 128], lhsT=hyh2[:, jb * 128:(jb + 1) * 128], rhs=w3b[:, cb0:cb0 + 128], start=True, stop=True),
                                 reads=[hyh2_b, hyw_b], writes=[pb])
                        S.op("dve", lambda e, pt=pt, dc=dc: e.tensor_tensor(out=dc[:, 0:256], in0=pt[:, 0:256], in1=dc[:, 0:256], op=ALU.mult), reads=[pb, dcb], writes=[dcb])
                        S.op("dve", lambda e, dc=dc, jb=jb: e.tensor_tensor(out=hyhp[:, jb, :], in0=dc[:, 0:128], in1=dc[:, 128:256], op=ALU.add), reads=[dcb], writes=[hyh_b])
                        S.op("dve", lambda e, dc=dc, jb=jb: e.tensor_tensor(out=hyhm[:, jb, :], in0=dc[:, 0:128], in1=dc[:, 128:256], op=ALU.subtract), reads=[dcb], writes=[hyh_b])
                    if l == 0 and cc == 0 and o_ == 0:
                        dump(3, arena[:, 6400:7680], [hyh_b])
                        dump(7, arena[:, 5120:6400], [hyvtok_b])
                    psl, pslb = ring[0], ring_b[0]
                    fws = None
                    for pair in list(range(2, 10)) + [0, 1]:
                        if pair == 0:
                            S.op("pool", lambda e, psl=psl: e.dma_start(out=psl[:, 0:1024].rearrange("p (t r) -> p t r", t=2), in_=hfwp_d.rearrange("(t p) r -> p t r", p=128)), writes=[pslb], dma=True)
                            S.op("pool", lambda e, psl=psl: e.dma_start(out=psl[:, 1024:2048].rearrange("p (r t) -> p r t", r=4), in_=hivp_d.rearrange("(r p) t -> p r t", p=128)), writes=[pslb], dma=True)
                        if pair < 2:
                            rc_re, rc_im = pair, 2 + pair
                            ntc, tb_base = 2, 0
                            lre = lambda tc, rc: psl[:, tc * 512 + rc * 128:tc * 512 + (rc + 1) * 128]
                            fb_ = pslb
                            yi = (rc_re, rc_im)
                        else:
                            pp_ = pair - 2
                            sgrp, a_ = pp_ // 2, pp_ % 2
                            rc_re, rc_im = 4 * sgrp + a_, 4 * sgrp + 2 + a_
                            if pp_ % 4 == 0:
                                half = pp_ // 4
                                fws, fwsb = ring[1 + half], ring_b[1 + half]
                                S.op("pool", lambda e, fws=fws, half=half: e.dma_start(out=fws[:, :].rearrange("p (t r) -> p t r", t=8),
                                                                                      in_=hfwg_d.rearrange("(t p) r -> p t r", p=128)[:, :, half * 1024:(half + 1) * 1024]), writes=[fwsb], dma=True)
                            ntc, tb_base = 8, 2
                            lre = lambda tc, rc, fws=fws: fws[:, tc * 1024 + (rc % 8) * 128:tc * 1024 + (rc % 8 + 1) * 128]
                            fb_ = fwsb
                            yi = (4 + rc_re, 4 + rc_im)
                        pu, pub = next_ps()
                        pk, pkb = next_ps()
                        for qi, rc in enumerate((rc_re, rc_im)):
                            for tc in range(ntc):
                                S.op("pe", lambda e, pu=pu, qi=qi, tc=tc, rc=rc, lre=lre, tb_base=tb_base, ntc=ntc: e.matmul(pu[:, qi * 128:(qi + 1) * 128], lhsT=lre(tc, rc), rhs=hyvtok[:, tb_base + tc, :],
                                                                                                                          start=(tc == 0), stop=(tc == ntc - 1)),
                                     reads=[fb_, hyvtok_b], writes=[pub])
                        for qi, rc in enumerate((rc_re, rc_im)):
                            hsrc = hyhp if qi == 0 else hyhm
                            for tc in range(ntc):
                                S.op("pe", lambda e, pk=pk, qi=qi, tc=tc, rc=rc, lre=lre, tb_base=tb_base, ntc=ntc, hsrc=hsrc: e.matmul(pk[:, qi * 128:(qi + 1) * 128], lhsT=lre(tc, rc), rhs=hsrc[:, tb_base + tc, :],
                                                                                                                                     start=(tc == 0), stop=(tc == ntc - 1)),
                                     reads=[fb_, hyh_b], writes=[pkb])
                        ut, utb = next_tmp()
                        S.op("act", lambda e, ut=ut, pu=pu: e.copy(out=ut[:, 0:256], in_=pu[:, 0:256]), reads=[pub], writes=[utb])
                        S.op("dve", lambda e, ut=ut, pk=pk: e.tensor_tensor(out=ut[:, 256:384], in0=ut[:, 0:128], in1=pk[:, 0:128], op=ALU.mult), reads=[utb, pkb], writes=[utb])
                        S.op("dve", lambda e, ut=ut, pk=pk: e.tensor_tensor(out=ut[:, 384:512], in0=ut[:, 128:256], in1=pk[:, 128:256], op=ALU.mult), reads=[utb, pkb], writes=[utb])
                        S.op("dve", lambda e, ut=ut, yi=yi: e.tensor_tensor(out=hyY[:, yi[0], :], in0=ut[:, 256:384], in1=ut[:, 384:512], op=ALU.subtract), reads=[utb], writes=[hyY_b])
                        S.op("dve", lambda e, ut=ut, pk=pk: e.tensor_tensor(out=ut[:, 256:384], in0=ut[:, 0:128], in1=pk[:, 128:256], op=ALU.mult), reads=[utb, pkb], writes=[utb])
                        S.op("dve", lambda e, ut=ut, pk=pk: e.tensor_tensor(out=ut[:, 384:512], in0=ut[:, 128:256], in1=pk[:, 0:128], op=ALU.mult), reads=[utb, pkb], writes=[utb])
                        S.op("dve", lambda e, ut=ut, yi=yi: e.tensor_tensor(out=hyY[:, yi[1], :], in0=ut[:, 256:384], in1=ut[:, 384:512], op=ALU.add), reads=[utb], writes=[hyY_b])
                    if l == 0 and cc == 0 and o_ == 0:
                        dump(4, arena[:, 8960:10240], [hyY_b])
                    ivs = []
                    for half in range(2):
                        isl, islb = ring[1 + half], ring_b[1 + half]
                        S.op("pool", lambda e, isl=isl, half=half: e.dma_start(out=isl[:, :].rearrange("p (r t) -> p r t", r=8),
                                                                              in_=hivg_d.rearrange("(r p) t -> p r t", p=128)[:, half * 8:(half + 1) * 8, :]), writes=[islb], dma=True)
                        ivs.append((isl, islb))
                    for ti, (t0, tl, c) in enumerate(TT):
                        pt, pb = next_ps()
                        if ti == 0:
                            for rc in range(4):
                                S.op("pe", lambda e, pt=pt, rc=rc: e.matmul(pt[:, 0:256], lhsT=hyY[:, rc, :], rhs=psl[:, 1024 + rc * 256:1024 + (rc + 1) * 256], start=(rc == 0), stop=(rc == 3)),
                                     reads=[hyY_b, pslb], writes=[pb])
                        else:
                            for rc in range(16):
                                isl, islb = ivs[rc // 8]
                                S.op("pe", lambda e, pt=pt, rc=rc, isl=isl, t0=t0: e.matmul(pt[:, 0:512], lhsT=hyY[:, 4 + rc, :], rhs=isl[:, (rc % 8) * 1024 + (t0 - 256):(rc % 8) * 1024 + (t0 - 256) + 512],
                                                                                          start=(rc == 0), stop=(rc == 15)),
                                     reads=[hyY_b, islb], writes=[pb])
                        if o_ == 0:
                            S.op("dve", lambda e, pt=pt, t0=t0, tl=tl, cc=cc: e.scalar_tensor_tensor(out=hyF[:, t0:t0 + tl], in0=hyF[:, t0:t0 + tl], scalar=hbias(l, 0, cc), in1=pt[:, 0:tl], op0=ALU.mult, op1=ALU.add),
                                 reads=[pb, vec3_b, hyF_b], writes=[hyF_b])
                            S.op("dve", lambda e, t0=t0, tl=tl: e.tensor_tensor(out=hyF[:, t0:t0 + tl], in0=hyF[:, t0:t0 + tl], in1=hyx1[:, t0:t0 + tl], op=ALU.mult), reads=[hyF_b, hyx1_b], writes=[hyF_b])
                            if l == 0 and cc == 0 and ti == 2:
                                dump(5, hyF[:, :], [hyF_b])
                        else:
                            tm, tmb = next_tmp()
                            dst = hyob0 if cc == 0 else hyx2
                            dstb = hyob0_b if cc == 0 else hyx2_b
                            S.op("dve", lambda e, pt=pt, tm=tm, t0=t0, tl=tl, cc=cc: e.scalar_tensor_tensor(out=tm[:, 0:tl], in0=hyF[:, t0:t0 + tl], scalar=hbias(l, 1, cc), in1=pt[:, 0:tl], op0=ALU.mult, op1=ALU.add),
                                 reads=[pb, vec3_b, hyF_b], writes=[tmb])
                            S.op("dve", lambda e, tm=tm, dst=dst, t0=t0, tl=tl: e.tensor_tensor(out=dst[:, t0:t0 + tl], in0=tm[:, 0:tl], in1=hyx2[:, t0:t0 + tl], op=ALU.mult), reads=[tmb, hyx2_b], writes=[dstb, hyx2_b])
            if l == 0:
                dump(6, hyob0[:, :], [hyob0_b])
            osrc = [hyob0, hyx2]
            osb = [hyob0_b, hyx2_b]
            wo, wob = ring[0], ring_b[0]
            _ri[0] = 1
            S.op("pool", lambda e, wo=wo: e.dma_start(out=wo[:, 0:2048].rearrange("p (h n) -> p h n", h=2), in_=wout_d[l, 256:512, :].rearrange("(h p) n -> p h n", p=128)), writes=[wob], dma=True)
            for ti, (t0, tl, c) in enumerate(TT):
                pt, pb = next_ps()
                for cc in range(2):
                    et, etb = next_e()
                    S.op("act", lambda e, et=et, cc=cc, t0=t0, tl=tl: e.activation(out=et[:, 0:tl], in_=osrc[cc][:, t0:t0 + tl], func=AF.Square), reads=[osb[cc]], writes=[etb])
                    S.op("pe", lambda e, pt=pt, et=et, cc=cc, tl=tl: e.matmul(pt[:, 0:tl], lhsT=ones_bf[:], rhs=et[:, 0:tl], start=(cc == 0), stop=(cc == 1)), reads=[ones_b, etb], writes=[pb])
                tm, tmb = next_tmp()
                S.op("act", lambda e, pt=pt, tm=tm, tl=tl: e.activation(out=tm[:, 0:tl], in_=pt[:, 0:tl], func=AF.Sqrt, scale=1.0 / 256, bias=EPS), reads=[pb], writes=[tmb])
                S.op("dve", lambda e, tm=tm, tl=tl: e.reciprocal(out=tm[:, 0:tl], in_=tm[:, 0:tl]), reads=[tmb], writes=[tmb])
                for cc in range(2):
                    S.op("dve", lambda e, tm=tm, cc=cc, t0=t0, tl=tl: e.scalar_tensor_tensor(out=osrc[cc][:, t0:t0 + tl], in0=osrc[cc][:, t0:t0 + tl], scalar=mixgain128(l, 2 + cc), in1=tm[:, 0:tl],
                                                                                            op0=ALU.mult, op1=ALU.mult),
                         reads=[osb[cc], tmb, vec3_b], writes=[osb[cc]])
            wout_partial(l, 2, 128, wo, wob, lambda pi, t0, tl: (osrc[pi][:, t0:t0 + tl], [osb[pi]]))

        def gla_group(l):
            fence()
            wv = win_d[l].rearrange("(k p) n -> p k n", p=128)
            wsl, wslb = ring[0], ring_b[0]
            S.op("pool", lambda e: e.dma_start(out=wsl[:, 0:6400].rearrange("p (k n) -> p k n", k=8), in_=wv[:, :, 1792:2592]), writes=[wslb], dma=True)
            S.op("sp", lambda e: e.dma_start(out=gtri, in_=tri_d.rearrange("a p c -> p a c")), writes=[gconst_b], dma=True)
            S.op("pool", lambda e: e.dma_start(out=ggw.rearrange("p (z c) -> p z c", z=2), in_=gw_d[l].rearrange("z r c -> r z c")), writes=[gconst_b], dma=True)
            S.op("pool", lambda e: e.dma_start(out=ggb, in_=gb_d[l].rearrange("z c -> (z c)").rearrange("(o n) -> o n", o=1)), writes=[gconst_b], dma=True)

            def wcol(k, c, n):
                return wsl[:, k * 800 + c:k * 800 + c + n]
            for ti, (t0, tl, c) in enumerate(TT if GLA_PART >= 2 else []):
                for (dst, dstb, cb, m, idx) in ((gqT, gqT_b, 0, 64, 0), (gqT, gqT_b, 64, 64, 1), (gkT, gkT_b, 128, 64, 0), (gkT, gkT_b, 192, 64, 1),
                                                (ggdT, ggdT_b, 512, 16, 0), (ggdT, ggdT_b, 528, 16, 1)):
                    pt, pb = next_ps(6)
                    for k in range(8):
                        S.op("pe", lambda e, pt=pt, k=k, cb=cb, m=m, t0=t0, tl=tl: e.matmul(pt[0:m, 0:tl], lhsT=wcol(k, cb, m), rhs=u[:, k, t0:t0 + tl], start=(k == 0), stop=(k == 7)),
                             reads=[wslb, u_b[ti]], writes=[pb])
                    S.op("act", lambda e, pt=pt, dst=dst, m=m, idx=idx, t0=t0, tl=tl: e.copy(out=dst[0:m, idx, t0:t0 + tl], in_=pt[0:m, 0:tl]), reads=[pb], writes=[dstb])
            for tb in range(NTB if GLA_PART >= 3 else 0):
                tti = 0 if tb < 2 else (1 if tb < 6 else 2)
                pt, pb = next_ps(6)
                for k in range(8):
                    S.op("pe", lambda e, pt=pt, k=k, tb=tb: e.matmul(pt[:, 0:384], lhsT=u[:, k, tb * 128:(tb + 1) * 128], rhs=wcol(k, 128, 384), start=(k == 0), stop=(k == 7)),
                         reads=[wslb, u_b[tti]], writes=[pb])
                if GLA_VAR == 1:
                    continue
                S.op("act", lambda e, pt=pt, tb=tb: e.copy(out=gktok[:, tb, :], in_=pt[:, 0:128]), reads=[pb], writes=[gktok_b])
                if GLA_VAR == 2:
                    continue
                if GLA_VAR == 3:
                    S.op("act", lambda e, pt=pt, tb=tb: e.copy(out=gvtok[:, tb, :], in_=pt[:, 128:384]), reads=[pb], writes=[gvtok_b])
                    continue
                S.op("act", lambda e, pt=pt, tb=tb: e.copy(out=gvtok[:, tb, :], in_=pt[:, 128:384]), reads=[pb], writes=[gvtok_b])
            SCALE = 32.0 ** -0.5
            gqk2 = [gqk, arena[0:64, 12928:13440]]
            gke2 = [gke, arena[:, 13440:13568]]
            geb2 = [geb, arena[0:64, 13568:14592].bitcast(F32).rearrange("p (a t) -> p a t", a=4)]
            gqk2_b = [gqk_b, Buf()]
            gke2_b = [gke_b, Buf()]
            geb2_b = [geb_b, Buf()]
            ARENA_BUFS.extend([gqk2_b[1], gke2_b[1], geb2_b[1]])

            def gla_block(z, tb):
                gebz, gebz_b = geb2[z], geb2_b[z]
                slot = 0 if tb < 2 else 1 + (tb - 2) // 2
                first = (tb % 2 == 0) if z == 0 else (tb % 2 == 1)
                last = not first
                if first:
                    for p in range(2):
                        zi = z * 2 + p
                        if tb in (0, 1):
                            S.op("dve", lambda e, zi=zi: e.memset(gS[:, zi, :], 0.0), writes=[gS_b])
                        elif (z == 0 and tb == 2) or (z == 1 and tb == 9):
                            S.op("sp", lambda e, zi=zi, p=p, z=z: e.dma_start(out=gS[:, zi, :], in_=gs0_d[l, z, 2 * p:2 * p + 2].rearrange("h d v -> (h d) v")), writes=[gS_b], dma=True)
                        else:
                            S.op("dve", lambda e, zi=zi: e.tensor_scalar(out=gS[:, zi, :], in0=gS[:, zi, :], scalar1=hyflag[0:64, 0:1], scalar2=None, op0=ALU.mult), reads=[gS_b, hysc_b], writes=[gS_b])
                        S.op("act", lambda e, zi=zi: e.copy(out=gSb[:, zi, :], in_=gS[:, zi, :]), reads=[gS_b], writes=[gS_b])
                yield
                pl, plb = next_ps(4)
                S.op("pe", lambda e, pl=pl, tb=tb, z=z: e.matmul(pl[:, 0:128], lhsT=ggdT[:, z, tb * 128:(tb + 1) * 128], rhs=ggw[:, z * 128:(z + 1) * 128], start=True, stop=False),
                     reads=[ggdT_b, gconst_b], writes=[plb])
                S.op("pe", lambda e, pl=pl, z=z: e.matmul(pl[:, 0:128], lhsT=ones_bf[0:1, 0:128], rhs=ggb[:, z * 128:(z + 1) * 128], start=False, stop=True),
                     reads=[ones_b, gconst_b], writes=[plb])
                yield
                gp, gpb = tmp[z], tmp_b[z]
                S.op("act", lambda e, pl=pl, gp=gp: e.activation(out=gp[:, 0:128], in_=pl[:, 0:128], func=AF.Exp, scale=-1.0), reads=[plb], writes=[gpb])
                S.op("act", lambda e, gp=gp: e.activation(out=gp[:, 0:128], in_=gp[:, 0:128], func=AF.Ln, bias=1.0), reads=[gpb], writes=[gpb])
                yield
                for p in range(2):
                    pc, pcb = next_ps(4)
                    S.op("pe", lambda e, pc=pc, gp=gp, p=p, z=z: e.matmul(pc[0:64, 0:128], lhsT=gp[:, p * 64:(p + 1) * 64], rhs=gtri[:, z, :], start=True, stop=True),
                         reads=[gpb, gconst_b], writes=[pcb])
                    S.op("act", lambda e, pc=pc, p=p: e.activation(out=gebz[:, 2 * p, :], in_=pc[0:64, 0:128], func=AF.Exp, scale=-1.0 / 16), reads=[pcb], writes=[gebz_b])
                    S.op("act", lambda e, pc=pc, p=p: e.activation(out=gebz[:, 2 * p + 1, :], in_=pc[0:64, 0:128], func=AF.Exp, scale=1.0 / 16), reads=[pcb], writes=[gebz_b])
                yield
                qk, qkb = gqk2[z], gqk2_b[z]
                for p in range(2):
                    S.op("dve", lambda e, qk=qk, p=p, tb=tb: e.scalar_tensor_tensor(out=qk[0:64, p * 128:(p + 1) * 128], in0=gqT[:, p, tb * 128:(tb + 1) * 128], scalar=SCALE, in1=gebz[:, 2 * p, :],
                                                                                   op0=ALU.mult, op1=ALU.mult), reads=[gqT_b, gebz_b], writes=[qkb])
                    S.op("dve", lambda e, qk=qk, p=p, tb=tb: e.tensor_tensor(out=qk[0:64, 256 + p * 128:256 + (p + 1) * 128], in0=gkT[:, p, tb * 128:(tb + 1) * 128], in1=gebz[:, 2 * p + 1, :], op=ALU.mult),
                         reads=[gkT_b, gebz_b], writes=[qkb])
                yield
                pf, pfb = next_ps(4)
                S.op("pe", lambda e, pf=pf, gp=gp, z=z: e.matmul(pf[:, 0:128], lhsT=gtri[:, 2 + z, :], rhs=gp[:, 0:128], start=True, stop=True), reads=[gpb, gconst_b], writes=[pfb])
                S.op("act", lambda e, pf=pf, gp=gp: e.activation(out=gp[:, 128:256], in_=pf[:, 0:128], func=AF.Exp, scale=-1.0 / 16), reads=[pfb], writes=[gpb])
                yield
                ke, keb = gke2[z], gke2_b[z]
                S.op("dve", lambda e, ke=ke, gp=gp, tb=tb: e.tensor_tensor(out=ke[:, 0:128], in0=gktok[:, tb, :], in1=gp[:, 128:256], op=ALU.mult), reads=[gktok_b, gpb], writes=[keb])
                yield
                pcs = [(ps[6 - 2 * z], ps_b[6 - 2 * z]), (ps[7 - 2 * z], ps_b[7 - 2 * z])]
                for h in range(4):
                    p, sidx = h // 2, h % 2
                    zi = z * 2 + p
                    pa, pab = next_ps(4)
                    S.op("pe", lambda e, pa=pa, qk=qk, p=p, sidx=sidx: e.matmul(pa[:, 0:128], lhsT=qk[32 * sidx:32 * sidx + 32, 256 + p * 128:256 + (p + 1) * 128],
                                                                               rhs=qk[32 * sidx:32 * sidx + 32, p * 128:(p + 1) * 128], start=True, stop=True),
                         reads=[qkb], writes=[pab])
                    yield
                    am, amb = next_e()
                    S.op("dve", lambda e, pa=pa, am=am, z=z: e.tensor_tensor(out=am[:, 0:128], in0=pa[:, 0:128], in1=gtri[:, z, :], op=ALU.mult), reads=[pab, gconst_b], writes=[amb])
                    yield
                    po, pob = next_ps(4)
                    S.op("pe", lambda e, po=po, am=am, h=h, tb=tb: e.matmul(po[0:64, 0:128], lhsT=gvtok[:, tb, h * 64:(h + 1) * 64], rhs=am[:, 0:128], start=True, stop=False),
                         reads=[gvtok_b, amb], writes=[pob])
                    S.op("pe", lambda e, po=po, qk=qk, zi=zi, p=p, sidx=sidx: e.matmul(po[0:64, 0:128], lhsT=gSb[32 * sidx:32 * sidx + 32, zi, :], rhs=qk[32 * sidx:32 * sidx + 32, p * 128:(p + 1) * 128],
                                                                                      start=False, stop=True),
                         reads=[gS_b, qkb], writes=[pob])
                    yield
                    if (z == 0 and tb <= 4) or (z == 1 and tb >= 5):
                        S.op("act", lambda e, po=po, h=h, tb=tb: e.copy(out=godT[:, h, tb * 128:(tb + 1) * 128], in_=po[0:64, 0:128]), reads=[pob], writes=[godT_b])
                    else:
                        S.op("dve", lambda e, po=po, h=h, tb=tb: e.tensor_tensor(out=godT[:, h, tb * 128:(tb + 1) * 128], in0=godT[:, h, tb * 128:(tb + 1) * 128], in1=po[0:64, 0:128], op=ALU.add),
                             reads=[pob, godT_b], writes=[godT_b])
                    if sidx == 1:
                        pcx, pcxb = pcs[p]
                        S.op("pe", lambda e, pcx=pcx, ke=ke, p=p, tb=tb: e.matmul(pcx[0:64, 0:128], lhsT=ke[:, p * 64:(p + 1) * 64], rhs=gvtok[:, tb, p * 128:(p + 1) * 128], start=True, stop=True),
                             reads=[keb, gvtok_b], writes=[pcxb])
                yield
                for p in range(2):
                    zi = z * 2 + p
                    pcx, pcxb = pcs[p]
                    dcol = 127 if z == 0 else 0
                    for sx in range(2):
                        S.op("dve", lambda e, pcx=pcx, zi=zi, p=p, dcol=dcol, sx=sx: e.scalar_tensor_tensor(out=gS[32 * sx:32 * sx + 32, zi, :], in0=gS[32 * sx:32 * sx + 32, zi, :],
                                                                                                           scalar=gebz[32 * sx:32 * sx + 32, 2 * p, dcol:dcol + 1], in1=pcx[32 * sx:32 * sx + 32, 64 * sx:64 * sx + 64],
                                                                                                           op0=ALU.mult, op1=ALU.add), reads=[gS_b, gebz_b, pcxb], writes=[gS_b])
                    S.op("act", lambda e, zi=zi: e.copy(out=gSb[:, zi, :], in_=gS[:, zi, :]), reads=[gS_b], writes=[gS_b])
                    if last:
                        S.op("sp", lambda e, zi=zi, p=p, z=z, slot=slot: e.dma_start(out=gout_d[l, z, slot, 2 * p:2 * p + 2].rearrange("h d v -> (h d) v"), in_=gS[:, zi, :]),
                             reads=[gS_b], dma=True, is_out=True)

            for step in range(NTB):
                gens = [gla_block(0, step), gla_block(1, NTB - 1 - step)]
                while gens:
                    for g_ in list(gens):
                        try:
                            next(g_)
                        except StopIteration:
                            gens.remove(g_)
            if l == 0:
                dump(0, arena[0:64, 0:1280], [gqT_b], np_=64)
                dump(1, arena[:, 5120:6400], [gktok_b])
                dump(2, a2[0:64, 0:1280], [godT_b], np_=64)
                dump(3, a2[0:64, 3840:5120], [godT_b], np_=64)
                dump(6, arena[:, 6400:7680], [gvtok_b])
                dump(4, arena[0:64, 11264:12288].bitcast(F32), [geb_b], n=512, np_=64)
                dump(5, arena[0:64, 10496:11008].bitcast(F32), [gS_b], n=256, np_=64)
            wo, wob = ring[1], ring_b[1]
            S.op("pool", lambda e: e.dma_start(out=wo[0:64, 0:4096].rearrange("p (h n) -> p h n", h=4), in_=wout_d[l, 768:1024, :].rearrange("(h p) n -> p h n", p=64)), writes=[wob], dma=True)
            if GLA_PART < 4:
                return
            for ti, (t0, tl, c) in enumerate(TT):
                for h in range(4):
                    et, etb = next_e()
                    S.op("act", lambda e, et=et, h=h, t0=t0, tl=tl: e.activation(out=et[0:64, 0:tl], in_=godT[:, h, t0:t0 + tl], func=AF.Square), reads=[godT_b], writes=[etb])
                    pt, pb = next_ps(6)
                    S.op("pe", lambda e, pt=pt, et=et, tl=tl: e.matmul(pt[0:64, 0:tl], lhsT=ones_bf[0:64, 0:64], rhs=et[0:64, 0:tl], start=True, stop=True), reads=[ones_b, etb], writes=[pb])
                    S.op("act", lambda e, pt=pt, tl=tl: e.activation(out=rden[:, 0:tl], in_=pt[0:64, 0:tl], func=AF.Sqrt, scale=1.0 / 64, bias=EPS), reads=[pb], writes=[rden_b])
                    S.op("dve", lambda e, tl=tl: e.reciprocal(out=rden[:, 0:tl], in_=rden[:, 0:tl]), reads=[rden_b], writes=[rden_b])
                    S.op("dve", lambda e, h=h, t0=t0, tl=tl: e.scalar_tensor_tensor(out=godT[:, h, t0:t0 + tl], in0=godT[:, h, t0:t0 + tl], scalar=mixgain64(l, 12 + h), in1=rden[:, 0:tl], op0=ALU.mult, op1=ALU.mult),
                         reads=[godT_b, rden_b, vec64_b], writes=[godT_b])
                    pr, prb = next_ps(6)
                    for k in range(8):
                        S.op("pe", lambda e, pr=pr, k=k, h=h, t0=t0, tl=tl: e.matmul(pr[0:64, 0:tl], lhsT=wcol(k, 544 + h * 64, 64), rhs=u[:, k, t0:t0 + tl], start=(k == 0), stop=(k == 7)),
                             reads=[wslb, u_b[ti]], writes=[prb])
                    tm, tmb = next_tmp()
                    S.op("act", lambda e, pr=pr, tm=tm, tl=tl: e.activation(out=tm[0:64, 0:tl], in_=pr[0:64, 0:tl], func=AF.Silu), reads=[prb], writes=[tmb])
                    S.op("dve", lambda e, tm=tm, h=h, t0=t0, tl=tl: e.tensor_tensor(out=godT[:, h, t0:t0 + tl], in0=godT[:, h, t0:t0 + tl], in1=tm[0:64, 0:tl], op=ALU.mult), reads=[godT_b, tmb], writes=[godT_b])
            _ri[0] = 2
            wout_partial(l, 4, 64, wo, wob, lambda pi, t0, tl: (godT[:, pi, t0:t0 + tl], [godT_b]))

        for l in range(NLAYERS):
            if STAGES["ffn1"]:
                S.phase = "L%d_ffn1" % l
                norm_mod(l, 0)
                ffn(l, 0)
            if l == 0:
                mod_hook(9)
            if STAGES["mixer"]:
                S.phase = "L%d_norm2" % l
                norm_mod(l, 1)
                if STAGES.get("A", True):
                    S.phase = "L%d_attnA" % l
                    attention_group(l, 0)
                if STAGES.get("C", True):
                    S.phase = "L%d_attnC" % l
                    attention_group(l, 1)
                if STAGES.get("B", True):
                    S.phase = "L%d_hyena" % l
                    hyena_group(l)
                if STAGES.get("D", True):
                    S.phase = "L%d_gla" % l
                    gla_group(l)
            if STAGES["ffn2"]:
                S.phase = "L%d_ffn2" % l
                norm_mod(l, 2)
                ffn(l, 1)
        S.phase = "final"

        for ti, (t0, tl, c) in enumerate(TT):
            S.op("act", lambda e, t0=t0, tl=tl: e.activation(out=u[:, :, t0:t0 + tl], in_=xres[:, :, t0:t0 + tl], func=AF.Square), reads=[xres_b[ti]], writes=[u_b[ti]])
            pt, pb = next_ps()
            for k in range(8):
                S.op("pe", lambda e, pt=pt, k=k, t0=t0, tl=tl: e.matmul(pt[:, 0:tl], lhsT=ones_bf[:], rhs=u[:, k, t0:t0 + tl], start=(k == 0), stop=(k == 7)),
                     reads=[ones_b, u_b[ti]], writes=[pb])
            S.op("act", lambda e, pt=pt, t0=t0, tl=tl: e.activation(out=rstd[:, t0:t0 + tl], in_=pt[:, 0:tl], func=AF.Sqrt, scale=1.0 / D, bias=EPS), reads=[pb], writes=[rstd_b])
            S.op("dve", lambda e, t0=t0, tl=tl: e.reciprocal(out=rstd[:, t0:t0 + tl], in_=rstd[:, t0:t0 + tl]), reads=[rstd_b], writes=[rstd_b])
            for k in range(8):
                S.op("dve", lambda e, k=k, t0=t0, tl=tl: e.scalar_tensor_tensor(out=xres[:, k, t0:t0 + tl], in0=xres[:, k, t0:t0 + tl], scalar=finalgT[:, k:k + 1],
                                                                               in1=rstd[:, t0:t0 + tl], op0=ALU.mult, op1=ALU.mult),
                     reads=[xres_b[ti], rstd_b, vecs_b[1]], writes=[xres_b[ti]])
        for tb in range(NTB):
            sg, sgb = stg[tb % 2], stg_b[tb % 2]
            tti = 0 if tb < 2 else (1 if tb < 6 else 2)
            for half in range(2):
                pt, pb = next_ps()
                for q in range(4):
                    k = half * 4 + q
                    S.op("pe", lambda e, pt=pt, k=k, q=q, tb=tb: e.transpose(pt[:, q * 128:(q + 1) * 128], xres[:, k, tb * 128:(tb + 1) * 128], ident[:]),
                         reads=[xres_b[tti], ident_b], writes=[pb])
                if half == 0:
                    S.op("act", lambda e, pt=pt, sg=sg, half=half: e.copy(out=sg[:, half * 512:(half + 1) * 512], in_=pt[:, :]), reads=[pb], writes=[sgb])
                else:
                    S.op("dve", lambda e, pt=pt, sg=sg, half=half: e.tensor_copy(out=sg[:, half * 512:(half + 1) * 512], in_=pt[:, :]), reads=[pb], writes=[sgb])
            S.op("sp", lambda e, sg=sg, tb=tb: e.dma_start(out=y_d[tb * 128:(tb + 1) * 128, :], in_=sg[:]), reads=[sgb], dma=True, is_out=True)

        S.emit(st)
    return nc


N_CORES = 8


def core_tokens(c):
    if c < 2:
        return [30 + c], c
    base = 5 * (c - 2)
    return [base + i for i in range(5)], None


def rope_tables():
    rows = 1024 // 64
    r = np.repeat(np.arange(rows, dtype=np.float32), 64)
    col = np.tile(np.arange(64, dtype=np.float32), rows)
    nf = 16
    inv = (10000.0 ** (-np.arange(nf, dtype=np.float32) / nf)).astype(np.float32)
    ang = np.concatenate([r[:, None] * inv, col[:, None] * inv], axis=-1).astype(np.float32)
    return np.cos(ang).astype(np.float32), np.sin(ang).astype(np.float32)


def attn_masks(is_sample):
    m = np.zeros((4, 128, 1920), np.float32)
    a = np.arange(128)[:, None]
    x = np.arange(1920)[None, :]
    if is_sample:
        band = (np.abs(x - 896 - a) <= 128).astype(np.float32)
        m[0] = band
        m[1] = band
        m[2] = 1.0
        m[3] = 1.0
    else:
        ev = ((x >= 896) & (x < 1152)).astype(np.float32) * np.ones((128, 1), np.float32)
        od = ((x >= 768) & (x < 1024)).astype(np.float32) * np.ones((128, 1), np.float32)
        m[0] = ev
        m[1] = od
        m[2] = ev
        m[3] = od
    return m.astype(NPBF16)


def hy_tables(L):
    t = np.linspace(0.0, 1.0, L, dtype=np.float32)[:, None]
    w = ((2.0 * math.pi / L) * np.arange(L, dtype=np.float32)[:, None]).astype(np.float32)
    bands = np.linspace(1e-4, 15, 16, dtype=np.float32)[None, :]
    z = np.concatenate([t, np.cos(bands * w), -np.sin(bands * w)], axis=-1).astype(np.float32)
    deltas = np.linspace(math.log(1e-2) / 1.5, math.log(1e-2) / 0.3, 256, dtype=np.float32)
    dec = np.exp(-t * np.abs(deltas)).astype(np.float32)
    dec2 = np.stack([dec, dec], 1)
    dec2[0, 1] = 0.0
    r = np.arange(2 * L)
    f = 256 * (r // 512) + (r % 256)
    is_im = (r % 512) >= 256
    th = np.pi * (f[None, :] + 0.5) * np.arange(L)[:, None].astype(np.float64) / L
    FW = np.where(is_im[None, :], -np.sin(th), np.cos(th))
    IV = FW.T / L
    return z, dec2, FW.astype(np.float32), IV.astype(np.float32)


def blockdiag4(m):
    a, b = m.shape
    o = np.zeros((4 * a, 4 * b), m.dtype)
    for i in range(4):
        o[i * a:(i + 1) * a, i * b:(i + 1) * b] = m
    return o


_NC_CACHE = {}
SHARED_KEYS = ["w_mod", "b_mod", "norm_g", "ffn_w_in", "ffn_w_out", "final_g", "w_in", "w_out", "mix_g", "swa_sink", "qk_norm_g",
               "hy_conv_w", "hy_conv_b", "hy_w1", "hy_b1", "hy_w2", "hy_b2", "hy_w3", "hy_freq", "hy_bias", "gla_gate_w", "gla_gate_b"]


def kernel(**inp):
    f32 = np.float32
    x_prompt = np.asarray(inp["x_prompt"], f32)
    x_sample = np.asarray(inp["x_sample"], f32)
    c = np.asarray(inp["c"], f32)
    c_ctx = np.asarray(inp["c_ctx"], f32)
    if "nc" not in _NC_CACHE:
        _NC_CACHE["nc"] = build_program()
    nc = _NC_CACHE["nc"]
    shared = {k: np.ascontiguousarray(inp[k], f32) for k in SHARED_KEYS}
    shared["ident_in"] = np.eye(128, dtype=f32)
    cos_s, sin_s = rope_tables()
    caches = [np.asarray(inp[k], f32) for k in ("cache_swa_k", "cache_swa_v", "cache_gqa_k", "cache_gqa_v")]
    z_p, dec_p, fw_p, iv_p = hy_tables(256)
    z_s, dec_s, fw_s, iv_s = hy_tables(1024)
    shared["hy_fw_p"] = fw_p.astype(NPBF16)
    r_ = np.arange(128)[:, None]
    c_ = np.arange(128)[None, :]
    shared["tri_in"] = np.stack([r_ <= c_, r_ >= c_, r_ > c_, r_ < c_], 0).astype(f32)
    state_gla = np.asarray(inp["state_gla"], f32)
    shared["hy_iv_p"] = iv_p.astype(NPBF16)
    hy_prompt = dict(hy_zT=np.ascontiguousarray(np.concatenate([z_p] * 5, 0).T), hy_dec=np.ascontiguousarray(np.concatenate([dec_p] * 5, 0)),
                     hy_fw_g=blockdiag4(fw_p).astype(NPBF16), hy_iv_g=blockdiag4(iv_p).astype(NPBF16), hy_flag=np.zeros((128, 1), f32))
    hy_sample = dict(hy_zT=np.ascontiguousarray(np.concatenate([z_p, z_s], 0).T), hy_dec=np.ascontiguousarray(np.concatenate([dec_p, dec_s], 0)),
                     hy_fw_g=fw_s.astype(NPBF16), hy_iv_g=iv_s.astype(NPBF16), hy_flag=np.ones((128, 1), f32))
    in_maps = []
    for core in range(N_CORES):
        pids, sid = core_tokens(core)
        m = dict(shared)
        cos = np.ones((T, 32), f32)
        sin = np.zeros((T, 32), f32)
        if sid is None:
            xs = np.concatenate([x_prompt[p] for p in pids], 0)
            cond = np.stack([c_ctx, c_ctx], 0)
            m["ctx_kv"] = np.zeros((4, 2, 512, 128), f32)
            m["ctx_bias"] = np.full((128, 1), -30000.0, f32)
            m["gla_s0"] = np.zeros((2, 2, 4, 32, 64), f32)
        else:
            xs = np.concatenate([x_prompt[pids[0]], x_sample[sid]], 0)
            cond = np.stack([c_ctx, c[sid]], 0)
            cos[256:] = cos_s
            sin[256:] = sin_s
            m["ctx_kv"] = np.ascontiguousarray(np.stack([cc[sid].reshape(2, 512, 128) for cc in caches], 0), f32)
            m["ctx_bias"] = np.zeros((128, 1), f32)
            m["gla_s0"] = np.ascontiguousarray(state_gla[sid], f32)
        m["attn_mask"] = attn_masks(sid is not None)
        m.update(hy_sample if sid is not None else hy_prompt)
        m["rope_cos"] = cos
        m["rope_sin"] = sin
        m["x_in"] = np.ascontiguousarray(xs, f32)
        m["cond_in"] = np.ascontiguousarray(cond, f32)
        in_maps.append(m)
    if ONE_CORE:
        res = run_bass_kernel_spmd(nc, in_maps[2:3], core_ids=[0])
        LAST["outs"] = res.results
        return None
    res = run_bass_kernel_spmd(nc, in_maps, core_ids=list(range(N_CORES)))
    outs = res.results
    LAST["outs"] = outs
    B, SEQ = x_prompt.shape[0], x_prompt.shape[1]
    y_prompt = np.zeros((B, SEQ, D), f32)
    y_sample = np.zeros(x_sample.shape, f32)
    kvs = [np.zeros((B, 2, SEQ, 2, 64), f32) for _ in range(4)]
    new_state = np.zeros((B, 2, 2, 4, 32, 64), f32)
    for core in range(N_CORES):
        pids, sid = core_tokens(core)
        y = np.asarray(outs[core]["y"], f32)
        kvo = np.asarray(outs[core]["kv_out"], f32)
        gso = np.asarray(outs[core]["gla_out"], f32)
        if sid is not None:
            y_sample[sid] = y[256:]
        for i, p in enumerate(pids):
            y_prompt[p] = y[i * 256:(i + 1) * 256]
            for a in range(4):
                kvs[a][p] = kvo[a, :, i * 256:(i + 1) * 256, :].reshape(2, SEQ, 2, 64)
            new_state[p] = gso[:, :, i]
    return (y_prompt, y_sample, kvs[0], kvs[1], kvs[2], kvs[3], new_state)
```

```python
import math
from contextlib import ExitStack
import numpy as np
import ml_dtypes
import concourse.bass as bass
import concourse.mybir as mybir
from concourse.bass_utils import run_bass_kernel_spmd

F32 = mybir.dt.float32
BF16 = mybir.dt.bfloat16
AF = mybir.ActivationFunctionType
ALU = mybir.AluOpType
AX = mybir.AxisListType
NPBF16 = ml_dtypes.bfloat16

STAGES = {"ffn1": True, "mixer": True, "ffn2": True, "A": True, "C": True, "B": True, "D": True}
NLAYERS = 2
DEBUG = False
PROFILE_SCOPES = False
PROFILE_ENGINE = "pe"
GLA_STEPS = 99
GLA_PART = 9
GLA_VAR = 0
ONE_CORE = False
LAST = {}

ENGS = ("pe", "act", "dve", "pool", "sp")
DMA_NSEM = {"sp": 12, "act": 4, "pool": 12}


class Buf:
    __slots__ = ("name", "w", "r")

    def __init__(self, name=""):
        self.name = name
        self.w = None
        self.r = []


class Op:
    __slots__ = ("eng", "idx", "fn", "deps", "signal", "dma", "dsem", "dval", "cnt", "phase")

    def __init__(self, eng, idx, fn, dma):
        self.eng = eng
        self.idx = idx
        self.fn = fn
        self.deps = []
        self.signal = False
        self.dma = dma
        self.dsem = None
        self.dval = 0
        self.cnt = 0


class Sched:
    def __init__(self, nc, same_engine_sync=True):
        self.nc = nc
        self.ops = {e: [] for e in ENGS}
        self.ndma = {e: 0 for e in DMA_NSEM}
        self.dma_ops = {e: [] for e in DMA_NSEM}
        self.same = same_engine_sync
        self.out_dmas = []
        self.phase = None

    def op(self, eng, fn, reads=(), writes=(), dma=False, is_out=False):
        lst = self.ops[eng]
        o = Op(eng, len(lst), fn, dma)
        o.phase = self.phase
        deps = {}
        for b in reads:
            if b.w is not None:
                deps[id(b.w)] = b.w
        for b in writes:
            if b.w is not None:
                deps[id(b.w)] = b.w
            for r in b.r:
                deps[id(r)] = r
        if dma:
            j = self.ndma[eng]
            k = DMA_NSEM[eng]
            o.dsem = (eng, j % k)
            o.dval = 16 * (j // k + 1)
            if j >= k:
                p = self.dma_ops[eng][j - k]
                deps[id(p)] = p
            self.ndma[eng] += 1
            self.dma_ops[eng].append(o)
            if is_out:
                self.out_dmas.append(o)
        best = {}
        for d in deps.values():
            if d is o:
                continue
            if d.dma:
                o.deps.append(d)
                continue
            if d.eng == eng and (eng in ("pe", "sp") or not self.same):
                continue
            if d.eng not in best or best[d.eng].idx < d.idx:
                best[d.eng] = d
        for d in best.values():
            o.deps.append(d)
            d.signal = True
        for b in reads:
            if not dma:
                b.r = [r for r in b.r if r.dma or r.eng != eng]
            b.r.append(o)
        for b in writes:
            b.w = o
            b.r = []
        lst.append(o)
        return o

    def emit(self, stack):
        nc = self.nc
        CH = 2000
        fin = Op("sp", len(self.ops["sp"]), None, False)
        fin.deps = list(self.out_dmas)
        fin.phase = None
        self.ops["sp"].append(fin)
        for e in ENGS:
            c = 0
            for o in self.ops[e]:
                if o.signal:
                    c += 1
                o.cnt = c
        esem = {}
        for e in ENGS:
            n = (self.ops[e][-1].cnt if self.ops[e] else 0)
            for i in range(max(1, (n + CH - 1) // CH)):
                esem[(e, i)] = stack.enter_context(nc.semaphore("es_%s%d" % (e, i)))
        dsem = {}
        for e, k in DMA_NSEM.items():
            for i in range(k):
                dsem[(e, i)] = stack.enter_context(nc.semaphore("ds_%s%d" % (e, i)))
        block = stack.enter_context(nc.Block())

        def run(e, engine):
            known = {}
            kn_eng = {}
            cur = [None, None]

            def set_phase(ph):
                if not PROFILE_SCOPES or ph == cur[0] or e != PROFILE_ENGINE:
                    return
                if cur[1] is not None:
                    cur[1].__exit__(None, None, None)
                    cur[1] = None
                cur[0] = ph
                if ph is not None:
                    cur[1] = nc.named_scope(ph)
                    cur[1].__enter__()

            for o in self.ops[e] + [None]:
                if o is None:
                    set_phase(None)
                    break
                set_phase(o.phase)
                need = {}
                for d in o.deps:
                    if d.dma:
                        key, val = ("d",) + d.dsem, d.dval
                        if known.get(key, 0) >= val:
                            continue
                    else:
                        if kn_eng.get(d.eng, 0) >= d.cnt:
                            continue
                        key, val = ("e", d.eng, (d.cnt - 1) // CH), (d.cnt - 1) % CH + 1
                        kn_eng[d.eng] = d.cnt
                    if need.get(key, 0) < val:
                        need[key] = val
                for key, val in need.items():
                    s = dsem[key[1:]] if key[0] == "d" else esem[key[1:]]
                    engine.wait_ge(s, val)
                    if key[0] == "d":
                        known[key] = val
                if o.fn is None:
                    continue
                ins = o.fn(engine)
                if o.dma:
                    ins.then_inc(dsem[o.dsem], 16)
                elif o.signal:
                    ins.then_inc(esem[(e, (o.cnt - 1) // CH)], 1)

        @block.tensor
        def _(eng):
            run("pe", eng)

        @block.scalar
        def _(eng):
            run("act", eng)

        @block.vector
        def _(eng):
            run("dve", eng)

        @block.gpsimd
        def _(eng):
            run("pool", eng)

        @block.sync
        def _(eng):
            run("sp", eng)


D = 1024
T = 1280
TT = [(0, 256, 0), (256, 512, 1), (768, 512, 1)]
NTB = T // 128
DFF = 2816
NHC = DFF // 128
EPS = 1e-6
RING_SLOTS = 3
SLOT_ELEMS = 8192


class Prog:
    pass


def build_program():
    nc = bass.Bass("TRN2", target_bir_lowering=False)
    P = Prog()
    st = ExitStack()
    with st:
        S = Sched(nc)

        def din(name, shape, dt=F32):
            return nc.dram_tensor(name, list(shape), dt, kind="ExternalInput").ap()

        def dout(name, shape, dt=F32):
            return nc.dram_tensor(name, list(shape), dt, kind="ExternalOutput").ap()

        _n = [0]

        def sb(shape, dt, name=None):
            _n[0] += 1
            return st.enter_context(nc.sbuf_tensor(name or ("t%d" % _n[0]), list(shape), dt))

        x_d = din("x_in", [T, D])
        cond_d = din("cond_in", [2, D])
        wmod_d = din("w_mod", [2, D, 9 * D])
        bmod_d = din("b_mod", [2, 9 * D])
        normg_d = din("norm_g", [2, 3, D])
        fwin_d = din("ffn_w_in", [2, 2, D, 2 * DFF])
        fwout_d = din("ffn_w_out", [2, 2, DFF, D])
        finalg_d = din("final_g", [D])
        ident_d = din("ident_in", [128, 128])
        y_d = dout("y", [T, D])
        win_d = din("w_in", [2, D, 2592])
        wout_d = din("w_out", [2, D, D])
        mixg_d = din("mix_g", [2, D])
        sink_d = din("swa_sink", [2, 4])
        qkg_d = din("qk_norm_g", [2, 2, 64])
        cos_d = din("rope_cos", [T, 32])
        sin_d = din("rope_sin", [T, 32])
        ctx_d = din("ctx_kv", [4, 2, 512, 128])
        ctxbias_d = din("ctx_bias", [128, 1])
        amask_d = din("attn_mask", [4, 128, 1920], BF16)
        kv_d = dout("kv_out", [4, 2, T, 128])
        hcw_d = din("hy_conv_w", [2, 3, 768])
        hcb_d = din("hy_conv_b", [2, 768])
        hw1_d = din("hy_w1", [2, 33, 64])
        hb1_d = din("hy_b1", [2, 64])
        hw2_d = din("hy_w2", [2, 64, 64])
        hb2_d = din("hy_b2", [2, 64])
        hw3_d = din("hy_w3", [2, 64, 1024])
        hfr_d = din("hy_freq", [2, 2, 64])
        hbias_d = din("hy_bias", [2, 2, 256])
        hz_d = din("hy_zT", [33, T])
        hdec_d = din("hy_dec", [T, 2, 256])
        hfwg_d = din("hy_fw_g", [1024, 2048], BF16)
        hivg_d = din("hy_iv_g", [2048, 1024], BF16)
        hfwp_d = din("hy_fw_p", [256, 512], BF16)
        hivp_d = din("hy_iv_p", [512, 256], BF16)
        hflag_d = din("hy_flag", [128, 1])
        gw_d = din("gla_gate_w", [2, 2, 16, 128])
        gb_d = din("gla_gate_b", [2, 2, 128])
        gs0_d = din("gla_s0", [2, 2, 4, 32, 64])
        tri_d = din("tri_in", [4, 128, 128])
        gout_d = dout("gla_out", [2, 2, 5, 4, 32, 64])

        xres = sb([128, 8, T], F32, "xres")
        u = sb([128, 8, T], BF16, "u")
        hid = sb([128, 12, T], BF16, "hid")
        ring = [sb([128, SLOT_ELEMS], BF16, "ring%d" % i) for i in range(RING_SLOTS)]
        ring_b = [Buf("ring%d" % i) for i in range(RING_SLOTS)]
        stg = [sb([128, D], F32, "stg%d" % i) for i in range(2)]
        stg_b = [Buf() for _ in range(2)]
        tmp = [sb([128, 512], F32, "tmp%d" % i) for i in range(3)]
        tmp_b = [Buf() for _ in range(3)]
        rstd = sb([128, T], F32, "rstd")
        rstd_b = Buf()
        ident = sb([128, 128], F32, "ident")
        ident_b = Buf()
        ones_bf = sb([128, 128], BF16, "ones")
        ones_b = Buf()
        vecs_in = [sb([128, 128], F32, "vin%d" % i) for i in range(2)]
        vecs = [sb([128, 128], F32, "vec%d" % i) for i in range(2)]
        vecs_b = [Buf() for _ in range(2)]
        vin_b = [Buf() for _ in range(2)]
        condT = sb([128, 8, 2], BF16, "condT")
        condT_b = Buf()
        mod = sb([128, 2, 72, 2], F32, "mod")
        mod_b = Buf()
        modA = sb([128, 2, 3, 8, 2], F32, "modA")
        modG = sb([128, 2, 3, 8, 2], F32, "modG")
        modA_b = Buf()
        xres_b = [Buf("xres%d" % i) for i in range(3)]
        u_b = [Buf("u%d" % i) for i in range(3)]
        hid_b = [[Buf() for _ in range(3)] for _ in range(12)]

        arena = hid[:].rearrange("p a b -> p (a b)")
        qT = arena[0:64, 0:5120].rearrange("p (h t) -> p h t", h=4)
        kT = arena[0:64, 5120:7680].rearrange("p (h t) -> p h t", h=2)
        vtok = arena[:, 7680:8980].rearrange("p (b g f) -> p b g f", b=NTB, g=2)
        oT = arena[0:64, 8980:14100].rearrange("p (h t) -> p h t", h=4)
        qT_b, kT_b, vtok_b, oT_b = Buf("qT"), Buf("kT"), Buf("vtok"), Buf("oT")
        amask = sb([128, 4, 1920], BF16, "amask")
        amask_b = Buf()
        ropec = sb([128, NTB, 32], F32, "ropec")
        ropes = sb([128, NTB, 32], F32, "ropes")
        rope_b = Buf()
        gq = sb([128, 2, 6, 64], F32, "gq")
        gq_b = Buf()
        ctxbias = sb([128, 1], F32, "ctxbias")
        esink = sb([64, 8], F32, "esink")
        misc_b = Buf()
        ctxkT = sb([64, 2, 512], BF16, "ctxkT")
        ctxv = sb([128, 4, 2, 65], BF16, "ctxv")
        esink128 = sb([128, 8], F32, "esink128")
        ones_f = sb([128, 64], F32, "ones_f")
        rrow = sb([128, 512], F32, "rrow")
        rrow_b = Buf()
        ctxkT_b, ctxv_b = Buf(), Buf()
        kvst = [sb([128, 2, 128], F32, "kvst%d" % i) for i in range(2)]
        kvst_b = [Buf() for _ in range(2)]
        qkn = [sb([128, 384], F32, "qkn%d" % i) for i in range(2)]
        qkn_b = [Buf() for _ in range(2)]
        qkr = [sb([128, 384], F32, "qkr%d" % i) for i in range(2)]
        qkr_b = [Buf() for _ in range(2)]
        rtmp2 = [sb([128, 192], F32, "rtmp%d" % i) for i in range(2)]
        rtmp2_b = [Buf() for _ in range(2)]
        ssq2 = [sb([128, 8], F32, "ssq%d" % i) for i in range(2)]
        ssq2_b = [Buf() for _ in range(2)]
        etile = [sb([128, 512], BF16, "etile%d" % i) for i in range(4)]
        etile_b = [Buf() for _ in range(4)]
        rden = sb([64, 512], F32, "rden")
        rden_b = Buf()
        vin64 = sb([128, 64], F32, "vin64")
        vec64 = sb([64, 128], F32, "vec64")
        vec64_b = Buf()

        hyF = arena[:, 0:2560].bitcast(F32)
        hyx1 = arena[:, 2560:3840]
        hyx2 = arena[:, 3840:5120]
        hyvtok = arena[:, 5120:6400].rearrange("p (b c) -> p b c", b=NTB)
        hyhp = arena[:, 6400:7680].rearrange("p (b c) -> p b c", b=NTB)
        hyhm = arena[:, 7680:8960].rearrange("p (b c) -> p b c", b=NTB)
        hyY = arena[:, 8960:11520].rearrange("p (r c) -> p r c", r=20)
        hyob0 = arena[:, 11520:12800]
        hyh2 = arena[0:64, 12800:14080]
        hyF_b, hyx1_b, hyx2_b, hyvtok_b, hyh_b, hyY_b, hyob0_b, hyh2_b = (Buf() for _ in range(8))
        ARENA_BUFS = [qT_b, kT_b, vtok_b, oT_b, hyF_b, hyx1_b, hyx2_b, hyvtok_b, hyh_b, hyY_b, hyob0_b, hyh2_b]
        gqT = arena[0:64, 0:2560].rearrange("p (a t) -> p a t", a=2)
        gkT = arena[0:64, 2560:5120].rearrange("p (a t) -> p a t", a=2)
        gktok = arena[:, 5120:6400].rearrange("p (b c) -> p b c", b=NTB)
        gvtok = arena[:, 6400:8960].rearrange("p (b c) -> p b c", b=NTB)
        gtri = arena[:, 8960:9984].bitcast(F32).rearrange("p (a c) -> p a c", a=4)
        ggw = arena[0:16, 9984:10240]
        ggb = arena[0:1, 10240:10496]
        gS = arena[0:64, 10496:11008].bitcast(F32).rearrange("p (a v) -> p a v", a=4)
        gSb = arena[0:64, 11008:11264].rearrange("p (a v) -> p a v", a=4)
        geb = arena[0:64, 11264:12288].bitcast(F32).rearrange("p (a t) -> p a t", a=4)
        gqk = arena[0:64, 12288:12800]
        gke = arena[:, 12800:12928]
        gqk_b, gke_b = Buf(), Buf()
        a2 = amask[:].rearrange("p a n -> p (a n)")
        godT = a2[0:64, 0:5120].rearrange("p (h t) -> p h t", h=4)
        ggdT = a2[0:16, 5120:7680].rearrange("p (z t) -> p z t", z=2)
        gqT_b, gkT_b, gktok_b, gvtok_b, gconst_b, gS_b, geb_b, godT_b, ggdT_b = (Buf() for _ in range(9))
        fdummy = sb([128, 1], F32, "fdummy")
        vin3 = sb([128, 128], F32, "vin3")
        vec3 = sb([128, 128], F32, "vec3")
        vec3_b = Buf()
        w3b = sb([64, 1024], BF16, "w3b")
        hyw12 = sb([64, 128], F32, "hyw12")
        hyw_b = Buf()
        hysc = sb([128, 2, 6, 4], F32, "hysc")
        hyfb = sb([64, 4], F32, "hyfb")
        hyflag = sb([128, 1], F32, "hyflag")
        hysc_b = Buf()

        ps = [st.enter_context(nc.psum_tensor("ps%d" % i, [128, 512], F32)) for i in range(8)]
        ps_b = [Buf("ps%d" % i) for i in range(8)]
        _pi = [0]

        def next_ps(n=8):
            i = _pi[0] % n
            _pi[0] += 1
            return ps[i], ps_b[i]

        _ri = [0]

        def next_slot():
            i = _ri[0] % RING_SLOTS
            _ri[0] += 1
            return ring[i], ring_b[i]

        _ti = [0]

        def next_tmp():
            i = _ti[0] % 3
            _ti[0] += 1
            return tmp[i], tmp_b[i]

        S.op("sp", lambda e: e.dma_start(out=ident[:], in_=ident_d), writes=[ident_b], dma=True)
        S.op("dve", lambda e: e.memset(ones_bf[:], 1.0), writes=[ones_b])

        S.op("dve", lambda e: e.memset(vecs_in[0][:], 0.0), writes=[vin_b[0]])
        S.op("dve", lambda e: e.memset(vecs_in[1][:], 0.0), writes=[vin_b[1]])
        S.op("sp", lambda e: e.dma_start(out=vecs_in[0][0:72, :], in_=bmod_d[0].rearrange("(c p) -> c p", p=128)), writes=[vin_b[0]], dma=True)
        S.op("sp", lambda e: e.dma_start(out=vecs_in[0][72:120, :], in_=normg_d.rearrange("l i (k p) -> (l i k) p", p=128)), writes=[vin_b[0]], dma=True)
        S.op("sp", lambda e: e.dma_start(out=vecs_in[1][0:72, :], in_=bmod_d[1].rearrange("(c p) -> c p", p=128)), writes=[vin_b[1]], dma=True)
        S.op("sp", lambda e: e.dma_start(out=vecs_in[1][72:88, :], in_=cond_d.rearrange("c (k p) -> (c k) p", p=128)), writes=[vin_b[1]], dma=True)
        S.op("sp", lambda e: e.dma_start(out=vecs_in[1][88:96, :], in_=finalg_d.rearrange("(k p) -> k p", p=128)), writes=[vin_b[1]], dma=True)
        for i in range(2):
            pt, pb = next_ps()
            S.op("pe", lambda e, i=i, pt=pt: e.transpose(pt[:, 0:128], vecs_in[i][:], ident[:]), reads=[vin_b[i], ident_b], writes=[pb])
            S.op("dve", lambda e, i=i, pt=pt: e.tensor_copy(out=vecs[i][:], in_=pt[:, 0:128]), reads=[pb], writes=[vecs_b[i]])

        def bmodT(l):
            return vecs[l][:, 0:72]

        def normgT(l, i):
            o = 72 + (l * 3 + i) * 8
            return vecs[0][:, o:o + 8]

        finalgT = vecs[1][:, 88:96]
        S.op("act", lambda e: e.activation(out=condT[:].rearrange("p k c -> p c k"), in_=vecs[1][:, 72:88].rearrange("p (c k) -> p c k", c=2), func=AF.Silu),
             reads=[vecs_b[1]], writes=[condT_b])

        S.phase = "load_x"
        for tb in range(NTB):
            sg, sgb = stg[tb % 2], stg_b[tb % 2]
            S.op("sp", lambda e, sg=sg, tb=tb: e.dma_start(out=sg[:], in_=x_d[tb * 128:(tb + 1) * 128, :]), writes=[sgb], dma=True)
            tti = 0 if tb < 2 else (1 if tb < 6 else 2)
            for half in range(2):
                pt, pb = next_ps()
                for q in range(4):
                    k = half * 4 + q
                    S.op("pe", lambda e, pt=pt, sg=sg, k=k, q=q: e.transpose(pt[:, q * 128:(q + 1) * 128], sg[:, k * 128:(k + 1) * 128], ident[:]),
                         reads=[sgb, ident_b], writes=[pb])
                eng = "act" if half == 0 else "dve"
                if eng == "act":
                    S.op("act", lambda e, pt=pt, half=half, tb=tb: e.copy(out=xres[:, half * 4:half * 4 + 4, tb * 128:(tb + 1) * 128], in_=pt[:, :].rearrange("p (a b) -> p a b", a=4)),
                         reads=[pb], writes=[xres_b[tti]])
                else:
                    S.op("dve", lambda e, pt=pt, half=half, tb=tb: e.tensor_copy(out=xres[:, half * 4:half * 4 + 4, tb * 128:(tb + 1) * 128], in_=pt[:, :].rearrange("p (a b) -> p a b", a=4)),
                         reads=[pb], writes=[xres_b[tti]])

        S.phase = "modulation"
        def mod_block(l, jb):
            sl, slb = next_slot()
            S.op("pool", lambda e, sl=sl: e.dma_start(out=sl[:, :].rearrange("p (k n) -> p k n", k=8),
                                                      in_=wmod_d[l].rearrange("(k p) n -> p k n", p=128)[:, :, jb * 1024:(jb + 1) * 1024]),
                 writes=[slb], dma=True)
            pt, pb = next_ps(6)
            for cc in range(8):
                for k in range(8):
                    S.op("pe", lambda e, pt=pt, sl=sl, cc=cc, k=k: e.matmul(pt[:, 2 * cc:2 * cc + 2], lhsT=sl[:, k * 1024 + cc * 128:k * 1024 + (cc + 1) * 128],
                                                                             rhs=condT[:, k, :], start=(k == 0), stop=(k == 7)),
                         reads=[slb, condT_b], writes=[pb])
            S.op("dve", lambda e, pt=pt: e.tensor_tensor(out=mod[:, l, jb * 8:(jb + 1) * 8, :], in0=pt[:, 0:16].rearrange("p (a c) -> p a c", c=2),
                                                         in1=bmodT(l)[:, jb * 8:(jb + 1) * 8].unsqueeze(2).broadcast_to([128, 8, 2]), op=ALU.add),
                 reads=[pb, vecs_b[l]], writes=[mod_b])

        def mod_finish(l):
            for i in range(3):
                S.op("dve", lambda e, i=i: e.tensor_scalar(out=modA[:, l, i], in0=mod[:, l, (3 * i + 1) * 8:(3 * i + 2) * 8, :], scalar1=1.0, scalar2=None, op0=ALU.add),
                     reads=[mod_b], writes=[modA_b])
                S.op("dve", lambda e, i=i: e.tensor_tensor(out=modA[:, l, i], in0=modA[:, l, i], in1=normgT(l, i).unsqueeze(2).broadcast_to([128, 8, 2]), op=ALU.mult),
                     reads=[modA_b, vecs_b[0]], writes=[modA_b])
                S.op("dve", lambda e, i=i: e.tensor_scalar(out=modG[:, l, i], in0=mod[:, l, (3 * i + 2) * 8:(3 * i + 3) * 8, :], scalar1=(1.0 if i == 1 else 0.5), scalar2=None, op0=ALU.mult),
                     reads=[mod_b], writes=[modA_b])

        for jb in range(9):
            mod_block(0, jb)
        mod_finish(0)
        MOD_PENDING = [(1, jb) for jb in range(9)] if NLAYERS > 1 else []

        def mod_hook(n=1):
            for _ in range(n):
                if MOD_PENDING:
                    l_, jb_ = MOD_PENDING.pop(0)
                    mod_block(l_, jb_)
                    if not MOD_PENDING:
                        mod_finish(l_)

        def modB(l, i, k, c):
            return mod[:, l, (3 * i) * 8 + k, c:c + 1]

        def norm_mod(l, i):
            for ti, (t0, tl, c) in enumerate(TT):
                S.op("act", lambda e, t0=t0, tl=tl: e.activation(out=u[:, :, t0:t0 + tl], in_=xres[:, :, t0:t0 + tl], func=AF.Square),
                     reads=[xres_b[ti]], writes=[u_b[ti]])
                pt, pb = next_ps()
                for k in range(8):
                    S.op("pe", lambda e, pt=pt, k=k, t0=t0, tl=tl: e.matmul(pt[:, 0:tl], lhsT=ones_bf[:], rhs=u[:, k, t0:t0 + tl], start=(k == 0), stop=(k == 7)),
                         reads=[ones_b, u_b[ti]], writes=[pb])
                S.op("act", lambda e, pt=pt, t0=t0, tl=tl: e.activation(out=rstd[:, t0:t0 + tl], in_=pt[:, 0:tl], func=AF.Sqrt, scale=1.0 / D, bias=EPS),
                     reads=[pb], writes=[rstd_b])
                S.op("dve", lambda e, t0=t0, tl=tl: e.reciprocal(out=rstd[:, t0:t0 + tl], in_=rstd[:, t0:t0 + tl]), reads=[rstd_b], writes=[rstd_b])
                for k in range(8):
                    tm, tmb = next_tmp()
                    S.op("dve", lambda e, tm=tm, k=k, t0=t0, tl=tl, c=c: e.scalar_tensor_tensor(out=tm[:, 0:tl], in0=xres[:, k, t0:t0 + tl], scalar=modA[:, l, i, k, c:c + 1],
                                                                                               in1=rstd[:, t0:t0 + tl], op0=ALU.mult, op1=ALU.mult),
                         reads=[xres_b[ti], rstd_b, modA_b], writes=[tmb])
                    S.op("act", lambda e, tm=tm, k=k, t0=t0, tl=tl, c=c: e.activation(out=u[:, k, t0:t0 + tl], in_=tm[:, 0:tl], func=AF.Identity, bias=modB(l, i, k, c), scale=1.0),
                         reads=[tmb, mod_b], writes=[u_b[ti]])

        def ffn(l, i):
            gi = 0 if i == 0 else 2
            win = fwin_d[l, i].rearrange("(k p) n -> p k n", p=128)
            wout = fwout_d[l, i].rearrange("(j p) n -> p j n", p=128)
            groups = [(0, 4), (4, 4), (8, 4), (12, 4), (16, 4), (20, 2)]
            halves = [groups[0:3], groups[3:6]]
            for hgroups in halves:
                h0 = hgroups[0][0]
                nh = sum(g[1] for g in hgroups)
                for (j0, nj) in hgroups:
                    sl, slb = next_slot()
                    S.op("pool", lambda e, sl=sl, j0=j0, nj=nj: e.dma_start(out=sl[:, 0:8 * nj * 128].rearrange("p (k n) -> p k n", k=8), in_=win[:, :, j0 * 128:(j0 + nj) * 128]),
                         writes=[slb], dma=True)
                    S.op("pool", lambda e, sl=sl, j0=j0, nj=nj: e.dma_start(out=sl[:, 4096:4096 + 8 * nj * 128].rearrange("p (k n) -> p k n", k=8),
                                                                            in_=win[:, :, DFF + j0 * 128:DFF + (j0 + nj) * 128]),
                         writes=[slb], dma=True)
                    for jj in range(nj):
                        j = j0 + jj
                        for ti, (t0, tl, c) in enumerate(TT):
                            pa, pab = next_ps()
                            pbt, pbb = next_ps()
                            for k in range(8):
                                S.op("pe", lambda e, pa=pa, sl=sl, k=k, jj=jj, nj=nj, t0=t0, tl=tl: e.matmul(pa[:, 0:tl], lhsT=sl[:, k * nj * 128 + jj * 128:k * nj * 128 + (jj + 1) * 128],
                                                                                                           rhs=u[:, k, t0:t0 + tl], start=(k == 0), stop=(k == 7)),
                                     reads=[slb, u_b[ti]], writes=[pab])
                            for k in range(8):
                                S.op("pe", lambda e, pbt=pbt, sl=sl, k=k, jj=jj, nj=nj, t0=t0, tl=tl: e.matmul(pbt[:, 0:tl], lhsT=sl[:, 4096 + k * nj * 128 + jj * 128:4096 + k * nj * 128 + (jj + 1) * 128],
                                                                                                             rhs=u[:, k, t0:t0 + tl], start=(k == 0), stop=(k == 7)),
                                     reads=[slb, u_b[ti]], writes=[pbb])
                            tm, tmb = next_tmp()
                            S.op("act", lambda e, pa=pa, tm=tm, tl=tl: e.activation(out=tm[:, 0:tl], in_=pa[:, 0:tl], func=AF.Silu), reads=[pab], writes=[tmb])
                            S.op("dve", lambda e, pbt=pbt, tm=tm, j=j, h0=h0, t0=t0, tl=tl: e.tensor_tensor(out=hid[:, j - h0, t0:t0 + tl], in0=tm[:, 0:tl], in1=pbt[:, 0:tl], op=ALU.mult),
                                 reads=[tmb, pbb], writes=[hid_b[j - h0][ti]])
                    if l == 0 and i == 0:
                        mod_hook(1)
                if l == 0 and i == 0:
                    mod_hook(1)
                slots = []
                jj0 = 0
                while jj0 < nh:
                    n = min(8, nh - jj0)
                    sl, slb = next_slot()
                    S.op("pool", lambda e, sl=sl, jj0=jj0, n=n, h0=h0: e.dma_start(out=sl[:, 0:n * 1024].rearrange("p (j n) -> p j n", j=n), in_=wout[:, h0 + jj0:h0 + jj0 + n, :]),
                         writes=[slb], dma=True)
                    slots.append((sl, slb, jj0, n))
                    jj0 += n
                for f in range(8):
                    for ti, (t0, tl, c) in enumerate(TT):
                        pt, pb = next_ps()
                        for (sl, slb, jj0, n) in slots:
                            for q in range(n):
                                jj = jj0 + q
                                S.op("pe", lambda e, pt=pt, sl=sl, q=q, f=f, jj=jj, t0=t0, tl=tl, nh=nh: e.matmul(pt[:, 0:tl], lhsT=sl[:, q * 1024 + f * 128:q * 1024 + (f + 1) * 128],
                                                                                                                rhs=hid[:, jj, t0:t0 + tl], start=(jj == 0), stop=(jj == nh - 1)),
                                     reads=[slb, hid_b[jj][ti]], writes=[pb])
                        S.op("dve", lambda e, pt=pt, f=f, t0=t0, tl=tl, c=c: e.scalar_tensor_tensor(out=xres[:, f, t0:t0 + tl], in0=pt[:, 0:tl], scalar=modG[:, l, gi, f, c:c + 1],
                                                                                                   in1=xres[:, f, t0:t0 + tl], op0=ALU.mult, op1=ALU.add),
                             reads=[pb, modA_b, xres_b[ti]], writes=[xres_b[ti]])

        S.phase = "consts"
        S.op("sp", lambda e: e.dma_start(out=ropec[:], in_=cos_d.rearrange("(b p) f -> p b f", p=128)), writes=[rope_b], dma=True)
        S.op("sp", lambda e: e.dma_start(out=ropes[:], in_=sin_d.rearrange("(b p) f -> p b f", p=128)), writes=[rope_b], dma=True)
        for l in range(2):
            for hh in range(6):
                S.op("sp", lambda e, l=l, hh=hh: e.dma_start(out=gq[:, l, hh, :], in_=qkg_d[l, (0 if hh < 4 else 1):(1 if hh < 4 else 2), :].broadcast_to([128, 64])),
                     writes=[gq_b], dma=True)
        S.op("sp", lambda e: e.dma_start(out=ctxbias[:], in_=ctxbias_d), writes=[misc_b], dma=True)
        S.op("sp", lambda e: e.dma_start(out=esink[:], in_=sink_d.rearrange("l h -> (l h)").rearrange("(o n) -> o n", o=1).broadcast_to([64, 8])), writes=[misc_b], dma=True)
        S.op("act", lambda e: e.activation(out=esink[:], in_=esink[:], func=AF.Exp), reads=[misc_b], writes=[misc_b])
        S.op("sp", lambda e: e.dma_start(out=esink128[:], in_=sink_d.rearrange("l h -> (l h)").rearrange("(o n) -> o n", o=1).broadcast_to([128, 8])), writes=[misc_b], dma=True)
        S.op("act", lambda e: e.activation(out=esink128[:], in_=esink128[:], func=AF.Exp), reads=[misc_b], writes=[misc_b])
        S.op("dve", lambda e: e.memset(ones_f[:], 1.0), writes=[misc_b])
        S.op("dve", lambda e: e.memset(vin64[:], 0.0), writes=[vec64_b])
        S.op("sp", lambda e: e.dma_start(out=vin64[0:32, :], in_=mixg_d.rearrange("l (c p) -> (l c) p", p=64)), writes=[vec64_b], dma=True)
        S.op("sp", lambda e: e.dma_start(out=vin64[32:36, :], in_=hfr_d.rearrange("l i d -> (l i) d")), writes=[vec64_b], dma=True)
        S.op("sp", lambda e: e.dma_start(out=vin64[36:38, :], in_=hb1_d), writes=[vec64_b], dma=True)
        S.op("sp", lambda e: e.dma_start(out=vin64[38:40, :], in_=hb2_d), writes=[vec64_b], dma=True)
        pt, pb = next_ps()
        S.op("pe", lambda e, pt=pt: e.transpose(pt[0:64, 0:128], vin64[:], ident[:]), reads=[vec64_b, ident_b], writes=[pb])
        S.op("dve", lambda e, pt=pt: e.tensor_copy(out=vec64[:], in_=pt[0:64, 0:128]), reads=[pb], writes=[vec64_b])

        def mixgain64(l, piece):
            return vec64[:, l * 16 + piece:l * 16 + piece + 1]

        _ei = [0]

        def next_e():
            i = _ei[0] % 4
            _ei[0] += 1
            return etile[i], etile_b[i]

        PS_O, PS_D = 6, 7

        def wout_partial(l, pieces, ksz, wslot, wslot_b, ysrc):
            for f in range(8):
                for ti, (t0, tl, c) in enumerate(TT):
                    pt, pb = next_ps(6)
                    for pi in range(pieces):
                        yap, ybufs = ysrc(pi, t0, tl)
                        S.op("pe", lambda e, pt=pt, pi=pi, f=f, tl=tl, yap=yap: e.matmul(pt[:, 0:tl], lhsT=wslot[0:ksz, pi * 1024 + f * 128:pi * 1024 + (f + 1) * 128], rhs=yap,
                                                                                         start=(pi == 0), stop=(pi == pieces - 1)),
                             reads=[wslot_b] + ybufs, writes=[pb])
                    S.op("dve", lambda e, pt=pt, f=f, t0=t0, tl=tl, c=c: e.scalar_tensor_tensor(out=xres[:, f, t0:t0 + tl], in0=pt[:, 0:tl], scalar=modG[:, l, 1, f, c:c + 1],
                                                                                               in1=xres[:, f, t0:t0 + tl], op0=ALU.mult, op1=ALU.add),
                         reads=[pb, modA_b, xres_b[ti]], writes=[xres_b[ti]])

        def attention_group(l, grp):
            c0 = 0 if grp == 0 else 1280
            wv = win_d[l].rearrange("(k p) n -> p k n", p=128)
            fence()
            if grp == 0 or not STAGES.get("A", True):
                S.op("sp", lambda e: e.dma_start(out=amask[:], in_=amask_d.rearrange("a p n -> p a n")), writes=[amask_b], dma=True)
            sl, slb = next_slot()
            S.op("pool", lambda e, sl=sl: e.dma_start(out=sl[:, 0:4096].rearrange("p (k n) -> p k n", k=8), in_=wv[:, :, c0:c0 + 512]), writes=[slb], dma=True)
            for g_ in range(2):
                S.op("pool", lambda e, g_=g_: e.dma_start(out=ctxv[:, :, g_, 0:64], in_=ctx_d[2 * grp + 1, l].rearrange("(b p) f -> p b f", p=128)[:, :, g_ * 64:(g_ + 1) * 64]),
                     reads=[u_b[0]], writes=[ctxv_b], dma=True)
            S.op("dve", lambda e: e.memset(ctxv[:, :, :, 64:65], 1.0), writes=[ctxv_b])
            S.op("dve", lambda e: e.memset(vtok[:, :, :, 64:65], 1.0), writes=[vtok_b])
            sg, sgb = stg[0], stg_b[0]
            S.op("sp", lambda e, sg=sg: e.dma_start(out=sg[:, 0:512].rearrange("p (b f) -> p b f", b=4), in_=ctx_d[2 * grp, l].rearrange("(b p) f -> p b f", p=128)),
                 reads=[u_b[0]], writes=[sgb], dma=True)
            for g in range(2):
                pt, pb = next_ps(6)
                for b in range(4):
                    S.op("pe", lambda e, pt=pt, sg=sg, b=b, g=g: e.transpose(pt[0:64, b * 128:(b + 1) * 128], sg[:, b * 128 + g * 64:b * 128 + (g + 1) * 64], ident[:]),
                         reads=[sgb, ident_b], writes=[pb])
                S.op("act", lambda e, pt=pt, g=g: e.copy(out=ctxkT[:, g, :], in_=pt[0:64, :]), reads=[pb], writes=[ctxkT_b])
            def prep_block(tb):
                tti = 0 if tb < 2 else (1 if tb < 6 else 2)
                sq_, sqb_ = ssq2[tb % 2], ssq2_b[tb % 2]
                pp, ppb = next_ps(6)
                for k in range(8):
                    S.op("pe", lambda e, pp=pp, sl=sl, k=k, tb=tb: e.matmul(pp[:, :], lhsT=u[:, k, tb * 128:(tb + 1) * 128], rhs=sl[:, k * 512:(k + 1) * 512], start=(k == 0), stop=(k == 7)),
                         reads=[slb, u_b[tti]], writes=[ppb])
                yield
                kv, kvb = kvst[tb % 2], kvst_b[tb % 2]
                qn, qnb = qkn[tb % 2], qkn_b[tb % 2]
                qr, qrb = qkr[tb % 2], qkr_b[tb % 2]
                S.op("act", lambda e, pp=pp, tb=tb: e.copy(out=vtok[:, tb, :, 0:64], in_=pp[:, 384:512].rearrange("p (g f) -> p g f", g=2)), reads=[ppb], writes=[vtok_b])
                S.op("act", lambda e, pp=pp, kv=kv: e.copy(out=kv[:, 1, :], in_=pp[:, 384:512]), reads=[ppb], writes=[kvb])
                if grp == 0:
                    S.op("act", lambda e, pp=pp, qn=qn: e.copy(out=qn[:, :], in_=pp[:, 0:384]), reads=[ppb], writes=[qnb])
                else:
                    S.op("act", lambda e, pp=pp, qr=qr: e.activation(out=qr[:, :], in_=pp[:, 0:384], func=AF.Square), reads=[ppb], writes=[qrb])
                    yield
                    S.op("dve", lambda e, qr=qr: e.reduce_sum(out=sq_[:, 0:6], in_=qr[:, :].rearrange("p (h d) -> p h d", h=6), axis=AX.X), reads=[qrb], writes=[sqb_])
                    yield
                    S.op("act", lambda e: e.activation(out=sq_[:, 0:6], in_=sq_[:, 0:6], func=AF.Sqrt, scale=1.0 / 64, bias=EPS), reads=[sqb_], writes=[sqb_])
                    yield
                    S.op("dve", lambda e: e.reciprocal(out=sq_[:, 0:6], in_=sq_[:, 0:6]), reads=[sqb_], writes=[sqb_])
                    S.op("dve", lambda e, pp=pp, qn=qn: e.tensor_tensor(out=qn[:, :].rearrange("p (h d) -> p h d", h=6), in0=pp[:, 0:384].rearrange("p (h d) -> p h d", h=6),
                                                                        in1=sq_[:, 0:6].unsqueeze(2).broadcast_to([128, 6, 64]), op=ALU.mult),
                         reads=[ppb, sqb_], writes=[qnb])
                    S.op("dve", lambda e, qn=qn: e.tensor_tensor(out=qn[:, :], in0=qn[:, :], in1=gq[:, l].rearrange("p h d -> p (h d)"), op=ALU.mult), reads=[qnb, gq_b], writes=[qnb])
                yield
                S.op("dve", lambda e, qn=qn, kv=kv: e.tensor_copy(out=kv[:, 0, :], in_=qn[:, 256:384]), reads=[qnb], writes=[kvb])
                S.op("sp", lambda e, kv=kv, tb=tb: e.dma_start(out=kv_d[2 * grp:2 * grp + 2, l, tb * 128:(tb + 1) * 128, :].rearrange("a t f -> t a f"), in_=kv[:]),
                     reads=[kvb], dma=True, is_out=True)
                x1 = qn[:, :].rearrange("p (h two d) -> p h two d", h=6, two=2)[:, :, 0, :]
                x2 = qn[:, :].rearrange("p (h two d) -> p h two d", h=6, two=2)[:, :, 1, :]
                o1 = qr[:, :].rearrange("p (h two d) -> p h two d", h=6, two=2)[:, :, 0, :]
                o2 = qr[:, :].rearrange("p (h two d) -> p h two d", h=6, two=2)[:, :, 1, :]
                cc = ropec[:, tb, :].unsqueeze(1).broadcast_to([128, 6, 32])
                ss = ropes[:, tb, :].unsqueeze(1).broadcast_to([128, 6, 32])
                rt = rtmp2[tb % 2][:, :].rearrange("p (h d) -> p h d", h=6)
                rtb = rtmp2_b[tb % 2]
                sq_, sqb_ = ssq2[tb % 2], ssq2_b[tb % 2]
                yield
                S.op("dve", lambda e, o1=o1, x1=x1, cc=cc: e.tensor_tensor(out=o1, in0=x1, in1=cc, op=ALU.mult), reads=[qnb, rope_b], writes=[qrb])
                S.op("dve", lambda e, rt=rt, x2=x2, ss=ss: e.tensor_tensor(out=rt, in0=x2, in1=ss, op=ALU.mult), reads=[qnb, rope_b], writes=[rtb])
                yield
                S.op("dve", lambda e, o1=o1, rt=rt: e.tensor_tensor(out=o1, in0=o1, in1=rt, op=ALU.subtract), reads=[qrb, rtb], writes=[qrb])
                yield
                S.op("dve", lambda e, o2=o2, x2=x2, cc=cc: e.tensor_tensor(out=o2, in0=x2, in1=cc, op=ALU.mult), reads=[qnb, rope_b], writes=[qrb])
                S.op("dve", lambda e, rt=rt, x1=x1, ss=ss: e.tensor_tensor(out=rt, in0=x1, in1=ss, op=ALU.mult), reads=[qnb, rope_b], writes=[rtb])
                yield
                S.op("dve", lambda e, o2=o2, rt=rt: e.tensor_tensor(out=o2, in0=o2, in1=rt, op=ALU.add), reads=[qrb, rtb], writes=[qrb])
                yield
                pq, pqb = next_ps(6)
                for h in range(4):
                    S.op("pe", lambda e, pq=pq, qr=qr, h=h: e.transpose(pq[0:64, h * 128:(h + 1) * 128], qr[:, h * 64:(h + 1) * 64], ident[:]), reads=[qrb, ident_b], writes=[pqb])
                yield
                S.op("act", lambda e, pq=pq, tb=tb: e.copy(out=qT[:, :, tb * 128:(tb + 1) * 128], in_=pq[0:64, :].rearrange("p (h t) -> p h t", h=4)), reads=[pqb], writes=[qT_b])
                yield
                pk, pkb = next_ps(6)
                for g in range(2):
                    S.op("pe", lambda e, pk=pk, qr=qr, g=g: e.transpose(pk[0:64, g * 128:(g + 1) * 128], qr[:, 256 + g * 64:256 + (g + 1) * 64], ident[:]), reads=[qrb, ident_b], writes=[pkb])
                yield
                S.op("act", lambda e, pk=pk, tb=tb: e.copy(out=kT[:, :, tb * 128:(tb + 1) * 128], in_=pk[0:64, 0:256].rearrange("p (h t) -> p h t", h=2)), reads=[pkb], writes=[kT_b])

            for tb0 in range(0, NTB, 2):
                gens = [prep_block(tb0), prep_block(tb0 + 1)]
                while gens:
                    for g_ in list(gens):
                        try:
                            next(g_)
                        except StopIteration:
                            gens.remove(g_)
            _segc = [0]
            pending = [None]
            for h in range(4):
                g = h // 2
                segs = [(0, 256, [("loc", 0, None), ("loc", 128, None)]),
                        (256, 512, None), (768, 512, None)]
                for (q0, nq, keys) in segs:
                    if keys is None:
                        keys = [("ctx", b, None) for b in range(4)]
                        for j in range(8):
                            off = 896 - 128 * j + (q0 - 256)
                            keys.append(("loc", 256 + 128 * j, amask[:, 2 * grp + (j % 2), off:off + nq]))
                    po, pob = ps[6 + _segc[0] % 2], ps_b[6 + _segc[0] % 2]
                    _segc[0] += 1
                    nk = len(keys)
                    def stage1(ki, keys=keys, nq=nq, h=h, g=g, q0=q0):
                        kind, kpos, mk = keys[ki]
                        pss, pssb = next_ps(6)
                        et, etb = next_e()
                        if kind == "ctx":
                            S.op("pe", lambda e, pss=pss, kpos=kpos: e.matmul(pss[:, 0:nq], lhsT=ctxkT[:, g, kpos * 128:(kpos + 1) * 128], rhs=qT[:, h, q0:q0 + nq], start=True, stop=True),
                                 reads=[ctxkT_b, qT_b], writes=[pssb])
                            S.op("act", lambda e, pss=pss, et=et: e.activation(out=et[:, 0:nq], in_=pss[:, 0:nq], func=AF.Exp, scale=0.125, bias=ctxbias[:, 0:1]),
                                 reads=[pssb, misc_b], writes=[etb])
                            vap = ctxv[:, kpos, g, :]
                            vb = ctxv_b
                        else:
                            S.op("pe", lambda e, pss=pss, kpos=kpos: e.matmul(pss[:, 0:nq], lhsT=kT[:, g, kpos:kpos + 128], rhs=qT[:, h, q0:q0 + nq], start=True, stop=True),
                                 reads=[kT_b, qT_b], writes=[pssb])
                            S.op("act", lambda e, pss=pss, et=et: e.activation(out=et[:, 0:nq], in_=pss[:, 0:nq], func=AF.Exp, scale=0.125), reads=[pssb], writes=[etb])
                            if mk is not None:
                                S.op("dve", lambda e, et=et, mk=mk: e.tensor_tensor(out=et[:, 0:nq], in0=et[:, 0:nq], in1=mk, op=ALU.mult), reads=[etb, amask_b], writes=[etb])
                            vap = vtok[:, kpos // 128, g, :]
                            vb = vtok_b
                        return et, etb, vap, vb

                    def stage2(ki, st1, nq=nq, nk=nk, po=po, pob=pob):
                        et, etb, vap, vb = st1
                        S.op("pe", lambda e, vap=vap, et=et: e.matmul(po[0:65, 0:nq], lhsT=vap, rhs=et[:, 0:nq], start=(ki == 0), stop=(ki == nk - 1)),
                             reads=[vb, etb], writes=[pob])

                    pend = [stage1(0)]
                    if nk > 1:
                        pend.append(stage1(1))
                    if pending[0] is not None:
                        pending[0]()
                        pending[0] = None
                    for ki in range(nk):
                        cur_ = pend.pop(0)
                        if ki + 2 < nk:
                            pend.append(stage1(ki + 2))
                        stage2(ki, cur_)
                    def epilogue(h=h, q0=q0, nq=nq, po=po, pob=pob):
                        if grp == 0:
                            S.op("dve", lambda e: e.tensor_scalar(out=rrow[64:65, 0:nq], in0=po[64:65, 0:nq], scalar1=esink128[64:65, l * 4 + h:l * 4 + h + 1], scalar2=None, op0=ALU.add),
                                 reads=[pob, misc_b], writes=[rrow_b])
                            S.op("dve", lambda e: e.reciprocal(out=rrow[64:65, 0:nq], in_=rrow[64:65, 0:nq]), reads=[rrow_b], writes=[rrow_b])
                        else:
                            S.op("dve", lambda e: e.reciprocal(out=rrow[64:65, 0:nq], in_=po[64:65, 0:nq]), reads=[pob], writes=[rrow_b])
                        pbk, pbkb = next_ps(6)
                        S.op("pe", lambda e: e.matmul(pbk[0:64, 0:nq], lhsT=ones_f[64:65, 0:64], rhs=rrow[64:65, 0:nq], start=True, stop=True), reads=[misc_b, rrow_b], writes=[pbkb])
                        S.op("act", lambda e: e.copy(out=rden[:, 0:nq], in_=pbk[0:64, 0:nq]), reads=[pbkb], writes=[rden_b])
                        S.op("dve", lambda e: e.tensor_tensor(out=oT[:, h, q0:q0 + nq], in0=po[0:64, 0:nq], in1=rden[:, 0:nq], op=ALU.mult),
                             reads=[pob, rden_b], writes=[oT_b])

                    pending[0] = epilogue
            if pending[0] is not None:
                pending[0]()
                pending[0] = None
            wsl, wslb = next_slot()
            S.op("pool", lambda e, wsl=wsl: e.dma_start(out=wsl[0:64, 0:4096].rearrange("p (h n) -> p h n", h=4),
                                                        in_=wout_d[l, (0 if grp == 0 else 512):(256 if grp == 0 else 768), :].rearrange("(h p) n -> p h n", p=64)),
                 writes=[wslb], dma=True)
            for ti, (t0, tl, c) in enumerate(TT):
                pt, pb = next_ps(6)
                for h in range(4):
                    et, etb = next_e()
                    S.op("act", lambda e, et=et, h=h, t0=t0, tl=tl: e.activation(out=et[0:64, 0:tl], in_=oT[:, h, t0:t0 + tl], func=AF.Square), reads=[oT_b], writes=[etb])
                    S.op("pe", lambda e, pt=pt, et=et, h=h, tl=tl: e.matmul(pt[0:64, 0:tl], lhsT=ones_bf[0:64, 0:64], rhs=et[0:64, 0:tl], start=(h == 0), stop=(h == 3)),
                         reads=[ones_b, etb], writes=[pb])
                S.op("act", lambda e, pt=pt, tl=tl: e.activation(out=rden[:, 0:tl], in_=pt[0:64, 0:tl], func=AF.Sqrt, scale=1.0 / 256, bias=EPS), reads=[pb], writes=[rden_b])
                S.op("dve", lambda e, tl=tl: e.reciprocal(out=rden[:, 0:tl], in_=rden[:, 0:tl]), reads=[rden_b], writes=[rden_b])
                for h in range(4):
                    piece = (0 if grp == 0 else 8) + h
                    S.op("dve", lambda e, h=h, t0=t0, tl=tl, piece=piece: e.scalar_tensor_tensor(out=oT[:, h, t0:t0 + tl], in0=oT[:, h, t0:t0 + tl], scalar=mixgain64(l, piece),
                                                                                                in1=rden[:, 0:tl], op0=ALU.mult, op1=ALU.mult),
                         reads=[oT_b, rden_b, vec64_b], writes=[oT_b])
            wout_partial(l, 4, 64, wsl, wslb, lambda pi, t0, tl: (oT[:, pi, t0:t0 + tl], [oT_b]))

        dbg_d = dout("dbg", [8, 128, T]) if DEBUG else None

        def dump(idx, ap, bufs, n=T, np_=128):
            if not DEBUG:
                return
            S.op("pool", lambda e: e.dma_start(out=dbg_d[idx, 0:np_, 0:n], in_=ap), reads=list(bufs), dma=True, is_out=True)

        ARENA_BUFS += [gqk_b, gke_b, gqT_b, gkT_b, gktok_b, gvtok_b, gconst_b, gS_b, geb_b, godT_b, ggdT_b, amask_b]

        def fence():
            S.op("dve", lambda e: e.memset(fdummy[:], 0.0), reads=ARENA_BUFS, writes=ARENA_BUFS)

        S.op("dve", lambda e: e.memset(vin3[:], 0.0), writes=[vec3_b])
        S.op("sp", lambda e: e.dma_start(out=vin3[0:36, :], in_=hcw_d.rearrange("l t (c p) -> (l t c) p", p=128)), writes=[vec3_b], dma=True)
        S.op("sp", lambda e: e.dma_start(out=vin3[36:48, :], in_=hcb_d.rearrange("l (c p) -> (l c) p", p=128)), writes=[vec3_b], dma=True)
        S.op("sp", lambda e: e.dma_start(out=vin3[48:56, :], in_=hbias_d.rearrange("l o (c p) -> (l o c) p", p=128)), writes=[vec3_b], dma=True)
        S.op("sp", lambda e: e.dma_start(out=vin3[56:72, :], in_=mixg_d.rearrange("l (c p) -> (l c) p", p=128)), writes=[vec3_b], dma=True)
        pt, pb = next_ps()
        S.op("pe", lambda e, pt=pt: e.transpose(pt[:, 0:128], vin3[:], ident[:]), reads=[vec3_b, ident_b], writes=[pb])
        S.op("dve", lambda e, pt=pt: e.tensor_copy(out=vec3[:], in_=pt[:, 0:128]), reads=[pb], writes=[vec3_b])

        def hcw(l, tap, fc):
            o = (l * 3 + tap) * 6 + fc
            return vec3[:, o:o + 1]

        def hcb(l, fc):
            o = 36 + l * 6 + fc
            return vec3[:, o:o + 1]

        def hbias(l, o_, cc):
            o = 48 + (l * 2 + o_) * 2 + cc
            return vec3[:, o:o + 1]

        def mixgain128(l, chunk):
            o = 56 + l * 8 + chunk
            return vec3[:, o:o + 1]

        S.op("sp", lambda e: e.dma_start(out=hyflag[:], in_=hflag_d), writes=[hysc_b], dma=True)
        for l in range(2):
            for fc in range(6):
                S.op("dve", lambda e, l=l, fc=fc: e.tensor_tensor(out=hysc[:, l, fc, 0:1], in0=hcw(l, 0, fc), in1=hyflag[:, 0:1], op=ALU.mult), reads=[vec3_b, hysc_b], writes=[hysc_b])
                S.op("dve", lambda e, l=l, fc=fc: e.tensor_tensor(out=hysc[:, l, fc, 1:2], in0=hcw(l, 2, fc), in1=hyflag[:, 0:1], op=ALU.mult), reads=[vec3_b, hysc_b], writes=[hysc_b])
                S.op("dve", lambda e, l=l, fc=fc: e.tensor_tensor(out=hysc[:, l, fc, 2:3], in0=hysc[:, l, fc, 0:1], in1=hcw(l, 0, fc), op=ALU.subtract), reads=[vec3_b, hysc_b], writes=[hysc_b])
                S.op("dve", lambda e, l=l, fc=fc: e.tensor_tensor(out=hysc[:, l, fc, 3:4], in0=hysc[:, l, fc, 1:2], in1=hcw(l, 2, fc), op=ALU.subtract), reads=[vec3_b, hysc_b], writes=[hysc_b])
            for i in range(2):
                S.op("dve", lambda e, l=l, i=i: e.tensor_tensor(out=hyfb[:, l * 2 + i:l * 2 + i + 1], in0=vec64[:, 32 + l * 2 + i:33 + l * 2 + i],
                                                                  in1=vec64[:, 36 + i * 2 + l:37 + i * 2 + l], op=ALU.mult), reads=[vec64_b], writes=[hysc_b])

        PI = math.pi

        def hyena_group(l):
            fence()
            wv = win_d[l].rearrange("(k p) n -> p k n", p=128)
            wsl, wslb = ring[0], ring_b[0]
            S.op("pool", lambda e: e.dma_start(out=w3b[:], in_=hw3_d[l]), reads=[hyw_b], writes=[hyw_b], dma=True)
            S.op("sp", lambda e: e.dma_start(out=hyw12[0:33, 0:64], in_=hw1_d[l]), writes=[hyw_b], dma=True)
            S.op("sp", lambda e: e.dma_start(out=hyw12[:, 64:128], in_=hw2_d[l]), writes=[hyw_b], dma=True)
            for ti, (t0, tl, c) in enumerate(TT):
                zt, ztb = next_tmp()
                S.op("sp", lambda e, zt=zt, t0=t0, tl=tl: e.dma_start(out=zt[0:33, 0:tl], in_=hz_d[:, t0:t0 + tl]), writes=[ztb], dma=True)
                cur, curb = zt, ztb
                for i in range(2):
                    pt, pb = next_ps()
                    kk = 33 if i == 0 else 64
                    wap = hyw12[0:33, 0:64] if i == 0 else hyw12[:, 64:128]
                    S.op("pe", lambda e, pt=pt, wap=wap, cur=cur, kk=kk, tl=tl: e.matmul(pt[0:64, 0:tl], lhsT=wap, rhs=cur[0:kk, 0:tl], start=True, stop=True), reads=[hyw_b, curb], writes=[pb])
                    a1, a1b = next_tmp()
                    S.op("dve", lambda e, pt=pt, a1=a1, i=i, tl=tl: e.tensor_scalar(out=a1[0:64, 0:tl], in0=pt[0:64, 0:tl], scalar1=vec64[:, 32 + l * 2 + i:33 + l * 2 + i],
                                                                                   scalar2=hyfb[:, l * 2 + i:l * 2 + i + 1], op0=ALU.mult, op1=ALU.add),
                         reads=[pb, vec64_b, hysc_b], writes=[a1b])
                    S.op("dve", lambda e, a1=a1, tl=tl: e.tensor_scalar(out=rden[:, 0:tl], in0=a1[0:64, 0:tl], scalar1=1.0 / (2.0 * PI), scalar2=12582912.0, op0=ALU.mult, op1=ALU.add), reads=[a1b], writes=[rden_b])
                    S.op("dve", lambda e, tl=tl: e.tensor_scalar(out=rden[:, 0:tl], in0=rden[:, 0:tl], scalar1=12582912.0, scalar2=None, op0=ALU.subtract), reads=[rden_b], writes=[rden_b])
                    S.op("dve", lambda e, a1=a1, tl=tl: e.scalar_tensor_tensor(out=a1[0:64, 0:tl], in0=rden[:, 0:tl], scalar=-2.0 * PI, in1=a1[0:64, 0:tl], op0=ALU.mult, op1=ALU.add), reads=[rden_b, a1b], writes=[a1b])
                    if i == 0:
                        S.op("act", lambda e, a1=a1, tl=tl: e.activation(out=a1[0:64, 0:tl], in_=a1[0:64, 0:tl], func=AF.Sin), reads=[a1b], writes=[a1b])
                        cur, curb = a1, a1b
                    else:
                        S.op("act", lambda e, a1=a1, t0=t0, tl=tl: e.activation(out=hyh2[:, t0:t0 + tl], in_=a1[0:64, 0:tl], func=AF.Sin), reads=[a1b], writes=[hyh2_b])
            for cc in range(2):
                for part in range(3):
                    S.op("pool", lambda e, part=part, cc=cc: e.dma_start(out=wsl[:, part * 1024:(part + 1) * 1024].rearrange("p (k n) -> p k n", k=8),
                                                                        in_=wv[:, :, 512 + (part * 2 + cc) * 128:512 + (part * 2 + cc + 1) * 128]), writes=[wslb], dma=True)
                for part in range(3):
                    fc = part * 2 + cc
                    pts = []
                    for ti, (t0, tl, c) in enumerate(TT):
                        pt, pb = next_ps()
                        for k in range(8):
                            S.op("pe", lambda e, pt=pt, wsl=wsl, k=k, part=part, t0=t0, tl=tl: e.matmul(pt[:, 0:tl], lhsT=wsl[:, part * 1024 + k * 128:part * 1024 + (k + 1) * 128], rhs=u[:, k, t0:t0 + tl],
                                                                                                      start=(k == 0), stop=(k == 7)),
                                 reads=[wslb, u_b[ti]], writes=[pb])
                        pts.append((pt, pb))

                    def zcol(t):
                        ti_ = 0 if t < 256 else (1 if t < 768 else 2)
                        return pts[ti_][0][:, t - TT[ti_][0]:t - TT[ti_][0] + 1], pts[ti_][1]

                    for ti, (t0, tl, c) in enumerate(TT):
                        pt, pb = pts[ti]
                        ac, acb = next_tmp()
                        S.op("act", lambda e, ac=ac, pt=pt, tl=tl, fc=fc: e.activation(out=ac[:, 0:tl], in_=pt[:, 0:tl], func=AF.Identity, scale=hcw(l, 1, fc), bias=hcb(l, fc)),
                             reads=[pb, vec3_b], writes=[acb])
                        S.op("dve", lambda e, ac=ac, pt=pt, tl=tl, fc=fc: e.scalar_tensor_tensor(out=ac[:, 1:tl], in0=pt[:, 0:tl - 1], scalar=hcw(l, 0, fc), in1=ac[:, 1:tl], op0=ALU.mult, op1=ALU.add),
                             reads=[pb, vec3_b, acb], writes=[acb])
                        S.op("dve", lambda e, ac=ac, pt=pt, tl=tl, fc=fc: e.scalar_tensor_tensor(out=ac[:, 0:tl - 1], in0=pt[:, 1:tl], scalar=hcw(l, 2, fc), in1=ac[:, 0:tl - 1], op0=ALU.mult, op1=ALU.add),
                             reads=[pb, vec3_b, acb], writes=[acb])
                        fix = []
                        if t0 == 256:
                            fix = [(512 - t0, 511, 2), (511 - t0, 512, 3), (767 - t0, 768, 1)]
                        elif t0 == 768:
                            fix = [(0, 767, 0), (1024 - t0, 1023, 2), (1023 - t0, 1024, 3)]
                        for (col, src, si) in fix:
                            zap, zb = zcol(src)
                            S.op("dve", lambda e, ac=ac, col=col, zap=zap, si=si, fc=fc: e.scalar_tensor_tensor(out=ac[:, col:col + 1], in0=zap, scalar=hysc[:, l, fc, si:si + 1],
                                                                                                              in1=ac[:, col:col + 1], op0=ALU.mult, op1=ALU.add),
                                 reads=[zb, hysc_b, acb], writes=[acb])
                        if part == 0:
                            S.op("act", lambda e, ac=ac, t0=t0, tl=tl: e.copy(out=hyF[:, t0:t0 + tl], in_=ac[:, 0:tl]), reads=[acb], writes=[hyF_b])
                        elif part == 1:
                            S.op("act", lambda e, ac=ac, t0=t0, tl=tl: e.copy(out=hyx1[:, t0:t0 + tl], in_=ac[:, 0:tl]), reads=[acb], writes=[hyx1_b])
                        else:
                            S.op("act", lambda e, ac=ac, t0=t0, tl=tl: e.copy(out=hyx2[:, t0:t0 + tl], in_=ac[:, 0:tl]), reads=[acb], writes=[hyx2_b])
                if l == 0 and cc == 0:
                    dump(0, hyF[:, :], [hyF_b])
                    dump(1, hyx1[:, :], [hyx1_b])
                    dump(2, hyh2[:, :], [hyh2_b], np_=64)
                for o_ in range(2):
                    for tb0 in range(0, NTB, 4):
                        nb = min(4, NTB - tb0)
                        pt, pb = next_ps()
                        for q in range(nb):
                            tb = tb0 + q
                            S.op("pe", lambda e, pt=pt, q=q, tb=tb: e.transpose(pt[:, q * 128:(q + 1) * 128], hyF[:, tb * 128:(tb + 1) * 128], ident[:]), reads=[hyF_b, ident_b], writes=[pb])
                        S.op("act", lambda e, pt=pt, tb0=tb0, nb=nb: e.copy(out=hyvtok[:, tb0:tb0 + nb, :], in_=pt[:, 0:nb * 128].rearrange("p (b c) -> p b c", b=nb)), reads=[pb], writes=[hyvtok_b])
                    for jb in range(NTB):
                        dc, dcb = next_tmp()
                        S.op("sp", lambda e, dc=dc, jb=jb, cc=cc: e.dma_start(out=dc[:, 0:256].rearrange("p (d c) -> p d c", d=2), in_=hdec_d[jb * 128:(jb + 1) * 128, :, cc * 128:(cc + 1) * 128]),
                             writes=[dcb], dma=True)
                        pt, pb = next_ps()
                        for dr in range(2):
                            cb0 = dr * 512 + o_ * 256 + cc * 128
                            S.op("pe", lambda e, pt=pt, dr=dr, cb0=cb0, jb=jb: e.matmul(pt[:, dr * 128:(dr + 1) * 128], lhsT=hyh2[:, jb * 128:(jb + 1) * 128], rhs=w3b[:, cb0:cb0 + 128], start=True, stop=True),
                                 reads=[hyh2_b, hyw_b], writes=[pb])
                        S.op("dve", lambda e, pt=pt, dc=dc: e.tensor_tensor(out=dc[:, 0:256], in0=pt[:, 0:256], in1=dc[:, 0:256], op=ALU.mult), reads=[pb, dcb], writes=[dcb])
                        S.op("dve", lambda e, dc=dc, jb=jb: e.tensor_tensor(out=hyhp[:, jb, :], in0=dc[:, 0:128], in1=dc[:, 128:256], op=ALU.add), reads=[dcb], writes=[hyh_b])
                        S.op("dve", lambda e, dc=dc, jb=jb: e.tensor_tensor(out=hyhm[:, jb, :], in0=dc[:, 0:128], in1=dc[:, 128:256], op=ALU.subtract), reads=[dcb], writes=[hyh_b])
                    if l == 0 and cc == 0 and o_ == 0:
                        dump(3, arena[:, 6400:7680], [hyh_b])
                        dump(7, arena[:, 5120:6400], [hyvtok_b])
                    psl, pslb = ring[0], ring_b[0]
                    fws = None
                    for pair in list(range(2, 10)) + [0, 1]:
                        if pair == 0:
                            S.op("pool", lambda e, psl=psl: e.dma_start(out=psl[:, 0:1024].rearrange("p (t r) -> p t r", t=2), in_=hfwp_d.rearrange("(t p) r -> p t r", p=128)), writes=[pslb], dma=True)
                            S.op("pool", lambda e, psl=psl: e.dma_start(out=psl[:, 1024:2048].rearrange("p (r t) -> p r t", r=4), in_=hivp_d.rearrange("(r p) t -> p r t", p=128)), writes=[pslb], dma=True)
                        if pair < 2:
                            rc_re, rc_im = pair, 2 + pair
                            ntc, tb_base = 2, 0
                            lre = lambda tc, rc: psl[:, tc * 512 + rc * 128:tc * 512 + (rc + 1) * 128]
                            fb_ = pslb
                            yi = (rc_re, rc_im)
                        else:
                            pp_ = pair - 2
                            sgrp, a_ = pp_ // 2, pp_ % 2
                            rc_re, rc_im = 4 * sgrp + a_, 4 * sgrp + 2 + a_
                            if pp_ % 4 == 0:
                                half = pp_ // 4
                                fws, fwsb = ring[1 + half], ring_b[1 + half]
                                S.op("pool", lambda e, fws=fws, half=half: e.dma_start(out=fws[:, :].rearrange("p (t r) -> p t r", t=8),
                                                                                      in_=hfwg_d.rearrange("(t p) r -> p t r", p=128)[:, :, half * 1024:(half + 1) * 1024]), writes=[fwsb], dma=True)
                            ntc, tb_base = 8, 2
                            lre = lambda tc, rc, fws=fws: fws[:, tc * 1024 + (rc % 8) * 128:tc * 1024 + (rc % 8 + 1) * 128]
                            fb_ = fwsb
                            yi = (4 + rc_re, 4 + rc_im)
                        pu, pub = next_ps()
                        pk, pkb = next_ps()
                        for qi, rc in enumerate((rc_re, rc_im)):
                            for tc in range(ntc):
                                S.op("pe", lambda e, pu=pu, qi=qi, tc=tc, rc=rc, lre=lre, tb_base=tb_base, ntc=ntc: e.matmul(pu[:, qi * 128:(qi + 1) * 128], lhsT=lre(tc, rc), rhs=hyvtok[:, tb_base + tc, :],
                                                                                                                          start=(tc == 0), stop=(tc == ntc - 1)),
                                     reads=[fb_, hyvtok_b], writes=[pub])
                        for qi, rc in enumerate((rc_re, rc_im)):
                            hsrc = hyhp if qi == 0 else hyhm
                            for tc in range(ntc):
                                S.op("pe", lambda e, pk=pk, qi=qi, tc=tc, rc=rc, lre=lre, tb_base=tb_base, ntc=ntc, hsrc=hsrc: e.matmul(pk[:, qi * 128:(qi + 1) * 128], lhsT=lre(tc, rc), rhs=hsrc[:, tb_base + tc, :],
                                                                                                                                     start=(tc == 0), stop=(tc == ntc - 1)),
                                     reads=[fb_, hyh_b], writes=[pkb])
                        ut, utb = next_tmp()
                        S.op("act", lambda e, ut=ut, pu=pu: e.copy(out=ut[:, 0:256], in_=pu[:, 0:256]), reads=[pub], writes=[utb])
                        S.op("dve", lambda e, ut=ut, pk=pk: e.tensor_tensor(out=ut[:, 256:384], in0=ut[:, 0:128], in1=pk[:, 0:128], op=ALU.mult), reads=[utb, pkb], writes=[utb])
                        S.op("dve", lambda e, ut=ut, pk=pk: e.tensor_tensor(out=ut[:, 384:512], in0=ut[:, 128:256], in1=pk[:, 128:256], op=ALU.mult), reads=[utb, pkb], writes=[utb])
                        S.op("dve", lambda e, ut=ut, yi=yi: e.tensor_tensor(out=hyY[:, yi[0], :], in0=ut[:, 256:384], in1=ut[:, 384:512], op=ALU.subtract), reads=[utb], writes=[hyY_b])
                        S.op("dve", lambda e, ut=ut, pk=pk: e.tensor_tensor(out=ut[:, 256:384], in0=ut[:, 0:128], in1=pk[:, 128:256], op=ALU.mult), reads=[utb, pkb], writes=[utb])
                        S.op("dve", lambda e, ut=ut, pk=pk: e.tensor_tensor(out=ut[:, 384:512], in0=ut[:, 128:256], in1=pk[:, 0:128], op=ALU.mult), reads=[utb, pkb], writes=[utb])
                        S.op("dve", lambda e, ut=ut, yi=yi: e.tensor_tensor(out=hyY[:, yi[1], :], in0=ut[:, 256:384], in1=ut[:, 384:512], op=ALU.add), reads=[utb], writes=[hyY_b])
                    if l == 0 and cc == 0 and o_ == 0:
                        dump(4, arena[:, 8960:10240], [hyY_b])
                    ivs = []
                    for half in range(2):
                        isl, islb = ring[1 + half], ring_b[1 + half]
                        S.op("pool", lambda e, isl=isl, half=half: e.dma_start(out=isl[:, :].rearrange("p (r t) -> p r t", r=8),
                                                                              in_=hivg_d.rearrange("(r p) t -> p r t", p=128)[:, half * 8:(half + 1) * 8, :]), writes=[islb], dma=True)
                        ivs.append((isl, islb))
                    for ti, (t0, tl, c) in enumerate(TT):
                        pt, pb = next_ps()
                        if ti == 0:
                            for rc in range(4):
                                S.op("pe", lambda e, pt=pt, rc=rc: e.matmul(pt[:, 0:256], lhsT=hyY[:, rc, :], rhs=psl[:, 1024 + rc * 256:1024 + (rc + 1) * 256], start=(rc == 0), stop=(rc == 3)),
                                     reads=[hyY_b, pslb], writes=[pb])
                        else:
                            for rc in range(16):
                                isl, islb = ivs[rc // 8]
                                S.op("pe", lambda e, pt=pt, rc=rc, isl=isl, t0=t0: e.matmul(pt[:, 0:512], lhsT=hyY[:, 4 + rc, :], rhs=isl[:, (rc % 8) * 1024 + (t0 - 256):(rc % 8) * 1024 + (t0 - 256) + 512],
                                                                                          start=(rc == 0), stop=(rc == 15)),
                                     reads=[hyY_b, islb], writes=[pb])
                        if o_ == 0:
                            S.op("dve", lambda e, pt=pt, t0=t0, tl=tl, cc=cc: e.scalar_tensor_tensor(out=hyF[:, t0:t0 + tl], in0=hyF[:, t0:t0 + tl], scalar=hbias(l, 0, cc), in1=pt[:, 0:tl], op0=ALU.mult, op1=ALU.add),
                                 reads=[pb, vec3_b, hyF_b], writes=[hyF_b])
                            S.op("dve", lambda e, t0=t0, tl=tl: e.tensor_tensor(out=hyF[:, t0:t0 + tl], in0=hyF[:, t0:t0 + tl], in1=hyx1[:, t0:t0 + tl], op=ALU.mult), reads=[hyF_b, hyx1_b], writes=[hyF_b])
                            if l == 0 and cc == 0 and ti == 2:
                                dump(5, hyF[:, :], [hyF_b])
                        else:
                            tm, tmb = next_tmp()
                            dst = hyob0 if cc == 0 else hyx2
                            dstb = hyob0_b if cc == 0 else hyx2_b
                            S.op("dve", lambda e, pt=pt, tm=tm, t0=t0, tl=tl, cc=cc: e.scalar_tensor_tensor(out=tm[:, 0:tl], in0=hyF[:, t0:t0 + tl], scalar=hbias(l, 1, cc), in1=pt[:, 0:tl], op0=ALU.mult, op1=ALU.add),
                                 reads=[pb, vec3_b, hyF_b], writes=[tmb])
                            S.op("dve", lambda e, tm=tm, dst=dst, t0=t0, tl=tl: e.tensor_tensor(out=dst[:, t0:t0 + tl], in0=tm[:, 0:tl], in1=hyx2[:, t0:t0 + tl], op=ALU.mult), reads=[tmb, hyx2_b], writes=[dstb, hyx2_b])
            if l == 0:
                dump(6, hyob0[:, :], [hyob0_b])
            osrc = [hyob0, hyx2]
            osb = [hyob0_b, hyx2_b]
            wo, wob = ring[0], ring_b[0]
            _ri[0] = 1
            S.op("pool", lambda e, wo=wo: e.dma_start(out=wo[:, 0:2048].rearrange("p (h n) -> p h n", h=2), in_=wout_d[l, 256:512, :].rearrange("(h p) n -> p h n", p=128)), writes=[wob], dma=True)
            for ti, (t0, tl, c) in enumerate(TT):
                pt, pb = next_ps()
                for cc in range(2):
                    et, etb = next_e()
                    S.op("act", lambda e, et=et, cc=cc, t0=t0, tl=tl: e.activation(out=et[:, 0:tl], in_=osrc[cc][:, t0:t0 + tl], func=AF.Square), reads=[osb[cc]], writes=[etb])
                    S.op("pe", lambda e, pt=pt, et=et, cc=cc, tl=tl: e.matmul(pt[:, 0:tl], lhsT=ones_bf[:], rhs=et[:, 0:tl], start=(cc == 0), stop=(cc == 1)), reads=[ones_b, etb], writes=[pb])
                tm, tmb = next_tmp()
                S.op("act", lambda e, pt=pt, tm=tm, tl=tl: e.activation(out=tm[:, 0:tl], in_=pt[:, 0:tl], func=AF.Sqrt, scale=1.0 / 256, bias=EPS), reads=[pb], writes=[tmb])
                S.op("dve", lambda e, tm=tm, tl=tl: e.reciprocal(out=tm[:, 0:tl], in_=tm[:, 0:tl]), reads=[tmb], writes=[tmb])
                for cc in range(2):
                    S.op("dve", lambda e, tm=tm, cc=cc, t0=t0, tl=tl: e.scalar_tensor_tensor(out=osrc[cc][:, t0:t0 + tl], in0=osrc[cc][:, t0:t0 + tl], scalar=mixgain128(l, 2 + cc), in1=tm[:, 0:tl],
                                                                                            op0=ALU.mult, op1=ALU.mult),
                         reads=[osb[cc], tmb, vec3_b], writes=[osb[cc]])
            wout_partial(l, 2, 128, wo, wob, lambda pi, t0, tl: (osrc[pi][:, t0:t0 + tl], [osb[pi]]))

        def gla_group(l):
            fence()
            wv = win_d[l].rearrange("(k p) n -> p k n", p=128)
            wsl, wslb = ring[0], ring_b[0]
            S.op("pool", lambda e: e.dma_start(out=wsl[:, 0:6400].rearrange("p (k n) -> p k n", k=8), in_=wv[:, :, 1792:2592]), writes=[wslb], dma=True)
            S.op("sp", lambda e: e.dma_start(out=gtri, in_=tri_d.rearrange("a p c -> p a c")), writes=[gconst_b], dma=True)
            S.op("pool", lambda e: e.dma_start(out=ggw.rearrange("p (z c) -> p z c", z=2), in_=gw_d[l].rearrange("z r c -> r z c")), writes=[gconst_b], dma=True)
            S.op("pool", lambda e: e.dma_start(out=ggb, in_=gb_d[l].rearrange("z c -> (z c)").rearrange("(o n) -> o n", o=1)), writes=[gconst_b], dma=True)

            def wcol(k, c, n):
                return wsl[:, k * 800 + c:k * 800 + c + n]
            for ti, (t0, tl, c) in enumerate(TT if GLA_PART >= 2 else []):
                for (dst, dstb, cb, m, idx) in ((gqT, gqT_b, 0, 64, 0), (gqT, gqT_b, 64, 64, 1), (gkT, gkT_b, 128, 64, 0), (gkT, gkT_b, 192, 64, 1),
                                                (ggdT, ggdT_b, 512, 16, 0), (ggdT, ggdT_b, 528, 16, 1)):
                    pt, pb = next_ps(6)
                    for k in range(8):
                        S.op("pe", lambda e, pt=pt, k=k, cb=cb, m=m, t0=t0, tl=tl: e.matmul(pt[0:m, 0:tl], lhsT=wcol(k, cb, m), rhs=u[:, k, t0:t0 + tl], start=(k == 0), stop=(k == 7)),
                             reads=[wslb, u_b[ti]], writes=[pb])
                    S.op("act", lambda e, pt=pt, dst=dst, m=m, idx=idx, t0=t0, tl=tl: e.copy(out=dst[0:m, idx, t0:t0 + tl], in_=pt[0:m, 0:tl]), reads=[pb], writes=[dstb])
            for tb in range(NTB if GLA_PART >= 3 else 0):
                tti = 0 if tb < 2 else (1 if tb < 6 else 2)
                pt, pb = next_ps(6)
                for k in range(8):
                    S.op("pe", lambda e, pt=pt, k=k, tb=tb: e.matmul(pt[:, 0:384], lhsT=u[:, k, tb * 128:(tb + 1) * 128], rhs=wcol(k, 128, 384), start=(k == 0), stop=(k == 7)),
                         reads=[wslb, u_b[tti]], writes=[pb])
                if GLA_VAR == 1:
                    continue
                S.op("act", lambda e, pt=pt, tb=tb: e.copy(out=gktok[:, tb, :], in_=pt[:, 0:128]), reads=[pb], writes=[gktok_b])
                if GLA_VAR == 2:
                    continue
                if GLA_VAR == 3:
                    S.op("act", lambda e, pt=pt, tb=tb: e.copy(out=gvtok[:, tb, :], in_=pt[:, 128:384]), reads=[pb], writes=[gvtok_b])
                    continue
                S.op("act", lambda e, pt=pt, tb=tb: e.copy(out=gvtok[:, tb, :], in_=pt[:, 128:384]), reads=[pb], writes=[gvtok_b])
            SCALE = 32.0 ** -0.5
            gqk2 = [gqk, arena[0:64, 12928:13440]]
            gke2 = [gke, arena[:, 13440:13568]]
            geb2 = [geb, arena[0:64, 13568:14592].bitcast(F32).rearrange("p (a t) -> p a t", a=4)]
            gqk2_b = [gqk_b, Buf()]
            gke2_b = [gke_b, Buf()]
            geb2_b = [geb_b, Buf()]
            ARENA_BUFS.extend([gqk2_b[1], gke2_b[1], geb2_b[1]])

            def gla_block(z, tb):
                gebz, gebz_b = geb2[z], geb2_b[z]
                slot = 0 if tb < 2 else 1 + (tb - 2) // 2
                first = (tb % 2 == 0) if z == 0 else (tb % 2 == 1)
                last = not first
                if first:
                    for p in range(2):
                        zi = z * 2 + p
                        if tb in (0, 1):
                            S.op("dve", lambda e, zi=zi: e.memset(gS[:, zi, :], 0.0), writes=[gS_b])
                        elif (z == 0 and tb == 2) or (z == 1 and tb == 9):
                            S.op("sp", lambda e, zi=zi, p=p, z=z: e.dma_start(out=gS[:, zi, :], in_=gs0_d[l, z, 2 * p:2 * p + 2].rearrange("h d v -> (h d) v")), writes=[gS_b], dma=True)
                        else:
                            S.op("dve", lambda e, zi=zi: e.tensor_scalar(out=gS[:, zi, :], in0=gS[:, zi, :], scalar1=hyflag[0:64, 0:1], scalar2=None, op0=ALU.mult), reads=[gS_b, hysc_b], writes=[gS_b])
                        S.op("act", lambda e, zi=zi: e.copy(out=gSb[:, zi, :], in_=gS[:, zi, :]), reads=[gS_b], writes=[gS_b])
                yield
                pl, plb = next_ps(4)
                S.op("pe", lambda e, pl=pl, tb=tb, z=z: e.matmul(pl[:, 0:128], lhsT=ggdT[:, z, tb * 128:(tb + 1) * 128], rhs=ggw[:, z * 128:(z + 1) * 128], start=True, stop=False),
                     reads=[ggdT_b, gconst_b], writes=[plb])
                S.op("pe", lambda e, pl=pl, z=z: e.matmul(pl[:, 0:128], lhsT=ones_bf[0:1, 0:128], rhs=ggb[:, z * 128:(z + 1) * 128], start=False, stop=True),
                     reads=[ones_b, gconst_b], writes=[plb])
                yield
                gp, gpb = tmp[z], tmp_b[z]
                S.op("act", lambda e, pl=pl, gp=gp: e.activation(out=gp[:, 0:128], in_=pl[:, 0:128], func=AF.Exp, scale=-1.0), reads=[plb], writes=[gpb])
                S.op("act", lambda e, gp=gp: e.activation(out=gp[:, 0:128], in_=gp[:, 0:128], func=AF.Ln, bias=1.0), reads=[gpb], writes=[gpb])
                yield
                for p in range(2):
                    pc, pcb = next_ps(4)
                    S.op("pe", lambda e, pc=pc, gp=gp, p=p, z=z: e.matmul(pc[0:64, 0:128], lhsT=gp[:, p * 64:(p + 1) * 64], rhs=gtri[:, z, :], start=True, stop=True),
                         reads=[gpb, gconst_b], writes=[pcb])
                    S.op("act", lambda e, pc=pc, p=p: e.activation(out=gebz[:, 2 * p, :], in_=pc[0:64, 0:128], func=AF.Exp, scale=-1.0 / 16), reads=[pcb], writes=[gebz_b])
                    S.op("act", lambda e, pc=pc, p=p: e.activation(out=gebz[:, 2 * p + 1, :], in_=pc[0:64, 0:128], func=AF.Exp, scale=1.0 / 16), reads=[pcb], writes=[gebz_b])
                yield
                qk, qkb = gqk2[z], gqk2_b[z]
                for p in range(2):
                    S.op("dve", lambda e, qk=qk, p=p, tb=tb: e.scalar_tensor_tensor(out=qk[0:64, p * 128:(p + 1) * 128], in0=gqT[:, p, tb * 128:(tb + 1) * 128], scalar=SCALE, in1=gebz[:, 2 * p, :],
                                                                                   op0=ALU.mult, op1=ALU.mult), reads=[gqT_b, gebz_b], writes=[qkb])
                    S.op("dve", lambda e, qk=qk, p=p, tb=tb: e.tensor_tensor(out=qk[0:64, 256 + p * 128:256 + (p + 1) * 128], in0=gkT[:, p, tb * 128:(tb + 1) * 128], in1=gebz[:, 2 * p + 1, :], op=ALU.mult),
                         reads=[gkT_b, gebz_b], writes=[qkb])
                yield
                pf, pfb = next_ps(4)
                S.op("pe", lambda e, pf=pf, gp=gp, z=z: e.matmul(pf[:, 0:128], lhsT=gtri[:, 2 + z, :], rhs=gp[:, 0:128], start=True, stop=True), reads=[gpb, gconst_b], writes=[pfb])
                S.op("act", lambda e, pf=pf, gp=gp: e.activation(out=gp[:, 128:256], in_=pf[:, 0:128], func=AF.Exp, scale=-1.0 / 16), reads=[pfb], writes=[gpb])
                yield
                ke, keb = gke2[z], gke2_b[z]
                S.op("dve", lambda e, ke=ke, gp=gp, tb=tb: e.tensor_tensor(out=ke[:, 0:128], in0=gktok[:, tb, :], in1=gp[:, 128:256], op=ALU.mult), reads=[gktok_b, gpb], writes=[keb])
                yield
                pcs = [(ps[6 - 2 * z], ps_b[6 - 2 * z]), (ps[7 - 2 * z], ps_b[7 - 2 * z])]
                for h in range(4):
                    p, sidx = h // 2, h % 2
                    zi = z * 2 + p
                    pa, pab = next_ps(4)
                    S.op("pe", lambda e, pa=pa, qk=qk, p=p, sidx=sidx: e.matmul(pa[:, 0:128], lhsT=qk[32 * sidx:32 * sidx + 32, 256 + p * 128:256 + (p + 1) * 128],
                                                                               rhs=qk[32 * sidx:32 * sidx + 32, p * 128:(p + 1) * 128], start=True, stop=True),
                         reads=[qkb], writes=[pab])
                    yield
                    am, amb = next_e()
                    S.op("dve", lambda e, pa=pa, am=am, z=z: e.tensor_tensor(out=am[:, 0:128], in0=pa[:, 0:128], in1=gtri[:, z, :], op=ALU.mult), reads=[pab, gconst_b], writes=[amb])
                    yield
                    po, pob = next_ps(4)
                    S.op("pe", lambda e, po=po, am=am, h=h, tb=tb: e.matmul(po[0:64, 0:128], lhsT=gvtok[:, tb, h * 64:(h + 1) * 64], rhs=am[:, 0:128], start=True, stop=False),
                         reads=[gvtok_b, amb], writes=[pob])
                    S.op("pe", lambda e, po=po, qk=qk, zi=zi, p=p, sidx=sidx: e.matmul(po[0:64, 0:128], lhsT=gSb[32 * sidx:32 * sidx + 32, zi, :], rhs=qk[32 * sidx:32 * sidx + 32, p * 128:(p + 1) * 128],
                                                                                      start=False, stop=True),
                         reads=[gS_b, qkb], writes=[pob])
                    yield
                    if (z == 0 and tb <= 4) or (z == 1 and tb >= 5):
                        S.op("act", lambda e, po=po, h=h, tb=tb: e.copy(out=godT[:, h, tb * 128:(tb + 1) * 128], in_=po[0:64, 0:128]), reads=[pob], writes=[godT_b])
                    else:
                        S.op("dve", lambda e, po=po, h=h, tb=tb: e.tensor_tensor(out=godT[:, h, tb * 128:(tb + 1) * 128], in0=godT[:, h, tb * 128:(tb + 1) * 128], in1=po[0:64, 0:128], op=ALU.add),
                             reads=[pob, godT_b], writes=[godT_b])
                    if sidx == 1:
                        pcx, pcxb = pcs[p]
                        S.op("pe", lambda e, pcx=pcx, ke=ke, p=p, tb=tb: e.matmul(pcx[0:64, 0:128], lhsT=ke[:, p * 64:(p + 1) * 64], rhs=gvtok[:, tb, p * 128:(p + 1) * 128], start=True, stop=True),
                             reads=[keb, gvtok_b], writes=[pcxb])
                yield
                for p in range(2):
                    zi = z * 2 + p
                    pcx, pcxb = pcs[p]
                    dcol = 127 if z == 0 else 0
                    for sx in range(2):
                        S.op("dve", lambda e, pcx=pcx, zi=zi, p=p, dcol=dcol, sx=sx: e.scalar_tensor_tensor(out=gS[32 * sx:32 * sx + 32, zi, :], in0=gS[32 * sx:32 * sx + 32, zi, :],
                                                                                                           scalar=gebz[32 * sx:32 * sx + 32, 2 * p, dcol:dcol + 1], in1=pcx[32 * sx:32 * sx + 32, 64 * sx:64 * sx + 64],
                                                                                                           op0=ALU.mult, op1=ALU.add), reads=[gS_b, gebz_b, pcxb], writes=[gS_b])
                    S.op("act", lambda e, zi=zi: e.copy(out=gSb[:, zi, :], in_=gS[:, zi, :]), reads=[gS_b], writes=[gS_b])
                    if last:
                        S.op("sp", lambda e, zi=zi, p=p, z=z, slot=slot: e.dma_start(out=gout_d[l, z, slot, 2 * p:2 * p + 2].rearrange("h d v -> (h d) v"), in_=gS[:, zi, :]),
                             reads=[gS_b], dma=True, is_out=True)

            for step in range(NTB):
                gens = [gla_block(0, step), gla_block(1, NTB - 1 - step)]
                while gens:
                    for g_ in list(gens):
                        try:
                            next(g_)
                        except StopIteration:
                            gens.remove(g_)
            if l == 0:
                dump(0, arena[0:64, 0:1280], [gqT_b], np_=64)
                dump(1, arena[:, 5120:6400], [gktok_b])
                dump(2, a2[0:64, 0:1280], [godT_b], np_=64)
                dump(3, a2[0:64, 3840:5120], [godT_b], np_=64)
                dump(6, arena[:, 6400:7680], [gvtok_b])
                dump(4, arena[0:64, 11264:12288].bitcast(F32), [geb_b], n=512, np_=64)
                dump(5, arena[0:64, 10496:11008].bitcast(F32), [gS_b], n=256, np_=64)
            wo, wob = ring[1], ring_b[1]
            S.op("pool", lambda e: e.dma_start(out=wo[0:64, 0:4096].rearrange("p (h n) -> p h n", h=4), in_=wout_d[l, 768:1024, :].rearrange("(h p) n -> p h n", p=64)), writes=[wob], dma=True)
            if GLA_PART < 4:
                return
            gsr = arena[0:64, 0:5120].rearrange("p (h t) -> p h t", h=4)
            for ti, (t0, tl, c) in enumerate(TT):
                for h in range(4):
                    pr, prb = next_ps(6)
                    for k in range(8):
                        S.op("pe", lambda e, pr=pr, k=k, h=h, t0=t0, tl=tl: e.matmul(pr[0:64, 0:tl], lhsT=wcol(k, 544 + h * 64, 64), rhs=u[:, k, t0:t0 + tl], start=(k == 0), stop=(k == 7)),
                             reads=[wslb, u_b[ti]], writes=[prb])
                    S.op("act", lambda e, pr=pr, h=h, t0=t0, tl=tl: e.activation(out=gsr[:, h, t0:t0 + tl], in_=pr[0:64, 0:tl], func=AF.Silu), reads=[prb], writes=[gqT_b, gkT_b])
            rbufs = [(rden[:, :], rden_b), (rrow[0:64, :], rrow_b), (tmp[0][0:64, :], tmp_b[0]), (tmp[1][0:64, :], tmp_b[1])]

            def head_norm(h, t0, tl):
                rb, rbb = rbufs[h]
                et, etb = next_e()
                S.op("act", lambda e: e.activation(out=et[0:64, 0:tl], in_=godT[:, h, t0:t0 + tl], func=AF.Square), reads=[godT_b], writes=[etb])
                yield
                pt, pb = next_ps(6)
                S.op("pe", lambda e: e.matmul(pt[0:64, 0:tl], lhsT=ones_bf[0:64, 0:64], rhs=et[0:64, 0:tl], start=True, stop=True), reads=[ones_b, etb], writes=[pb])
                yield
                S.op("act", lambda e: e.activation(out=rb[:, 0:tl], in_=pt[0:64, 0:tl], func=AF.Sqrt, scale=1.0 / 64, bias=EPS), reads=[pb], writes=[rbb])
                yield
                S.op("dve", lambda e: e.reciprocal(out=rb[:, 0:tl], in_=rb[:, 0:tl]), reads=[rbb], writes=[rbb])
                yield
                S.op("dve", lambda e: e.scalar_tensor_tensor(out=godT[:, h, t0:t0 + tl], in0=godT[:, h, t0:t0 + tl], scalar=mixgain64(l, 12 + h), in1=rb[:, 0:tl], op0=ALU.mult, op1=ALU.mult),
                     reads=[godT_b, rbb, vec64_b], writes=[godT_b])
                yield
                S.op("dve", lambda e: e.tensor_tensor(out=godT[:, h, t0:t0 + tl], in0=godT[:, h, t0:t0 + tl], in1=gsr[:, h, t0:t0 + tl], op=ALU.mult), reads=[godT_b, gqT_b, gkT_b], writes=[godT_b])

            for ti, (t0, tl, c) in enumerate(TT):
                gens = [head_norm(h, t0, tl) for h in range(4)]
                while gens:
                    for g_ in list(gens):
                        try:
                            next(g_)
                        except StopIteration:
                            gens.remove(g_)
            _ri[0] = 2
            wout_partial(l, 4, 64, wo, wob, lambda pi, t0, tl: (godT[:, pi, t0:t0 + tl], [godT_b]))

        for l in range(NLAYERS):
            if STAGES["ffn1"]:
                S.phase = "L%d_ffn1" % l
                norm_mod(l, 0)
                ffn(l, 0)
            if l == 0:
                mod_hook(9)
            if STAGES["mixer"]:
                S.phase = "L%d_norm2" % l
                norm_mod(l, 1)
                if STAGES.get("A", True):
                    S.phase = "L%d_attnA" % l
                    attention_group(l, 0)
                if STAGES.get("C", True):
                    S.phase = "L%d_attnC" % l
                    attention_group(l, 1)
                if STAGES.get("B", True):
                    S.phase = "L%d_hyena" % l
                    hyena_group(l)
                if STAGES.get("D", True):
                    S.phase = "L%d_gla" % l
                    gla_group(l)
            if STAGES["ffn2"]:
                S.phase = "L%d_ffn2" % l
                norm_mod(l, 2)
                ffn(l, 1)
        S.phase = "final"

        for ti, (t0, tl, c) in enumerate(TT):
            S.op("act", lambda e, t0=t0, tl=tl: e.activation(out=u[:, :, t0:t0 + tl], in_=xres[:, :, t0:t0 + tl], func=AF.Square), reads=[xres_b[ti]], writes=[u_b[ti]])
            pt, pb = next_ps()
            for k in range(8):
                S.op("pe", lambda e, pt=pt, k=k, t0=t0, tl=tl: e.matmul(pt[:, 0:tl], lhsT=ones_bf[:], rhs=u[:, k, t0:t0 + tl], start=(k == 0), stop=(k == 7)),
                     reads=[ones_b, u_b[ti]], writes=[pb])
            S.op("act", lambda e, pt=pt, t0=t0, tl=tl: e.activation(out=rstd[:, t0:t0 + tl], in_=pt[:, 0:tl], func=AF.Sqrt, scale=1.0 / D, bias=EPS), reads=[pb], writes=[rstd_b])
            S.op("dve", lambda e, t0=t0, tl=tl: e.reciprocal(out=rstd[:, t0:t0 + tl], in_=rstd[:, t0:t0 + tl]), reads=[rstd_b], writes=[rstd_b])
            for k in range(8):
                S.op("dve", lambda e, k=k, t0=t0, tl=tl: e.scalar_tensor_tensor(out=xres[:, k, t0:t0 + tl], in0=xres[:, k, t0:t0 + tl], scalar=finalgT[:, k:k + 1],
                                                                               in1=rstd[:, t0:t0 + tl], op0=ALU.mult, op1=ALU.mult),
                     reads=[xres_b[ti], rstd_b, vecs_b[1]], writes=[xres_b[ti]])
        for tb in range(NTB):
            sg, sgb = stg[tb % 2], stg_b[tb % 2]
            tti = 0 if tb < 2 else (1 if tb < 6 else 2)
            for half in range(2):
                pt, pb = next_ps()
                for q in range(4):
                    k = half * 4 + q
                    S.op("pe", lambda e, pt=pt, k=k, q=q, tb=tb: e.transpose(pt[:, q * 128:(q + 1) * 128], xres[:, k, tb * 128:(tb + 1) * 128], ident[:]),
                         reads=[xres_b[tti], ident_b], writes=[pb])
                if half == 0:
                    S.op("act", lambda e, pt=pt, sg=sg, half=half: e.copy(out=sg[:, half * 512:(half + 1) * 512], in_=pt[:, :]), reads=[pb], writes=[sgb])
                else:
                    S.op("dve", lambda e, pt=pt, sg=sg, half=half: e.tensor_copy(out=sg[:, half * 512:(half + 1) * 512], in_=pt[:, :]), reads=[pb], writes=[sgb])
            S.op("sp", lambda e, sg=sg, tb=tb: e.dma_start(out=y_d[tb * 128:(tb + 1) * 128, :], in_=sg[:]), reads=[sgb], dma=True, is_out=True)

        S.emit(st)
    return nc


N_CORES = 8


def core_tokens(c):
    if c < 2:
        return [30 + c], c
    base = 5 * (c - 2)
    return [base + i for i in range(5)], None


def rope_tables():
    rows = 1024 // 64
    r = np.repeat(np.arange(rows, dtype=np.float32), 64)
    col = np.tile(np.arange(64, dtype=np.float32), rows)
    nf = 16
    inv = (10000.0 ** (-np.arange(nf, dtype=np.float32) / nf)).astype(np.float32)
    ang = np.concatenate([r[:, None] * inv, col[:, None] * inv], axis=-1).astype(np.float32)
    return np.cos(ang).astype(np.float32), np.sin(ang).astype(np.float32)


def attn_masks(is_sample):
    m = np.zeros((4, 128, 1920), np.float32)
    a = np.arange(128)[:, None]
    x = np.arange(1920)[None, :]
    if is_sample:
        band = (np.abs(x - 896 - a) <= 128).astype(np.float32)
        m[0] = band
        m[1] = band
        m[2] = 1.0
        m[3] = 1.0
    else:
        ev = ((x >= 896) & (x < 1152)).astype(np.float32) * np.ones((128, 1), np.float32)
        od = ((x >= 768) & (x < 1024)).astype(np.float32) * np.ones((128, 1), np.float32)
        m[0] = ev
        m[1] = od
        m[2] = ev
        m[3] = od
    return m.astype(NPBF16)


def hy_tables(L):
    t = np.linspace(0.0, 1.0, L, dtype=np.float32)[:, None]
    w = ((2.0 * math.pi / L) * np.arange(L, dtype=np.float32)[:, None]).astype(np.float32)
    bands = np.linspace(1e-4, 15, 16, dtype=np.float32)[None, :]
    z = np.concatenate([t, np.cos(bands * w), -np.sin(bands * w)], axis=-1).astype(np.float32)
    deltas = np.linspace(math.log(1e-2) / 1.5, math.log(1e-2) / 0.3, 256, dtype=np.float32)
    dec = np.exp(-t * np.abs(deltas)).astype(np.float32)
    dec2 = np.stack([dec, dec], 1)
    dec2[0, 1] = 0.0
    r = np.arange(2 * L)
    f = 256 * (r // 512) + (r % 256)
    is_im = (r % 512) >= 256
    th = np.pi * (f[None, :] + 0.5) * np.arange(L)[:, None].astype(np.float64) / L
    FW = np.where(is_im[None, :], -np.sin(th), np.cos(th))
    IV = FW.T / L
    return z, dec2, FW.astype(np.float32), IV.astype(np.float32)


def blockdiag4(m):
    a, b = m.shape
    o = np.zeros((4 * a, 4 * b), m.dtype)
    for i in range(4):
        o[i * a:(i + 1) * a, i * b:(i + 1) * b] = m
    return o


_NC_CACHE = {}
SHARED_KEYS = ["w_mod", "b_mod", "norm_g", "ffn_w_in", "ffn_w_out", "final_g", "w_in", "w_out", "mix_g", "swa_sink", "qk_norm_g",
               "hy_conv_w", "hy_conv_b", "hy_w1", "hy_b1", "hy_w2", "hy_b2", "hy_w3", "hy_freq", "hy_bias", "gla_gate_w", "gla_gate_b"]


def kernel(**inp):
    f32 = np.float32
    x_prompt = np.asarray(inp["x_prompt"], f32)
    x_sample = np.asarray(inp["x_sample"], f32)
    c = np.asarray(inp["c"], f32)
    c_ctx = np.asarray(inp["c_ctx"], f32)
    if "nc" not in _NC_CACHE:
        _NC_CACHE["nc"] = build_program()
    nc = _NC_CACHE["nc"]
    shared = {k: np.ascontiguousarray(inp[k], f32) for k in SHARED_KEYS}
    shared["ident_in"] = np.eye(128, dtype=f32)
    cos_s, sin_s = rope_tables()
    caches = [np.asarray(inp[k], f32) for k in ("cache_swa_k", "cache_swa_v", "cache_gqa_k", "cache_gqa_v")]
    z_p, dec_p, fw_p, iv_p = hy_tables(256)
    z_s, dec_s, fw_s, iv_s = hy_tables(1024)
    shared["hy_fw_p"] = fw_p.astype(NPBF16)
    r_ = np.arange(128)[:, None]
    c_ = np.arange(128)[None, :]
    shared["tri_in"] = np.stack([r_ <= c_, r_ >= c_, r_ > c_, r_ < c_], 0).astype(f32)
    state_gla = np.asarray(inp["state_gla"], f32)
    shared["hy_iv_p"] = iv_p.astype(NPBF16)
    hy_prompt = dict(hy_zT=np.ascontiguousarray(np.concatenate([z_p] * 5, 0).T), hy_dec=np.ascontiguousarray(np.concatenate([dec_p] * 5, 0)),
                     hy_fw_g=blockdiag4(fw_p).astype(NPBF16), hy_iv_g=blockdiag4(iv_p).astype(NPBF16), hy_flag=np.zeros((128, 1), f32))
    hy_sample = dict(hy_zT=np.ascontiguousarray(np.concatenate([z_p, z_s], 0).T), hy_dec=np.ascontiguousarray(np.concatenate([dec_p, dec_s], 0)),
                     hy_fw_g=fw_s.astype(NPBF16), hy_iv_g=iv_s.astype(NPBF16), hy_flag=np.ones((128, 1), f32))
    in_maps = []
    for core in range(N_CORES):
        pids, sid = core_tokens(core)
        m = dict(shared)
        cos = np.ones((T, 32), f32)
        sin = np.zeros((T, 32), f32)
        if sid is None:
            xs = np.concatenate([x_prompt[p] for p in pids], 0)
            cond = np.stack([c_ctx, c_ctx], 0)
            m["ctx_kv"] = np.zeros((4, 2, 512, 128), f32)
            m["ctx_bias"] = np.full((128, 1), -30000.0, f32)
            m["gla_s0"] = np.zeros((2, 2, 4, 32, 64), f32)
        else:
            xs = np.concatenate([x_prompt[pids[0]], x_sample[sid]], 0)
            cond = np.stack([c_ctx, c[sid]], 0)
            cos[256:] = cos_s
            sin[256:] = sin_s
            m["ctx_kv"] = np.ascontiguousarray(np.stack([cc[sid].reshape(2, 512, 128) for cc in caches], 0), f32)
            m["ctx_bias"] = np.zeros((128, 1), f32)
            m["gla_s0"] = np.ascontiguousarray(state_gla[sid], f32)
        m["attn_mask"] = attn_masks(sid is not None)
        m.update(hy_sample if sid is not None else hy_prompt)
        m["rope_cos"] = cos
        m["rope_sin"] = sin
        m["x_in"] = np.ascontiguousarray(xs, f32)
        m["cond_in"] = np.ascontiguousarray(cond, f32)
        in_maps.append(m)
    if ONE_CORE:
        res = run_bass_kernel_spmd(nc, in_maps[2:3], core_ids=[0])
        LAST["outs"] = res.results
        return None
    res = run_bass_kernel_spmd(nc, in_maps, core_ids=list(range(N_CORES)))
    outs = res.results
    LAST["outs"] = outs
    B, SEQ = x_prompt.shape[0], x_prompt.shape[1]
    y_prompt = np.zeros((B, SEQ, D), f32)
    y_sample = np.zeros(x_sample.shape, f32)
    kvs = [np.zeros((B, 2, SEQ, 2, 64), f32) for _ in range(4)]
    new_state = np.zeros((B, 2, 2, 4, 32, 64), f32)
    for core in range(N_CORES):
        pids, sid = core_tokens(core)
        y = np.asarray(outs[core]["y"], f32)
        kvo = np.asarray(outs[core]["kv_out"], f32)
        gso = np.asarray(outs[core]["gla_out"], f32)
        if sid is not None:
            y_sample[sid] = y[256:]
        for i, p in enumerate(pids):
            y_prompt[p] = y[i * 256:(i + 1) * 256]
            for a in range(4):
                kvs[a][p] = kvo[a, :, i * 256:(i + 1) * 256, :].reshape(2, SEQ, 2, 64)
            new_state[p] = gso[:, :, i]
    return (y_prompt, y_sample, kvs[0], kvs[1], kvs[2], kvs[3], new_state)
```
